# Optimizing a Trainium2 kernel written in Bass

```python
import math
import jax, jax.numpy as jnp
from jax import lax
import numpy as np

D_MODEL = 1024
BATCH = 8
SEQ = 4096
DEPTH = 4

N_A = DEPTH // 2
N_B = DEPTH - N_A
N_VRES = max(N_A - 1, 0)
RWKV_HEAD = 64
RWKV_HEADS = D_MODEL // RWKV_HEAD
DECAY_LORA = 64
AAA_LORA = 64
MV_LORA = 32
GATE_LORA = 160
GN_EPS = 64e-5
DIFF_HEAD = 64
DIFF_HEADS = D_MODEL // (2 * DIFF_HEAD)
QK_WIDTH = DIFF_HEADS * 2 * DIFF_HEAD
V_WIDTH = DIFF_HEADS * 2 * DIFF_HEAD
ROT_DIM = DIFF_HEAD // 4
ROPE_THETA = 500000.0
Q_BLOCK = 128
D_FF = ((8 * D_MODEL // 3 + 127) // 128) * 128
CONV_W = 3
NORM_EPS = 1e-6

kernel_name = "yoco_rwkv7_diffattn_convglu_adaln"


def rms_norm(x, g):
    xf = x.astype(jnp.float32)
    y = xf * lax.rsqrt(jnp.mean(xf * xf, axis=-1, keepdims=True) + NORM_EPS)
    return (y * g.astype(jnp.float32)).astype(x.dtype)


def modulate(h, shift, scale):
    return h * (1.0 + scale[:, None, :]) + shift[:, None, :]


def rope_tables(seq):
    pos = jnp.arange(seq, dtype=jnp.float32)
    inv = ROPE_THETA ** (-jnp.arange(0, ROT_DIM, 2, dtype=jnp.float32) / ROT_DIM)
    ang = pos[:, None] * inv[None, :]
    return jnp.cos(ang), jnp.sin(ang)


def partial_rope(t, cos, sin):
    half = ROT_DIM // 2
    cs = cos[None, :, None, None, :].astype(t.dtype)
    sn = sin[None, :, None, None, :].astype(t.dtype)
    x1 = t[..., :half]
    x2 = t[..., half:ROT_DIM]
    return jnp.concatenate([x1 * cs - x2 * sn, x2 * cs + x1 * sn, t[..., ROT_DIM:]], axis=-1)


def conv_glu_ffn(h, w_up, conv_w, conv_b, w_down):
    s = h.shape[1]
    u = h @ w_up
    up = jnp.pad(u, ((0, 0), (CONV_W - 1, 0), (0, 0)))
    u = conv_b + conv_w[0] * up[:, 0:s]
    for j in range(1, CONV_W):
        u = u + conv_w[j] * up[:, j:j + s]
    gate, val = jnp.split(u, 2, axis=-1)
    return (jax.nn.silu(gate) * val) @ w_down


def wkv7_scan(r, w, k, v, a_in, b_in):
    bsz, _, nh, n = r.shape

    def step(state, inp):
        rt, wt, kt, vt, at, bt = inp
        sa = jnp.einsum('bhvk,bhk->bhv', state, at)
        state = state * wt[:, :, None, :] + sa[..., None] * bt[:, :, None, :] + vt[..., None] * kt[:, :, None, :]
        y = jnp.einsum('bhvk,bhk->bhv', state, rt)
        return state, y

    xs = (jnp.moveaxis(r, 1, 0), jnp.moveaxis(w, 1, 0), jnp.moveaxis(k, 1, 0),
          jnp.moveaxis(v, 1, 0), jnp.moveaxis(a_in, 1, 0), jnp.moveaxis(b_in, 1, 0))
    s0 = jnp.zeros((bsz, nh, n, n), jnp.float32)
    _, ys = lax.scan(step, s0, xs)
    return jnp.moveaxis(ys, 0, 1)


def rwkv7_time_mix(h, v_first, vres, mu, w_rkv, w0, w1, w2, a0, a1, a2, g1, g2, k_k, k_a, r_k, ln_w, ln_b, w_o):
    bsz, s, d = h.shape
    xx = jnp.pad(h, ((0, 0), (1, 0), (0, 0)))[:, :-1] - h
    xr = h + xx * mu[0]
    xw = h + xx * mu[1]
    xk = h + xx * mu[2]
    xv = h + xx * mu[3]
    xa = h + xx * mu[4]
    xg = h + xx * mu[5]
    r = xr @ w_rkv[0]
    k = xk @ w_rkv[1]
    v = xv @ w_rkv[2]
    w = -jax.nn.softplus(-(w0 + jnp.tanh(xw @ w1) @ w2)) - 0.5
    if vres is None:
        v_first = v
    else:
        v0, v1, v2 = vres
        v = v + (v_first - v) * jax.nn.sigmoid(v0 + (xv @ v1) @ v2)
    a = jax.nn.sigmoid(a0 + (xa @ a1) @ a2)
    g = jax.nn.sigmoid(xg @ g1) @ g2

    def heads(t):
        return t.reshape(bsz, s, RWKV_HEADS, RWKV_HEAD).astype(jnp.float32)

    kk = heads(k * k_k)
    kk = kk / jnp.maximum(jnp.sqrt(jnp.sum(kk * kk, axis=-1, keepdims=True)), 1e-12)
    k = k * (1.0 + (a - 1.0) * k_a)
    rh, kh, vh, ah = heads(r), heads(k), heads(v), heads(a)
    decay = jnp.exp(-jnp.exp(heads(w)))
    o = wkv7_scan(rh, decay, kh, vh, -kk, kk * ah)
    mean = jnp.mean(o, axis=-1, keepdims=True)
    var = jnp.mean(jnp.square(o - mean), axis=-1, keepdims=True)
    o = ((o - mean) * lax.rsqrt(var + GN_EPS)).reshape(bsz, s, d) * ln_w + ln_b
    bonus = jnp.sum(rh * kh * r_k.astype(jnp.float32), axis=-1, keepdims=True) * vh
    o = o + bonus.reshape(bsz, s, d)
    return (o * g).astype(h.dtype) @ w_o, v_first


def diff_attention(h, k_sh, v_sh, cos, sin, w_q, lam_vecs, subln, w_o, lam_init):
    bsz, s, d = h.shape
    q = (h @ w_q).reshape(bsz, s, DIFF_HEADS, 2, DIFF_HEAD)
    q = partial_rope(q, cos, sin) * (DIFF_HEAD ** -0.5)
    lv = lam_vecs.astype(jnp.float32)
    lam = jnp.exp(jnp.sum(lv[0] * lv[1])) - jnp.exp(jnp.sum(lv[2] * lv[3])) + lam_init
    nqb = s // Q_BLOCK
    qb = q.reshape(bsz, nqb, Q_BLOCK, DIFF_HEADS, 2, DIFF_HEAD).transpose(1, 0, 2, 3, 4, 5)
    kpos = jnp.arange(s)

    def block(args):
        qblk, bi = args
        sc = jnp.einsum('bqhcd,bkhcd->bhcqk', qblk, k_sh, preferred_element_type=jnp.float32)
        qpos = bi * Q_BLOCK + jnp.arange(Q_BLOCK)
        mask = kpos[None, :] <= qpos[:, None]
        sc = jnp.where(mask, sc, -jnp.inf)
        p = jax.nn.softmax(sc, axis=-1)
        attn = p[:, :, 0] - lam * p[:, :, 1]
        return jnp.einsum('bhqk,bkhe->bqhe', attn.astype(v_sh.dtype), v_sh)

    o = lax.map(block, (qb, jnp.arange(nqb)))
    o = o.transpose(1, 0, 2, 3, 4).reshape(bsz, s, DIFF_HEADS, 2 * DIFF_HEAD)
    o = rms_norm(o, subln) * (1.0 - lam_init)
    return o.reshape(bsz, s, d) @ w_o


def setup_inputs(seed: int = 0) -> dict:
    key = jax.random.key(seed)
    ks = iter(jax.random.split(key, 40))
    D = D_MODEL
    F2 = 2 * D_FF

    def nrm(shape, scale):
        return jax.random.normal(next(ks), shape, jnp.float32) * scale

    x = nrm((BATCH, SEQ, D), 1.0)
    c = nrm((BATCH, D), 1.0)
    ada_w = nrm((DEPTH, D, 6 * D), 0.5 * D ** -0.5)
    ada_b = nrm((DEPTH, 6 * D), 0.02)
    norm1 = 1.0 + nrm((DEPTH, D), 0.02)
    norm2 = 1.0 + nrm((DEPTH, D), 0.02)
    final_norm = 1.0 + nrm((D,), 0.02)
    a_mu = jax.random.uniform(next(ks), (N_A, 6, D), jnp.float32, 0.0, 1.0)
    a_w_rkv = nrm((N_A, 3, D, D), D ** -0.5)
    a_w0 = jax.random.uniform(next(ks), (N_A, D), jnp.float32, -2.5, 0.5)
    a_w1 = nrm((N_A, D, DECAY_LORA), D ** -0.5)
    a_w2 = nrm((N_A, DECAY_LORA, D), 0.3 * DECAY_LORA ** -0.5)
    a_a0 = nrm((N_A, D), 0.5)
    a_a1 = nrm((N_A, D, AAA_LORA), D ** -0.5)
    a_a2 = nrm((N_A, AAA_LORA, D), 0.3 * AAA_LORA ** -0.5)
    a_v0 = nrm((N_VRES, D), 0.5)
    a_v1 = nrm((N_VRES, D, MV_LORA), D ** -0.5)
    a_v2 = nrm((N_VRES, MV_LORA, D), 0.3 * MV_LORA ** -0.5)
    a_g1 = nrm((N_A, D, GATE_LORA), D ** -0.5)
    a_g2 = nrm((N_A, GATE_LORA, D), GATE_LORA ** -0.5)
    a_k_k = 0.85 + nrm((N_A, D), 0.1)
    a_k_a = 1.0 + nrm((N_A, D), 0.1)
    a_r_k = nrm((N_A, RWKV_HEADS, RWKV_HEAD), 0.1)
    a_ln_w = 1.0 + nrm((N_A, D), 0.02)
    a_ln_b = nrm((N_A, D), 0.02)
    a_w_o = nrm((N_A, D, D), D ** -0.5)
    kv_norm = 1.0 + nrm((D,), 0.02)
    w_kv = nrm((D, QK_WIDTH + V_WIDTH), D ** -0.5)
    b_w_q = nrm((N_B, D, QK_WIDTH), D ** -0.5)
    b_lam = nrm((N_B, 4, DIFF_HEAD), 0.1)
    b_subln = 1.0 + nrm((N_B, 2 * DIFF_HEAD), 0.02)
    b_w_o = nrm((N_B, V_WIDTH, D), V_WIDTH ** -0.5)
    ffn_w_up = nrm((DEPTH, D, F2), D ** -0.5)
    ffn_conv_w = nrm((DEPTH, CONV_W, F2), CONV_W ** -0.5)
    ffn_conv_b = nrm((DEPTH, F2), 0.02)
    ffn_w_down = nrm((DEPTH, D_FF, D), D_FF ** -0.5)
    return {"x": x, "c": c, "ada_w": ada_w, "ada_b": ada_b, "norm1": norm1, "norm2": norm2,
            "final_norm": final_norm, "a_mu": a_mu, "a_w_rkv": a_w_rkv, "a_w0": a_w0, "a_w1": a_w1,
            "a_w2": a_w2, "a_a0": a_a0, "a_a1": a_a1, "a_a2": a_a2, "a_v0": a_v0, "a_v1": a_v1,
            "a_v2": a_v2, "a_g1": a_g1, "a_g2": a_g2, "a_k_k": a_k_k, "a_k_a": a_k_a, "a_r_k": a_r_k,
            "a_ln_w": a_ln_w, "a_ln_b": a_ln_b, "a_w_o": a_w_o, "kv_norm": kv_norm, "w_kv": w_kv,
            "b_w_q": b_w_q, "b_lam": b_lam, "b_subln": b_subln, "b_w_o": b_w_o,
            "ffn_w_up": ffn_w_up, "ffn_conv_w": ffn_conv_w, "ffn_conv_b": ffn_conv_b,
            "ffn_w_down": ffn_w_down}


def reference(x, c, ada_w, ada_b, norm1, norm2, final_norm,
              a_mu, a_w_rkv, a_w0, a_w1, a_w2, a_a0, a_a1, a_a2, a_v0, a_v1, a_v2,
              a_g1, a_g2, a_k_k, a_k_a, a_r_k, a_ln_w, a_ln_b, a_w_o,
              kv_norm, w_kv, b_w_q, b_lam, b_subln, b_w_o,
              ffn_w_up, ffn_conv_w, ffn_conv_b, ffn_w_down):
    bsz, s, d = x.shape
    cos, sin = rope_tables(s)
    c_act = jax.nn.silu(c)
    v_first = None
    k_sh = None
    v_sh = None
    for l in range(DEPTH):
        mod = c_act @ ada_w[l] + ada_b[l]
        sh1, sc1, g1, sh2, sc2, g2 = jnp.split(mod, 6, axis=-1)
        h = modulate(rms_norm(x, norm1[l]), sh1, sc1)
        if l < N_A:
            vres = None if l == 0 else (a_v0[l - 1], a_v1[l - 1], a_v2[l - 1])
            y, v_first = rwkv7_time_mix(h, v_first, vres, a_mu[l], a_w_rkv[l], a_w0[l], a_w1[l], a_w2[l],
                                        a_a0[l], a_a1[l], a_a2[l], a_g1[l], a_g2[l], a_k_k[l], a_k_a[l],
                                        a_r_k[l], a_ln_w[l], a_ln_b[l], a_w_o[l])
        else:
            j = l - N_A
            if j == 0:
                kv = rms_norm(x, kv_norm) @ w_kv
                k_sh = partial_rope(kv[..., :QK_WIDTH].reshape(bsz, s, DIFF_HEADS, 2, DIFF_HEAD), cos, sin)
                v_sh = kv[..., QK_WIDTH:].reshape(bsz, s, DIFF_HEADS, 2 * DIFF_HEAD)
            lam_init = 0.8 - 0.6 * math.exp(-0.3 * l)
            y = diff_attention(h, k_sh, v_sh, cos, sin, b_w_q[j], b_lam[j], b_subln[j], b_w_o[j], lam_init)
        x = x + g1[:, None, :] * y
        h2 = modulate(rms_norm(x, norm2[l]), sh2, sc2)
        x = x + g2[:, None, :] * conv_glu_ffn(h2, ffn_w_up[l], ffn_conv_w[l], ffn_conv_b[l], ffn_w_down[l])
    return rms_norm(x, final_norm)
```

```python
import math
import numpy as np
import concourse.bass as bass
import concourse.mybir as mybir
from concourse.bass_utils import run_bass_kernel_spmd
from contextlib import ExitStack
import types

F32 = mybir.dt.float32
BF16 = mybir.dt.bfloat16
AF = mybir.ActivationFunctionType
ALU = mybir.AluOpType
AX = mybir.AxisListType

D = 1024
T = 4096
NJ = 8
DFF = 2816
F2 = 5632
NF = 44
NG = 22
C0 = math.exp(-0.5)
NCORES = 8

ENGS = ["pe", "act", "dve", "pool", "sp"]


def freeze(fn):
    if fn.__closure__ is None:
        return fn
    cells = []
    for c in fn.__closure__:
        try:
            cells.append(types.CellType(c.cell_contents))
        except ValueError:
            cells.append(c)
    return types.FunctionType(fn.__code__, fn.__globals__, fn.__name__, fn.__defaults__, tuple(cells))


class Buf:
    __slots__ = ("name", "lw", "rd", "dsem", "excl")

    def __init__(self, name):
        self.name = name
        self.lw = None
        self.rd = {}
        self.dsem = None
        self.excl = False


class Tl:
    __slots__ = ("ap", "b")

    def __init__(self, ap, b):
        self.ap = ap
        self.b = b

    def __getitem__(self, k):
        return self.ap[k]


class Prog:
    def __init__(self, nc, es, n_dma_sems=40):
        self.nc = nc
        self.es = es
        self.q = {e: [] for e in ENGS}
        self.cnt = {e: 0 for e in ENGS}
        self.sems = {}
        self.semkey = 0
        self.esem = {e: self._newsem("c_" + e) for e in ENGS}
        self.seen = {e: {} for e in ENGS}
        self.bar = self._newsem("bar")
        self.nbar = 0
        self.dma_pool = [self._newsem("d%d" % i) for i in range(n_dma_sems)]
        self.dma_cnt = {k: 0 for k in self.dma_pool}
        self.dma_free = list(self.dma_pool)
        self.ninstr = 0

    def _newsem(self, name):
        s = self.es.enter_context(self.nc.semaphore(name))
        self.semkey += 1
        self.sems[self.semkey] = s
        return self.semkey

    def buf(self, name):
        return Buf(name)

    def dma_buf(self, name):
        b = Buf(name)
        b.dsem = self.dma_free.pop(0)
        return b

    def release(self, bufs):
        for b in bufs:
            if b.dsem is not None:
                self.dma_free.append(b.dsem)
                b.dsem = None

    def _waits(self, e, reads, writes, is_dma=False):
        need = {}
        seen = self.seen[e]

        def add(ev, raw):
            key, val, src = ev
            if src == e and not is_dma and (e in ("pe", "sp") or not raw):
                return
            if seen.get(key, 0) >= val:
                return
            if need.get(key, 0) < val:
                need[key] = val

        for b in reads:
            if b.lw is not None:
                add(b.lw, True)
            if b.excl:
                for src, ev in b.rd.items():
                    if src != e:
                        add(ev, False)
        for b in writes:
            if b.lw is not None:
                add(b.lw, False)
            for ev in b.rd.values():
                add(ev, False)
        out = []
        for key, val in need.items():
            seen[key] = val
            out.append((self.sems[key], val))
        return out

    def _emit(self, e, fn, waits, sem, inc):
        fn = freeze(fn)

        attach = (inc == 1 and e in ("act", "dve", "pool") and len(waits) > 0)

        def run(eng, fn=fn, waits=waits, sem=sem, inc=inc, attach=attach):
            for (s, v) in (waits[:-1] if attach else waits):
                eng.wait_ge(s, v)
            ins = fn(eng)
            if attach:
                ins._wait_ge(waits[-1][0], waits[-1][1])
            ins.then_inc(sem, inc)
        self.q[e].append(run)
        self.ninstr += 1 + len(waits)

    def op(self, e, fn, reads=(), writes=()):
        waits = self._waits(e, reads, writes)
        self.cnt[e] += 1
        key = self.esem[e]
        self._emit(e, fn, waits, self.sems[key], 1)
        ev = (key, self.cnt[e], e)
        for b in writes:
            b.lw = ev
            b.rd = {}
        for b in reads:
            if b not in writes:
                b.rd[e] = ev

    def dma(self, e, fn, owner, reads=(), writes=()):
        assert owner.dsem is not None, owner.name
        waits = self._waits(e, reads, writes, is_dma=True)
        key = owner.dsem
        self.dma_cnt[key] += 16
        self._emit(e, fn, waits, self.sems[key], 16)
        src = "dma%d" % key
        ev = (key, self.dma_cnt[key], src)
        for b in writes:
            b.lw = ev
            b.rd = {}
        for b in reads:
            if b not in writes:
                b.rd[src] = ev

    def barrier(self):
        g = "sp"
        gw = []
        for e in ENGS:
            if e == g or self.cnt[e] == 0:
                continue
            key = self.esem[e]
            if self.seen[g].get(key, 0) < self.cnt[e]:
                gw.append((self.sems[key], self.cnt[e]))
        for key in self.dma_pool:
            val = self.dma_cnt[key]
            if val > 0 and self.seen[g].get(key, 0) < val:
                gw.append((self.sems[key], val))
        self.nbar += 1
        bsem = self.sems[self.bar]

        def run_g(eng, gw=gw, bsem=bsem):
            for (s, v) in gw:
                eng.wait_ge(s, v)
            eng.sem_inc(bsem, 1)
        self.q[g].append(run_g)
        for e in ENGS:
            if e != g:
                self.q[e].append(lambda eng, sem=bsem, val=self.nbar: eng.wait_ge(sem, val))
        for e in ENGS:
            for e2 in ENGS:
                self.seen[e][self.esem[e2]] = self.cnt[e2]
            for key in self.dma_pool:
                self.seen[e][key] = self.dma_cnt[key]
        for e in ENGS:
            if self.cnt[e] > 12000:
                self.esem[e] = self._newsem("c_%s_%d" % (e, self.nbar))
                self.cnt[e] = 0

    def finish(self):
        nc = self.nc
        self.barrier()
        with nc.Block() as block:
            @block.tensor
            def _(eng):
                for f in self.q["pe"]:
                    f(eng)

            @block.scalar
            def _(eng):
                for f in self.q["act"]:
                    f(eng)

            @block.vector
            def _(eng):
                for f in self.q["dve"]:
                    f(eng)

            @block.gpsimd
            def _(eng):
                for f in self.q["pool"]:
                    f(eng)

            @block.sync
            def _(eng):
                for f in self.q["sp"]:
                    f(eng)


class ColPack:
    def __init__(self):
        self.off = {}
        self.n = 0
        self.items = []

    def add(self, name, length):
        assert length % 128 == 0
        self.off[name] = self.n
        self.n += length // 128
        self.items.append((name, length))

    def pack(self, vecs):
        arr = np.zeros((128, self.n), np.float32)
        for name, length in self.items:
            v = np.asarray(vecs[name], np.float32).reshape(length // 128, 128)
            arr[:, self.off[name]:self.off[name] + length // 128] = v.T
        return arr


def make_colpack():
    cp = ColPack()
    for l in range(4):
        cp.add("ada_b%d" % l, 6 * D)
        cp.add("norm1_%d" % l, D)
        cp.add("norm2_%d" % l, D)
        for i in range(3):
            cp.add("cw%d_%d" % (i, l), F2)
        cp.add("cb_%d" % l, F2)
    cp.add("final_norm", D)
    cp.add("kv_norm", D)
    for l in range(2):
        for i in range(6):
            cp.add("mu%d_%d" % (i, l), D)
        for nm in ("w0", "a0", "k_k", "k_a", "r_k"):
            cp.add("%s_%d" % (nm, l), D)
    return cp


CP = make_colpack()


def make_consts():
    c = {}
    c["ident"] = np.eye(128, dtype=np.float32)
    c["ones"] = np.ones((128, 128), np.float32)
    bd = np.zeros((128, 128), np.float32)
    bd[:64, :64] = 1
    bd[64:, 64:] = 1
    c["bd"] = bd
    ind = np.zeros((128, 8, 16), np.float32)
    for p in range(128):
        for j in range(8):
            ind[p, j, 2 * j + p // 64] = 1
    c["ind"] = ind.reshape(128, 128)
    s = np.arange(128)[:, None]
    t = np.arange(128)[None, :]
    strict = (t > s).astype(np.float32)
    incl = (t >= s).astype(np.float32)
    c["mask4"] = np.concatenate([strict, incl, strict, incl], axis=1)
    c["maskL"] = (t < s).astype(np.float32)
    pos = np.arange(T, dtype=np.float64)
    inv = 500000.0 ** (-np.arange(0, 16, 2, dtype=np.float64) / 16)
    ct = np.ones((128, T), np.float64)
    st = np.zeros((128, T), np.float64)
    rm = np.zeros((128, 128), np.float32)
    for cc in range(2):
        for d in range(16):
            p = cc * 64 + d
            ang = pos * inv[d % 8]
            ct[p] = np.cos(ang)
            st[p] = np.sin(ang)
            if d < 8:
                rm[p + 8, p] = -1.0
            else:
                rm[p - 8, p] = 1.0
    c["ropec"] = ct.astype(np.float32)
    c["ropes"] = st.astype(np.float32)
    c["roper"] = rm
    return c


class Builder:
    def __init__(self, cfg):
        self.cfg = cfg
        self.nc = bass.Bass("TRN2", target_bir_lowering=False)
        self.uid = 0
        self.rr = 0

    def dram_in(self, name, shape, dt=F32):
        return self.nc.dram_tensor(name, list(shape), dt, kind="ExternalInput").ap()

    def tile(self, es, name, shape, dt=F32, dma=False):
        self.uid += 1
        t = es.enter_context(self.nc.sbuf_tensor("%s_%d" % (name, self.uid), list(shape), dt))
        b = self.P.dma_buf(name) if dma else self.P.buf(name)
        if dma:
            self.phase_dma_bufs.append(b)
        return Tl(t, b)

    def eng_rr(self, engs=("dve", "pool", "act")):
        self.rr += 1
        return engs[self.rr % len(engs)]

    def copy(self, e, out_ap, in_ap, reads, writes):
        P = self.P
        if e == "act":
            P.op("act", lambda g: g.copy(out=out_ap, in_=in_ap), reads=reads, writes=writes)
        else:
            P.op(e, lambda g: g.tensor_copy(out=out_ap, in_=in_ap), reads=reads, writes=writes)

    def psum_next(self):
        self.psi = (self.psi + 1) % 6
        return self.ps[self.psi]

    def build(self):
        nc = self.nc
        cfg = self.cfg
        inp = {}
        inp["xT"] = self.dram_in("xT", [D, T])
        inp["ccol"] = self.dram_in("ccol", [128, 8])
        inp["cols"] = self.dram_in("cols", [128, CP.n])
        for k in ("ident", "ones", "bd", "ind", "maskL", "roper"):
            inp[k] = self.dram_in(k, [128, 128])
        inp["mask4"] = self.dram_in("mask4", [128, 512])
        inp["ropec"] = self.dram_in("ropec", [128, T])
        inp["ropes"] = self.dram_in("ropes", [128, T])
        inp["ada_w"] = self.dram_in("ada_w", [4, D, 6 * D])
        inp["a_w_rkv"] = self.dram_in("a_w_rkv", [2, 3, D, D])
        inp["a_w1"] = self.dram_in("a_w1", [2, D, 64])
        inp["a_w2"] = self.dram_in("a_w2", [2, 64, D])
        inp["a_a1"] = self.dram_in("a_a1", [2, D, 64])
        inp["a_a2"] = self.dram_in("a_a2", [2, 64, D])
        inp["a_v1"] = self.dram_in("a_v1", [1, D, 32])
        inp["a_v2"] = self.dram_in("a_v2", [1, 32, D])
        inp["a_g1"] = self.dram_in("a_g1", [2, D, 160])
        inp["a_g2"] = self.dram_in("a_g2", [2, 160, D])
        inp["a_w_o"] = self.dram_in("a_w_o", [2, D, D])
        inp["rows"] = self.dram_in("rows", [7 * D + 512])
        inp["w_kv"] = self.dram_in("w_kv", [D, 2 * D])
        inp["b_w_q"] = self.dram_in("b_w_q", [2, D, D])
        inp["b_w_o"] = self.dram_in("b_w_o", [2, D, D])
        inp["ffn_w_up"] = self.dram_in("ffn_w_up", [4, D, F2])
        inp["ffn_w_down"] = self.dram_in("ffn_w_down", [4, DFF, D])
        self.inp = inp
        self.outT = nc.dram_tensor("outT", [D, T], F32, kind="ExternalOutput").ap()
        self.X = nc.dram_tensor("Xs", [D, T], F32).ap()
        self.VF = nc.dram_tensor("VFs", [T, D], F32).ap()
        self.KT = nc.dram_tensor("KTs", [D, T], BF16).ap()
        self.VS = nc.dram_tensor("VSs", [T, D], BF16).ap()
        self.QT = nc.dram_tensor("QTs", [D, T], BF16).ap()
        self.YT = nc.dram_tensor("YTs", [D, T], BF16).ap()
        self.dbg = {}
        for name, shape in cfg.get("dbg", {}).items():
            self.dbg[name] = nc.dram_tensor("dbg_" + name, list(shape), F32, kind="ExternalOutput").ap()

        with ExitStack() as es:
            self.P = P = Prog(nc, es)
            self.phase_dma_bufs = []
            self.ps = []
            for i in range(7):
                t = es.enter_context(nc.psum_tensor("psb%d" % i, [128, 512], F32))
                self.ps.append(Tl(t, P.buf("psb%d" % i)))
                self.ps[-1].b.excl = True
            t = es.enter_context(nc.psum_tensor("psbf", [128, 1024], BF16))
            self.psbf = Tl(t, P.buf("psbf"))
            self.psbf.b.excl = True
            self.psi = 0
            g = es
            self.cols = self.tile(g, "cols", [128, CP.n], dma=True)
            self.ccol = self.tile(g, "ccol", [128, 8], dma=True)
            self.cst = {}
            for k in ("ident", "ones", "bd", "ind", "maskL", "roper"):
                self.cst[k] = self.tile(g, k, [128, 128], dma=True)
            self.cst["mask4"] = self.tile(g, "mask4", [128, 512], dma=True)
            P.dma("sp", lambda e: e.dma_start(out=self.cols[:], in_=inp["cols"]), self.cols.b, writes=[self.cols.b])
            P.dma("sp", lambda e: e.dma_start(out=self.ccol[:], in_=inp["ccol"]), self.ccol.b, writes=[self.ccol.b])
            for k, tl in self.cst.items():
                P.dma("act", lambda e, tl=tl, k=k: e.dma_start(out=tl[:], in_=inp[k]), tl.b, writes=[tl.b])
            self.eps = self.tile(g, "eps", [128, 1])
            P.op("pool", lambda e: e.memset(self.eps[:], 1e-6), writes=[self.eps.b])
            self.identb = self.tile(g, "identb", [128, 128], BF16)
            self.copy("pool", self.identb[:], self.cst["ident"][:], [self.cst["ident"].b], [self.identb.b])
            self.modc = self.tile(g, "modc", [128, 192])
            self.der = self.tile(g, "der", [128, 4, 2, 8])

            self.phase_mod()
            xsrc = inp["xT"]
            for l in cfg["layers"]:
                if cfg.get("mixer", True):
                    if l < 2:
                        self.phase_rwkv(l, xsrc)
                    else:
                        if l == 2 or cfg.get("force_kv", False):
                            self.phase_kv(xsrc)
                        self.phase_attn(l, xsrc)
                    xsrc = self.X
                if cfg.get("ffn", True):
                    self.phase_ffn(l, xsrc)
                    xsrc = self.X
            self.phase_final(xsrc, cfg.get("final", True))
            P.finish()
        return nc

    def col(self, name, j0=0, n=None):
        o = CP.off[name] + j0
        if n is None:
            n = 1
        return self.cols[:, o:o + n]

    def end_phase(self):
        self.P.barrier()
        self.P.release(self.phase_dma_bufs)
        self.phase_dma_bufs = []

    def phase_mod(self):
        P, nc, inp = self.P, self.nc, self.inp
        with ExitStack() as es:
            cact = self.tile(es, "cact", [128, 8])
            P.op("act", lambda e: e.activation(out=cact[:], in_=self.ccol[:], func=AF.Silu),
                 reads=[self.ccol.b], writes=[cact.b])
            A = [self.tile(es, "adaA%d" % i, [128, 8, 768], dma=True) for i in range(2)]
            psm = self.ps[0]
            it = 0
            for l in range(4):
                for blk in range(8):
                    a = A[it % 2]
                    it += 1
                    src = inp["ada_w"][l].rearrange("(kc p) n -> p kc n", p=128)[:, :, blk * 768:(blk + 1) * 768]
                    P.dma("sp" if it % 2 else "act", lambda e, a=a, src=src: e.dma_start(out=a[:], in_=src), a.b, writes=[a.b])
                    for n_ in range(6):
                        colidx = l * 48 + blk * 6 + n_
                        for kc in range(8):
                            P.op("pe", lambda e, a=a, kc=kc, n_=n_, colidx=colidx: e.matmul(
                                psm[:, colidx:colidx + 1], lhsT=a[:, kc, n_ * 128:(n_ + 1) * 128], rhs=cact[:, kc:kc + 1],
                                start=(kc == 0), stop=(kc == 7)), reads=[a.b, cact.b], writes=[psm.b])
            for l in range(4):
                o = CP.off["ada_b%d" % l]
                P.op("dve", lambda e, l=l, o=o: e.tensor_tensor(out=self.modc[:, l * 48:(l + 1) * 48], in0=psm[:, l * 48:(l + 1) * 48],
                                                                in1=self.cols[:, o:o + 48], op=ALU.add),
                     reads=[psm.b, self.cols.b], writes=[self.modc.b])
            for l in range(4):
                for which in range(2):
                    sc = self.modc[:, l * 48 + which * 24 + 8: l * 48 + which * 24 + 16]
                    nm = self.col("norm%d_%d" % (which + 1, l), 0, 8)
                    P.op("dve", lambda e, l=l, which=which, sc=sc, nm=nm: e.scalar_tensor_tensor(
                        out=self.der[:, l, which, :], in0=sc, scalar=1.0, in1=nm, op0=ALU.add, op1=ALU.mult),
                        reads=[self.modc.b, self.cols.b], writes=[self.der.b])
            if "modc" in self.dbg:
                dt = self.tile(es, "dbgm", [128, 192], dma=True)
                self.copy("dve", dt[:], self.modc[:], [self.modc.b], [dt.b])
                P.dma("sp", lambda e: e.dma_start(out=self.dbg["modc"], in_=dt[:]), dt.b, reads=[dt.b])
            self.end_phase()

    def modcol(self, l, i, j0=0, n=8):
        o = l * 48 + i * 8 + j0
        return self.modc[:, o:o + n]

    def rmsnorm(self, x, N, sq, G, S, out_ap, out_b, extra_reads=()):
        P = self.P
        ones = self.cst["ones"]
        P.op("act", lambda e: e.activation(out=sq[:], in_=x[:], func=AF.Square), reads=[x.b], writes=[sq.b])
        ps = self.psum_next()
        for j in range(8):
            P.op("pe", lambda e, j=j: e.matmul(ps[:, 0:N], lhsT=ones[:], rhs=sq[:, j, :], start=(j == 0), stop=(j == 7)),
                 reads=[ones.b, sq.b], writes=[ps.b])
        rs = self.rstd
        P.op("act", lambda e: e.activation(out=rs[:, 0:N], in_=ps[:, 0:N], func=AF.Sqrt, bias=self.eps[:], scale=1.0 / D),
             reads=[ps.b, self.eps.b], writes=[rs.b])
        P.op("dve", lambda e: e.reciprocal(out=rs[:, 0:N], in_=rs[:, 0:N]), reads=[rs.b], writes=[rs.b])
        P.op("dve", lambda e: e.tensor_tensor(out=sq[:], in0=x[:], in1=rs[:, 0:N].unsqueeze(1).to_broadcast([128, 8, N]), op=ALU.mult),
             reads=[x.b, rs.b], writes=[sq.b])
        if S is None:
            P.op("pool", lambda e: e.tensor_tensor(out=out_ap, in0=sq[:], in1=G.unsqueeze(2).to_broadcast([128, 8, N]), op=ALU.mult),
                 reads=[sq.b, self.cols.b, self.der.b] + list(extra_reads), writes=[out_b])
        else:
            P.op("pool", lambda e: e.tensor_tensor(out=sq[:], in0=sq[:], in1=G.unsqueeze(2).to_broadcast([128, 8, N]), op=ALU.mult),
                 reads=[sq.b, self.cols.b, self.der.b], writes=[sq.b])
            P.op("pool", lambda e: e.tensor_tensor(out=out_ap, in0=sq[:], in1=S.unsqueeze(2).to_broadcast([128, 8, N]), op=ALU.add),
                 reads=[sq.b, self.modc.b] + list(extra_reads), writes=[out_b])

    def load_cast(self, es_stage, pieces, wb):
        P = self.P
        if not hasattr(self, "_stg") or self._stg is None:
            self._stg = [self.tile(es_stage, "stg%d" % i, [128, 1024], dma=True) for i in range(2)]
            self._stgi = 0
        for (dst, src, p, a, b) in pieces:
            assert a * b <= 1024
            st = self._stg[self._stgi % 2]
            self._stgi += 1
            sv = st[0:p, 0:a * b].rearrange("p (a b) -> p a b", a=a)
            q = ("sp", "act")[self._stgi % 2]
            P.dma(q, lambda e, sv=sv, src=src: e.dma_start(out=sv, in_=src), st.b, writes=[st.b])
            self.copy(self.eng_rr(("pool", "dve", "act")), dst, sv, [st.b], [wb])

    def phase_ffn(self, l, xsrc):
        P, nc, inp = self.P, self.nc, self.inp
        TB = 256
        NB = T // TB
        with ExitStack() as es:
            self._stg = None
            wup = self.tile(es, "wup", [128, 8, F2], BF16)
            wdn = self.tile(es, "wdn", [128, NG, D], BF16)
            pieces = []
            srcu = inp["ffn_w_up"][l].rearrange("(kc p) n -> p kc n", p=128)
            for kc in range(8):
                for (n0, n1) in ((0, 1024), (1024, 2048), (2048, 3072), (3072, 4096), (4096, 5120), (5120, F2)):
                    pieces.append((wup[:, kc:kc + 1, n0:n1], srcu[:, kc:kc + 1, n0:n1], 128, 1, n1 - n0))
            self.load_cast(es, pieces, wup.b)
            srcd = inp["ffn_w_down"][l].rearrange("(kc p) n -> p kc n", p=128)
            pieces = [(wdn[:, i:i + 1, :], srcd[:, i:i + 1, :], 128, 1, D) for i in range(0, NG)]
            self.load_cast(es, pieces, wdn.b)

            xs = [self.tile(es, "fx%d" % i, [128, 8, TB], dma=True) for i in range(2)]
            sq = self.tile(es, "fsq", [128, 8, TB])
            self.rstd = self.tile(es, "frstd", [128, TB])
            h2s = [self.tile(es, "fh2_%d" % i, [128, 8, TB + 2], BF16) for i in range(2)]
            for i in range(2):
                P.op("pool", lambda e: e.memset(h2s[i][:], 0.0), writes=[h2s[i].b])
            NCV = 6
            cv = [self.tile(es, "fcv%d" % i, [128, TB]) for i in range(NCV)]
            sgl = [self.tile(es, "fsg%d" % i, [128, TB]) for i in range(3)]
            hm = self.tile(es, "fhm", [128, NG, TB], BF16)
            G2 = self.der[:, l, 1, :]
            S2 = self.modcol(l, 3)
            g2c = self.modcol(l, 5)
            cw = [CP.off["cw%d_%d" % (i, l)] for i in range(3)]
            cb = CP.off["cb_%d" % l]
            xr = lambda ap: ap.rearrange("(j p) t -> p j t", p=128)

            def norm(nb):
                x = xs[nb % 2]
                h2 = h2s[nb % 2]
                t0 = nb * TB
                P.dma("sp", lambda e: e.dma_start(out=x[:], in_=xr(xsrc)[:, :, t0:t0 + TB]), x.b, writes=[x.b])
                if nb > 0:
                    hp = h2s[(nb - 1) % 2]
                    P.op("pool", lambda e: e.tensor_copy(out=h2[:, :, 0:2], in_=hp[:, :, TB:TB + 2]), reads=[hp.b], writes=[h2.b])
                self.rmsnorm(x, TB, sq, G2, S2, h2[:, :, 2:TB + 2], h2.b)

            def up(nb):
                h2 = h2s[nb % 2]
                ui = 0
                for i in range(NG):
                    for which in range(2):
                        n = i + which * NG
                        ps = self.psum_next()
                        for kc in range(8):
                            P.op("pe", lambda e: e.matmul(ps[:, 0:TB + 2], lhsT=wup[:, kc, n * 128:(n + 1) * 128], rhs=h2[:, kc, :],
                                                          start=(kc == 0), stop=(kc == 7)), reads=[wup.b, h2.b], writes=[ps.b])
                        c = cv[ui % NCV]
                        ui += 1
                        P.op("act", lambda e: e.activation(out=c[:], in_=ps[:, 2:TB + 2], func=AF.Identity,
                                                           bias=self.cols[:, cb + n:cb + n + 1], scale=self.cols[:, cw[2] + n:cw[2] + n + 1]),
                             reads=[ps.b, self.cols.b], writes=[c.b])
                        P.op("dve", lambda e: e.scalar_tensor_tensor(out=c[:], in0=ps[:, 1:TB + 1], scalar=self.cols[:, cw[1] + n:cw[1] + n + 1],
                                                                     in1=c[:], op0=ALU.mult, op1=ALU.add), reads=[ps.b, c.b, self.cols.b], writes=[c.b])
                        P.op("dve", lambda e: e.scalar_tensor_tensor(out=c[:], in0=ps[:, 0:TB], scalar=self.cols[:, cw[0] + n:cw[0] + n + 1],
                                                                     in1=c[:], op0=ALU.mult, op1=ALU.add), reads=[ps.b, c.b, self.cols.b], writes=[c.b])
                        s_ = sgl[i % 3]
                        if which == 0:
                            P.op("act", lambda e: e.activation(out=s_[:], in_=c[:], func=AF.Silu), reads=[c.b], writes=[s_.b])
                        else:
                            P.op("pool", lambda e: e.tensor_tensor(out=hm[:, i, :], in0=s_[:], in1=c[:], op=ALU.mult),
                                 reads=[s_.b, c.b], writes=[hm.b])

            def down(nb):
                x = xs[nb % 2]
                t0 = nb * TB
                for nj in range(8):
                    ps = self.psum_next()
                    for i in range(NG):
                        P.op("pe", lambda e: e.matmul(ps[:, 0:TB], lhsT=wdn[:, i, nj * 128:(nj + 1) * 128], rhs=hm[:, i, :],
                                                      start=(i == 0), stop=(i == NG - 1)), reads=[wdn.b, hm.b], writes=[ps.b])
                    P.op("dve", lambda e: e.scalar_tensor_tensor(out=x[:, nj, :], in0=ps[:, 0:TB], scalar=g2c[:, nj:nj + 1],
                                                                 in1=x[:, nj, :], op0=ALU.mult, op1=ALU.add),
                         reads=[ps.b, x.b, self.modc.b], writes=[x.b])
                P.dma("sp", lambda e: e.dma_start(out=xr(self.X)[:, :, t0:t0 + TB], in_=x[:]), x.b, reads=[x.b])

            norm(0)
            for nb in range(NB):
                up(nb)
                if nb + 1 < NB:
                    norm(nb + 1)
                down(nb)
            self.end_phase()

    def phase_final(self, xsrc, do_norm):
        P = self.P
        TB = 512
        with ExitStack() as es:
            xs = [self.tile(es, "nx%d" % i, [128, 8, TB], dma=True) for i in range(2)]
            sq = self.tile(es, "nsq", [128, 8, TB])
            self.rstd = self.tile(es, "nrstd", [128, TB])
            G = self.col("final_norm", 0, 8)
            for nb in range(T // TB):
                x = xs[nb % 2]
                t0 = nb * TB
                src = xsrc.rearrange("(j p) t -> p j t", p=128)[:, :, t0:t0 + TB]
                P.dma("sp", lambda e, x=x, src=src: e.dma_start(out=x[:], in_=src), x.b, writes=[x.b])
                if do_norm:
                    self.rmsnorm(x, TB, sq, G, None, x[:], x.b)
                dst = self.outT.rearrange("(j p) t -> p j t", p=128)[:, :, t0:t0 + TB]
                P.dma("act", lambda e, x=x, dst=dst: e.dma_start(out=dst, in_=x[:]), x.b, reads=[x.b])
            self.end_phase()

    def phase_rwkv(self, l, xsrc):
        P, nc, inp = self.P, self.nc, self.inp
        CH = 128
        NCH = self.cfg.get("nch", T // CH)
        ident = self.cst["ident"]
        with ExitStack() as es:
            self._stg = None
            reuse_stage = self.cfg.get("stage_reuse", True)
            ses = ExitStack() if reuse_stage else es
            Wr = self.tile(es, "Wr", [128, 8, D], BF16)
            Wk = self.tile(es, "Wk", [128, 8, D], BF16)
            Wv = self.tile(es, "Wv", [128, 8, D], BF16)
            Wo = self.tile(es, "Wo", [128, 8, D], BF16)
            w1b = self.tile(es, "w1b", [128, 8, 64], BF16)
            a1b = self.tile(es, "a1b", [128, 8, 64], BF16)
            g1b = self.tile(es, "g1b", [128, 8, 160], BF16)
            w2b = self.tile(es, "w2b", [64, 1, D], BF16)
            a2b = self.tile(es, "a2b", [64, 1, D], BF16)
            g2b = self.tile(es, "g2b", [128, 2, D], BF16)
            if l == 1:
                v1b = self.tile(es, "v1b", [128, 8, 32], BF16)
                v2b = self.tile(es, "v2b", [32, 1, D], BF16)
                v0r = self.tile(es, "v0r", [128, D], dma=True)
            lnw = self.tile(es, "lnw", [128, D], dma=True)
            lnb = self.tile(es, "lnb", [128, D], dma=True)
            omka = self.tile(es, "omka", [128, 8])
            eps2 = self.tile(es, "eps2", [128, 1])
            onesT = self.tile(es, "onesT", [128, 128])
            for W, src in ((Wr, inp["a_w_rkv"][l, 0]), (Wk, inp["a_w_rkv"][l, 1]), (Wv, inp["a_w_rkv"][l, 2]), (Wo, inp["a_w_o"][l])):
                s3 = src.rearrange("(kc p) n -> p kc n", p=128)
                self.load_cast(ses, [(W[:, kc:kc + 1, :], s3[:, kc:kc + 1, :], 128, 1, D) for kc in range(8)], W.b)
            self.load_cast(ses, [(w1b[:], inp["a_w1"][l].rearrange("(kc p) n -> p kc n", p=128), 128, 8, 64)], w1b.b)
            self.load_cast(ses, [(a1b[:], inp["a_a1"][l].rearrange("(kc p) n -> p kc n", p=128), 128, 8, 64)], a1b.b)
            sg1 = inp["a_g1"][l].rearrange("(kc p) n -> p kc n", p=128)
            self.load_cast(ses, [(g1b[:, 0:4, :], sg1[:, 0:4, :], 128, 4, 160), (g1b[:, 4:8, :], sg1[:, 4:8, :], 128, 4, 160)], g1b.b)
            self.load_cast(ses, [(w2b[:], inp["a_w2"][l].rearrange("(o p) n -> p o n", o=1), 64, 1, D)], w2b.b)
            self.load_cast(ses, [(a2b[:], inp["a_a2"][l].rearrange("(o p) n -> p o n", o=1), 64, 1, D)], a2b.b)
            self.load_cast(ses, [(g2b[:, 0:1, :], inp["a_g2"][l][0:128, :].rearrange("(o p) n -> p o n", o=1), 128, 1, D),
                                (g2b[0:32, 1:2, :], inp["a_g2"][l][128:160, :].rearrange("(o p) n -> p o n", o=1), 32, 1, D)], g2b.b)
            if l == 1:
                self.load_cast(ses, [(v1b[:], inp["a_v1"][0].rearrange("(kc p) n -> p kc n", p=128), 128, 8, 32)], v1b.b)
                self.load_cast(ses, [(v2b[:], inp["a_v2"][0].rearrange("(o p) n -> p o n", o=1), 32, 1, D)], v2b.b)
                P.dma("sp", lambda e: e.dma_start(out=v0r[:], in_=inp["rows"][4 * D:5 * D].partition_broadcast(128)), v0r.b, writes=[v0r.b])
            P.dma("sp", lambda e: e.dma_start(out=lnw[:], in_=inp["rows"][(2 * l) * D:(2 * l + 1) * D].partition_broadcast(128)), lnw.b, writes=[lnw.b])
            P.dma("sp", lambda e: e.dma_start(out=lnb[:], in_=inp["rows"][(2 * l + 1) * D:(2 * l + 2) * D].partition_broadcast(128)), lnb.b, writes=[lnb.b])
            P.op("dve", lambda e: e.tensor_scalar(out=omka[:], in0=self.col("k_a_%d" % l, 0, 8), scalar1=-1.0, scalar2=1.0, op0=ALU.mult, op1=ALU.add),
                 reads=[self.cols.b], writes=[omka.b])
            P.op("pool", lambda e: e.memset(eps2[:], 64e-5), writes=[eps2.b])
            P.op("pool", lambda e: e.memset(onesT[:], 1.0), writes=[onesT.b])

            if reuse_stage:
                P.barrier()
                ses.close()
                self._stg = None
            Hbd = self.tile(es, "Hbd", [128, 8, 128])
            Hb = self.tile(es, "Hb", [128, 8, 128], BF16)
            if self.cfg.get("t_hb", True):
                P.op("pool", lambda e: e.memset(Hb[:], 0.0), writes=[Hb.b])
            Vb = self.tile(es, "Vb", [128, D], BF16)
            P.op("pool", lambda e: e.memset(Hbd[:], 0.0), writes=[Hbd.b])
            h = self.tile(es, "h", [128, 8, 129])
            P.op("pool", lambda e: e.memset(h[:], 0.0), writes=[h.b])
            x = self.tile(es, "rx", [128, 8, CH], dma=True)
            self.rstd = self.tile(es, "rrstd", [128, CH])
            xx = self.tile(es, "xx", [128, 8, CH])
            tmpm = [self.tile(es, "tmpm%d" % i, [128, 8, CH], dma=(i == 0)) for i in range(2)]
            sq = tmpm[1] if self.cfg.get("t_sq", True) else self.tile(es, "rsq", [128, 8, CH])
            xm = [self.tile(es, "xm%d" % i, [128, 8, CH], BF16) for i in range(6)]
            lwt = self.tile(es, "lwt", [64, CH], BF16)
            lat = self.tile(es, "lat", [64, CH], BF16)
            lgt = self.tile(es, "lgt", [128, 2, CH], BF16)
            V = self.tile(es, "V", [128, D], dma=True)
            Yt = self.tile(es, "Yt", [128, D], dma=True)
            if l == 1:
                lvt = self.tile(es, "lvt", [32, CH], BF16)
                VFt = Tl(tmpm[0][:].rearrange("p a t -> p (a t)"), tmpm[0].b)
                sgv = Tl(tmpm[1][:].rearrange("p a t -> p (a t)")[:, 0:512], tmpm[1].b)
            ogb = self.tile(es, "ogb", [128, D], BF16)
            ogT = self.tile(es, "ogT", [128, 8, CH], BF16)
            st = self.tile(es, "stat", [128, 6, 16])
            rkb = self.tile(es, "rkb", [128, 16])

            NSLOT = self.cfg.get("nslot%d" % l, 4)

            def mkslot(si):
                S = {}

                def pt(name, shape=(128, 128), dt=F32):
                    S[name] = self.tile(es, "s%d_%s" % (si, name), list(shape), dt)
                for nm in ("r", "k", "sg", "a", "kk", "x", "L", "eL"):
                    pt(nm)
                for nm in ("rh", "bts", "kts", "WT", "U"):
                    pt(nm, (128, 128), BF16)
                pt("AR", (128, 2, 128), BF16)
                pt("F3", (128, 3, 128), BF16)
                pt("T3", (128, 3, 128), BF16)
                for i in range(2):
                    pt("SC%d" % i, (128, 4, 128), BF16)
                    pt("Ym%d" % i, (128, 128), BF16)
                    pt("ZY%d_0" % i, (128, 2, 128), BF16)
                    pt("ZY%d_1" % i, (128, 2, 128), BF16)
                    pt("Tt%d" % i, (128, 128), BF16)
                    pt("AV%d" % i, (128, 64), BF16)
                return S
            slots = [mkslot(i) for i in range(NSLOT)]
            mask4 = self.cst["mask4"]; maskL = self.cst["maskL"]; bd = self.cst["bd"]; ind = self.cst["ind"]
            G1 = self.der[:, l, 0, :]
            S1 = self.modcol(l, 0)
            g1c = self.modcol(l, 2)
            psB = self.ps[6]

            def c_(name, j):
                return self.col("%s_%d" % (name, l), j, 1)

            for c in range(NCH):
                t0 = c * CH
                src = xsrc.rearrange("(j p) t -> p j t", p=128)[:, :, t0:t0 + CH]
                P.dma("sp", lambda e, src=src: e.dma_start(out=x[:], in_=src), x.b, writes=[x.b])
                self.rmsnorm(x, CH, sq, G1, S1, h[:, :, 1:CH + 1], h.b)
                P.op("pool", lambda e: e.tensor_tensor(out=xx[:], in0=h[:, :, 0:CH], in1=h[:, :, 1:CH + 1], op=ALU.subtract),
                     reads=[h.b], writes=[xx.b])
                for i in range(6):
                    tm = tmpm[i % 2]
                    mu = self.col("mu%d_%d" % (i, l), 0, 8)
                    P.op("dve", lambda e, tm=tm, mu=mu: e.tensor_tensor(out=tm[:], in0=xx[:], in1=mu.unsqueeze(2).to_broadcast([128, 8, CH]), op=ALU.mult),
                         reads=[xx.b, self.cols.b], writes=[tm.b])
                    P.op("pool", lambda e, tm=tm, i=i: e.tensor_tensor(out=xm[i][:], in0=tm[:], in1=h[:, :, 1:CH + 1], op=ALU.add),
                         reads=[tm.b, h.b], writes=[xm[i].b])
                P.op("pool", lambda e: e.tensor_copy(out=h[:, :, 0:1], in_=h[:, :, CH:CH + 1]), reads=[h.b], writes=[h.b])
                if l == 1:
                    P.dma("sp", lambda e, t0=t0: e.dma_start(out=VFt[:], in_=self.VF[t0:t0 + CH, :]), VFt.b, writes=[VFt.b])

                psl = self.psum_next()
                for kc in range(8):
                    P.op("pe", lambda e, kc=kc: e.matmul(psl[0:64, 0:128], lhsT=w1b[:, kc, :], rhs=xm[1][:, kc, :], start=(kc == 0), stop=(kc == 7)),
                         reads=[w1b.b, xm[1].b], writes=[psl.b])
                for kc in range(8):
                    P.op("pe", lambda e, kc=kc: e.matmul(psl[0:64, 128:256], lhsT=a1b[:, kc, :], rhs=xm[4][:, kc, :], start=(kc == 0), stop=(kc == 7)),
                         reads=[a1b.b, xm[4].b], writes=[psl.b])
                for kc in range(8):
                    P.op("pe", lambda e, kc=kc: e.matmul(psl[:, 256:384], lhsT=g1b[:, kc, 0:128], rhs=xm[5][:, kc, :], start=(kc == 0), stop=(kc == 7)),
                         reads=[g1b.b, xm[5].b], writes=[psl.b])
                for kc in range(8):
                    P.op("pe", lambda e, kc=kc: e.matmul(psl[0:32, 384:512], lhsT=g1b[:, kc, 128:160], rhs=xm[5][:, kc, :], start=(kc == 0), stop=(kc == 7)),
                         reads=[g1b.b, xm[5].b], writes=[psl.b])
                P.op("act", lambda e: e.activation(out=lwt[:], in_=psl[0:64, 0:128], func=AF.Tanh), reads=[psl.b], writes=[lwt.b])
                P.op("act", lambda e: e.copy(out=lat[:], in_=psl[0:64, 128:256]), reads=[psl.b], writes=[lat.b])
                P.op("act", lambda e: e.activation(out=lgt[:, 0, :], in_=psl[:, 256:384], func=AF.Sigmoid), reads=[psl.b], writes=[lgt.b])
                P.op("act", lambda e: e.activation(out=lgt[0:32, 1, :], in_=psl[0:32, 384:512], func=AF.Sigmoid), reads=[psl.b], writes=[lgt.b])
                if l == 1:
                    psv = self.psum_next()
                    for kc in range(8):
                        P.op("pe", lambda e, kc=kc: e.matmul(psv[0:32, 0:128], lhsT=v1b[:, kc, :], rhs=xm[3][:, kc, :], start=(kc == 0), stop=(kc == 7)),
                             reads=[v1b.b, xm[3].b], writes=[psv.b])
                    P.op("act", lambda e: e.copy(out=lvt[:], in_=psv[0:32, 0:128]), reads=[psv.b], writes=[lvt.b])
                for half in range(2):
                    ps = self.psum_next()
                    for kc in range(8):
                        P.op("pe", lambda e, ps=ps, kc=kc, half=half: e.matmul(ps[:, :], lhsT=xm[3][:, kc, :], rhs=Wv[:, kc, half * 512:(half + 1) * 512],
                                                                              start=(kc == 0), stop=(kc == 7)), reads=[xm[3].b, Wv.b], writes=[ps.b])
                    P.op("act", lambda e, ps=ps, half=half: e.copy(out=V[:, half * 512:(half + 1) * 512], in_=ps[:, :]), reads=[ps.b], writes=[V.b])
                if l == 1:
                    for half in range(2):
                        ps = self.psum_next()
                        hs = slice(half * 512, (half + 1) * 512)
                        P.op("pe", lambda e, ps=ps, hs=hs: e.matmul(ps[:, :], lhsT=lvt[:], rhs=v2b[0:32, 0, hs], start=True, stop=True),
                             reads=[lvt.b, v2b.b], writes=[ps.b])
                        P.op("dve", lambda e, ps=ps, hs=hs: e.tensor_tensor(out=sgv[:], in0=ps[:, :], in1=v0r[:, hs], op=ALU.add),
                             reads=[ps.b, v0r.b], writes=[sgv.b])
                        P.op("act", lambda e: e.activation(out=sgv[:], in_=sgv[:], func=AF.Sigmoid), reads=[sgv.b], writes=[sgv.b])
                        P.op("pool", lambda e, hs=hs: e.tensor_tensor(out=VFt[:, hs], in0=VFt[:, hs], in1=V[:, hs], op=ALU.subtract),
                             reads=[VFt.b, V.b], writes=[VFt.b])
                        P.op("pool", lambda e, hs=hs: e.tensor_tensor(out=VFt[:, hs], in0=VFt[:, hs], in1=sgv[:], op=ALU.mult),
                             reads=[VFt.b, sgv.b], writes=[VFt.b])
                        P.op("pool", lambda e, hs=hs: e.tensor_tensor(out=V[:, hs], in0=V[:, hs], in1=VFt[:, hs], op=ALU.add),
                             reads=[VFt.b, V.b], writes=[V.b])
                else:
                    P.dma("act", lambda e, t0=t0: e.dma_start(out=self.VF[t0:t0 + CH, :], in_=V[:]), V.b, reads=[V.b])
                if "rw_v" in self.dbg:
                    P.dma("act", lambda e, t0=t0: e.dma_start(out=self.dbg["rw_v"][t0:t0 + CH, :], in_=V[:]), V.b, reads=[V.b])

                if self.cfg.get("t_vb", True):
                    P.op("pool", lambda e: e.tensor_copy(out=Vb[:], in_=V[:]), reads=[V.b], writes=[Vb.b])

                def pair_gen(j, S):
                    CUT = self.cfg.get("rw_cut", 99)
                    js = slice(j * 128, (j + 1) * 128)
                    r_, k_, sg, a_, kk, tx, L_, eL = S["r"], S["k"], S["sg"], S["a"], S["kk"], S["x"], S["L"], S["eL"]
                    rh, bts, kts, WT, U_, AR, F3, T3 = S["rh"], S["bts"], S["kts"], S["WT"], S["U"], S["AR"], S["F3"], S["T3"]
                    SC = [S["SC0"], S["SC1"]]; Ym = [S["Ym0"], S["Ym1"]]; Tt = [S["Tt0"], S["Tt1"]]; AV = [S["AV0"], S["AV1"]]
                    ZY = [[S["ZY0_0"], S["ZY0_1"]], [S["ZY1_0"], S["ZY1_1"]]]
                    psA = self.psum_next()
                    for kc in range(8):
                        P.op("pe", lambda e: e.matmul(psA[:, 0:128], lhsT=Wr[:, kc, js], rhs=xm[0][:, kc, :], start=(kc == 0), stop=(kc == 7)),
                             reads=[Wr.b, xm[0].b], writes=[psA.b])
                    for kc in range(8):
                        P.op("pe", lambda e: e.matmul(psA[:, 128:256], lhsT=Wk[:, kc, js], rhs=xm[2][:, kc, :], start=(kc == 0), stop=(kc == 7)),
                             reads=[Wk.b, xm[2].b], writes=[psA.b])
                    P.op("pe", lambda e: e.matmul(psA[:, 256:384], lhsT=w2b[:, 0, js], rhs=lwt[:], start=True, stop=True), reads=[w2b.b, lwt.b], writes=[psA.b])
                    P.op("pe", lambda e: e.matmul(psA[:, 384:512], lhsT=a2b[:, 0, js], rhs=lat[:], start=True, stop=True), reads=[a2b.b, lat.b], writes=[psA.b])
                    P.op("act", lambda e: e.copy(out=r_[:], in_=psA[:, 0:128]), reads=[psA.b], writes=[r_.b])
                    P.op("act", lambda e: e.copy(out=k_[:], in_=psA[:, 128:256]), reads=[psA.b], writes=[k_.b])
                    P.op("act", lambda e: e.activation(out=sg[:], in_=psA[:, 256:384], func=AF.Sigmoid, bias=c_("w0", j), scale=1.0),
                         reads=[psA.b, self.cols.b], writes=[sg.b])
                    P.op("act", lambda e: e.activation(out=a_[:], in_=psA[:, 384:512], func=AF.Sigmoid, bias=c_("a0", j), scale=1.0),
                         reads=[psA.b, self.cols.b], writes=[a_.b])
                    yield
                    if CUT <= 1:
                        return
                    P.op("dve", lambda e: e.tensor_scalar(out=kk[:], in0=k_[:], scalar1=c_("k_k", j), scalar2=None, op0=ALU.mult),
                         reads=[k_.b, self.cols.b], writes=[kk.b])
                    P.op("pool", lambda e: e.tensor_tensor(out=tx[:], in0=kk[:], in1=kk[:], op=ALU.mult), reads=[kk.b], writes=[tx.b])
                    ps = self.psum_next()
                    P.op("pe", lambda e: e.matmul(ps[:, 0:128], lhsT=bd[:], rhs=tx[:], start=True, stop=True), reads=[bd.b, tx.b], writes=[ps.b])
                    P.op("act", lambda e: e.activation(out=tx[:], in_=ps[:, 0:128], func=AF.Sqrt), reads=[ps.b], writes=[tx.b])
                    yield
                    if CUT <= 2:
                        return
                    P.op("dve", lambda e: e.tensor_scalar(out=tx[:], in0=tx[:], scalar1=1e-12, scalar2=None, op0=ALU.max), reads=[tx.b], writes=[tx.b])
                    P.op("dve", lambda e: e.reciprocal(out=tx[:], in_=tx[:]), reads=[tx.b], writes=[tx.b])
                    P.op("pool", lambda e: e.tensor_tensor(out=kk[:], in0=kk[:], in1=tx[:], op=ALU.mult), reads=[kk.b, tx.b], writes=[kk.b])
                    P.op("dve", lambda e: e.tensor_scalar(out=tx[:], in0=a_[:], scalar1=c_("k_a", j), scalar2=omka[:, j:j + 1], op0=ALU.mult, op1=ALU.add),
                         reads=[a_.b, self.cols.b, omka.b], writes=[tx.b])
                    P.op("pool", lambda e: e.tensor_tensor(out=k_[:], in0=k_[:], in1=tx[:], op=ALU.mult), reads=[k_.b, tx.b], writes=[k_.b])
                    P.op("pool", lambda e: e.tensor_tensor(out=a_[:], in0=kk[:], in1=a_[:], op=ALU.mult), reads=[kk.b, a_.b], writes=[a_.b])
                    P.op("dve", lambda e: e.scalar_tensor_tensor(out=tx[:], in0=r_[:], scalar=c_("r_k", j), in1=k_[:], op0=ALU.mult, op1=ALU.mult),
                         reads=[r_.b, k_.b, self.cols.b], writes=[tx.b])
                    P.op("pe", lambda e: e.matmul(psB[:, 0:16], lhsT=tx[:], rhs=ind[:, j * 16:(j + 1) * 16], start=(j == 0), stop=(j == 7)),
                         reads=[tx.b, ind.b], writes=[psB.b])
                    yield
                    if CUT <= 3:
                        return
                    P.op("dve", lambda e: e.tensor_tensor_scan(out=L_[:], data0=onesT[:], data1=sg[:], initial=0.0, op0=ALU.mult, op1=ALU.add),
                         reads=[onesT.b, sg.b], writes=[L_.b])
                    P.op("pool", lambda e: e.tensor_tensor(out=sg[:], in0=L_[:], in1=sg[:], op=ALU.subtract), reads=[L_.b, sg.b], writes=[sg.b])
                    P.op("act", lambda e: e.activation(out=eL[:], in_=L_[:], func=AF.Exp, scale=-C0), reads=[L_.b], writes=[eL.b])
                    P.op("act", lambda e: e.activation(out=sg[:], in_=sg[:], func=AF.Exp, scale=-C0), reads=[sg.b], writes=[sg.b])
                    P.op("act", lambda e: e.activation(out=L_[:], in_=L_[:], func=AF.Exp, scale=C0), reads=[L_.b], writes=[L_.b])
                    yield
                    if CUT <= 4:
                        return
                    enL = L_
                    eE = sg
                    P.op("pool", lambda e: e.tensor_tensor(out=r_[:], in0=r_[:], in1=eL[:], op=ALU.mult), reads=[r_.b, eL.b], writes=[r_.b])
                    P.op("pool", lambda e: e.tensor_copy(out=rh[:], in_=r_[:]), reads=[r_.b], writes=[rh.b])
                    P.op("dve", lambda e: e.scalar_tensor_tensor(out=tx[:], in0=kk[:], scalar=-1.0, in1=eE[:], op0=ALU.mult, op1=ALU.mult),
                         reads=[kk.b, eE.b], writes=[tx.b])
                    P.op("pool", lambda e: e.tensor_copy(out=F3[:, 0, :], in_=tx[:]), reads=[tx.b], writes=[F3.b])
                    P.op("dve", lambda e: e.tensor_scalar(out=AR[:, 0, :], in0=tx[:], scalar1=enL[:, 63:64], scalar2=None, op0=ALU.mult),
                         reads=[tx.b, enL.b], writes=[AR.b])
                    P.op("dve", lambda e: e.tensor_scalar(out=AR[:, 1, :], in0=r_[:], scalar1=enL[:, 63:64], scalar2=None, op0=ALU.mult),
                         reads=[r_.b, enL.b], writes=[AR.b])
                    P.op("pool", lambda e: e.tensor_tensor(out=a_[:], in0=a_[:], in1=enL[:], op=ALU.mult), reads=[a_.b, enL.b], writes=[a_.b])
                    P.op("pool", lambda e: e.tensor_tensor(out=k_[:], in0=k_[:], in1=enL[:], op=ALU.mult), reads=[k_.b, enL.b], writes=[k_.b])
                    yield
                    if CUT <= 5:
                        return
                    P.op("dve", lambda e: e.tensor_scalar(out=bts[:], in0=a_[:], scalar1=eL[:, 63:64], scalar2=None, op0=ALU.mult),
                         reads=[a_.b, eL.b], writes=[bts.b])
                    P.op("dve", lambda e: e.tensor_scalar(out=kts[:], in0=k_[:], scalar1=eL[:, 63:64], scalar2=None, op0=ALU.mult),
                         reads=[k_.b, eL.b], writes=[kts.b])
                    P.op("dve", lambda e: e.tensor_scalar(out=F3[:, 1, :], in0=a_[:], scalar1=eL[:, 127:128], scalar2=None, op0=ALU.mult),
                         reads=[a_.b, eL.b], writes=[F3.b])
                    P.op("dve", lambda e: e.tensor_scalar(out=F3[:, 2, :], in0=k_[:], scalar1=eL[:, 127:128], scalar2=None, op0=ALU.mult),
                         reads=[k_.b, eL.b], writes=[F3.b])
                    yield
                    if CUT <= 6:
                        return
                    for q in range(3):
                        P.op("pe", lambda e: e.transpose(out=self.psbf[:, q * 128:(q + 1) * 128], in_=F3[:, q, :], identity=self.identb[:]),
                             reads=[F3.b, self.identb.b], writes=[self.psbf.b])
                    P.op("act", lambda e: e.copy(out=T3[:].rearrange("p a t -> p (a t)"), in_=self.psbf[:, 0:384]), reads=[self.psbf.b], writes=[T3.b])
                    yield
                    if CUT <= 7:
                        return
                    for par in range(2):
                        rows = slice(64 * par, 64 * par + 64)
                        pss = self.psum_next()
                        arv = AR[rows, :, :].rearrange("p a t -> p (a t)")
                        P.op("pe", lambda e: e.matmul(pss[:, 0:256], lhsT=bts[rows, :], rhs=arv, start=True, stop=True),
                             reads=[bts.b, AR.b], writes=[pss.b])
                        P.op("pe", lambda e: e.matmul(pss[:, 256:512], lhsT=kts[rows, :], rhs=arv, start=True, stop=True),
                             reads=[kts.b, AR.b], writes=[pss.b])
                        P.op("dve", lambda e: e.tensor_tensor(out=SC[par][:].rearrange("p a t -> p (a t)"), in0=pss[:, :], in1=mask4[:], op=ALU.mult),
                             reads=[pss.b, mask4.b], writes=[SC[par].b])
                        ps3 = self.psum_next()
                        P.op("pe", lambda e: e.matmul(ps3[:, 0:128], lhsT=AR[rows, 0, :], rhs=bts[rows, :], start=True, stop=True),
                             reads=[bts.b, AR.b], writes=[ps3.b])
                        P.op("dve", lambda e: e.tensor_tensor(out=Ym[par][:], in0=ps3[:, 0:128], in1=maskL[:], op=ALU.mult),
                             reads=[ps3.b, maskL.b], writes=[Ym[par].b])
                        P.op("pool", lambda e: e.tensor_tensor(out=Tt[par][:], in0=SC[par][:, 0, :], in1=self.identb[:], op=ALU.add),
                             reads=[SC[par].b, self.identb.b], writes=[Tt[par].b])
                        yield
                    cur = [(SC[0][:, 0, :], SC[0].b, Ym[0][:], Ym[0].b), (SC[1][:, 0, :], SC[1].b, Ym[1][:], Ym[1].b)]
                    for step in range(1, 7):
                        for par in range(2):
                            Zap, Zb, Yap, Yb = cur[par]
                            zy = ZY[par][step % 2]
                            psz = self.psum_next()
                            if step < 6:
                                P.op("pe", lambda e: e.matmul(psz[:, 0:128], lhsT=Yap, rhs=Zap, start=True, stop=True), reads=[Zb, Yb], writes=[psz.b])
                            P.op("pe", lambda e: e.matmul(psz[:, 128:256], lhsT=Zap, rhs=Yap, start=True, stop=True), reads=[Zb, Yb], writes=[psz.b])
                            ev = "act"
                            if step < 6:
                                self.copy(ev, zy[:].rearrange("p a t -> p (a t)"), psz[:, 0:256], [psz.b], [zy.b])
                            else:
                                self.copy(ev, zy[:, 1, :], psz[:, 128:256], [psz.b], [zy.b])
                            cur[par] = (zy[:, 0, :], zy.b, zy[:, 1, :], zy.b)
                            yield
                            tt = Tt[par]
                            psp = self.psum_next()
                            P.op("pe", lambda e: e.matmul(psp[:, 0:128], lhsT=zy[:, 1, :], rhs=tt[:], start=True, stop=True),
                                 reads=[zy.b, tt.b], writes=[psp.b])
                            P.op("dve", lambda e: e.tensor_tensor(out=tt[:], in0=psp[:, 0:128], in1=tt[:], op=ALU.add),
                                 reads=[psp.b, tt.b], writes=[tt.b])
                            yield
                    for par in range(2):
                        rows = slice(64 * par, 64 * par + 64)
                        hc = slice((2 * j + par) * 64, (2 * j + par + 1) * 64)
                        psw = self.psum_next()
                        P.op("pe", lambda e: e.matmul(psw[:, 0:128], lhsT=T3[:, 0, :], rhs=Tt[par][:], start=True, stop=True),
                             reads=[T3.b, Tt[par].b], writes=[psw.b])
                        P.op("pe", lambda e: e.matmul(psw[:, 128:192], lhsT=SC[par][:, 2, :], rhs=Vb[:, hc], start=True, stop=True),
                             reads=[SC[par].b, Vb.b], writes=[psw.b])
                        P.op("act", lambda e: e.copy(out=WT[rows, :], in_=psw[rows, 0:128]), reads=[psw.b], writes=[WT.b])
                        P.op("act", lambda e: e.copy(out=AV[par][:], in_=psw[:, 128:192]), reads=[psw.b], writes=[AV[par].b])
                        yield
                    psu = self.psum_next()
                    P.op("pe", lambda e: e.matmul(psu[:, 0:128], lhsT=WT[:], rhs=Hb[:, j, :], start=True, stop=True),
                         reads=[WT.b, Hb.b], writes=[psu.b])
                    for par in range(2):
                        cs = slice(64 * par, 64 * par + 64)
                        P.op("pe", lambda e: e.matmul(psu[:, cs], lhsT=Tt[par][:], rhs=AV[par][:], start=False, stop=(par == 1), skip_group_check=True),
                             reads=[Tt[par].b, AV[par].b], writes=[psu.b])
                    P.op("act", lambda e: e.copy(out=U_[:], in_=psu[:, 0:128]), reads=[psu.b], writes=[U_.b])
                    yield
                    if CUT <= 8:
                        return
                    psy = self.psum_next()
                    P.op("pe", lambda e: e.matmul(psy[:, 0:128], lhsT=rh[:], rhs=Hb[:, j, :], start=True, stop=True),
                         reads=[rh.b, Hb.b], writes=[psy.b])
                    for par in range(2):
                        hc = slice((2 * j + par) * 64, (2 * j + par + 1) * 64)
                        cs = slice(64 * par, 64 * par + 64)
                        P.op("pe", lambda e: e.matmul(psy[:, cs], lhsT=SC[par][:, 1, :], rhs=U_[:, cs], start=False, stop=False, skip_group_check=True),
                             reads=[SC[par].b, U_.b], writes=[psy.b])
                        P.op("pe", lambda e: e.matmul(psy[:, cs], lhsT=SC[par][:, 3, :], rhs=Vb[:, hc], start=False, stop=(par == 1), skip_group_check=True),
                             reads=[SC[par].b, Vb.b], writes=[psy.b])
                    P.op("act", lambda e: e.copy(out=Yt[:, js], in_=psy[:, 0:128]), reads=[psy.b], writes=[Yt.b])
                    psh = self.psum_next()
                    P.op("pe", lambda e: e.matmul(psh[:, 0:128], lhsT=T3[:, 2, :], rhs=Vb[:, js], start=True, stop=False),
                         reads=[T3.b, Vb.b], writes=[psh.b])
                    P.op("pe", lambda e: e.matmul(psh[:, 0:128], lhsT=T3[:, 1, :], rhs=U_[:], start=False, stop=True),
                         reads=[T3.b, U_.b], writes=[psh.b])
                    for par in range(2):
                        rows = slice(64 * par, 64 * par + 64)
                        P.op("dve", lambda e: e.scalar_tensor_tensor(out=Hbd[rows, j, rows], in0=Hbd[rows, j, rows], scalar=eL[rows, 127:128],
                                                                     in1=psh[rows, rows], op0=ALU.mult, op1=ALU.add),
                             reads=[Hbd.b, eL.b, psh.b], writes=[Hbd.b])
                        P.op("pool", lambda e: e.tensor_copy(out=Hb[rows, j, rows], in_=Hbd[rows, j, rows]), reads=[Hbd.b], writes=[Hb.b])
                    yield
                    if CUT <= 9:
                        return

                STAG = self.cfg.get("stagger", 9)
                pending = list(range(8))
                free_slots = list(range(NSLOT))
                active = []
                tick = 0
                next_start = 0
                while pending or active:
                    if pending and free_slots and tick >= next_start:
                        jn = pending.pop(0)
                        sl = free_slots.pop(0)
                        active.append((pair_gen(jn, slots[sl]), sl))
                        next_start = tick + STAG
                    for item in list(active):
                        try:
                            next(item[0])
                        except StopIteration:
                            active.remove(item)
                            free_slots.append(item[1])
                    tick += 1

                if "rw_o" in self.dbg:
                    P.dma("act", lambda e, t0=t0: e.dma_start(out=self.dbg["rw_o"][t0:t0 + CH, :], in_=Yt[:]), Yt.b, reads=[Yt.b])
                Y3 = Yt[:].rearrange("p (h n) -> p h n", n=64)
                S1t = tmpm[0][:].rearrange("p a t -> p (a t)")
                S1b = tmpm[0].b
                S2t = tmpm[1][:].rearrange("p a t -> p (a t)")
                S2b = tmpm[1].b
                P.op("act", lambda e: e.copy(out=rkb[:], in_=psB[:, 0:16]), reads=[psB.b], writes=[rkb.b])
                P.op("dve", lambda e: e.tensor_reduce(out=st[:, 0, :], in_=Y3, axis=AX.X, op=ALU.add), reads=[Yt.b], writes=[st.b])
                P.op("pool", lambda e: e.tensor_tensor(out=S1t, in0=Yt[:], in1=Yt[:], op=ALU.mult), reads=[Yt.b], writes=[S1b])
                P.op("dve", lambda e: e.tensor_reduce(out=st[:, 1, :], in_=S1t.rearrange("p (h n) -> p h n", n=64), axis=AX.X, op=ALU.add),
                     reads=[S1b], writes=[st.b])
                P.op("dve", lambda e: e.tensor_scalar(out=st[:, 2, :], in0=st[:, 0, :], scalar1=1.0 / 64, scalar2=None, op0=ALU.mult), reads=[st.b], writes=[st.b])
                P.op("dve", lambda e: e.tensor_tensor(out=st[:, 3, :], in0=st[:, 2, :], in1=st[:, 2, :], op=ALU.mult), reads=[st.b], writes=[st.b])
                P.op("dve", lambda e: e.scalar_tensor_tensor(out=st[:, 4, :], in0=st[:, 1, :], scalar=1.0 / 64, in1=st[:, 3, :], op0=ALU.mult, op1=ALU.subtract),
                     reads=[st.b], writes=[st.b])
                P.op("act", lambda e: e.activation(out=st[:, 5, :], in_=st[:, 4, :], func=AF.Sqrt, bias=eps2[:], scale=1.0), reads=[st.b, eps2.b], writes=[st.b])
                P.op("dve", lambda e: e.reciprocal(out=st[:, 5, :], in_=st[:, 5, :]), reads=[st.b], writes=[st.b])
                P.op("pool", lambda e: e.tensor_tensor(out=S1t.rearrange("p (h n) -> p h n", n=64), in0=Y3, in1=st[:, 2, :].unsqueeze(2).to_broadcast([128, 16, 64]), op=ALU.subtract),
                     reads=[Yt.b, st.b], writes=[S1b])
                P.op("dve", lambda e: e.tensor_tensor(out=S1t.rearrange("p (h n) -> p h n", n=64), in0=S1t.rearrange("p (h n) -> p h n", n=64),
                                                      in1=st[:, 5, :].unsqueeze(2).to_broadcast([128, 16, 64]), op=ALU.mult),
                     reads=[S1b, st.b], writes=[S1b])
                P.op("pool", lambda e: e.tensor_tensor(out=S1t, in0=S1t, in1=lnw[:], op=ALU.mult), reads=[S1b, lnw.b], writes=[S1b])
                P.op("dve", lambda e: e.tensor_tensor(out=S1t, in0=S1t, in1=lnb[:], op=ALU.add), reads=[S1b, lnb.b], writes=[S1b])
                P.op("pool", lambda e: e.tensor_tensor(out=S2t.rearrange("p (h n) -> p h n", n=64), in0=V[:].rearrange("p (h n) -> p h n", n=64),
                                                       in1=rkb[:].unsqueeze(2).to_broadcast([128, 16, 64]), op=ALU.mult),
                     reads=[V.b, rkb.b], writes=[S2b])
                P.op("dve", lambda e: e.tensor_tensor(out=S1t, in0=S1t, in1=S2t, op=ALU.add), reads=[S1b, S2b], writes=[S1b])
                for half in range(2):
                    ps = self.psum_next()
                    hs = slice(half * 512, (half + 1) * 512)
                    P.op("pe", lambda e: e.matmul(ps[:, :], lhsT=lgt[:, 0, :], rhs=g2b[:, 0, hs], start=True, stop=False),
                         reads=[lgt.b, g2b.b], writes=[ps.b])
                    P.op("pe", lambda e: e.matmul(ps[:, :], lhsT=lgt[0:32, 1, :], rhs=g2b[0:32, 1, hs], start=False, stop=True),
                         reads=[lgt.b, g2b.b], writes=[ps.b])
                    P.op("dve", lambda e: e.tensor_tensor(out=ogb[:, hs], in0=S1t[:, hs], in1=ps[:, :], op=ALU.mult), reads=[S1b, ps.b], writes=[ogb.b])
                for jj in range(8):
                    P.op("pe", lambda e, jj=jj: e.transpose(out=self.psbf[:, jj * 128:(jj + 1) * 128], in_=ogb[:, jj * 128:(jj + 1) * 128], identity=self.identb[:]),
                         reads=[ogb.b, self.identb.b], writes=[self.psbf.b])
                P.op("act", lambda e: e.copy(out=ogT[:].rearrange("p a t -> p (a t)"), in_=self.psbf[:, :]), reads=[self.psbf.b], writes=[ogT.b])
                for half in range(2):
                    ps = self.psum_next()
                    for jj in range(4):
                        nj = half * 4 + jj
                        for kc in range(8):
                            P.op("pe", lambda e, ps=ps, jj=jj, nj=nj, kc=kc: e.matmul(ps[:, jj * 128:(jj + 1) * 128], lhsT=Wo[:, kc, nj * 128:(nj + 1) * 128], rhs=ogT[:, kc, :],
                                                                                    start=(kc == 0), stop=(kc == 7)), reads=[Wo.b, ogT.b], writes=[ps.b])
                    for jj in range(4):
                        nj = half * 4 + jj
                        P.op("dve", lambda e, ps=ps, jj=jj, nj=nj: e.scalar_tensor_tensor(out=x[:, nj, :], in0=ps[:, jj * 128:(jj + 1) * 128], scalar=g1c[:, nj:nj + 1],
                                                                                         in1=x[:, nj, :], op0=ALU.mult, op1=ALU.add),
                             reads=[ps.b, x.b, self.modc.b], writes=[x.b])
                dst = self.X.rearrange("(j p) t -> p j t", p=128)[:, :, t0:t0 + CH]
                P.dma("sp", lambda e, dst=dst: e.dma_start(out=dst, in_=x[:]), x.b, reads=[x.b])
            self.end_phase()

    def rope_proj(self, es, W, hb, ncol0, dst_blk, CTt, STt, tmpA, tmpB):
        P = self.P
        roper = self.cst["roper"]
        tmpAs, tmpBs = tmpA, tmpB
        for hh in range(8):
            tmpA = tmpAs[hh % len(tmpAs)]
            tmpB = tmpBs[hh % len(tmpBs)]
            ps = self.psum_next()
            for kc in range(8):
                P.op("pe", lambda e: e.matmul(ps[:, :], lhsT=W[:, kc, ncol0 + hh * 128:ncol0 + (hh + 1) * 128], rhs=hb[:, kc, :],
                                              start=(kc == 0), stop=(kc == 7)), reads=[W.b, hb.b], writes=[ps.b])
            P.op("act", lambda e: e.copy(out=tmpA[:], in_=ps[:, :]), reads=[ps.b], writes=[tmpA.b])
            ps2 = self.psum_next()
            P.op("pe", lambda e: e.matmul(ps2[:, :], lhsT=roper[:], rhs=tmpA[:], start=True, stop=True), reads=[roper.b, tmpA.b], writes=[ps2.b])
            P.op("dve", lambda e: e.tensor_tensor(out=tmpB[:], in0=ps2[:, :], in1=STt[:], op=ALU.mult), reads=[ps2.b, STt.b], writes=[tmpB.b])
            P.op("pool", lambda e: e.tensor_tensor(out=tmpA[:], in0=tmpA[:], in1=CTt[:], op=ALU.mult), reads=[tmpA.b, CTt.b], writes=[tmpA.b])
            P.op("pool", lambda e: e.tensor_tensor(out=dst_blk[:, hh, :], in0=tmpA[:], in1=tmpB[:], op=ALU.add), reads=[tmpA.b, tmpB.b], writes=[dst_blk.b])

    def phase_kv(self, xsrc):
        P, nc, inp = self.P, self.nc, self.inp
        TB = 512
        with ExitStack() as es:
            self._stg = None
            Wkv = self.tile(es, "Wkv", [128, 8, 2 * D], BF16)
            s3 = inp["w_kv"].rearrange("(kc p) n -> p kc n", p=128)
            pieces = []
            for kc in range(8):
                for hf in range(2):
                    pieces.append((Wkv[:, kc:kc + 1, hf * D:(hf + 1) * D], s3[:, kc:kc + 1, hf * D:(hf + 1) * D], 128, 1, D))
            self.load_cast(es, pieces, Wkv.b)
            xs_ = [self.tile(es, "kx%d" % i, [128, 8, TB], dma=True) for i in range(2)]
            sq = self.tile(es, "ksq", [128, 8, TB])
            self.rstd = self.tile(es, "krstd", [128, TB])
            hbs_ = [self.tile(es, "khb%d" % i, [128, 8, TB], BF16) for i in range(2)]
            CTt = self.tile(es, "kCT", [128, TB], dma=True)
            STt = self.tile(es, "kST", [128, TB], dma=True)
            tmpA = [self.tile(es, "ktA%d" % i, [128, TB]) for i in range(3)]
            tmpB = [self.tile(es, "ktB%d" % i, [128, TB]) for i in range(3)]
            Kblks = [self.tile(es, "Kblk%d" % i, [128, 8, TB], BF16, dma=True) for i in range(2)]
            Vblks = [self.tile(es, "Vblk%d" % i, [128, 4, D], BF16, dma=True) for i in range(2)]
            G = self.col("kv_norm", 0, 8)
            for nb in range(T // TB):
                t0 = nb * TB
                x, hb, Kblk, Vblk = xs_[nb % 2], hbs_[nb % 2], Kblks[nb % 2], Vblks[nb % 2]
                src = xsrc.rearrange("(j p) t -> p j t", p=128)[:, :, t0:t0 + TB]
                P.dma("sp", lambda e: e.dma_start(out=x[:], in_=src), x.b, writes=[x.b])
                P.dma("act", lambda e: e.dma_start(out=CTt[:], in_=inp["ropec"][:, t0:t0 + TB]), CTt.b, writes=[CTt.b])
                P.dma("act", lambda e: e.dma_start(out=STt[:], in_=inp["ropes"][:, t0:t0 + TB]), STt.b, writes=[STt.b])
                self.rmsnorm(x, TB, sq, G, None, hb[:], hb.b)
                self.rope_proj(es, Wkv, hb, 0, Kblk, CTt, STt, tmpA, tmpB)
                dst = self.KT.rearrange("(j p) t -> p j t", p=128)[:, :, t0:t0 + TB]
                P.dma("sp", lambda e: e.dma_start(out=dst, in_=Kblk[:]), Kblk.b, reads=[Kblk.b])
                for tl in range(4):
                    for half in range(2):
                        ps = self.psum_next()
                        for kc in range(8):
                            P.op("pe", lambda e: e.matmul(ps[:, :], lhsT=hb[:, kc, tl * 128:(tl + 1) * 128], rhs=Wkv[:, kc, D + half * 512:D + (half + 1) * 512],
                                                          start=(kc == 0), stop=(kc == 7)), reads=[hb.b, Wkv.b], writes=[ps.b])
                        self.copy(("act", "dve")[half], Vblk[:, tl, half * 512:(half + 1) * 512], ps[:, :], [ps.b], [Vblk.b])
                dstv = self.VS[t0:t0 + TB, :].rearrange("(a p) e -> p a e", p=128)
                P.dma("sp", lambda e: e.dma_start(out=dstv, in_=Vblk[:]), Vblk.b, reads=[Vblk.b])
            self.end_phase()

    def phase_attn(self, l, xsrc):
        P, nc, inp = self.P, self.nc, self.inp
        jl = l - 2
        TB = 512
        lam_init = 0.8 - 0.6 * math.exp(-0.3 * l)
        G1 = self.der[:, l, 0, :]
        S1 = self.modcol(l, 0)
        g1c = self.modcol(l, 2)
        with ExitStack() as es:
            self._stg = None
            Wq = self.tile(es, "Wq", [128, 8, D], BF16)
            s3 = inp["b_w_q"][jl].rearrange("(kc p) n -> p kc n", p=128)
            self.load_cast(es, [(Wq[:, kc:kc + 1, :], s3[:, kc:kc + 1, :], 128, 1, D) for kc in range(8)], Wq.b)
            xs_ = [self.tile(es, "qx%d" % i, [128, 8, TB], dma=True) for i in range(2)]
            sq = self.tile(es, "qsq", [128, 8, TB])
            self.rstd = self.tile(es, "qrstd", [128, TB])
            hbs_ = [self.tile(es, "qhb%d" % i, [128, 8, TB], BF16) for i in range(2)]
            CTt = self.tile(es, "qCT", [128, TB], dma=True)
            STt = self.tile(es, "qST", [128, TB], dma=True)
            tmpA = [self.tile(es, "qtA%d" % i, [128, TB]) for i in range(3)]
            tmpB = [self.tile(es, "qtB%d" % i, [128, TB]) for i in range(3)]
            Qblks = [self.tile(es, "Qblk%d" % i, [128, 8, TB], BF16, dma=True) for i in range(2)]
            for nb in range(T // TB):
                t0 = nb * TB
                x, hb, Qblk = xs_[nb % 2], hbs_[nb % 2], Qblks[nb % 2]
                src = xsrc.rearrange("(j p) t -> p j t", p=128)[:, :, t0:t0 + TB]
                P.dma("sp", lambda e: e.dma_start(out=x[:], in_=src), x.b, writes=[x.b])
                P.dma("act", lambda e: e.dma_start(out=CTt[:], in_=inp["ropec"][:, t0:t0 + TB]), CTt.b, writes=[CTt.b])
                P.dma("act", lambda e: e.dma_start(out=STt[:], in_=inp["ropes"][:, t0:t0 + TB]), STt.b, writes=[STt.b])
                self.rmsnorm(x, TB, sq, G1, S1, hb[:], hb.b)
                self.rope_proj(es, Wq, hb, 0, Qblk, CTt, STt, tmpA, tmpB)
                dst = self.QT.rearrange("(j p) t -> p j t", p=128)[:, :, t0:t0 + TB]
                P.dma("sp", lambda e: e.dma_start(out=dst, in_=Qblk[:]), Qblk.b, reads=[Qblk.b])
            self.end_phase()

        with ExitStack() as es:
            NH = self.cfg.get("nheads", 8)
            NQB = self.cfg.get("nqb", T // TB)
            lamv = self.tile(es, "lamv", [128, 256], dma=True)
            o_l = 7 * D + jl * 256
            P.dma("sp", lambda e: e.dma_start(out=lamv[:], in_=inp["rows"][o_l:o_l + 256].partition_broadcast(128)), lamv.b, writes=[lamv.b])
            subw = self.tile(es, "subw", [128, 128], dma=True)
            o_s = 5 * D + jl * D
            P.dma("sp", lambda e: e.dma_start(out=subw[:], in_=inp["rows"][o_s:o_s + 128].partition_broadcast(128)), subw.b, writes=[subw.b])
            P.op("dve", lambda e: e.tensor_scalar(out=subw[:], in0=subw[:], scalar1=(1.0 - lam_init), scalar2=None, op0=ALU.mult), reads=[subw.b], writes=[subw.b])
            lt = self.tile(es, "lt", [128, 2, 64])
            ls = self.tile(es, "ls", [128, 4])
            P.op("dve", lambda e: e.tensor_tensor(out=lt[:, 0, :], in0=lamv[:, 0:64], in1=lamv[:, 64:128], op=ALU.mult), reads=[lamv.b], writes=[lt.b])
            P.op("dve", lambda e: e.tensor_tensor(out=lt[:, 1, :], in0=lamv[:, 128:192], in1=lamv[:, 192:256], op=ALU.mult), reads=[lamv.b], writes=[lt.b])
            P.op("dve", lambda e: e.tensor_reduce(out=ls[:, 0:2], in_=lt[:], axis=AX.X, op=ALU.add), reads=[lt.b], writes=[ls.b])
            P.op("act", lambda e: e.activation(out=ls[:, 0:2], in_=ls[:, 0:2], func=AF.Exp), reads=[ls.b], writes=[ls.b])
            P.op("dve", lambda e: e.tensor_tensor(out=ls[:, 2:3], in0=ls[:, 1:2], in1=ls[:, 0:1], op=ALU.subtract), reads=[ls.b], writes=[ls.b])
            P.op("dve", lambda e: e.tensor_scalar(out=ls[:, 3:4], in0=ls[:, 2:3], scalar1=-lam_init, scalar2=None, op0=ALU.add), reads=[ls.b], writes=[ls.b])
            neglam = ls[:, 3:4]
            cmaskb = self.tile(es, "cmaskb", [128, 128], BF16)
            self.copy("pool", cmaskb[:], self.cst["mask4"][:, 128:256], [self.cst["mask4"].b], [cmaskb.b])
            eps1 = self.eps

            KTh = [self.tile(es, "KTh%d" % i, [128, T], BF16, dma=True) for i in range(2)]
            QTh = [self.tile(es, "QTh%d" % i, [128, T], BF16, dma=True) for i in range(2)]
            Vh = [self.tile(es, "Vh%d" % i, [128, 32, 129], BF16, dma=True) for i in range(2)]
            YTh = [self.tile(es, "YTh%d" % i, [128, T], BF16, dma=True) for i in range(2)]
            for i in range(2):
                P.op("pool", lambda e: e.memset(Vh[i][:, :, 128:129], 1.0), writes=[Vh[i].b])
            ET = [self.tile(es, "ET%d" % i, [128, 512], BF16) for i in range(4)]
            Oc = [self.tile(es, "Oc%d" % i, [128, 4, 129]) for i in range(2)]
            rz = self.tile(es, "rz", [128, 2, 4])
            y = self.tile(es, "ay", [128, 4, 128])
            ysq = self.tile(es, "aysq", [128, 4, 128])
            ss = self.tile(es, "ass", [128, 4])
            ynb = self.tile(es, "aynb", [128, 4, 128], BF16)
            if "at_o" in self.dbg:
                self.dbg_tile = self.tile(es, "dbgt", [128, 4, 128], dma=True)
            eti = 0
            for hh in range(NH):
                kt_, qt_, vh_, yt_ = KTh[hh % 2], QTh[hh % 2], Vh[hh % 2], YTh[hh % 2]
                hs = slice(hh * 128, (hh + 1) * 128)
                P.dma("sp", lambda e: e.dma_start(out=kt_[:], in_=self.KT[hs, :]), kt_.b, writes=[kt_.b])
                P.dma("act", lambda e: e.dma_start(out=qt_[:], in_=self.QT[hs, :]), qt_.b, writes=[qt_.b])
                P.dma("sp", lambda e: e.dma_start(out=vh_[:, :, 0:128], in_=self.VS.rearrange("(kt p) e -> p kt e", p=128)[:, :, hs]), vh_.b, writes=[vh_.b])
                sbanks = [self.ps[0], self.ps[1], self.ps[6]]
                tasks = []
                for qb in range(NQB):
                    for cc in range(2):
                        for kt in range(4 * qb + 4):
                            tasks.append((qb, cc, kt))

                def score(ti):
                    qb, cc, kt = tasks[ti]
                    rows = slice(64 * cc, 64 * cc + 64)
                    c0 = max(kt - 4 * qb, 0) * 128
                    pS = sbanks[ti % 3]
                    P.op("pe", lambda e: e.matmul(pS[:, c0:512], lhsT=kt_[rows, kt * 128:(kt + 1) * 128], rhs=qt_[rows, qb * 512 + c0:(qb + 1) * 512],
                                                  start=True, stop=True), reads=[kt_.b, qt_.b], writes=[pS.b])

                def combine(qb):
                    qs = slice(qb * 512, (qb + 1) * 512)
                    P.op("dve", lambda e: e.reciprocal(out=rz[:, 0, :], in_=Oc[0][:, :, 128]), reads=[Oc[0].b], writes=[rz.b])
                    P.op("dve", lambda e: e.reciprocal(out=rz[:, 1, :], in_=Oc[1][:, :, 128]), reads=[Oc[1].b], writes=[rz.b])
                    P.op("dve", lambda e: e.tensor_scalar(out=rz[:, 1, :], in0=rz[:, 1, :], scalar1=neglam, scalar2=None, op0=ALU.mult), reads=[rz.b, ls.b], writes=[rz.b])
                    P.op("pool", lambda e: e.tensor_tensor(out=y[:], in0=Oc[0][:, :, 0:128], in1=rz[:, 0, :].unsqueeze(2).to_broadcast([128, 4, 128]), op=ALU.mult),
                         reads=[Oc[0].b, rz.b], writes=[y.b])
                    P.op("pool", lambda e: e.tensor_tensor(out=ysq[:], in0=Oc[1][:, :, 0:128], in1=rz[:, 1, :].unsqueeze(2).to_broadcast([128, 4, 128]), op=ALU.mult),
                         reads=[Oc[1].b, rz.b], writes=[ysq.b])
                    P.op("dve", lambda e: e.tensor_tensor(out=y[:], in0=y[:], in1=ysq[:], op=ALU.add), reads=[y.b, ysq.b], writes=[y.b])
                    if "at_o" in self.dbg:
                        dtl = self.dbg_tile
                        self.copy("dve", dtl[:], y[:], [y.b], [dtl.b])
                        dd = self.dbg["at_o"][qb * 512:(qb + 1) * 512, hs].rearrange("(a p) e -> p a e", p=128)
                        P.dma("act", lambda e: e.dma_start(out=dd, in_=dtl[:]), dtl.b, reads=[dtl.b])
                    P.op("pool", lambda e: e.tensor_tensor(out=ysq[:], in0=y[:], in1=y[:], op=ALU.mult), reads=[y.b], writes=[ysq.b])
                    P.op("dve", lambda e: e.tensor_reduce(out=ss[:], in_=ysq[:], axis=AX.X, op=ALU.add), reads=[ysq.b], writes=[ss.b])
                    P.op("act", lambda e: e.activation(out=ss[:], in_=ss[:], func=AF.Sqrt, bias=eps1[:], scale=1.0 / 128), reads=[ss.b, eps1.b], writes=[ss.b])
                    P.op("dve", lambda e: e.reciprocal(out=ss[:], in_=ss[:]), reads=[ss.b], writes=[ss.b])
                    P.op("pool", lambda e: e.tensor_tensor(out=y[:], in0=y[:], in1=ss[:].unsqueeze(2).to_broadcast([128, 4, 128]), op=ALU.mult),
                         reads=[y.b, ss.b], writes=[y.b])
                    P.op("dve", lambda e: e.tensor_tensor(out=ynb[:], in0=y[:], in1=subw[:].unsqueeze(1).to_broadcast([128, 4, 128]), op=ALU.mult),
                         reads=[y.b, subw.b], writes=[ynb.b])
                    for qt in range(4):
                        P.op("pe", lambda e: e.transpose(out=self.psbf[:, qt * 128:(qt + 1) * 128], in_=ynb[:, qt, :], identity=self.identb[:]),
                             reads=[ynb.b, self.identb.b], writes=[self.psbf.b])
                    P.op("act", lambda e: e.copy(out=yt_[:, qs], in_=self.psbf[:, 0:512]), reads=[self.psbf.b], writes=[yt_.b])

                LOOK = 2
                for ti in range(min(LOOK, len(tasks))):
                    score(ti)
                for ti in range(len(tasks)):
                    if ti + LOOK < len(tasks):
                        score(ti + LOOK)
                    qb, cc, kt = tasks[ti]
                    r = kt - 4 * qb
                    c0 = max(r, 0) * 128
                    pS = sbanks[ti % 3]
                    pO = [self.ps[2 + 2 * cc], self.ps[3 + 2 * cc]]
                    et = ET[ti % 4]
                    P.op("act", lambda e: e.activation(out=et[:, c0:512], in_=pS[:, c0:512], func=AF.Exp, scale=0.125), reads=[pS.b], writes=[et.b])
                    if r >= 0:
                        P.op("pool", lambda e: e.tensor_tensor(out=et[:, c0:c0 + 128], in0=et[:, c0:c0 + 128], in1=cmaskb[:], op=ALU.mult),
                             reads=[et.b, cmaskb.b], writes=[et.b])
                    for qt in range(max(r, 0), 4):
                        po = pO[qt // 2]
                        oc = (qt % 2) * 129
                        P.op("pe", lambda e: e.matmul(po[:, oc:oc + 129], lhsT=et[:, qt * 128:(qt + 1) * 128], rhs=vh_[:, kt, :],
                                                      start=(kt == 0 and qt % 2 == 0), stop=(kt == 4 * qb + qt), skip_group_check=True),
                             reads=[et.b, vh_.b], writes=[po.b])
                    if kt == 4 * qb + 3:
                        for i2 in range(2):
                            self.copy(("act", "dve")[i2], Oc[cc][:, 2 * i2:2 * i2 + 2, :].rearrange("p a e -> p (a e)"), pO[i2][:, 0:258], [pO[i2].b], [Oc[cc].b])
                        if cc == 1:
                            combine(qb)
                P.dma("sp", lambda e: e.dma_start(out=self.YT[hs, :], in_=yt_[:]), yt_.b, reads=[yt_.b])
            self.end_phase()

        with ExitStack() as es:
            self._stg = None
            Wo = self.tile(es, "aWo", [128, 8, D], BF16)
            s3 = inp["b_w_o"][jl].rearrange("(kc p) n -> p kc n", p=128)
            self.load_cast(es, [(Wo[:, kc:kc + 1, :], s3[:, kc:kc + 1, :], 128, 1, D) for kc in range(8)], Wo.b)
            xs = [self.tile(es, "cx%d" % i, [128, 8, TB], dma=True) for i in range(2)]
            ys = [self.tile(es, "cy%d" % i, [128, 8, TB], BF16, dma=True) for i in range(2)]
            for nb in range(T // TB):
                t0 = nb * TB
                x = xs[nb % 2]
                yb = ys[nb % 2]
                src = xsrc.rearrange("(j p) t -> p j t", p=128)[:, :, t0:t0 + TB]
                P.dma("sp", lambda e: e.dma_start(out=x[:], in_=src), x.b, writes=[x.b])
                srcy = self.YT.rearrange("(j p) t -> p j t", p=128)[:, :, t0:t0 + TB]
                P.dma("act", lambda e: e.dma_start(out=yb[:], in_=srcy), yb.b, writes=[yb.b])
                for nj in range(8):
                    ps = self.psum_next()
                    for kc in range(8):
                        P.op("pe", lambda e: e.matmul(ps[:, :], lhsT=Wo[:, kc, nj * 128:(nj + 1) * 128], rhs=yb[:, kc, :], start=(kc == 0), stop=(kc == 7)),
                             reads=[Wo.b, yb.b], writes=[ps.b])
                    P.op("dve", lambda e: e.scalar_tensor_tensor(out=x[:, nj, :], in0=ps[:, :], scalar=g1c[:, nj:nj + 1], in1=x[:, nj, :], op0=ALU.mult, op1=ALU.add),
                         reads=[ps.b, x.b, self.modc.b], writes=[x.b])
                dst = self.X.rearrange("(j p) t -> p j t", p=128)[:, :, t0:t0 + TB]
                P.dma("sp", lambda e: e.dma_start(out=dst, in_=x[:]), x.b, reads=[x.b])
            self.end_phase()


_CONSTS = None


def prepare_inputs(inputs):
    global _CONSTS
    if _CONSTS is None:
        _CONSTS = make_consts()
    f = lambda a: np.ascontiguousarray(np.asarray(a, np.float32))
    vecs = {}
    for l in range(4):
        vecs["ada_b%d" % l] = inputs["ada_b"][l]
        vecs["norm1_%d" % l] = inputs["norm1"][l]
        vecs["norm2_%d" % l] = inputs["norm2"][l]
        for i in range(3):
            vecs["cw%d_%d" % (i, l)] = inputs["ffn_conv_w"][l][i]
        vecs["cb_%d" % l] = inputs["ffn_conv_b"][l]
    vecs["final_norm"] = inputs["final_norm"]
    vecs["kv_norm"] = inputs["kv_norm"]
    for l in range(2):
        for i in range(6):
            vecs["mu%d_%d" % (i, l)] = inputs["a_mu"][l][i]
        vecs["w0_%d" % l] = inputs["a_w0"][l]
        vecs["a0_%d" % l] = inputs["a_a0"][l]
        vecs["k_k_%d" % l] = inputs["a_k_k"][l]
        vecs["k_a_%d" % l] = inputs["a_k_a"][l]
        vecs["r_k_%d" % l] = np.asarray(inputs["a_r_k"][l]).reshape(-1)
    cols = CP.pack(vecs)
    rows = np.concatenate([
        f(inputs["a_ln_w"][0]), f(inputs["a_ln_b"][0]), f(inputs["a_ln_w"][1]), f(inputs["a_ln_b"][1]),
        f(inputs["a_v0"][0]),
        np.tile(f(inputs["b_subln"][0]), 8), np.tile(f(inputs["b_subln"][1]), 8),
        f(inputs["b_lam"][0]).reshape(-1), f(inputs["b_lam"][1]).reshape(-1)])
    shared = dict(_CONSTS)
    shared["cols"] = cols
    shared["rows"] = rows
    for k in ("ada_w", "a_w_rkv", "a_w1", "a_w2", "a_a1", "a_a2", "a_v1", "a_v2", "a_g1", "a_g2", "a_w_o",
              "w_kv", "b_w_q", "b_w_o", "ffn_w_up", "ffn_w_down"):
        shared[k] = f(inputs[k])
    x = np.asarray(inputs["x"], np.float32)
    c = np.asarray(inputs["c"], np.float32)
    in_maps = []
    for b in range(NCORES):
        m = dict(shared)
        m["xT"] = np.ascontiguousarray(x[b].T)
        m["ccol"] = np.ascontiguousarray(c[b].reshape(8, 128).T)
        in_maps.append(m)
    return in_maps


_NC_CACHE = {}


def run(inputs, cfg, key="full", ncores=NCORES):
    if key not in _NC_CACHE:
        _NC_CACHE[key] = Builder(cfg).build()
    nc = _NC_CACHE[key]
    in_maps = prepare_inputs(inputs)[:ncores]
    res = run_bass_kernel_spmd(nc, in_maps, core_ids=list(range(ncores)))
    return res


def kernel(**inputs):
    cfg = {"layers": [0, 1, 2, 3]}
    res = run(inputs, cfg)
    out = np.stack([np.ascontiguousarray(r["outT"].T) for r in res.results], axis=0)
    return out.astype(np.float32)
```

```python
import math
import numpy as np
import concourse.bass as bass
import concourse.mybir as mybir
from concourse.bass_utils import run_bass_kernel_spmd
from contextlib import ExitStack
import types

F32 = mybir.dt.float32
BF16 = mybir.dt.bfloat16
AF = mybir.ActivationFunctionType
ALU = mybir.AluOpType
AX = mybir.AxisListType

D = 1024
T = 4096
NJ = 8
DFF = 2816
F2 = 5632
NF = 44
NG = 22
C0 = math.exp(-0.5)
NCORES = 8

ENGS = ["pe", "act", "dve", "pool", "sp"]


def freeze(fn):
    if fn.__closure__ is None:
        return fn
    cells = []
    for c in fn.__closure__:
        try:
            cells.append(types.CellType(c.cell_contents))
        except ValueError:
            cells.append(c)
    return types.FunctionType(fn.__code__, fn.__globals__, fn.__name__, fn.__defaults__, tuple(cells))


class Buf:
    __slots__ = ("name", "lw", "rd", "dsem", "excl")

    def __init__(self, name):
        self.name = name
        self.lw = None
        self.rd = {}
        self.dsem = None
        self.excl = False


class Tl:
    __slots__ = ("ap", "b")

    def __init__(self, ap, b):
        self.ap = ap
        self.b = b

    def __getitem__(self, k):
        return self.ap[k]


class Prog:
    def __init__(self, nc, es, n_dma_sems=40):
        self.nc = nc
        self.es = es
        self.q = {e: [] for e in ENGS}
        self.cnt = {e: 0 for e in ENGS}
        self.sems = {}
        self.semkey = 0
        self.esem = {e: self._newsem("c_" + e) for e in ENGS}
        self.seen = {e: {} for e in ENGS}
        self.bar = self._newsem("bar")
        self.nbar = 0
        self.dma_pool = [self._newsem("d%d" % i) for i in range(n_dma_sems)]
        self.dma_cnt = {k: 0 for k in self.dma_pool}
        self.dma_free = list(self.dma_pool)
        self.ninstr = 0

    def _newsem(self, name):
        s = self.es.enter_context(self.nc.semaphore(name))
        self.semkey += 1
        self.sems[self.semkey] = s
        return self.semkey

    def buf(self, name):
        return Buf(name)

    def dma_buf(self, name):
        b = Buf(name)
        b.dsem = self.dma_free.pop(0)
        return b

    def release(self, bufs):
        for b in bufs:
            if b.dsem is not None:
                self.dma_free.append(b.dsem)
                b.dsem = None

    def _waits(self, e, reads, writes, is_dma=False):
        need = {}
        seen = self.seen[e]

        def add(ev, raw):
            key, val, src = ev
            if src == e and not is_dma and (e in ("pe", "sp") or not raw):
                return
            if seen.get(key, 0) >= val:
                return
            if need.get(key, 0) < val:
                need[key] = val

        for b in reads:
            if b.lw is not None:
                add(b.lw, True)
            if b.excl:
                for src, ev in b.rd.items():
                    if src != e:
                        add(ev, False)
        for b in writes:
            if b.lw is not None:
                add(b.lw, False)
            for ev in b.rd.values():
                add(ev, False)
        out = []
        for key, val in need.items():
            seen[key] = val
            out.append((self.sems[key], val))
        return out

    def _emit(self, e, fn, waits, sem, inc):
        fn = freeze(fn)

        attach = (inc == 1 and e in ("act", "dve", "pool") and len(waits) > 0)

        def run(eng, fn=fn, waits=waits, sem=sem, inc=inc, attach=attach):
            for (s, v) in (waits[:-1] if attach else waits):
                eng.wait_ge(s, v)
            ins = fn(eng)
            if attach:
                ins._wait_ge(waits[-1][0], waits[-1][1])
            ins.then_inc(sem, inc)
        self.q[e].append(run)
        self.ninstr += 1 + len(waits)

    def op(self, e, fn, reads=(), writes=()):
        waits = self._waits(e, reads, writes)
        self.cnt[e] += 1
        key = self.esem[e]
        self._emit(e, fn, waits, self.sems[key], 1)
        ev = (key, self.cnt[e], e)
        for b in writes:
            b.lw = ev
            b.rd = {}
        for b in reads:
            if b not in writes:
                b.rd[e] = ev

    def dma(self, e, fn, owner, reads=(), writes=()):
        assert owner.dsem is not None, owner.name
        waits = self._waits(e, reads, writes, is_dma=True)
        key = owner.dsem
        self.dma_cnt[key] += 16
        self._emit(e, fn, waits, self.sems[key], 16)
        src = "dma%d" % key
        ev = (key, self.dma_cnt[key], src)
        for b in writes:
            b.lw = ev
            b.rd = {}
        for b in reads:
            if b not in writes:
                b.rd[src] = ev

    def barrier(self):
        g = "sp"
        gw = []
        for e in ENGS:
            if e == g or self.cnt[e] == 0:
                continue
            key = self.esem[e]
            if self.seen[g].get(key, 0) < self.cnt[e]:
                gw.append((self.sems[key], self.cnt[e]))
        for key in self.dma_pool:
            val = self.dma_cnt[key]
            if val > 0 and self.seen[g].get(key, 0) < val:
                gw.append((self.sems[key], val))
        self.nbar += 1
        bsem = self.sems[self.bar]

        def run_g(eng, gw=gw, bsem=bsem):
            for (s, v) in gw:
                eng.wait_ge(s, v)
            eng.sem_inc(bsem, 1)
        self.q[g].append(run_g)
        for e in ENGS:
            if e != g:
                self.q[e].append(lambda eng, sem=bsem, val=self.nbar: eng.wait_ge(sem, val))
        for e in ENGS:
            for e2 in ENGS:
                self.seen[e][self.esem[e2]] = self.cnt[e2]
            for key in self.dma_pool:
                self.seen[e][key] = self.dma_cnt[key]
        for e in ENGS:
            if self.cnt[e] > 12000:
                self.esem[e] = self._newsem("c_%s_%d" % (e, self.nbar))
                self.cnt[e] = 0

    def finish(self):
        nc = self.nc
        self.barrier()
        with nc.Block() as block:
            @block.tensor
            def _(eng):
                for f in self.q["pe"]:
                    f(eng)

            @block.scalar
            def _(eng):
                for f in self.q["act"]:
                    f(eng)

            @block.vector
            def _(eng):
                for f in self.q["dve"]:
                    f(eng)

            @block.gpsimd
            def _(eng):
                for f in self.q["pool"]:
                    f(eng)

            @block.sync
            def _(eng):
                for f in self.q["sp"]:
                    f(eng)


class ColPack:
    def __init__(self):
        self.off = {}
        self.n = 0
        self.items = []

    def add(self, name, length):
        assert length % 128 == 0
        self.off[name] = self.n
        self.n += length // 128
        self.items.append((name, length))

    def pack(self, vecs):
        arr = np.zeros((128, self.n), np.float32)
        for name, length in self.items:
            v = np.asarray(vecs[name], np.float32).reshape(length // 128, 128)
            arr[:, self.off[name]:self.off[name] + length // 128] = v.T
        return arr


def make_colpack():
    cp = ColPack()
    for l in range(4):
        cp.add("ada_b%d" % l, 6 * D)
        cp.add("norm1_%d" % l, D)
        cp.add("norm2_%d" % l, D)
        for i in range(3):
            cp.add("cw%d_%d" % (i, l), F2)
        cp.add("cb_%d" % l, F2)
    cp.add("final_norm", D)
    cp.add("kv_norm", D)
    for l in range(2):
        for i in range(6):
            cp.add("mu%d_%d" % (i, l), D)
        for nm in ("w0", "a0", "k_k", "k_a", "r_k"):
            cp.add("%s_%d" % (nm, l), D)
    return cp


CP = make_colpack()


def make_consts():
    c = {}
    c["ident"] = np.eye(128, dtype=np.float32)
    c["ones"] = np.ones((128, 128), np.float32)
    bd = np.zeros((128, 128), np.float32)
    bd[:64, :64] = 1
    bd[64:, 64:] = 1
    c["bd"] = bd
    ind = np.zeros((128, 8, 16), np.float32)
    for p in range(128):
        for j in range(8):
            ind[p, j, 2 * j + p // 64] = 1
    c["ind"] = ind.reshape(128, 128)
    s = np.arange(128)[:, None]
    t = np.arange(128)[None, :]
    strict = (t > s).astype(np.float32)
    incl = (t >= s).astype(np.float32)
    c["mask4"] = np.concatenate([strict, incl, strict, incl], axis=1)
    c["maskL"] = (t < s).astype(np.float32)
    pos = np.arange(T, dtype=np.float64)
    inv = 500000.0 ** (-np.arange(0, 16, 2, dtype=np.float64) / 16)
    ct = np.ones((128, T), np.float64)
    st = np.zeros((128, T), np.float64)
    rm = np.zeros((128, 128), np.float32)
    for cc in range(2):
        for d in range(16):
            p = cc * 64 + d
            ang = pos * inv[d % 8]
            ct[p] = np.cos(ang)
            st[p] = np.sin(ang)
            if d < 8:
                rm[p + 8, p] = -1.0
            else:
                rm[p - 8, p] = 1.0
    c["ropec"] = ct.astype(np.float32)
    c["ropes"] = st.astype(np.float32)
    c["roper"] = rm
    return c


class Builder:
    def __init__(self, cfg):
        self.cfg = cfg
        self.nc = bass.Bass("TRN2", target_bir_lowering=False)
        self.uid = 0
        self.rr = 0

    def dram_in(self, name, shape, dt=F32):
        return self.nc.dram_tensor(name, list(shape), dt, kind="ExternalInput").ap()

    def tile(self, es, name, shape, dt=F32, dma=False):
        self.uid += 1
        t = es.enter_context(self.nc.sbuf_tensor("%s_%d" % (name, self.uid), list(shape), dt))
        b = self.P.dma_buf(name) if dma else self.P.buf(name)
        if dma:
            self.phase_dma_bufs.append(b)
        return Tl(t, b)

    def eng_rr(self, engs=("dve", "pool", "act")):
        self.rr += 1
        return engs[self.rr % len(engs)]

    def copy(self, e, out_ap, in_ap, reads, writes):
        P = self.P
        if e == "act":
            P.op("act", lambda g: g.copy(out=out_ap, in_=in_ap), reads=reads, writes=writes)
        else:
            P.op(e, lambda g: g.tensor_copy(out=out_ap, in_=in_ap), reads=reads, writes=writes)

    def psum_next(self):
        self.psi = (self.psi + 1) % 6
        return self.ps[self.psi]

    def build(self):
        nc = self.nc
        cfg = self.cfg
        inp = {}
        inp["xT"] = self.dram_in("xT", [D, T])
        inp["ccol"] = self.dram_in("ccol", [128, 8])
        inp["cols"] = self.dram_in("cols", [128, CP.n])
        for k in ("ident", "ones", "bd", "ind", "maskL", "roper"):
            inp[k] = self.dram_in(k, [128, 128])
        inp["mask4"] = self.dram_in("mask4", [128, 512])
        inp["ropec"] = self.dram_in("ropec", [128, T])
        inp["ropes"] = self.dram_in("ropes", [128, T])
        inp["ada_w"] = self.dram_in("ada_w", [4, D, 6 * D])
        inp["a_w_rkv"] = self.dram_in("a_w_rkv", [2, 3, D, D])
        inp["a_w1"] = self.dram_in("a_w1", [2, D, 64])
        inp["a_w2"] = self.dram_in("a_w2", [2, 64, D])
        inp["a_a1"] = self.dram_in("a_a1", [2, D, 64])
        inp["a_a2"] = self.dram_in("a_a2", [2, 64, D])
        inp["a_v1"] = self.dram_in("a_v1", [1, D, 32])
        inp["a_v2"] = self.dram_in("a_v2", [1, 32, D])
        inp["a_g1"] = self.dram_in("a_g1", [2, D, 160])
        inp["a_g2"] = self.dram_in("a_g2", [2, 160, D])
        inp["a_w_o"] = self.dram_in("a_w_o", [2, D, D])
        inp["rows"] = self.dram_in("rows", [7 * D + 512])
        inp["w_kv"] = self.dram_in("w_kv", [D, 2 * D])
        inp["b_w_q"] = self.dram_in("b_w_q", [2, D, D])
        inp["b_w_o"] = self.dram_in("b_w_o", [2, D, D])
        inp["ffn_w_up"] = self.dram_in("ffn_w_up", [4, D, F2])
        inp["ffn_w_down"] = self.dram_in("ffn_w_down", [4, DFF, D])
        self.inp = inp
        self.outT = nc.dram_tensor("outT", [D, T], F32, kind="ExternalOutput").ap()
        self.X = nc.dram_tensor("Xs", [D, T], F32).ap()
        self.VF = nc.dram_tensor("VFs", [T, D], F32).ap()
        self.KT = nc.dram_tensor("KTs", [D, T], BF16).ap()
        self.VS = nc.dram_tensor("VSs", [T, D], BF16).ap()
        self.QT = nc.dram_tensor("QTs", [D, T], BF16).ap()
        self.YT = nc.dram_tensor("YTs", [D, T], BF16).ap()
        self.dbg = {}
        for name, shape in cfg.get("dbg", {}).items():
            self.dbg[name] = nc.dram_tensor("dbg_" + name, list(shape), F32, kind="ExternalOutput").ap()

        with ExitStack() as es:
            self.P = P = Prog(nc, es)
            self.phase_dma_bufs = []
            self.ps = []
            for i in range(7):
                t = es.enter_context(nc.psum_tensor("psb%d" % i, [128, 512], F32))
                self.ps.append(Tl(t, P.buf("psb%d" % i)))
                self.ps[-1].b.excl = True
            t = es.enter_context(nc.psum_tensor("psbf", [128, 1024], BF16))
            self.psbf = Tl(t, P.buf("psbf"))
            self.psbf.b.excl = True
            self.psi = 0
            g = es
            self.cols = self.tile(g, "cols", [128, CP.n], dma=True)
            self.ccol = self.tile(g, "ccol", [128, 8], dma=True)
            self.cst = {}
            for k in ("ident", "ones", "bd", "ind", "maskL", "roper"):
                self.cst[k] = self.tile(g, k, [128, 128], dma=True)
            self.cst["mask4"] = self.tile(g, "mask4", [128, 512], dma=True)
            P.dma("sp", lambda e: e.dma_start(out=self.cols[:], in_=inp["cols"]), self.cols.b, writes=[self.cols.b])
            P.dma("sp", lambda e: e.dma_start(out=self.ccol[:], in_=inp["ccol"]), self.ccol.b, writes=[self.ccol.b])
            for k, tl in self.cst.items():
                P.dma("act", lambda e, tl=tl, k=k: e.dma_start(out=tl[:], in_=inp[k]), tl.b, writes=[tl.b])
            self.eps = self.tile(g, "eps", [128, 1])
            P.op("pool", lambda e: e.memset(self.eps[:], 1e-6), writes=[self.eps.b])
            self.identb = self.tile(g, "identb", [128, 128], BF16)
            self.copy("pool", self.identb[:], self.cst["ident"][:], [self.cst["ident"].b], [self.identb.b])
            self.modc = self.tile(g, "modc", [128, 192])
            self.der = self.tile(g, "der", [128, 4, 2, 8])

            self.phase_mod()
            xsrc = inp["xT"]
            for l in cfg["layers"]:
                if cfg.get("mixer", True):
                    if l < 2:
                        self.phase_rwkv(l, xsrc)
                    else:
                        if l == 2 or cfg.get("force_kv", False):
                            self.phase_kv(xsrc)
                        self.phase_attn(l, xsrc)
                    xsrc = self.X
                if cfg.get("ffn", True):
                    self.phase_ffn(l, xsrc)
                    xsrc = self.X
            self.phase_final(xsrc, cfg.get("final", True))
            P.finish()
        return nc

    def col(self, name, j0=0, n=None):
        o = CP.off[name] + j0
        if n is None:
            n = 1
        return self.cols[:, o:o + n]

    def end_phase(self):
        self.P.barrier()
        self.P.release(self.phase_dma_bufs)
        self.phase_dma_bufs = []

    def phase_mod(self):
        P, nc, inp = self.P, self.nc, self.inp
        with ExitStack() as es:
            cact = self.tile(es, "cact", [128, 8])
            P.op("act", lambda e: e.activation(out=cact[:], in_=self.ccol[:], func=AF.Silu),
                 reads=[self.ccol.b], writes=[cact.b])
            A = [self.tile(es, "adaA%d" % i, [128, 8, 768], dma=True) for i in range(2)]
            psm = self.ps[0]
            it = 0
            for l in range(4):
                for blk in range(8):
                    a = A[it % 2]
                    it += 1
                    src = inp["ada_w"][l].rearrange("(kc p) n -> p kc n", p=128)[:, :, blk * 768:(blk + 1) * 768]
                    P.dma("sp" if it % 2 else "act", lambda e, a=a, src=src: e.dma_start(out=a[:], in_=src), a.b, writes=[a.b])
                    for n_ in range(6):
                        colidx = l * 48 + blk * 6 + n_
                        for kc in range(8):
                            P.op("pe", lambda e, a=a, kc=kc, n_=n_, colidx=colidx: e.matmul(
                                psm[:, colidx:colidx + 1], lhsT=a[:, kc, n_ * 128:(n_ + 1) * 128], rhs=cact[:, kc:kc + 1],
                                start=(kc == 0), stop=(kc == 7)), reads=[a.b, cact.b], writes=[psm.b])
            for l in range(4):
                o = CP.off["ada_b%d" % l]
                P.op("dve", lambda e, l=l, o=o: e.tensor_tensor(out=self.modc[:, l * 48:(l + 1) * 48], in0=psm[:, l * 48:(l + 1) * 48],
                                                                in1=self.cols[:, o:o + 48], op=ALU.add),
                     reads=[psm.b, self.cols.b], writes=[self.modc.b])
            for l in range(4):
                for which in range(2):
                    sc = self.modc[:, l * 48 + which * 24 + 8: l * 48 + which * 24 + 16]
                    nm = self.col("norm%d_%d" % (which + 1, l), 0, 8)
                    P.op("dve", lambda e, l=l, which=which, sc=sc, nm=nm: e.scalar_tensor_tensor(
                        out=self.der[:, l, which, :], in0=sc, scalar=1.0, in1=nm, op0=ALU.add, op1=ALU.mult),
                        reads=[self.modc.b, self.cols.b], writes=[self.der.b])
            if "modc" in self.dbg:
                dt = self.tile(es, "dbgm", [128, 192], dma=True)
                self.copy("dve", dt[:], self.modc[:], [self.modc.b], [dt.b])
                P.dma("sp", lambda e: e.dma_start(out=self.dbg["modc"], in_=dt[:]), dt.b, reads=[dt.b])
            self.end_phase()

    def modcol(self, l, i, j0=0, n=8):
        o = l * 48 + i * 8 + j0
        return self.modc[:, o:o + n]

    def rmsnorm(self, x, N, sq, G, S, out_ap, out_b, extra_reads=()):
        P = self.P
        ones = self.cst["ones"]
        P.op("act", lambda e: e.activation(out=sq[:], in_=x[:], func=AF.Square), reads=[x.b], writes=[sq.b])
        ps = self.psum_next()
        for j in range(8):
            P.op("pe", lambda e, j=j: e.matmul(ps[:, 0:N], lhsT=ones[:], rhs=sq[:, j, :], start=(j == 0), stop=(j == 7)),
                 reads=[ones.b, sq.b], writes=[ps.b])
        rs = self.rstd
        P.op("act", lambda e: e.activation(out=rs[:, 0:N], in_=ps[:, 0:N], func=AF.Sqrt, bias=self.eps[:], scale=1.0 / D),
             reads=[ps.b, self.eps.b], writes=[rs.b])
        P.op("dve", lambda e: e.reciprocal(out=rs[:, 0:N], in_=rs[:, 0:N]), reads=[rs.b], writes=[rs.b])
        P.op("dve", lambda e: e.tensor_tensor(out=sq[:], in0=x[:], in1=rs[:, 0:N].unsqueeze(1).to_broadcast([128, 8, N]), op=ALU.mult),
             reads=[x.b, rs.b], writes=[sq.b])
        if S is None:
            P.op("pool", lambda e: e.tensor_tensor(out=out_ap, in0=sq[:], in1=G.unsqueeze(2).to_broadcast([128, 8, N]), op=ALU.mult),
                 reads=[sq.b, self.cols.b, self.der.b] + list(extra_reads), writes=[out_b])
        else:
            P.op("pool", lambda e: e.tensor_tensor(out=sq[:], in0=sq[:], in1=G.unsqueeze(2).to_broadcast([128, 8, N]), op=ALU.mult),
                 reads=[sq.b, self.cols.b, self.der.b], writes=[sq.b])
            P.op("pool", lambda e: e.tensor_tensor(out=out_ap, in0=sq[:], in1=S.unsqueeze(2).to_broadcast([128, 8, N]), op=ALU.add),
                 reads=[sq.b, self.modc.b] + list(extra_reads), writes=[out_b])

    def load_cast(self, es_stage, pieces, wb):
        P = self.P
        if not hasattr(self, "_stg") or self._stg is None:
            self._stg = [self.tile(es_stage, "stg%d" % i, [128, 1024], dma=True) for i in range(2)]
            self._stgi = 0
        for (dst, src, p, a, b) in pieces:
            assert a * b <= 1024
            st = self._stg[self._stgi % 2]
            self._stgi += 1
            sv = st[0:p, 0:a * b].rearrange("p (a b) -> p a b", a=a)
            q = ("sp", "act")[self._stgi % 2]
            P.dma(q, lambda e, sv=sv, src=src: e.dma_start(out=sv, in_=src), st.b, writes=[st.b])
            self.copy(self.eng_rr(("pool", "dve", "act")), dst, sv, [st.b], [wb])

    def phase_ffn(self, l, xsrc):
        P, nc, inp = self.P, self.nc, self.inp
        TB = 256
        NB = T // TB
        with ExitStack() as es:
            self._stg = None
            wup = self.tile(es, "wup", [128, 8, F2], BF16)
            wdn = self.tile(es, "wdn", [128, NG, D], BF16)
            pieces = []
            srcu = inp["ffn_w_up"][l].rearrange("(kc p) n -> p kc n", p=128)
            for kc in range(8):
                for (n0, n1) in ((0, 1024), (1024, 2048), (2048, 3072), (3072, 4096), (4096, 5120), (5120, F2)):
                    pieces.append((wup[:, kc:kc + 1, n0:n1], srcu[:, kc:kc + 1, n0:n1], 128, 1, n1 - n0))
            self.load_cast(es, pieces, wup.b)
            srcd = inp["ffn_w_down"][l].rearrange("(kc p) n -> p kc n", p=128)
            pieces = [(wdn[:, i:i + 1, :], srcd[:, i:i + 1, :], 128, 1, D) for i in range(0, NG)]
            self.load_cast(es, pieces, wdn.b)

            xs = [self.tile(es, "fx%d" % i, [128, 8, TB], dma=True) for i in range(2)]
            sq = self.tile(es, "fsq", [128, 8, TB])
            self.rstd = self.tile(es, "frstd", [128, TB])
            h2s = [self.tile(es, "fh2_%d" % i, [128, 8, TB + 2], BF16) for i in range(2)]
            for i in range(2):
                P.op("pool", lambda e: e.memset(h2s[i][:], 0.0), writes=[h2s[i].b])
            NCV = 6
            cv = [self.tile(es, "fcv%d" % i, [128, TB]) for i in range(NCV)]
            sgl = [self.tile(es, "fsg%d" % i, [128, TB]) for i in range(3)]
            hm = self.tile(es, "fhm", [128, NG, TB], BF16)
            G2 = self.der[:, l, 1, :]
            S2 = self.modcol(l, 3)
            g2c = self.modcol(l, 5)
            cw = [CP.off["cw%d_%d" % (i, l)] for i in range(3)]
            cb = CP.off["cb_%d" % l]
            xr = lambda ap: ap.rearrange("(j p) t -> p j t", p=128)

            def norm(nb):
                x = xs[nb % 2]
                h2 = h2s[nb % 2]
                t0 = nb * TB
                P.dma("sp", lambda e: e.dma_start(out=x[:], in_=xr(xsrc)[:, :, t0:t0 + TB]), x.b, writes=[x.b])
                if nb > 0:
                    hp = h2s[(nb - 1) % 2]
                    P.op("pool", lambda e: e.tensor_copy(out=h2[:, :, 0:2], in_=hp[:, :, TB:TB + 2]), reads=[hp.b], writes=[h2.b])
                self.rmsnorm(x, TB, sq, G2, S2, h2[:, :, 2:TB + 2], h2.b)

            def up(nb):
                h2 = h2s[nb % 2]
                ui = 0
                for i in range(NG):
                    for which in range(2):
                        n = i + which * NG
                        ps = self.psum_next()
                        for kc in range(8):
                            P.op("pe", lambda e: e.matmul(ps[:, 0:TB + 2], lhsT=wup[:, kc, n * 128:(n + 1) * 128], rhs=h2[:, kc, :],
                                                          start=(kc == 0), stop=(kc == 7)), reads=[wup.b, h2.b], writes=[ps.b])
                        c = cv[ui % NCV]
                        ui += 1
                        P.op("act", lambda e: e.activation(out=c[:], in_=ps[:, 2:TB + 2], func=AF.Identity,
                                                           bias=self.cols[:, cb + n:cb + n + 1], scale=self.cols[:, cw[2] + n:cw[2] + n + 1]),
                             reads=[ps.b, self.cols.b], writes=[c.b])
                        P.op("dve", lambda e: e.scalar_tensor_tensor(out=c[:], in0=ps[:, 1:TB + 1], scalar=self.cols[:, cw[1] + n:cw[1] + n + 1],
                                                                     in1=c[:], op0=ALU.mult, op1=ALU.add), reads=[ps.b, c.b, self.cols.b], writes=[c.b])
                        P.op("dve", lambda e: e.scalar_tensor_tensor(out=c[:], in0=ps[:, 0:TB], scalar=self.cols[:, cw[0] + n:cw[0] + n + 1],
                                                                     in1=c[:], op0=ALU.mult, op1=ALU.add), reads=[ps.b, c.b, self.cols.b], writes=[c.b])
                        s_ = sgl[i % 3]
                        if which == 0:
                            P.op("act", lambda e: e.activation(out=s_[:], in_=c[:], func=AF.Silu), reads=[c.b], writes=[s_.b])
                        else:
                            P.op("pool", lambda e: e.tensor_tensor(out=hm[:, i, :], in0=s_[:], in1=c[:], op=ALU.mult),
                                 reads=[s_.b, c.b], writes=[hm.b])

            def down(nb):
                x = xs[nb % 2]
                t0 = nb * TB
                for nj in range(8):
                    ps = self.psum_next()
                    for i in range(NG):
                        P.op("pe", lambda e: e.matmul(ps[:, 0:TB], lhsT=wdn[:, i, nj * 128:(nj + 1) * 128], rhs=hm[:, i, :],
                                                      start=(i == 0), stop=(i == NG - 1)), reads=[wdn.b, hm.b], writes=[ps.b])
                    P.op("dve", lambda e: e.scalar_tensor_tensor(out=x[:, nj, :], in0=ps[:, 0:TB], scalar=g2c[:, nj:nj + 1],
                                                                 in1=x[:, nj, :], op0=ALU.mult, op1=ALU.add),
                         reads=[ps.b, x.b, self.modc.b], writes=[x.b])
                P.dma("sp", lambda e: e.dma_start(out=xr(self.X)[:, :, t0:t0 + TB], in_=x[:]), x.b, reads=[x.b])

            norm(0)
            for nb in range(NB):
                up(nb)
                if nb + 1 < NB:
                    norm(nb + 1)
                down(nb)
            self.end_phase()

    def phase_final(self, xsrc, do_norm):
        P = self.P
        TB = 512
        with ExitStack() as es:
            xs = [self.tile(es, "nx%d" % i, [128, 8, TB], dma=True) for i in range(2)]
            sq = self.tile(es, "nsq", [128, 8, TB])
            self.rstd = self.tile(es, "nrstd", [128, TB])
            G = self.col("final_norm", 0, 8)
            for nb in range(T // TB):
                x = xs[nb % 2]
                t0 = nb * TB
                src = xsrc.rearrange("(j p) t -> p j t", p=128)[:, :, t0:t0 + TB]
                P.dma("sp", lambda e, x=x, src=src: e.dma_start(out=x[:], in_=src), x.b, writes=[x.b])
                if do_norm:
                    self.rmsnorm(x, TB, sq, G, None, x[:], x.b)
                dst = self.outT.rearrange("(j p) t -> p j t", p=128)[:, :, t0:t0 + TB]
                P.dma("act", lambda e, x=x, dst=dst: e.dma_start(out=dst, in_=x[:]), x.b, reads=[x.b])
            self.end_phase()

    def phase_rwkv(self, l, xsrc):
        P, nc, inp = self.P, self.nc, self.inp
        CH = 128
        NCH = self.cfg.get("nch", T // CH)
        ident = self.cst["ident"]
        with ExitStack() as es:
            self._stg = None
            reuse_stage = self.cfg.get("stage_reuse", True)
            ses = ExitStack() if reuse_stage else es
            Wr = self.tile(es, "Wr", [128, 8, D], BF16)
            Wk = self.tile(es, "Wk", [128, 8, D], BF16)
            Wv = self.tile(es, "Wv", [128, 8, D], BF16)
            Wo = self.tile(es, "Wo", [128, 8, D], BF16)
            w1b = self.tile(es, "w1b", [128, 8, 64], BF16)
            a1b = self.tile(es, "a1b", [128, 8, 64], BF16)
            g1b = self.tile(es, "g1b", [128, 8, 160], BF16)
            w2b = self.tile(es, "w2b", [64, 1, D], BF16)
            a2b = self.tile(es, "a2b", [64, 1, D], BF16)
            g2b = self.tile(es, "g2b", [128, 2, D], BF16)
            if l == 1:
                v1b = self.tile(es, "v1b", [128, 8, 32], BF16)
                v2b = self.tile(es, "v2b", [32, 1, D], BF16)
                v0r = self.tile(es, "v0r", [128, D], dma=True)
            lnw = self.tile(es, "lnw", [128, D], dma=True)
            lnb = self.tile(es, "lnb", [128, D], dma=True)
            omka = self.tile(es, "omka", [128, 8])
            eps2 = self.tile(es, "eps2", [128, 1])
            onesT = self.tile(es, "onesT", [128, 128])
            for W, src in ((Wr, inp["a_w_rkv"][l, 0]), (Wk, inp["a_w_rkv"][l, 1]), (Wv, inp["a_w_rkv"][l, 2]), (Wo, inp["a_w_o"][l])):
                s3 = src.rearrange("(kc p) n -> p kc n", p=128)
                self.load_cast(ses, [(W[:, kc:kc + 1, :], s3[:, kc:kc + 1, :], 128, 1, D) for kc in range(8)], W.b)
            self.load_cast(ses, [(w1b[:], inp["a_w1"][l].rearrange("(kc p) n -> p kc n", p=128), 128, 8, 64)], w1b.b)
            self.load_cast(ses, [(a1b[:], inp["a_a1"][l].rearrange("(kc p) n -> p kc n", p=128), 128, 8, 64)], a1b.b)
            sg1 = inp["a_g1"][l].rearrange("(kc p) n -> p kc n", p=128)
            self.load_cast(ses, [(g1b[:, 0:4, :], sg1[:, 0:4, :], 128, 4, 160), (g1b[:, 4:8, :], sg1[:, 4:8, :], 128, 4, 160)], g1b.b)
            self.load_cast(ses, [(w2b[:], inp["a_w2"][l].rearrange("(o p) n -> p o n", o=1), 64, 1, D)], w2b.b)
            self.load_cast(ses, [(a2b[:], inp["a_a2"][l].rearrange("(o p) n -> p o n", o=1), 64, 1, D)], a2b.b)
            self.load_cast(ses, [(g2b[:, 0:1, :], inp["a_g2"][l][0:128, :].rearrange("(o p) n -> p o n", o=1), 128, 1, D),
                                (g2b[0:32, 1:2, :], inp["a_g2"][l][128:160, :].rearrange("(o p) n -> p o n", o=1), 32, 1, D)], g2b.b)
            if l == 1:
                self.load_cast(ses, [(v1b[:], inp["a_v1"][0].rearrange("(kc p) n -> p kc n", p=128), 128, 8, 32)], v1b.b)
                self.load_cast(ses, [(v2b[:], inp["a_v2"][0].rearrange("(o p) n -> p o n", o=1), 32, 1, D)], v2b.b)
                P.dma("sp", lambda e: e.dma_start(out=v0r[:], in_=inp["rows"][4 * D:5 * D].partition_broadcast(128)), v0r.b, writes=[v0r.b])
            P.dma("sp", lambda e: e.dma_start(out=lnw[:], in_=inp["rows"][(2 * l) * D:(2 * l + 1) * D].partition_broadcast(128)), lnw.b, writes=[lnw.b])
            P.dma("sp", lambda e: e.dma_start(out=lnb[:], in_=inp["rows"][(2 * l + 1) * D:(2 * l + 2) * D].partition_broadcast(128)), lnb.b, writes=[lnb.b])
            P.op("dve", lambda e: e.tensor_scalar(out=omka[:], in0=self.col("k_a_%d" % l, 0, 8), scalar1=-1.0, scalar2=1.0, op0=ALU.mult, op1=ALU.add),
                 reads=[self.cols.b], writes=[omka.b])
            P.op("pool", lambda e: e.memset(eps2[:], 64e-5), writes=[eps2.b])
            P.op("pool", lambda e: e.memset(onesT[:], 1.0), writes=[onesT.b])

            if reuse_stage:
                P.barrier()
                ses.close()
                self._stg = None
            Hbd = self.tile(es, "Hbd", [128, 8, 128])
            Hb = self.tile(es, "Hb", [128, 8, 128], BF16)
            if self.cfg.get("t_hb", True):
                P.op("pool", lambda e: e.memset(Hb[:], 0.0), writes=[Hb.b])
            Vb = self.tile(es, "Vb", [128, D], BF16)
            P.op("pool", lambda e: e.memset(Hbd[:], 0.0), writes=[Hbd.b])
            h = self.tile(es, "h", [128, 8, 129])
            P.op("pool", lambda e: e.memset(h[:], 0.0), writes=[h.b])
            x = self.tile(es, "rx", [128, 8, CH], dma=True)
            self.rstd = self.tile(es, "rrstd", [128, CH])
            xx = self.tile(es, "xx", [128, 8, CH])
            tmpm = [self.tile(es, "tmpm%d" % i, [128, 8, CH], dma=(i == 0)) for i in range(2)]
            sq = tmpm[1] if self.cfg.get("t_sq", True) else self.tile(es, "rsq", [128, 8, CH])
            xm = [self.tile(es, "xm%d" % i, [128, 8, CH], BF16) for i in range(6)]
            lwt = self.tile(es, "lwt", [64, CH], BF16)
            lat = self.tile(es, "lat", [64, CH], BF16)
            lgt = self.tile(es, "lgt", [128, 2, CH], BF16)
            V = self.tile(es, "V", [128, D], dma=True)
            Yt = self.tile(es, "Yt", [128, D], dma=True)
            if l == 1:
                lvt = self.tile(es, "lvt", [32, CH], BF16)
                VFt = Tl(tmpm[0][:].rearrange("p a t -> p (a t)"), tmpm[0].b)
                sgv = Tl(tmpm[1][:].rearrange("p a t -> p (a t)")[:, 0:512], tmpm[1].b)
            ogb = self.tile(es, "ogb", [128, D], BF16)
            ogT = self.tile(es, "ogT", [128, 8, CH], BF16)
            st = self.tile(es, "stat", [128, 6, 16])
            rkb = self.tile(es, "rkb", [128, 16])

            NSLOT = self.cfg.get("nslot%d" % l, 4)

            def mkslot(si):
                S = {}

                def pt(name, shape=(128, 128), dt=F32):
                    S[name] = self.tile(es, "s%d_%s" % (si, name), list(shape), dt)
                for nm in ("r", "k", "sg", "a", "kk", "x", "L", "eL"):
                    pt(nm)
                for nm in ("rh", "bts", "kts", "WT", "U"):
                    pt(nm, (128, 128), BF16)
                pt("AR", (128, 2, 128), BF16)
                pt("F3", (128, 3, 128), BF16)
                pt("T3", (128, 3, 128), BF16)
                for i in range(2):
                    pt("SC%d" % i, (128, 4, 128), BF16)
                    pt("Ym%d" % i, (128, 128), BF16)
                    pt("ZY%d_0" % i, (128, 2, 128), BF16)
                    pt("ZY%d_1" % i, (128, 2, 128), BF16)
                    pt("Tt%d" % i, (128, 128), BF16)
                    pt("AV%d" % i, (128, 64), BF16)
                return S
            slots = [mkslot(i) for i in range(NSLOT)]
            mask4 = self.cst["mask4"]; maskL = self.cst["maskL"]; bd = self.cst["bd"]; ind = self.cst["ind"]
            G1 = self.der[:, l, 0, :]
            S1 = self.modcol(l, 0)
            g1c = self.modcol(l, 2)
            psB = self.ps[6]

            def c_(name, j):
                return self.col("%s_%d" % (name, l), j, 1)

            for c in range(NCH):
                t0 = c * CH
                src = xsrc.rearrange("(j p) t -> p j t", p=128)[:, :, t0:t0 + CH]
                P.dma("sp", lambda e, src=src: e.dma_start(out=x[:], in_=src), x.b, writes=[x.b])
                self.rmsnorm(x, CH, sq, G1, S1, h[:, :, 1:CH + 1], h.b)
                P.op("pool", lambda e: e.tensor_tensor(out=xx[:], in0=h[:, :, 0:CH], in1=h[:, :, 1:CH + 1], op=ALU.subtract),
                     reads=[h.b], writes=[xx.b])
                for i in range(6):
                    tm = tmpm[i % 2]
                    mu = self.col("mu%d_%d" % (i, l), 0, 8)
                    P.op("dve", lambda e, tm=tm, mu=mu: e.tensor_tensor(out=tm[:], in0=xx[:], in1=mu.unsqueeze(2).to_broadcast([128, 8, CH]), op=ALU.mult),
                         reads=[xx.b, self.cols.b], writes=[tm.b])
                    P.op("pool", lambda e, tm=tm, i=i: e.tensor_tensor(out=xm[i][:], in0=tm[:], in1=h[:, :, 1:CH + 1], op=ALU.add),
                         reads=[tm.b, h.b], writes=[xm[i].b])
                P.op("pool", lambda e: e.tensor_copy(out=h[:, :, 0:1], in_=h[:, :, CH:CH + 1]), reads=[h.b], writes=[h.b])
                if l == 1:
                    P.dma("sp", lambda e, t0=t0: e.dma_start(out=VFt[:], in_=self.VF[t0:t0 + CH, :]), VFt.b, writes=[VFt.b])

                psl = self.psum_next()
                for kc in range(8):
                    P.op("pe", lambda e, kc=kc: e.matmul(psl[0:64, 0:128], lhsT=w1b[:, kc, :], rhs=xm[1][:, kc, :], start=(kc == 0), stop=(kc == 7)),
                         reads=[w1b.b, xm[1].b], writes=[psl.b])
                for kc in range(8):
                    P.op("pe", lambda e, kc=kc: e.matmul(psl[0:64, 128:256], lhsT=a1b[:, kc, :], rhs=xm[4][:, kc, :], start=(kc == 0), stop=(kc == 7)),
                         reads=[a1b.b, xm[4].b], writes=[psl.b])
                for kc in range(8):
                    P.op("pe", lambda e, kc=kc: e.matmul(psl[:, 256:384], lhsT=g1b[:, kc, 0:128], rhs=xm[5][:, kc, :], start=(kc == 0), stop=(kc == 7)),
                         reads=[g1b.b, xm[5].b], writes=[psl.b])
                for kc in range(8):
                    P.op("pe", lambda e, kc=kc: e.matmul(psl[0:32, 384:512], lhsT=g1b[:, kc, 128:160], rhs=xm[5][:, kc, :], start=(kc == 0), stop=(kc == 7)),
                         reads=[g1b.b, xm[5].b], writes=[psl.b])
                P.op("act", lambda e: e.activation(out=lwt[:], in_=psl[0:64, 0:128], func=AF.Tanh), reads=[psl.b], writes=[lwt.b])
                P.op("act", lambda e: e.copy(out=lat[:], in_=psl[0:64, 128:256]), reads=[psl.b], writes=[lat.b])
                P.op("act", lambda e: e.activation(out=lgt[:, 0, :], in_=psl[:, 256:384], func=AF.Sigmoid), reads=[psl.b], writes=[lgt.b])
                P.op("act", lambda e: e.activation(out=lgt[0:32, 1, :], in_=psl[0:32, 384:512], func=AF.Sigmoid), reads=[psl.b], writes=[lgt.b])
                if l == 1:
                    psv = self.psum_next()
                    for kc in range(8):
                        P.op("pe", lambda e, kc=kc: e.matmul(psv[0:32, 0:128], lhsT=v1b[:, kc, :], rhs=xm[3][:, kc, :], start=(kc == 0), stop=(kc == 7)),
                             reads=[v1b.b, xm[3].b], writes=[psv.b])
                    P.op("act", lambda e: e.copy(out=lvt[:], in_=psv[0:32, 0:128]), reads=[psv.b], writes=[lvt.b])
                for half in range(2):
                    ps = self.psum_next()
                    for kc in range(8):
                        P.op("pe", lambda e, ps=ps, kc=kc, half=half: e.matmul(ps[:, :], lhsT=xm[3][:, kc, :], rhs=Wv[:, kc, half * 512:(half + 1) * 512],
                                                                              start=(kc == 0), stop=(kc == 7)), reads=[xm[3].b, Wv.b], writes=[ps.b])
                    P.op("act", lambda e, ps=ps, half=half: e.copy(out=V[:, half * 512:(half + 1) * 512], in_=ps[:, :]), reads=[ps.b], writes=[V.b])
                if l == 1:
                    for half in range(2):
                        ps = self.psum_next()
                        hs = slice(half * 512, (half + 1) * 512)
                        P.op("pe", lambda e, ps=ps, hs=hs: e.matmul(ps[:, :], lhsT=lvt[:], rhs=v2b[0:32, 0, hs], start=True, stop=True),
                             reads=[lvt.b, v2b.b], writes=[ps.b])
                        P.op("dve", lambda e, ps=ps, hs=hs: e.tensor_tensor(out=sgv[:], in0=ps[:, :], in1=v0r[:, hs], op=ALU.add),
                             reads=[ps.b, v0r.b], writes=[sgv.b])
                        P.op("act", lambda e: e.activation(out=sgv[:], in_=sgv[:], func=AF.Sigmoid), reads=[sgv.b], writes=[sgv.b])
                        P.op("pool", lambda e, hs=hs: e.tensor_tensor(out=VFt[:, hs], in0=VFt[:, hs], in1=V[:, hs], op=ALU.subtract),
                             reads=[VFt.b, V.b], writes=[VFt.b])
                        P.op("pool", lambda e, hs=hs: e.tensor_tensor(out=VFt[:, hs], in0=VFt[:, hs], in1=sgv[:], op=ALU.mult),
                             reads=[VFt.b, sgv.b], writes=[VFt.b])
                        P.op("pool", lambda e, hs=hs: e.tensor_tensor(out=V[:, hs], in0=V[:, hs], in1=VFt[:, hs], op=ALU.add),
                             reads=[VFt.b, V.b], writes=[V.b])
                else:
                    P.dma("act", lambda e, t0=t0: e.dma_start(out=self.VF[t0:t0 + CH, :], in_=V[:]), V.b, reads=[V.b])
                if "rw_v" in self.dbg:
                    P.dma("act", lambda e, t0=t0: e.dma_start(out=self.dbg["rw_v"][t0:t0 + CH, :], in_=V[:]), V.b, reads=[V.b])

                if self.cfg.get("t_vb", True):
                    P.op("pool", lambda e: e.tensor_copy(out=Vb[:], in_=V[:]), reads=[V.b], writes=[Vb.b])

                def pair_gen(j, S):
                    CUT = self.cfg.get("rw_cut", 99)
                    EB = self.cfg.get("eng_b", "dve")

                    def smul(eng, out_ap, in_ap, sc_ap, rd, wr):
                        if eng == "act":
                            P.op("act", lambda e: e.activation(out=out_ap, in_=in_ap, func=AF.Copy, scale=sc_ap), reads=rd, writes=wr)
                        else:
                            P.op(eng, lambda e: e.tensor_scalar(out=out_ap, in0=in_ap, scalar1=sc_ap, scalar2=None, op0=ALU.mult), reads=rd, writes=wr)
                    js = slice(j * 128, (j + 1) * 128)
                    r_, k_, sg, a_, kk, tx, L_, eL = S["r"], S["k"], S["sg"], S["a"], S["kk"], S["x"], S["L"], S["eL"]
                    rh, bts, kts, WT, U_, AR, F3, T3 = S["rh"], S["bts"], S["kts"], S["WT"], S["U"], S["AR"], S["F3"], S["T3"]
                    SC = [S["SC0"], S["SC1"]]; Ym = [S["Ym0"], S["Ym1"]]; Tt = [S["Tt0"], S["Tt1"]]; AV = [S["AV0"], S["AV1"]]
                    ZY = [[S["ZY0_0"], S["ZY0_1"]], [S["ZY1_0"], S["ZY1_1"]]]
                    psA = self.psum_next()
                    for kc in range(8):
                        P.op("pe", lambda e: e.matmul(psA[:, 0:128], lhsT=Wr[:, kc, js], rhs=xm[0][:, kc, :], start=(kc == 0), stop=(kc == 7)),
                             reads=[Wr.b, xm[0].b], writes=[psA.b])
                    for kc in range(8):
                        P.op("pe", lambda e: e.matmul(psA[:, 128:256], lhsT=Wk[:, kc, js], rhs=xm[2][:, kc, :], start=(kc == 0), stop=(kc == 7)),
                             reads=[Wk.b, xm[2].b], writes=[psA.b])
                    P.op("pe", lambda e: e.matmul(psA[:, 256:384], lhsT=w2b[:, 0, js], rhs=lwt[:], start=True, stop=True), reads=[w2b.b, lwt.b], writes=[psA.b])
                    P.op("pe", lambda e: e.matmul(psA[:, 384:512], lhsT=a2b[:, 0, js], rhs=lat[:], start=True, stop=True), reads=[a2b.b, lat.b], writes=[psA.b])
                    P.op("act", lambda e: e.copy(out=r_[:], in_=psA[:, 0:128]), reads=[psA.b], writes=[r_.b])
                    P.op("act", lambda e: e.copy(out=k_[:], in_=psA[:, 128:256]), reads=[psA.b], writes=[k_.b])
                    P.op("act", lambda e: e.activation(out=sg[:], in_=psA[:, 256:384], func=AF.Sigmoid, bias=c_("w0", j), scale=1.0),
                         reads=[psA.b, self.cols.b], writes=[sg.b])
                    P.op("act", lambda e: e.activation(out=a_[:], in_=psA[:, 384:512], func=AF.Sigmoid, bias=c_("a0", j), scale=1.0),
                         reads=[psA.b, self.cols.b], writes=[a_.b])
                    yield
                    if CUT <= 1:
                        return
                    P.op("dve", lambda e: e.tensor_scalar(out=kk[:], in0=k_[:], scalar1=c_("k_k", j), scalar2=None, op0=ALU.mult),
                         reads=[k_.b, self.cols.b], writes=[kk.b])
                    P.op("pool", lambda e: e.tensor_tensor(out=tx[:], in0=kk[:], in1=kk[:], op=ALU.mult), reads=[kk.b], writes=[tx.b])
                    ps = self.psum_next()
                    P.op("pe", lambda e: e.matmul(ps[:, 0:128], lhsT=bd[:], rhs=tx[:], start=True, stop=True), reads=[bd.b, tx.b], writes=[ps.b])
                    P.op("act", lambda e: e.activation(out=tx[:], in_=ps[:, 0:128], func=AF.Sqrt), reads=[ps.b], writes=[tx.b])
                    yield
                    if CUT <= 2:
                        return
                    P.op("dve", lambda e: e.tensor_scalar(out=tx[:], in0=tx[:], scalar1=1e-12, scalar2=None, op0=ALU.max), reads=[tx.b], writes=[tx.b])
                    P.op("dve", lambda e: e.reciprocal(out=tx[:], in_=tx[:]), reads=[tx.b], writes=[tx.b])
                    P.op("pool", lambda e: e.tensor_tensor(out=kk[:], in0=kk[:], in1=tx[:], op=ALU.mult), reads=[kk.b, tx.b], writes=[kk.b])
                    P.op("dve", lambda e: e.tensor_scalar(out=tx[:], in0=a_[:], scalar1=c_("k_a", j), scalar2=omka[:, j:j + 1], op0=ALU.mult, op1=ALU.add),
                         reads=[a_.b, self.cols.b, omka.b], writes=[tx.b])
                    P.op("pool", lambda e: e.tensor_tensor(out=k_[:], in0=k_[:], in1=tx[:], op=ALU.mult), reads=[k_.b, tx.b], writes=[k_.b])
                    P.op("pool", lambda e: e.tensor_tensor(out=a_[:], in0=kk[:], in1=a_[:], op=ALU.mult), reads=[kk.b, a_.b], writes=[a_.b])
                    P.op("dve", lambda e: e.scalar_tensor_tensor(out=tx[:], in0=r_[:], scalar=c_("r_k", j), in1=k_[:], op0=ALU.mult, op1=ALU.mult),
                         reads=[r_.b, k_.b, self.cols.b], writes=[tx.b])
                    P.op("pe", lambda e: e.matmul(psB[:, 0:16], lhsT=tx[:], rhs=ind[:, j * 16:(j + 1) * 16], start=(j == 0), stop=(j == 7)),
                         reads=[tx.b, ind.b], writes=[psB.b])
                    yield
                    if CUT <= 3:
                        return
                    P.op("dve", lambda e: e.tensor_tensor_scan(out=L_[:], data0=onesT[:], data1=sg[:], initial=0.0, op0=ALU.mult, op1=ALU.add),
                         reads=[onesT.b, sg.b], writes=[L_.b])
                    P.op("pool", lambda e: e.tensor_tensor(out=sg[:], in0=L_[:], in1=sg[:], op=ALU.subtract), reads=[L_.b, sg.b], writes=[sg.b])
                    P.op("act", lambda e: e.activation(out=eL[:], in_=L_[:], func=AF.Exp, scale=-C0), reads=[L_.b], writes=[eL.b])
                    P.op("act", lambda e: e.activation(out=sg[:], in_=sg[:], func=AF.Exp, scale=-C0), reads=[sg.b], writes=[sg.b])
                    P.op("act", lambda e: e.activation(out=L_[:], in_=L_[:], func=AF.Exp, scale=C0), reads=[L_.b], writes=[L_.b])
                    yield
                    if CUT <= 4:
                        return
                    enL = L_
                    eE = sg
                    P.op("pool", lambda e: e.tensor_tensor(out=r_[:], in0=r_[:], in1=eL[:], op=ALU.mult), reads=[r_.b, eL.b], writes=[r_.b])
                    P.op("pool", lambda e: e.tensor_copy(out=rh[:], in_=r_[:]), reads=[r_.b], writes=[rh.b])
                    P.op("dve", lambda e: e.scalar_tensor_tensor(out=tx[:], in0=kk[:], scalar=-1.0, in1=eE[:], op0=ALU.mult, op1=ALU.mult),
                         reads=[kk.b, eE.b], writes=[tx.b])
                    P.op("pool", lambda e: e.tensor_copy(out=F3[:, 0, :], in_=tx[:]), reads=[tx.b], writes=[F3.b])
                    P.op("dve", lambda e: e.tensor_scalar(out=AR[:, 0, :], in0=tx[:], scalar1=enL[:, 63:64], scalar2=None, op0=ALU.mult),
                         reads=[tx.b, enL.b], writes=[AR.b])
                    P.op("dve", lambda e: e.tensor_scalar(out=AR[:, 1, :], in0=r_[:], scalar1=enL[:, 63:64], scalar2=None, op0=ALU.mult),
                         reads=[r_.b, enL.b], writes=[AR.b])
                    P.op("pool", lambda e: e.tensor_tensor(out=a_[:], in0=a_[:], in1=enL[:], op=ALU.mult), reads=[a_.b, enL.b], writes=[a_.b])
                    P.op("pool", lambda e: e.tensor_tensor(out=k_[:], in0=k_[:], in1=enL[:], op=ALU.mult), reads=[k_.b, enL.b], writes=[k_.b])
                    yield
                    if CUT <= 5:
                        return
                    smul(EB, bts[:], a_[:], eL[:, 63:64], [a_.b, eL.b], [bts.b])
                    smul(EB, kts[:], k_[:], eL[:, 63:64], [k_.b, eL.b], [kts.b])
                    smul(EB, F3[:, 1, :], a_[:], eL[:, 127:128], [a_.b, eL.b], [F3.b])
                    smul(EB, F3[:, 2, :], k_[:], eL[:, 127:128], [k_.b, eL.b], [F3.b])
                    yield
                    if CUT <= 6:
                        return
                    for q in range(3):
                        P.op("pe", lambda e: e.transpose(out=self.psbf[:, q * 128:(q + 1) * 128], in_=F3[:, q, :], identity=self.identb[:]),
                             reads=[F3.b, self.identb.b], writes=[self.psbf.b])
                    P.op("act", lambda e: e.copy(out=T3[:].rearrange("p a t -> p (a t)"), in_=self.psbf[:, 0:384]), reads=[self.psbf.b], writes=[T3.b])
                    yield
                    if CUT <= 7:
                        return
                    for par in range(2):
                        rows = slice(64 * par, 64 * par + 64)
                        pss = self.psum_next()
                        arv = AR[rows, :, :].rearrange("p a t -> p (a t)")
                        P.op("pe", lambda e: e.matmul(pss[:, 0:256], lhsT=bts[rows, :], rhs=arv, start=True, stop=True),
                             reads=[bts.b, AR.b], writes=[pss.b])
                        P.op("pe", lambda e: e.matmul(pss[:, 256:512], lhsT=kts[rows, :], rhs=arv, start=True, stop=True),
                             reads=[kts.b, AR.b], writes=[pss.b])
                        P.op("dve", lambda e: e.tensor_tensor(out=SC[par][:].rearrange("p a t -> p (a t)"), in0=pss[:, :], in1=mask4[:], op=ALU.mult),
                             reads=[pss.b, mask4.b], writes=[SC[par].b])
                        ps3 = self.psum_next()
                        P.op("pe", lambda e: e.matmul(ps3[:, 0:128], lhsT=AR[rows, 0, :], rhs=bts[rows, :], start=True, stop=True),
                             reads=[bts.b, AR.b], writes=[ps3.b])
                        P.op("dve", lambda e: e.tensor_tensor(out=Ym[par][:], in0=ps3[:, 0:128], in1=maskL[:], op=ALU.mult),
                             reads=[ps3.b, maskL.b], writes=[Ym[par].b])
                        P.op("pool", lambda e: e.tensor_tensor(out=Tt[par][:], in0=SC[par][:, 0, :], in1=self.identb[:], op=ALU.add),
                             reads=[SC[par].b, self.identb.b], writes=[Tt[par].b])
                        yield
                    cur = [(SC[0][:, 0, :], SC[0].b, Ym[0][:], Ym[0].b), (SC[1][:, 0, :], SC[1].b, Ym[1][:], Ym[1].b)]
                    for step in range(1, 7):
                        for par in range(2):
                            Zap, Zb, Yap, Yb = cur[par]
                            zy = ZY[par][step % 2]
                            psz = self.psum_next()
                            if step < 6:
                                P.op("pe", lambda e: e.matmul(psz[:, 0:128], lhsT=Yap, rhs=Zap, start=True, stop=True), reads=[Zb, Yb], writes=[psz.b])
                            P.op("pe", lambda e: e.matmul(psz[:, 128:256], lhsT=Zap, rhs=Yap, start=True, stop=True), reads=[Zb, Yb], writes=[psz.b])
                            ev = "act"
                            if step < 6:
                                self.copy(ev, zy[:].rearrange("p a t -> p (a t)"), psz[:, 0:256], [psz.b], [zy.b])
                            else:
                                self.copy(ev, zy[:, 1, :], psz[:, 128:256], [psz.b], [zy.b])
                            cur[par] = (zy[:, 0, :], zy.b, zy[:, 1, :], zy.b)
                            yield
                            tt = Tt[par]
                            psp = self.psum_next()
                            P.op("pe", lambda e: e.matmul(psp[:, 0:128], lhsT=zy[:, 1, :], rhs=tt[:], start=True, stop=True),
                                 reads=[zy.b, tt.b], writes=[psp.b])
                            P.op("dve", lambda e: e.tensor_tensor(out=tt[:], in0=psp[:, 0:128], in1=tt[:], op=ALU.add),
                                 reads=[psp.b, tt.b], writes=[tt.b])
                            yield
                    for par in range(2):
                        rows = slice(64 * par, 64 * par + 64)
                        hc = slice((2 * j + par) * 64, (2 * j + par + 1) * 64)
                        psw = self.psum_next()
                        P.op("pe", lambda e: e.matmul(psw[:, 0:128], lhsT=T3[:, 0, :], rhs=Tt[par][:], start=True, stop=True),
                             reads=[T3.b, Tt[par].b], writes=[psw.b])
                        P.op("pe", lambda e: e.matmul(psw[:, 128:192], lhsT=SC[par][:, 2, :], rhs=Vb[:, hc], start=True, stop=True),
                             reads=[SC[par].b, Vb.b], writes=[psw.b])
                        P.op("act", lambda e: e.copy(out=WT[rows, :], in_=psw[rows, 0:128]), reads=[psw.b], writes=[WT.b])
                        P.op("act", lambda e: e.copy(out=AV[par][:], in_=psw[:, 128:192]), reads=[psw.b], writes=[AV[par].b])
                        yield
                    psu = self.psum_next()
                    P.op("pe", lambda e: e.matmul(psu[:, 0:128], lhsT=WT[:], rhs=Hb[:, j, :], start=True, stop=True),
                         reads=[WT.b, Hb.b], writes=[psu.b])
                    for par in range(2):
                        cs = slice(64 * par, 64 * par + 64)
                        P.op("pe", lambda e: e.matmul(psu[:, cs], lhsT=Tt[par][:], rhs=AV[par][:], start=False, stop=(par == 1), skip_group_check=True),
                             reads=[Tt[par].b, AV[par].b], writes=[psu.b])
                    P.op("act", lambda e: e.copy(out=U_[:], in_=psu[:, 0:128]), reads=[psu.b], writes=[U_.b])
                    yield
                    if CUT <= 8:
                        return
                    psy = self.psum_next()
                    P.op("pe", lambda e: e.matmul(psy[:, 0:128], lhsT=rh[:], rhs=Hb[:, j, :], start=True, stop=True),
                         reads=[rh.b, Hb.b], writes=[psy.b])
                    for par in range(2):
                        hc = slice((2 * j + par) * 64, (2 * j + par + 1) * 64)
                        cs = slice(64 * par, 64 * par + 64)
                        P.op("pe", lambda e: e.matmul(psy[:, cs], lhsT=SC[par][:, 1, :], rhs=U_[:, cs], start=False, stop=False, skip_group_check=True),
                             reads=[SC[par].b, U_.b], writes=[psy.b])
                        P.op("pe", lambda e: e.matmul(psy[:, cs], lhsT=SC[par][:, 3, :], rhs=Vb[:, hc], start=False, stop=(par == 1), skip_group_check=True),
                             reads=[SC[par].b, Vb.b], writes=[psy.b])
                    P.op("act", lambda e: e.copy(out=Yt[:, js], in_=psy[:, 0:128]), reads=[psy.b], writes=[Yt.b])
                    psh = self.psum_next()
                    P.op("pe", lambda e: e.matmul(psh[:, 0:128], lhsT=T3[:, 2, :], rhs=Vb[:, js], start=True, stop=False),
                         reads=[T3.b, Vb.b], writes=[psh.b])
                    P.op("pe", lambda e: e.matmul(psh[:, 0:128], lhsT=T3[:, 1, :], rhs=U_[:], start=False, stop=True),
                         reads=[T3.b, U_.b], writes=[psh.b])
                    for par in range(2):
                        rows = slice(64 * par, 64 * par + 64)
                        P.op("dve", lambda e: e.scalar_tensor_tensor(out=Hbd[rows, j, rows], in0=Hbd[rows, j, rows], scalar=eL[rows, 127:128],
                                                                     in1=psh[rows, rows], op0=ALU.mult, op1=ALU.add),
                             reads=[Hbd.b, eL.b, psh.b], writes=[Hbd.b])
                        P.op("pool", lambda e: e.tensor_copy(out=Hb[rows, j, rows], in_=Hbd[rows, j, rows]), reads=[Hbd.b], writes=[Hb.b])
                    yield
                    if CUT <= 9:
                        return

                STAG = self.cfg.get("stagger", 0)
                pending = list(range(8))
                free_slots = list(range(NSLOT))
                active = []
                tick = 0
                next_start = 0
                while pending or active:
                    if pending and free_slots and tick >= next_start:
                        jn = pending.pop(0)
                        sl = free_slots.pop(0)
                        active.append((pair_gen(jn, slots[sl]), sl))
                        next_start = tick + STAG
                    for item in list(active):
                        try:
                            next(item[0])
                        except StopIteration:
                            active.remove(item)
                            free_slots.append(item[1])
                    tick += 1

                if "rw_o" in self.dbg:
                    P.dma("act", lambda e, t0=t0: e.dma_start(out=self.dbg["rw_o"][t0:t0 + CH, :], in_=Yt[:]), Yt.b, reads=[Yt.b])
                Y3 = Yt[:].rearrange("p (h n) -> p h n", n=64)
                S1t = tmpm[0][:].rearrange("p a t -> p (a t)")
                S1b = tmpm[0].b
                S2t = tmpm[1][:].rearrange("p a t -> p (a t)")
                S2b = tmpm[1].b
                P.op("act", lambda e: e.copy(out=rkb[:], in_=psB[:, 0:16]), reads=[psB.b], writes=[rkb.b])
                P.op("dve", lambda e: e.tensor_reduce(out=st[:, 0, :], in_=Y3, axis=AX.X, op=ALU.add), reads=[Yt.b], writes=[st.b])
                P.op("pool", lambda e: e.tensor_tensor(out=S1t, in0=Yt[:], in1=Yt[:], op=ALU.mult), reads=[Yt.b], writes=[S1b])
                P.op("dve", lambda e: e.tensor_reduce(out=st[:, 1, :], in_=S1t.rearrange("p (h n) -> p h n", n=64), axis=AX.X, op=ALU.add),
                     reads=[S1b], writes=[st.b])
                P.op("dve", lambda e: e.tensor_scalar(out=st[:, 2, :], in0=st[:, 0, :], scalar1=1.0 / 64, scalar2=None, op0=ALU.mult), reads=[st.b], writes=[st.b])
                P.op("dve", lambda e: e.tensor_tensor(out=st[:, 3, :], in0=st[:, 2, :], in1=st[:, 2, :], op=ALU.mult), reads=[st.b], writes=[st.b])
                P.op("dve", lambda e: e.scalar_tensor_tensor(out=st[:, 4, :], in0=st[:, 1, :], scalar=1.0 / 64, in1=st[:, 3, :], op0=ALU.mult, op1=ALU.subtract),
                     reads=[st.b], writes=[st.b])
                P.op("act", lambda e: e.activation(out=st[:, 5, :], in_=st[:, 4, :], func=AF.Sqrt, bias=eps2[:], scale=1.0), reads=[st.b, eps2.b], writes=[st.b])
                P.op("dve", lambda e: e.reciprocal(out=st[:, 5, :], in_=st[:, 5, :]), reads=[st.b], writes=[st.b])
                P.op("pool", lambda e: e.tensor_tensor(out=S1t.rearrange("p (h n) -> p h n", n=64), in0=Y3, in1=st[:, 2, :].unsqueeze(2).to_broadcast([128, 16, 64]), op=ALU.subtract),
                     reads=[Yt.b, st.b], writes=[S1b])
                P.op("dve", lambda e: e.tensor_tensor(out=S1t.rearrange("p (h n) -> p h n", n=64), in0=S1t.rearrange("p (h n) -> p h n", n=64),
                                                      in1=st[:, 5, :].unsqueeze(2).to_broadcast([128, 16, 64]), op=ALU.mult),
                     reads=[S1b, st.b], writes=[S1b])
                P.op("pool", lambda e: e.tensor_tensor(out=S1t, in0=S1t, in1=lnw[:], op=ALU.mult), reads=[S1b, lnw.b], writes=[S1b])
                P.op("dve", lambda e: e.tensor_tensor(out=S1t, in0=S1t, in1=lnb[:], op=ALU.add), reads=[S1b, lnb.b], writes=[S1b])
                P.op("pool", lambda e: e.tensor_tensor(out=S2t.rearrange("p (h n) -> p h n", n=64), in0=V[:].rearrange("p (h n) -> p h n", n=64),
                                                       in1=rkb[:].unsqueeze(2).to_broadcast([128, 16, 64]), op=ALU.mult),
                     reads=[V.b, rkb.b], writes=[S2b])
                P.op("dve", lambda e: e.tensor_tensor(out=S1t, in0=S1t, in1=S2t, op=ALU.add), reads=[S1b, S2b], writes=[S1b])
                for half in range(2):
                    ps = self.psum_next()
                    hs = slice(half * 512, (half + 1) * 512)
                    P.op("pe", lambda e: e.matmul(ps[:, :], lhsT=lgt[:, 0, :], rhs=g2b[:, 0, hs], start=True, stop=False),
                         reads=[lgt.b, g2b.b], writes=[ps.b])
                    P.op("pe", lambda e: e.matmul(ps[:, :], lhsT=lgt[0:32, 1, :], rhs=g2b[0:32, 1, hs], start=False, stop=True),
                         reads=[lgt.b, g2b.b], writes=[ps.b])
                    P.op("dve", lambda e: e.tensor_tensor(out=ogb[:, hs], in0=S1t[:, hs], in1=ps[:, :], op=ALU.mult), reads=[S1b, ps.b], writes=[ogb.b])
                for jj in range(8):
                    P.op("pe", lambda e, jj=jj: e.transpose(out=self.psbf[:, jj * 128:(jj + 1) * 128], in_=ogb[:, jj * 128:(jj + 1) * 128], identity=self.identb[:]),
                         reads=[ogb.b, self.identb.b], writes=[self.psbf.b])
                P.op("act", lambda e: e.copy(out=ogT[:].rearrange("p a t -> p (a t)"), in_=self.psbf[:, :]), reads=[self.psbf.b], writes=[ogT.b])
                for half in range(2):
                    ps = self.psum_next()
                    for jj in range(4):
                        nj = half * 4 + jj
                        for kc in range(8):
                            P.op("pe", lambda e, ps=ps, jj=jj, nj=nj, kc=kc: e.matmul(ps[:, jj * 128:(jj + 1) * 128], lhsT=Wo[:, kc, nj * 128:(nj + 1) * 128], rhs=ogT[:, kc, :],
                                                                                    start=(kc == 0), stop=(kc == 7)), reads=[Wo.b, ogT.b], writes=[ps.b])
                    for jj in range(4):
                        nj = half * 4 + jj
                        P.op("dve", lambda e, ps=ps, jj=jj, nj=nj: e.scalar_tensor_tensor(out=x[:, nj, :], in0=ps[:, jj * 128:(jj + 1) * 128], scalar=g1c[:, nj:nj + 1],
                                                                                         in1=x[:, nj, :], op0=ALU.mult, op1=ALU.add),
                             reads=[ps.b, x.b, self.modc.b], writes=[x.b])
                dst = self.X.rearrange("(j p) t -> p j t", p=128)[:, :, t0:t0 + CH]
                P.dma("sp", lambda e, dst=dst: e.dma_start(out=dst, in_=x[:]), x.b, reads=[x.b])
            self.end_phase()

    def rope_proj(self, es, W, hb, ncol0, dst_blk, CTt, STt, tmpA, tmpB):
        P = self.P
        roper = self.cst["roper"]
        tmpAs, tmpBs = tmpA, tmpB
        for hh in range(8):
            tmpA = tmpAs[hh % len(tmpAs)]
            tmpB = tmpBs[hh % len(tmpBs)]
            ps = self.psum_next()
            for kc in range(8):
                P.op("pe", lambda e: e.matmul(ps[:, :], lhsT=W[:, kc, ncol0 + hh * 128:ncol0 + (hh + 1) * 128], rhs=hb[:, kc, :],
                                              start=(kc == 0), stop=(kc == 7)), reads=[W.b, hb.b], writes=[ps.b])
            P.op("act", lambda e: e.copy(out=tmpA[:], in_=ps[:, :]), reads=[ps.b], writes=[tmpA.b])
            ps2 = self.psum_next()
            P.op("pe", lambda e: e.matmul(ps2[:, :], lhsT=roper[:], rhs=tmpA[:], start=True, stop=True), reads=[roper.b, tmpA.b], writes=[ps2.b])
            P.op("dve", lambda e: e.tensor_tensor(out=tmpB[:], in0=ps2[:, :], in1=STt[:], op=ALU.mult), reads=[ps2.b, STt.b], writes=[tmpB.b])
            P.op("pool", lambda e: e.tensor_tensor(out=tmpA[:], in0=tmpA[:], in1=CTt[:], op=ALU.mult), reads=[tmpA.b, CTt.b], writes=[tmpA.b])
            P.op("pool", lambda e: e.tensor_tensor(out=dst_blk[:, hh, :], in0=tmpA[:], in1=tmpB[:], op=ALU.add), reads=[tmpA.b, tmpB.b], writes=[dst_blk.b])

    def phase_kv(self, xsrc):
        P, nc, inp = self.P, self.nc, self.inp
        TB = 512
        with ExitStack() as es:
            self._stg = None
            Wkv = self.tile(es, "Wkv", [128, 8, 2 * D], BF16)
            s3 = inp["w_kv"].rearrange("(kc p) n -> p kc n", p=128)
            pieces = []
            for kc in range(8):
                for hf in range(2):
                    pieces.append((Wkv[:, kc:kc + 1, hf * D:(hf + 1) * D], s3[:, kc:kc + 1, hf * D:(hf + 1) * D], 128, 1, D))
            self.load_cast(es, pieces, Wkv.b)
            xs_ = [self.tile(es, "kx%d" % i, [128, 8, TB], dma=True) for i in range(2)]
            sq = self.tile(es, "ksq", [128, 8, TB])
            self.rstd = self.tile(es, "krstd", [128, TB])
            hbs_ = [self.tile(es, "khb%d" % i, [128, 8, TB], BF16) for i in range(2)]
            CTt = self.tile(es, "kCT", [128, TB], dma=True)
            STt = self.tile(es, "kST", [128, TB], dma=True)
            tmpA = [self.tile(es, "ktA%d" % i, [128, TB]) for i in range(3)]
            tmpB = [self.tile(es, "ktB%d" % i, [128, TB]) for i in range(3)]
            Kblks = [self.tile(es, "Kblk%d" % i, [128, 8, TB], BF16, dma=True) for i in range(2)]
            Vblks = [self.tile(es, "Vblk%d" % i, [128, 4, D], BF16, dma=True) for i in range(2)]
            G = self.col("kv_norm", 0, 8)
            for nb in range(T // TB):
                t0 = nb * TB
                x, hb, Kblk, Vblk = xs_[nb % 2], hbs_[nb % 2], Kblks[nb % 2], Vblks[nb % 2]
                src = xsrc.rearrange("(j p) t -> p j t", p=128)[:, :, t0:t0 + TB]
                P.dma("sp", lambda e: e.dma_start(out=x[:], in_=src), x.b, writes=[x.b])
                P.dma("act", lambda e: e.dma_start(out=CTt[:], in_=inp["ropec"][:, t0:t0 + TB]), CTt.b, writes=[CTt.b])
                P.dma("act", lambda e: e.dma_start(out=STt[:], in_=inp["ropes"][:, t0:t0 + TB]), STt.b, writes=[STt.b])
                self.rmsnorm(x, TB, sq, G, None, hb[:], hb.b)
                self.rope_proj(es, Wkv, hb, 0, Kblk, CTt, STt, tmpA, tmpB)
                dst = self.KT.rearrange("(j p) t -> p j t", p=128)[:, :, t0:t0 + TB]
                P.dma("sp", lambda e: e.dma_start(out=dst, in_=Kblk[:]), Kblk.b, reads=[Kblk.b])
                for tl in range(4):
                    for half in range(2):
                        ps = self.psum_next()
                        for kc in range(8):
                            P.op("pe", lambda e: e.matmul(ps[:, :], lhsT=hb[:, kc, tl * 128:(tl + 1) * 128], rhs=Wkv[:, kc, D + half * 512:D + (half + 1) * 512],
                                                          start=(kc == 0), stop=(kc == 7)), reads=[hb.b, Wkv.b], writes=[ps.b])
                        self.copy(("act", "dve")[half], Vblk[:, tl, half * 512:(half + 1) * 512], ps[:, :], [ps.b], [Vblk.b])
                dstv = self.VS[t0:t0 + TB, :].rearrange("(a p) e -> p a e", p=128)
                P.dma("sp", lambda e: e.dma_start(out=dstv, in_=Vblk[:]), Vblk.b, reads=[Vblk.b])
            self.end_phase()

    def phase_attn(self, l, xsrc):
        P, nc, inp = self.P, self.nc, self.inp
        jl = l - 2
        TB = 512
        lam_init = 0.8 - 0.6 * math.exp(-0.3 * l)
        G1 = self.der[:, l, 0, :]
        S1 = self.modcol(l, 0)
        g1c = self.modcol(l, 2)
        with ExitStack() as es:
            self._stg = None
            Wq = self.tile(es, "Wq", [128, 8, D], BF16)
            s3 = inp["b_w_q"][jl].rearrange("(kc p) n -> p kc n", p=128)
            self.load_cast(es, [(Wq[:, kc:kc + 1, :], s3[:, kc:kc + 1, :], 128, 1, D) for kc in range(8)], Wq.b)
            xs_ = [self.tile(es, "qx%d" % i, [128, 8, TB], dma=True) for i in range(2)]
            sq = self.tile(es, "qsq", [128, 8, TB])
            self.rstd = self.tile(es, "qrstd", [128, TB])
            hbs_ = [self.tile(es, "qhb%d" % i, [128, 8, TB], BF16) for i in range(2)]
            CTt = self.tile(es, "qCT", [128, TB], dma=True)
            STt = self.tile(es, "qST", [128, TB], dma=True)
            tmpA = [self.tile(es, "qtA%d" % i, [128, TB]) for i in range(3)]
            tmpB = [self.tile(es, "qtB%d" % i, [128, TB]) for i in range(3)]
            Qblks = [self.tile(es, "Qblk%d" % i, [128, 8, TB], BF16, dma=True) for i in range(2)]
            for nb in range(T // TB):
                t0 = nb * TB
                x, hb, Qblk = xs_[nb % 2], hbs_[nb % 2], Qblks[nb % 2]
                src = xsrc.rearrange("(j p) t -> p j t", p=128)[:, :, t0:t0 + TB]
                P.dma("sp", lambda e: e.dma_start(out=x[:], in_=src), x.b, writes=[x.b])
                P.dma("act", lambda e: e.dma_start(out=CTt[:], in_=inp["ropec"][:, t0:t0 + TB]), CTt.b, writes=[CTt.b])
                P.dma("act", lambda e: e.dma_start(out=STt[:], in_=inp["ropes"][:, t0:t0 + TB]), STt.b, writes=[STt.b])
                self.rmsnorm(x, TB, sq, G1, S1, hb[:], hb.b)
                self.rope_proj(es, Wq, hb, 0, Qblk, CTt, STt, tmpA, tmpB)
                dst = self.QT.rearrange("(j p) t -> p j t", p=128)[:, :, t0:t0 + TB]
                P.dma("sp", lambda e: e.dma_start(out=dst, in_=Qblk[:]), Qblk.b, reads=[Qblk.b])
            self.end_phase()

        with ExitStack() as es:
            NH = self.cfg.get("nheads", 8)
            NQB = self.cfg.get("nqb", T // TB)
            lamv = self.tile(es, "lamv", [128, 256], dma=True)
            o_l = 7 * D + jl * 256
            P.dma("sp", lambda e: e.dma_start(out=lamv[:], in_=inp["rows"][o_l:o_l + 256].partition_broadcast(128)), lamv.b, writes=[lamv.b])
            subw = self.tile(es, "subw", [128, 128], dma=True)
            o_s = 5 * D + jl * D
            P.dma("sp", lambda e: e.dma_start(out=subw[:], in_=inp["rows"][o_s:o_s + 128].partition_broadcast(128)), subw.b, writes=[subw.b])
            P.op("dve", lambda e: e.tensor_scalar(out=subw[:], in0=subw[:], scalar1=(1.0 - lam_init), scalar2=None, op0=ALU.mult), reads=[subw.b], writes=[subw.b])
            lt = self.tile(es, "lt", [128, 2, 64])
            ls = self.tile(es, "ls", [128, 4])
            P.op("dve", lambda e: e.tensor_tensor(out=lt[:, 0, :], in0=lamv[:, 0:64], in1=lamv[:, 64:128], op=ALU.mult), reads=[lamv.b], writes=[lt.b])
            P.op("dve", lambda e: e.tensor_tensor(out=lt[:, 1, :], in0=lamv[:, 128:192], in1=lamv[:, 192:256], op=ALU.mult), reads=[lamv.b], writes=[lt.b])
            P.op("dve", lambda e: e.tensor_reduce(out=ls[:, 0:2], in_=lt[:], axis=AX.X, op=ALU.add), reads=[lt.b], writes=[ls.b])
            P.op("act", lambda e: e.activation(out=ls[:, 0:2], in_=ls[:, 0:2], func=AF.Exp), reads=[ls.b], writes=[ls.b])
            P.op("dve", lambda e: e.tensor_tensor(out=ls[:, 2:3], in0=ls[:, 1:2], in1=ls[:, 0:1], op=ALU.subtract), reads=[ls.b], writes=[ls.b])
            P.op("dve", lambda e: e.tensor_scalar(out=ls[:, 3:4], in0=ls[:, 2:3], scalar1=-lam_init, scalar2=None, op0=ALU.add), reads=[ls.b], writes=[ls.b])
            neglam = ls[:, 3:4]
            cmaskb = self.tile(es, "cmaskb", [128, 128], BF16)
            self.copy("pool", cmaskb[:], self.cst["mask4"][:, 128:256], [self.cst["mask4"].b], [cmaskb.b])
            eps1 = self.eps

            KTh = [self.tile(es, "KTh%d" % i, [128, T], BF16, dma=True) for i in range(2)]
            QTh = [self.tile(es, "QTh%d" % i, [128, T], BF16, dma=True) for i in range(2)]
            Vh = [self.tile(es, "Vh%d" % i, [128, 32, 129], BF16, dma=True) for i in range(2)]
            YTh = [self.tile(es, "YTh%d" % i, [128, T], BF16, dma=True) for i in range(2)]
            for i in range(2):
                P.op("pool", lambda e: e.memset(Vh[i][:, :, 128:129], 1.0), writes=[Vh[i].b])
            ET = [self.tile(es, "ET%d" % i, [128, 512], BF16) for i in range(4)]
            Oc = [self.tile(es, "Oc%d" % i, [128, 4, 129]) for i in range(2)]
            rz = self.tile(es, "rz", [128, 2, 4])
            y = self.tile(es, "ay", [128, 4, 128])
            ysq = self.tile(es, "aysq", [128, 4, 128])
            ss = self.tile(es, "ass", [128, 4])
            ynb = self.tile(es, "aynb", [128, 4, 128], BF16)
            if "at_o" in self.dbg:
                self.dbg_tile = self.tile(es, "dbgt", [128, 4, 128], dma=True)
            eti = 0
            for hh in range(NH):
                kt_, qt_, vh_, yt_ = KTh[hh % 2], QTh[hh % 2], Vh[hh % 2], YTh[hh % 2]
                hs = slice(hh * 128, (hh + 1) * 128)
                P.dma("sp", lambda e: e.dma_start(out=kt_[:], in_=self.KT[hs, :]), kt_.b, writes=[kt_.b])
                P.dma("act", lambda e: e.dma_start(out=qt_[:], in_=self.QT[hs, :]), qt_.b, writes=[qt_.b])
                P.dma("sp", lambda e: e.dma_start(out=vh_[:, :, 0:128], in_=self.VS.rearrange("(kt p) e -> p kt e", p=128)[:, :, hs]), vh_.b, writes=[vh_.b])
                sbanks = [self.ps[0], self.ps[1], self.ps[6]]
                tasks = []
                for qb in range(NQB):
                    for cc in range(2):
                        for kt in range(4 * qb + 4):
                            tasks.append((qb, cc, kt))

                def score(ti):
                    qb, cc, kt = tasks[ti]
                    rows = slice(64 * cc, 64 * cc + 64)
                    c0 = max(kt - 4 * qb, 0) * 128
                    pS = sbanks[ti % 3]
                    P.op("pe", lambda e: e.matmul(pS[:, c0:512], lhsT=kt_[rows, kt * 128:(kt + 1) * 128], rhs=qt_[rows, qb * 512 + c0:(qb + 1) * 512],
                                                  start=True, stop=True), reads=[kt_.b, qt_.b], writes=[pS.b])

                def combine(qb):
                    qs = slice(qb * 512, (qb + 1) * 512)
                    P.op("dve", lambda e: e.reciprocal(out=rz[:, 0, :], in_=Oc[0][:, :, 128]), reads=[Oc[0].b], writes=[rz.b])
                    P.op("dve", lambda e: e.reciprocal(out=rz[:, 1, :], in_=Oc[1][:, :, 128]), reads=[Oc[1].b], writes=[rz.b])
                    P.op("dve", lambda e: e.tensor_scalar(out=rz[:, 1, :], in0=rz[:, 1, :], scalar1=neglam, scalar2=None, op0=ALU.mult), reads=[rz.b, ls.b], writes=[rz.b])
                    P.op("pool", lambda e: e.tensor_tensor(out=y[:], in0=Oc[0][:, :, 0:128], in1=rz[:, 0, :].unsqueeze(2).to_broadcast([128, 4, 128]), op=ALU.mult),
                         reads=[Oc[0].b, rz.b], writes=[y.b])
                    P.op("pool", lambda e: e.tensor_tensor(out=ysq[:], in0=Oc[1][:, :, 0:128], in1=rz[:, 1, :].unsqueeze(2).to_broadcast([128, 4, 128]), op=ALU.mult),
                         reads=[Oc[1].b, rz.b], writes=[ysq.b])
                    P.op("dve", lambda e: e.tensor_tensor(out=y[:], in0=y[:], in1=ysq[:], op=ALU.add), reads=[y.b, ysq.b], writes=[y.b])
                    if "at_o" in self.dbg:
                        dtl = self.dbg_tile
                        self.copy("dve", dtl[:], y[:], [y.b], [dtl.b])
                        dd = self.dbg["at_o"][qb * 512:(qb + 1) * 512, hs].rearrange("(a p) e -> p a e", p=128)
                        P.dma("act", lambda e: e.dma_start(out=dd, in_=dtl[:]), dtl.b, reads=[dtl.b])
                    P.op("pool", lambda e: e.tensor_tensor(out=ysq[:], in0=y[:], in1=y[:], op=ALU.mult), reads=[y.b], writes=[ysq.b])
                    P.op("dve", lambda e: e.tensor_reduce(out=ss[:], in_=ysq[:], axis=AX.X, op=ALU.add), reads=[ysq.b], writes=[ss.b])
                    P.op("act", lambda e: e.activation(out=ss[:], in_=ss[:], func=AF.Sqrt, bias=eps1[:], scale=1.0 / 128), reads=[ss.b, eps1.b], writes=[ss.b])
                    P.op("dve", lambda e: e.reciprocal(out=ss[:], in_=ss[:]), reads=[ss.b], writes=[ss.b])
                    P.op("pool", lambda e: e.tensor_tensor(out=y[:], in0=y[:], in1=ss[:].unsqueeze(2).to_broadcast([128, 4, 128]), op=ALU.mult),
                         reads=[y.b, ss.b], writes=[y.b])
                    P.op("dve", lambda e: e.tensor_tensor(out=ynb[:], in0=y[:], in1=subw[:].unsqueeze(1).to_broadcast([128, 4, 128]), op=ALU.mult),
                         reads=[y.b, subw.b], writes=[ynb.b])
                    for qt in range(4):
                        P.op("pe", lambda e: e.transpose(out=self.psbf[:, qt * 128:(qt + 1) * 128], in_=ynb[:, qt, :], identity=self.identb[:]),
                             reads=[ynb.b, self.identb.b], writes=[self.psbf.b])
                    P.op("act", lambda e: e.copy(out=yt_[:, qs], in_=self.psbf[:, 0:512]), reads=[self.psbf.b], writes=[yt_.b])

                LOOK = 2
                for ti in range(min(LOOK, len(tasks))):
                    score(ti)
                for ti in range(len(tasks)):
                    if ti + LOOK < len(tasks):
                        score(ti + LOOK)
                    qb, cc, kt = tasks[ti]
                    r = kt - 4 * qb
                    c0 = max(r, 0) * 128
                    pS = sbanks[ti % 3]
                    pO = [self.ps[2 + 2 * cc], self.ps[3 + 2 * cc]]
                    et = ET[ti % 4]
                    P.op("act", lambda e: e.activation(out=et[:, c0:512], in_=pS[:, c0:512], func=AF.Exp, scale=0.125), reads=[pS.b], writes=[et.b])
                    if r >= 0:
                        P.op("pool", lambda e: e.tensor_tensor(out=et[:, c0:c0 + 128], in0=et[:, c0:c0 + 128], in1=cmaskb[:], op=ALU.mult),
                             reads=[et.b, cmaskb.b], writes=[et.b])
                    for qt in range(max(r, 0), 4):
                        po = pO[qt // 2]
                        oc = (qt % 2) * 129
                        P.op("pe", lambda e: e.matmul(po[:, oc:oc + 129], lhsT=et[:, qt * 128:(qt + 1) * 128], rhs=vh_[:, kt, :],
                                                      start=(kt == 0 and qt % 2 == 0), stop=(kt == 4 * qb + qt), skip_group_check=True),
                             reads=[et.b, vh_.b], writes=[po.b])
                    if kt == 4 * qb + 3:
                        for i2 in range(2):
                            self.copy(("act", "dve")[i2], Oc[cc][:, 2 * i2:2 * i2 + 2, :].rearrange("p a e -> p (a e)"), pO[i2][:, 0:258], [pO[i2].b], [Oc[cc].b])
                        if cc == 1:
                            combine(qb)
                P.dma("sp", lambda e: e.dma_start(out=self.YT[hs, :], in_=yt_[:]), yt_.b, reads=[yt_.b])
            self.end_phase()

        with ExitStack() as es:
            self._stg = None
            Wo = self.tile(es, "aWo", [128, 8, D], BF16)
            s3 = inp["b_w_o"][jl].rearrange("(kc p) n -> p kc n", p=128)
            self.load_cast(es, [(Wo[:, kc:kc + 1, :], s3[:, kc:kc + 1, :], 128, 1, D) for kc in range(8)], Wo.b)
            xs = [self.tile(es, "cx%d" % i, [128, 8, TB], dma=True) for i in range(2)]
            ys = [self.tile(es, "cy%d" % i, [128, 8, TB], BF16, dma=True) for i in range(2)]
            for nb in range(T // TB):
                t0 = nb * TB
                x = xs[nb % 2]
                yb = ys[nb % 2]
                src = xsrc.rearrange("(j p) t -> p j t", p=128)[:, :, t0:t0 + TB]
                P.dma("sp", lambda e: e.dma_start(out=x[:], in_=src), x.b, writes=[x.b])
                srcy = self.YT.rearrange("(j p) t -> p j t", p=128)[:, :, t0:t0 + TB]
                P.dma("act", lambda e: e.dma_start(out=yb[:], in_=srcy), yb.b, writes=[yb.b])
                for nj in range(8):
                    ps = self.psum_next()
                    for kc in range(8):
                        P.op("pe", lambda e: e.matmul(ps[:, :], lhsT=Wo[:, kc, nj * 128:(nj + 1) * 128], rhs=yb[:, kc, :], start=(kc == 0), stop=(kc == 7)),
                             reads=[Wo.b, yb.b], writes=[ps.b])
                    P.op("dve", lambda e: e.scalar_tensor_tensor(out=x[:, nj, :], in0=ps[:, :], scalar=g1c[:, nj:nj + 1], in1=x[:, nj, :], op0=ALU.mult, op1=ALU.add),
                         reads=[ps.b, x.b, self.modc.b], writes=[x.b])
                dst = self.X.rearrange("(j p) t -> p j t", p=128)[:, :, t0:t0 + TB]
                P.dma("sp", lambda e: e.dma_start(out=dst, in_=x[:]), x.b, reads=[x.b])
            self.end_phase()


_CONSTS = None


def prepare_inputs(inputs):
    global _CONSTS
    if _CONSTS is None:
        _CONSTS = make_consts()
    f = lambda a: np.ascontiguousarray(np.asarray(a, np.float32))
    vecs = {}
    for l in range(4):
        vecs["ada_b%d" % l] = inputs["ada_b"][l]
        vecs["norm1_%d" % l] = inputs["norm1"][l]
        vecs["norm2_%d" % l] = inputs["norm2"][l]
        for i in range(3):
            vecs["cw%d_%d" % (i, l)] = inputs["ffn_conv_w"][l][i]
        vecs["cb_%d" % l] = inputs["ffn_conv_b"][l]
    vecs["final_norm"] = inputs["final_norm"]
    vecs["kv_norm"] = inputs["kv_norm"]
    for l in range(2):
        for i in range(6):
            vecs["mu%d_%d" % (i, l)] = inputs["a_mu"][l][i]
        vecs["w0_%d" % l] = inputs["a_w0"][l]
        vecs["a0_%d" % l] = inputs["a_a0"][l]
        vecs["k_k_%d" % l] = inputs["a_k_k"][l]
        vecs["k_a_%d" % l] = inputs["a_k_a"][l]
        vecs["r_k_%d" % l] = np.asarray(inputs["a_r_k"][l]).reshape(-1)
    cols = CP.pack(vecs)
    rows = np.concatenate([
        f(inputs["a_ln_w"][0]), f(inputs["a_ln_b"][0]), f(inputs["a_ln_w"][1]), f(inputs["a_ln_b"][1]),
        f(inputs["a_v0"][0]),
        np.tile(f(inputs["b_subln"][0]), 8), np.tile(f(inputs["b_subln"][1]), 8),
        f(inputs["b_lam"][0]).reshape(-1), f(inputs["b_lam"][1]).reshape(-1)])
    shared = dict(_CONSTS)
    shared["cols"] = cols
    shared["rows"] = rows
    for k in ("ada_w", "a_w_rkv", "a_w1", "a_w2", "a_a1", "a_a2", "a_v1", "a_v2", "a_g1", "a_g2", "a_w_o",
              "w_kv", "b_w_q", "b_w_o", "ffn_w_up", "ffn_w_down"):
        shared[k] = f(inputs[k])
    x = np.asarray(inputs["x"], np.float32)
    c = np.asarray(inputs["c"], np.float32)
    in_maps = []
    for b in range(NCORES):
        m = dict(shared)
        m["xT"] = np.ascontiguousarray(x[b].T)
        m["ccol"] = np.ascontiguousarray(c[b].reshape(8, 128).T)
        in_maps.append(m)
    return in_maps


_NC_CACHE = {}


def run(inputs, cfg, key="full", ncores=NCORES):
    if key not in _NC_CACHE:
        _NC_CACHE[key] = Builder(cfg).build()
    nc = _NC_CACHE[key]
    in_maps = prepare_inputs(inputs)[:ncores]
    res = run_bass_kernel_spmd(nc, in_maps, core_ids=list(range(ncores)))
    return res


def kernel(**inputs):
    cfg = {"layers": [0, 1, 2, 3]}
    res = run(inputs, cfg)
    out = np.stack([np.ascontiguousarray(r["outT"].T) for r in res.results], axis=0)
    return out.astype(np.float32)
```

```python
import math
import numpy as np
import concourse.bass as bass
import concourse.mybir as mybir
from concourse.bass_utils import run_bass_kernel_spmd
from contextlib import ExitStack
import types

F32 = mybir.dt.float32
BF16 = mybir.dt.bfloat16
AF = mybir.ActivationFunctionType
ALU = mybir.AluOpType
AX = mybir.AxisListType

D = 1024
T = 4096
NJ = 8
DFF = 2816
F2 = 5632
NF = 44
NG = 22
C0 = math.exp(-0.5)
NCORES = 8

ENGS = ["pe", "act", "dve", "pool", "sp"]


def freeze(fn):
    if fn.__closure__ is None:
        return fn
    cells = []
    for c in fn.__closure__:
        try:
            cells.append(types.CellType(c.cell_contents))
        except ValueError:
            cells.append(c)
    return types.FunctionType(fn.__code__, fn.__globals__, fn.__name__, fn.__defaults__, tuple(cells))


class Buf:
    __slots__ = ("name", "lw", "rd", "dsem", "excl")

    def __init__(self, name):
        self.name = name
        self.lw = None
        self.rd = {}
        self.dsem = None
        self.excl = False


class Tl:
    __slots__ = ("ap", "b")

    def __init__(self, ap, b):
        self.ap = ap
        self.b = b

    def __getitem__(self, k):
        return self.ap[k]


class Prog:
    def __init__(self, nc, es, n_dma_sems=40):
        self.nc = nc
        self.es = es
        self.q = {e: [] for e in ENGS}
        self.cnt = {e: 0 for e in ENGS}
        self.sems = {}
        self.semkey = 0
        self.esem = {e: self._newsem("c_" + e) for e in ENGS}
        self.seen = {e: {} for e in ENGS}
        self.bar = self._newsem("bar")
        self.nbar = 0
        self.dma_pool = [self._newsem("d%d" % i) for i in range(n_dma_sems)]
        self.dma_cnt = {k: 0 for k in self.dma_pool}
        self.dma_free = list(self.dma_pool)
        self.ninstr = 0

    def _newsem(self, name):
        s = self.es.enter_context(self.nc.semaphore(name))
        self.semkey += 1
        self.sems[self.semkey] = s
        return self.semkey

    def buf(self, name):
        return Buf(name)

    def dma_buf(self, name):
        b = Buf(name)
        b.dsem = self.dma_free.pop(0)
        return b

    def release(self, bufs):
        for b in bufs:
            if b.dsem is not None:
                self.dma_free.append(b.dsem)
                b.dsem = None

    def _waits(self, e, reads, writes, is_dma=False):
        need = {}
        seen = self.seen[e]

        def add(ev, raw):
            key, val, src = ev
            if src == e and not is_dma and (e in ("pe", "sp") or not raw):
                return
            if seen.get(key, 0) >= val:
                return
            if need.get(key, 0) < val:
                need[key] = val

        for b in reads:
            if b.lw is not None:
                add(b.lw, True)
            if b.excl:
                for src, ev in b.rd.items():
                    if src != e:
                        add(ev, False)
        for b in writes:
            if b.lw is not None:
                add(b.lw, False)
            for ev in b.rd.values():
                add(ev, False)
        out = []
        for key, val in need.items():
            seen[key] = val
            out.append((self.sems[key], val))
        return out

    def _emit(self, e, fn, waits, sem, inc):
        fn = freeze(fn)

        attach = (inc == 1 and e in ("act", "dve", "pool") and len(waits) > 0)

        def run(eng, fn=fn, waits=waits, sem=sem, inc=inc, attach=attach):
            for (s, v) in (waits[:-1] if attach else waits):
                eng.wait_ge(s, v)
            ins = fn(eng)
            if attach:
                ins._wait_ge(waits[-1][0], waits[-1][1])
            ins.then_inc(sem, inc)
        self.q[e].append(run)
        self.ninstr += 1 + len(waits)

    def op(self, e, fn, reads=(), writes=()):
        waits = self._waits(e, reads, writes)
        self.cnt[e] += 1
        key = self.esem[e]
        self._emit(e, fn, waits, self.sems[key], 1)
        ev = (key, self.cnt[e], e)
        for b in writes:
            b.lw = ev
            b.rd = {}
        for b in reads:
            if b not in writes:
                b.rd[e] = ev

    def dma(self, e, fn, owner, reads=(), writes=()):
        assert owner.dsem is not None, owner.name
        waits = self._waits(e, reads, writes, is_dma=True)
        key = owner.dsem
        self.dma_cnt[key] += 16
        self._emit(e, fn, waits, self.sems[key], 16)
        src = "dma%d" % key
        ev = (key, self.dma_cnt[key], src)
        for b in writes:
            b.lw = ev
            b.rd = {}
        for b in reads:
            if b not in writes:
                b.rd[src] = ev

    def barrier(self):
        g = "sp"
        gw = []
        for e in ENGS:
            if e == g or self.cnt[e] == 0:
                continue
            key = self.esem[e]
            if self.seen[g].get(key, 0) < self.cnt[e]:
                gw.append((self.sems[key], self.cnt[e]))
        for key in self.dma_pool:
            val = self.dma_cnt[key]
            if val > 0 and self.seen[g].get(key, 0) < val:
                gw.append((self.sems[key], val))
        self.nbar += 1
        bsem = self.sems[self.bar]

        def run_g(eng, gw=gw, bsem=bsem):
            for (s, v) in gw:
                eng.wait_ge(s, v)
            eng.sem_inc(bsem, 1)
        self.q[g].append(run_g)
        for e in ENGS:
            if e != g:
                self.q[e].append(lambda eng, sem=bsem, val=self.nbar: eng.wait_ge(sem, val))
        for e in ENGS:
            for e2 in ENGS:
                self.seen[e][self.esem[e2]] = self.cnt[e2]
            for key in self.dma_pool:
                self.seen[e][key] = self.dma_cnt[key]
        for e in ENGS:
            if self.cnt[e] > 12000:
                self.esem[e] = self._newsem("c_%s_%d" % (e, self.nbar))
                self.cnt[e] = 0

    def finish(self):
        nc = self.nc
        self.barrier()
        with nc.Block() as block:
            @block.tensor
            def _(eng):
                for f in self.q["pe"]:
                    f(eng)

            @block.scalar
            def _(eng):
                for f in self.q["act"]:
                    f(eng)

            @block.vector
            def _(eng):
                for f in self.q["dve"]:
                    f(eng)

            @block.gpsimd
            def _(eng):
                for f in self.q["pool"]:
                    f(eng)

            @block.sync
            def _(eng):
                for f in self.q["sp"]:
                    f(eng)


class ColPack:
    def __init__(self):
        self.off = {}
        self.n = 0
        self.items = []

    def add(self, name, length):
        assert length % 128 == 0
        self.off[name] = self.n
        self.n += length // 128
        self.items.append((name, length))

    def pack(self, vecs):
        arr = np.zeros((128, self.n), np.float32)
        for name, length in self.items:
            v = np.asarray(vecs[name], np.float32).reshape(length // 128, 128)
            arr[:, self.off[name]:self.off[name] + length // 128] = v.T
        return arr


def make_colpack():
    cp = ColPack()
    for l in range(4):
        cp.add("ada_b%d" % l, 6 * D)
        cp.add("norm1_%d" % l, D)
        cp.add("norm2_%d" % l, D)
        for i in range(3):
            cp.add("cw%d_%d" % (i, l), F2)
        cp.add("cb_%d" % l, F2)
    cp.add("final_norm", D)
    cp.add("kv_norm", D)
    for l in range(2):
        for i in range(6):
            cp.add("mu%d_%d" % (i, l), D)
        for nm in ("w0", "a0", "k_k", "k_a", "r_k"):
            cp.add("%s_%d" % (nm, l), D)
    return cp


CP = make_colpack()


def make_consts():
    c = {}
    c["ident"] = np.eye(128, dtype=np.float32)
    c["ones"] = np.ones((128, 128), np.float32)
    bd = np.zeros((128, 128), np.float32)
    bd[:64, :64] = 1
    bd[64:, 64:] = 1
    c["bd"] = bd
    ind = np.zeros((128, 8, 16), np.float32)
    for p in range(128):
        for j in range(8):
            ind[p, j, 2 * j + p // 64] = 1
    c["ind"] = ind.reshape(128, 128)
    s = np.arange(128)[:, None]
    t = np.arange(128)[None, :]
    strict = (t > s).astype(np.float32)
    incl = (t >= s).astype(np.float32)
    c["mask4"] = np.concatenate([strict, incl, strict, incl], axis=1)
    c["maskL"] = (t < s).astype(np.float32)
    pos = np.arange(T, dtype=np.float64)
    inv = 500000.0 ** (-np.arange(0, 16, 2, dtype=np.float64) / 16)
    ct = np.ones((128, T), np.float64)
    st = np.zeros((128, T), np.float64)
    rm = np.zeros((128, 128), np.float32)
    for cc in range(2):
        for d in range(16):
            p = cc * 64 + d
            ang = pos * inv[d % 8]
            ct[p] = np.cos(ang)
            st[p] = np.sin(ang)
            if d < 8:
                rm[p + 8, p] = -1.0
            else:
                rm[p - 8, p] = 1.0
    c["ropec"] = ct.astype(np.float32)
    c["ropes"] = st.astype(np.float32)
    c["roper"] = rm
    return c


class Builder:
    def __init__(self, cfg):
        self.cfg = cfg
        self.nc = bass.Bass("TRN2", target_bir_lowering=False)
        self.uid = 0
        self.rr = 0

    def dram_in(self, name, shape, dt=F32):
        return self.nc.dram_tensor(name, list(shape), dt, kind="ExternalInput").ap()

    def tile(self, es, name, shape, dt=F32, dma=False):
        self.uid += 1
        t = es.enter_context(self.nc.sbuf_tensor("%s_%d" % (name, self.uid), list(shape), dt))
        b = self.P.dma_buf(name) if dma else self.P.buf(name)
        if dma:
            self.phase_dma_bufs.append(b)
        return Tl(t, b)

    def eng_rr(self, engs=("dve", "pool", "act")):
        self.rr += 1
        return engs[self.rr % len(engs)]

    def copy(self, e, out_ap, in_ap, reads, writes):
        P = self.P
        if e == "act":
            P.op("act", lambda g: g.copy(out=out_ap, in_=in_ap), reads=reads, writes=writes)
        else:
            P.op(e, lambda g: g.tensor_copy(out=out_ap, in_=in_ap), reads=reads, writes=writes)

    def psum_next(self):
        self.psi = (self.psi + 1) % 6
        return self.ps[self.psi]

    def build(self):
        nc = self.nc
        cfg = self.cfg
        inp = {}
        inp["xT"] = self.dram_in("xT", [D, T])
        inp["ccol"] = self.dram_in("ccol", [128, 8])
        inp["cols"] = self.dram_in("cols", [128, CP.n])
        for k in ("ident", "ones", "bd", "ind", "maskL", "roper"):
            inp[k] = self.dram_in(k, [128, 128])
        inp["mask4"] = self.dram_in("mask4", [128, 512])
        inp["ropec"] = self.dram_in("ropec", [128, T])
        inp["ropes"] = self.dram_in("ropes", [128, T])
        inp["ada_w"] = self.dram_in("ada_w", [4, D, 6 * D])
        inp["a_w_rkv"] = self.dram_in("a_w_rkv", [2, 3, D, D])
        inp["a_w1"] = self.dram_in("a_w1", [2, D, 64])
        inp["a_w2"] = self.dram_in("a_w2", [2, 64, D])
        inp["a_a1"] = self.dram_in("a_a1", [2, D, 64])
        inp["a_a2"] = self.dram_in("a_a2", [2, 64, D])
        inp["a_v1"] = self.dram_in("a_v1", [1, D, 32])
        inp["a_v2"] = self.dram_in("a_v2", [1, 32, D])
        inp["a_g1"] = self.dram_in("a_g1", [2, D, 160])
        inp["a_g2"] = self.dram_in("a_g2", [2, 160, D])
        inp["a_w_o"] = self.dram_in("a_w_o", [2, D, D])
        inp["rows"] = self.dram_in("rows", [7 * D + 512])
        inp["w_kv"] = self.dram_in("w_kv", [D, 2 * D])
        inp["b_w_q"] = self.dram_in("b_w_q", [2, D, D])
        inp["b_w_o"] = self.dram_in("b_w_o", [2, D, D])
        inp["ffn_w_up"] = self.dram_in("ffn_w_up", [4, D, F2])
        inp["ffn_w_down"] = self.dram_in("ffn_w_down", [4, DFF, D])
        self.inp = inp
        self.outT = nc.dram_tensor("outT", [D, T], F32, kind="ExternalOutput").ap()
        self.X = nc.dram_tensor("Xs", [D, T], F32).ap()
        self.VF = nc.dram_tensor("VFs", [T, D], F32).ap()
        self.KT = nc.dram_tensor("KTs", [D, T], BF16).ap()
        self.VS = nc.dram_tensor("VSs", [T, D], BF16).ap()
        self.QT = nc.dram_tensor("QTs", [D, T], BF16).ap()
        self.YT = nc.dram_tensor("YTs", [D, T], BF16).ap()
        self.dbg = {}
        for name, shape in cfg.get("dbg", {}).items():
            self.dbg[name] = nc.dram_tensor("dbg_" + name, list(shape), F32, kind="ExternalOutput").ap()

        with ExitStack() as es:
            self.P = P = Prog(nc, es)
            self.phase_dma_bufs = []
            self.ps = []
            for i in range(7):
                t = es.enter_context(nc.psum_tensor("psb%d" % i, [128, 512], F32))
                self.ps.append(Tl(t, P.buf("psb%d" % i)))
                self.ps[-1].b.excl = True
            t = es.enter_context(nc.psum_tensor("psbf", [128, 1024], BF16))
            self.psbf = Tl(t, P.buf("psbf"))
            self.psbf.b.excl = True
            self.psi = 0
            g = es
            self.cols = self.tile(g, "cols", [128, CP.n], dma=True)
            self.ccol = self.tile(g, "ccol", [128, 8], dma=True)
            self.cst = {}
            for k in ("ident", "ones", "bd", "ind", "maskL", "roper"):
                self.cst[k] = self.tile(g, k, [128, 128], dma=True)
            self.cst["mask4"] = self.tile(g, "mask4", [128, 512], dma=True)
            P.dma("sp", lambda e: e.dma_start(out=self.cols[:], in_=inp["cols"]), self.cols.b, writes=[self.cols.b])
            P.dma("sp", lambda e: e.dma_start(out=self.ccol[:], in_=inp["ccol"]), self.ccol.b, writes=[self.ccol.b])
            for k, tl in self.cst.items():
                P.dma("act", lambda e, tl=tl, k=k: e.dma_start(out=tl[:], in_=inp[k]), tl.b, writes=[tl.b])
            self.eps = self.tile(g, "eps", [128, 1])
            P.op("pool", lambda e: e.memset(self.eps[:], 1e-6), writes=[self.eps.b])
            self.identb = self.tile(g, "identb", [128, 128], BF16)
            self.copy("pool", self.identb[:], self.cst["ident"][:], [self.cst["ident"].b], [self.identb.b])
            self.modc = self.tile(g, "modc", [128, 192])
            self.der = self.tile(g, "der", [128, 4, 2, 8])

            self.phase_mod()
            xsrc = inp["xT"]
            for l in cfg["layers"]:
                if cfg.get("mixer", True):
                    if l < 2:
                        self.phase_rwkv(l, xsrc)
                    else:
                        if l == 2 or cfg.get("force_kv", False):
                            self.phase_kv(xsrc)
                        self.phase_attn(l, xsrc)
                    xsrc = self.X
                fuse_final = (l == cfg["layers"][-1]) and cfg.get("ffn", True) and cfg.get("final", True) and cfg.get("fuse_final", True)
                if cfg.get("ffn", True):
                    self.phase_ffn(l, xsrc, fuse_final)
                    xsrc = self.X
            if not fuse_final:
                self.phase_final(xsrc, cfg.get("final", True))
            P.finish()
        return nc

    def col(self, name, j0=0, n=None):
        o = CP.off[name] + j0
        if n is None:
            n = 1
        return self.cols[:, o:o + n]

    def end_phase(self):
        self.P.barrier()
        self.P.release(self.phase_dma_bufs)
        self.phase_dma_bufs = []

    def phase_mod(self):
        P, nc, inp = self.P, self.nc, self.inp
        with ExitStack() as es:
            cact = self.tile(es, "cact", [128, 8])
            P.op("act", lambda e: e.activation(out=cact[:], in_=self.ccol[:], func=AF.Silu),
                 reads=[self.ccol.b], writes=[cact.b])
            A = [self.tile(es, "adaA%d" % i, [128, 8, 768], dma=True) for i in range(2)]
            psm = self.ps[0]
            it = 0
            for l in range(4):
                for blk in range(8):
                    a = A[it % 2]
                    it += 1
                    src = inp["ada_w"][l].rearrange("(kc p) n -> p kc n", p=128)[:, :, blk * 768:(blk + 1) * 768]
                    P.dma("sp" if it % 2 else "act", lambda e, a=a, src=src: e.dma_start(out=a[:], in_=src), a.b, writes=[a.b])
                    for n_ in range(6):
                        colidx = l * 48 + blk * 6 + n_
                        for kc in range(8):
                            P.op("pe", lambda e, a=a, kc=kc, n_=n_, colidx=colidx: e.matmul(
                                psm[:, colidx:colidx + 1], lhsT=a[:, kc, n_ * 128:(n_ + 1) * 128], rhs=cact[:, kc:kc + 1],
                                start=(kc == 0), stop=(kc == 7)), reads=[a.b, cact.b], writes=[psm.b])
            for l in range(4):
                o = CP.off["ada_b%d" % l]
                P.op("dve", lambda e, l=l, o=o: e.tensor_tensor(out=self.modc[:, l * 48:(l + 1) * 48], in0=psm[:, l * 48:(l + 1) * 48],
                                                                in1=self.cols[:, o:o + 48], op=ALU.add),
                     reads=[psm.b, self.cols.b], writes=[self.modc.b])
            for l in range(4):
                for which in range(2):
                    sc = self.modc[:, l * 48 + which * 24 + 8: l * 48 + which * 24 + 16]
                    nm = self.col("norm%d_%d" % (which + 1, l), 0, 8)
                    P.op("dve", lambda e, l=l, which=which, sc=sc, nm=nm: e.scalar_tensor_tensor(
                        out=self.der[:, l, which, :], in0=sc, scalar=1.0, in1=nm, op0=ALU.add, op1=ALU.mult),
                        reads=[self.modc.b, self.cols.b], writes=[self.der.b])
            if "modc" in self.dbg:
                dt = self.tile(es, "dbgm", [128, 192], dma=True)
                self.copy("dve", dt[:], self.modc[:], [self.modc.b], [dt.b])
                P.dma("sp", lambda e: e.dma_start(out=self.dbg["modc"], in_=dt[:]), dt.b, reads=[dt.b])
            self.end_phase()

    def modcol(self, l, i, j0=0, n=8):
        o = l * 48 + i * 8 + j0
        return self.modc[:, o:o + n]

    def rmsnorm(self, x, N, sq, G, S, out_ap, out_b, extra_reads=()):
        P = self.P
        ones = self.cst["ones"]
        P.op("act", lambda e: e.activation(out=sq[:], in_=x[:], func=AF.Square), reads=[x.b], writes=[sq.b])
        ps = self.psum_next()
        for j in range(8):
            P.op("pe", lambda e, j=j: e.matmul(ps[:, 0:N], lhsT=ones[:], rhs=sq[:, j, :], start=(j == 0), stop=(j == 7)),
                 reads=[ones.b, sq.b], writes=[ps.b])
        rs = self.rstd
        P.op("act", lambda e: e.activation(out=rs[:, 0:N], in_=ps[:, 0:N], func=AF.Sqrt, bias=self.eps[:], scale=1.0 / D),
             reads=[ps.b, self.eps.b], writes=[rs.b])
        P.op("dve", lambda e: e.reciprocal(out=rs[:, 0:N], in_=rs[:, 0:N]), reads=[rs.b], writes=[rs.b])
        P.op("dve", lambda e: e.tensor_tensor(out=sq[:], in0=x[:], in1=rs[:, 0:N].unsqueeze(1).to_broadcast([128, 8, N]), op=ALU.mult),
             reads=[x.b, rs.b], writes=[sq.b])
        if S is None:
            P.op("pool", lambda e: e.tensor_tensor(out=out_ap, in0=sq[:], in1=G.unsqueeze(2).to_broadcast([128, 8, N]), op=ALU.mult),
                 reads=[sq.b, self.cols.b, self.der.b] + list(extra_reads), writes=[out_b])
        else:
            P.op("pool", lambda e: e.tensor_tensor(out=sq[:], in0=sq[:], in1=G.unsqueeze(2).to_broadcast([128, 8, N]), op=ALU.mult),
                 reads=[sq.b, self.cols.b, self.der.b], writes=[sq.b])
            P.op("pool", lambda e: e.tensor_tensor(out=out_ap, in0=sq[:], in1=S.unsqueeze(2).to_broadcast([128, 8, N]), op=ALU.add),
                 reads=[sq.b, self.modc.b] + list(extra_reads), writes=[out_b])

    def load_cast(self, es_stage, pieces, wb):
        P = self.P
        if not hasattr(self, "_stg") or self._stg is None:
            self._stg = [self.tile(es_stage, "stg%d" % i, [128, 1024], dma=True) for i in range(2)]
            self._stgi = 0
        for (dst, src, p, a, b) in pieces:
            assert a * b <= 1024
            st = self._stg[self._stgi % 2]
            self._stgi += 1
            sv = st[0:p, 0:a * b].rearrange("p (a b) -> p a b", a=a)
            q = ("sp", "act")[self._stgi % 2]
            P.dma(q, lambda e, sv=sv, src=src: e.dma_start(out=sv, in_=src), st.b, writes=[st.b])
            self.copy(self.eng_rr(("pool", "dve", "act")), dst, sv, [st.b], [wb])

    def phase_ffn(self, l, xsrc, fuse_final=False):
        P, nc, inp = self.P, self.nc, self.inp
        TB = 256
        NB = T // TB
        with ExitStack() as es:
            self._stg = None
            wup = self.tile(es, "wup", [128, 8, F2], BF16)
            wdn = self.tile(es, "wdn", [128, NG, D], BF16)
            pieces = []
            srcu = inp["ffn_w_up"][l].rearrange("(kc p) n -> p kc n", p=128)
            for kc in range(8):
                for (n0, n1) in ((0, 1024), (1024, 2048), (2048, 3072), (3072, 4096), (4096, 5120), (5120, F2)):
                    pieces.append((wup[:, kc:kc + 1, n0:n1], srcu[:, kc:kc + 1, n0:n1], 128, 1, n1 - n0))
            self.load_cast(es, pieces, wup.b)
            srcd = inp["ffn_w_down"][l].rearrange("(kc p) n -> p kc n", p=128)
            pieces = [(wdn[:, i:i + 1, :], srcd[:, i:i + 1, :], 128, 1, D) for i in range(0, NG)]
            self.load_cast(es, pieces, wdn.b)

            xs = [self.tile(es, "fx%d" % i, [128, 8, TB], dma=True) for i in range(2)]
            sq = self.tile(es, "fsq", [128, 8, TB])
            self.rstd = self.tile(es, "frstd", [128, TB])
            h2s = [self.tile(es, "fh2_%d" % i, [128, 8, TB + 2], BF16) for i in range(2)]
            for i in range(2):
                P.op("pool", lambda e: e.memset(h2s[i][:], 0.0), writes=[h2s[i].b])
            NCV = 6
            cv = [self.tile(es, "fcv%d" % i, [128, TB]) for i in range(NCV)]
            sgl = [self.tile(es, "fsg%d" % i, [128, TB]) for i in range(3)]
            hm = self.tile(es, "fhm", [128, NG, TB], BF16)
            G2 = self.der[:, l, 1, :]
            S2 = self.modcol(l, 3)
            g2c = self.modcol(l, 5)
            cw = [CP.off["cw%d_%d" % (i, l)] for i in range(3)]
            cb = CP.off["cb_%d" % l]
            xr = lambda ap: ap.rearrange("(j p) t -> p j t", p=128)

            def norm(nb):
                x = xs[nb % 2]
                h2 = h2s[nb % 2]
                t0 = nb * TB
                P.dma("sp", lambda e: e.dma_start(out=x[:], in_=xr(xsrc)[:, :, t0:t0 + TB]), x.b, writes=[x.b])
                if nb > 0:
                    hp = h2s[(nb - 1) % 2]
                    P.op("pool", lambda e: e.tensor_copy(out=h2[:, :, 0:2], in_=hp[:, :, TB:TB + 2]), reads=[hp.b], writes=[h2.b])
                self.rmsnorm(x, TB, sq, G2, S2, h2[:, :, 2:TB + 2], h2.b)

            def up(nb):
                h2 = h2s[nb % 2]
                ui = 0
                for i in range(NG):
                    for which in range(2):
                        n = i + which * NG
                        ps = self.psum_next()
                        for kc in range(8):
                            P.op("pe", lambda e: e.matmul(ps[:, 0:TB + 2], lhsT=wup[:, kc, n * 128:(n + 1) * 128], rhs=h2[:, kc, :],
                                                          start=(kc == 0), stop=(kc == 7)), reads=[wup.b, h2.b], writes=[ps.b])
                        c = cv[ui % NCV]
                        ui += 1
                        P.op("act", lambda e: e.activation(out=c[:], in_=ps[:, 2:TB + 2], func=AF.Identity,
                                                           bias=self.cols[:, cb + n:cb + n + 1], scale=self.cols[:, cw[2] + n:cw[2] + n + 1]),
                             reads=[ps.b, self.cols.b], writes=[c.b])
                        P.op("dve", lambda e: e.scalar_tensor_tensor(out=c[:], in0=ps[:, 1:TB + 1], scalar=self.cols[:, cw[1] + n:cw[1] + n + 1],
                                                                     in1=c[:], op0=ALU.mult, op1=ALU.add), reads=[ps.b, c.b, self.cols.b], writes=[c.b])
                        P.op("dve", lambda e: e.scalar_tensor_tensor(out=c[:], in0=ps[:, 0:TB], scalar=self.cols[:, cw[0] + n:cw[0] + n + 1],
                                                                     in1=c[:], op0=ALU.mult, op1=ALU.add), reads=[ps.b, c.b, self.cols.b], writes=[c.b])
                        s_ = sgl[i % 3]
                        if which == 0:
                            P.op("act", lambda e: e.activation(out=s_[:], in_=c[:], func=AF.Silu), reads=[c.b], writes=[s_.b])
                        else:
                            P.op("pool", lambda e: e.tensor_tensor(out=hm[:, i, :], in0=s_[:], in1=c[:], op=ALU.mult),
                                 reads=[s_.b, c.b], writes=[hm.b])

            def down(nb):
                x = xs[nb % 2]
                t0 = nb * TB
                for nj in range(8):
                    ps = self.psum_next()
                    for i in range(NG):
                        P.op("pe", lambda e: e.matmul(ps[:, 0:TB], lhsT=wdn[:, i, nj * 128:(nj + 1) * 128], rhs=hm[:, i, :],
                                                      start=(i == 0), stop=(i == NG - 1)), reads=[wdn.b, hm.b], writes=[ps.b])
                    P.op("dve", lambda e: e.scalar_tensor_tensor(out=x[:, nj, :], in0=ps[:, 0:TB], scalar=g2c[:, nj:nj + 1],
                                                                 in1=x[:, nj, :], op0=ALU.mult, op1=ALU.add),
                         reads=[ps.b, x.b, self.modc.b], writes=[x.b])
                if fuse_final:
                    self.rmsnorm(x, TB, sq, self.col("final_norm", 0, 8), None, x[:], x.b)
                    P.dma("sp", lambda e: e.dma_start(out=xr(self.outT)[:, :, t0:t0 + TB], in_=x[:]), x.b, reads=[x.b])
                else:
                    P.dma("sp", lambda e: e.dma_start(out=xr(self.X)[:, :, t0:t0 + TB], in_=x[:]), x.b, reads=[x.b])

            norm(0)
            for nb in range(NB):
                up(nb)
                if nb + 1 < NB:
                    norm(nb + 1)
                down(nb)
            self.end_phase()

    def phase_final(self, xsrc, do_norm):
        P = self.P
        TB = 512
        with ExitStack() as es:
            xs = [self.tile(es, "nx%d" % i, [128, 8, TB], dma=True) for i in range(2)]
            sq = self.tile(es, "nsq", [128, 8, TB])
            self.rstd = self.tile(es, "nrstd", [128, TB])
            G = self.col("final_norm", 0, 8)
            for nb in range(T // TB):
                x = xs[nb % 2]
                t0 = nb * TB
                src = xsrc.rearrange("(j p) t -> p j t", p=128)[:, :, t0:t0 + TB]
                P.dma("sp", lambda e, x=x, src=src: e.dma_start(out=x[:], in_=src), x.b, writes=[x.b])
                if do_norm:
                    self.rmsnorm(x, TB, sq, G, None, x[:], x.b)
                dst = self.outT.rearrange("(j p) t -> p j t", p=128)[:, :, t0:t0 + TB]
                P.dma("act", lambda e, x=x, dst=dst: e.dma_start(out=dst, in_=x[:]), x.b, reads=[x.b])
            self.end_phase()

    def phase_rwkv(self, l, xsrc):
        P, nc, inp = self.P, self.nc, self.inp
        CH = 128
        NCH = self.cfg.get("nch", T // CH)
        ident = self.cst["ident"]
        with ExitStack() as es:
            self._stg = None
            reuse_stage = self.cfg.get("stage_reuse", True)
            ses = ExitStack() if reuse_stage else es
            Wr = self.tile(es, "Wr", [128, 8, D], BF16)
            Wk = self.tile(es, "Wk", [128, 8, D], BF16)
            Wv = self.tile(es, "Wv", [128, 8, D], BF16)
            Wo = self.tile(es, "Wo", [128, 8, D], BF16)
            w1b = self.tile(es, "w1b", [128, 8, 64], BF16)
            a1b = self.tile(es, "a1b", [128, 8, 64], BF16)
            g1b = self.tile(es, "g1b", [128, 8, 160], BF16)
            w2b = self.tile(es, "w2b", [64, 1, D], BF16)
            a2b = self.tile(es, "a2b", [64, 1, D], BF16)
            g2b = self.tile(es, "g2b", [128, 2, D], BF16)
            if l == 1:
                v1b = self.tile(es, "v1b", [128, 8, 32], BF16)
                v2b = self.tile(es, "v2b", [32, 1, D], BF16)
                v0r = self.tile(es, "v0r", [128, D], dma=True)
            lnw = self.tile(es, "lnw", [128, D], dma=True)
            lnb = self.tile(es, "lnb", [128, D], dma=True)
            omka = self.tile(es, "omka", [128, 8])
            eps2 = self.tile(es, "eps2", [128, 1])
            onesT = self.tile(es, "onesT", [128, 128])
            for W, src in ((Wr, inp["a_w_rkv"][l, 0]), (Wk, inp["a_w_rkv"][l, 1]), (Wv, inp["a_w_rkv"][l, 2]), (Wo, inp["a_w_o"][l])):
                s3 = src.rearrange("(kc p) n -> p kc n", p=128)
                self.load_cast(ses, [(W[:, kc:kc + 1, :], s3[:, kc:kc + 1, :], 128, 1, D) for kc in range(8)], W.b)
            self.load_cast(ses, [(w1b[:], inp["a_w1"][l].rearrange("(kc p) n -> p kc n", p=128), 128, 8, 64)], w1b.b)
            self.load_cast(ses, [(a1b[:], inp["a_a1"][l].rearrange("(kc p) n -> p kc n", p=128), 128, 8, 64)], a1b.b)
            sg1 = inp["a_g1"][l].rearrange("(kc p) n -> p kc n", p=128)
            self.load_cast(ses, [(g1b[:, 0:4, :], sg1[:, 0:4, :], 128, 4, 160), (g1b[:, 4:8, :], sg1[:, 4:8, :], 128, 4, 160)], g1b.b)
            self.load_cast(ses, [(w2b[:], inp["a_w2"][l].rearrange("(o p) n -> p o n", o=1), 64, 1, D)], w2b.b)
            self.load_cast(ses, [(a2b[:], inp["a_a2"][l].rearrange("(o p) n -> p o n", o=1), 64, 1, D)], a2b.b)
            self.load_cast(ses, [(g2b[:, 0:1, :], inp["a_g2"][l][0:128, :].rearrange("(o p) n -> p o n", o=1), 128, 1, D),
                                (g2b[0:32, 1:2, :], inp["a_g2"][l][128:160, :].rearrange("(o p) n -> p o n", o=1), 32, 1, D)], g2b.b)
            if l == 1:
                self.load_cast(ses, [(v1b[:], inp["a_v1"][0].rearrange("(kc p) n -> p kc n", p=128), 128, 8, 32)], v1b.b)
                self.load_cast(ses, [(v2b[:], inp["a_v2"][0].rearrange("(o p) n -> p o n", o=1), 32, 1, D)], v2b.b)
                P.dma("sp", lambda e: e.dma_start(out=v0r[:], in_=inp["rows"][4 * D:5 * D].partition_broadcast(128)), v0r.b, writes=[v0r.b])
            P.dma("sp", lambda e: e.dma_start(out=lnw[:], in_=inp["rows"][(2 * l) * D:(2 * l + 1) * D].partition_broadcast(128)), lnw.b, writes=[lnw.b])
            P.dma("sp", lambda e: e.dma_start(out=lnb[:], in_=inp["rows"][(2 * l + 1) * D:(2 * l + 2) * D].partition_broadcast(128)), lnb.b, writes=[lnb.b])
            P.op("dve", lambda e: e.tensor_scalar(out=omka[:], in0=self.col("k_a_%d" % l, 0, 8), scalar1=-1.0, scalar2=1.0, op0=ALU.mult, op1=ALU.add),
                 reads=[self.cols.b], writes=[omka.b])
            P.op("pool", lambda e: e.memset(eps2[:], 64e-5), writes=[eps2.b])
            P.op("pool", lambda e: e.memset(onesT[:], 1.0), writes=[onesT.b])

            if reuse_stage:
                P.barrier()
                ses.close()
                self._stg = None
            Hbd = self.tile(es, "Hbd", [128, 8, 128])
            Hb = self.tile(es, "Hb", [128, 8, 128], BF16)
            if self.cfg.get("t_hb", True):
                P.op("pool", lambda e: e.memset(Hb[:], 0.0), writes=[Hb.b])
            Vb = self.tile(es, "Vb", [128, D], BF16)
            P.op("pool", lambda e: e.memset(Hbd[:], 0.0), writes=[Hbd.b])
            h = self.tile(es, "h", [128, 8, 129])
            P.op("pool", lambda e: e.memset(h[:], 0.0), writes=[h.b])
            x = self.tile(es, "rx", [128, 8, CH], dma=True)
            self.rstd = self.tile(es, "rrstd", [128, CH])
            xx = self.tile(es, "xx", [128, 8, CH])
            tmpm = [self.tile(es, "tmpm%d" % i, [128, 8, CH], dma=(i == 0)) for i in range(2)]
            sq = tmpm[1] if self.cfg.get("t_sq", True) else self.tile(es, "rsq", [128, 8, CH])
            xm = [self.tile(es, "xm%d" % i, [128, 8, CH], BF16) for i in range(6)]
            lwt = self.tile(es, "lwt", [64, CH], BF16)
            lat = self.tile(es, "lat", [64, CH], BF16)
            lgt = self.tile(es, "lgt", [128, 2, CH], BF16)
            V = self.tile(es, "V", [128, D], dma=True)
            Yt = self.tile(es, "Yt", [128, D], dma=True)
            if l == 1:
                lvt = self.tile(es, "lvt", [32, CH], BF16)
                VFt = Tl(tmpm[0][:].rearrange("p a t -> p (a t)"), tmpm[0].b)
                sgv = Tl(tmpm[1][:].rearrange("p a t -> p (a t)")[:, 0:512], tmpm[1].b)
            ogb = self.tile(es, "ogb", [128, D], BF16)
            ogT = self.tile(es, "ogT", [128, 8, CH], BF16)
            st = self.tile(es, "stat", [128, 6, 16])
            rkb = self.tile(es, "rkb", [128, 16])

            NSLOT = self.cfg.get("nslot%d" % l, 4)

            def mkslot(si):
                S = {}

                def pt(name, shape=(128, 128), dt=F32):
                    S[name] = self.tile(es, "s%d_%s" % (si, name), list(shape), dt)
                for nm in ("r", "k", "sg", "a", "kk", "x", "L", "eL"):
                    pt(nm)
                for nm in ("rh", "bts", "kts", "WT", "U"):
                    pt(nm, (128, 128), BF16)
                pt("AR", (128, 2, 128), BF16)
                pt("F3", (128, 3, 128), BF16)
                pt("T3", (128, 3, 128), BF16)
                for i in range(2):
                    pt("SC%d" % i, (128, 4, 128), BF16)
                    pt("Ym%d" % i, (128, 128), BF16)
                    pt("ZY%d_0" % i, (128, 2, 128), BF16)
                    pt("ZY%d_1" % i, (128, 2, 128), BF16)
                    pt("Tt%d" % i, (128, 128), BF16)
                    pt("AV%d" % i, (128, 64), BF16)
                return S
            slots = [mkslot(i) for i in range(NSLOT)]
            mask4 = self.cst["mask4"]; maskL = self.cst["maskL"]; bd = self.cst["bd"]; ind = self.cst["ind"]
            G1 = self.der[:, l, 0, :]
            S1 = self.modcol(l, 0)
            g1c = self.modcol(l, 2)
            psB = self.ps[6]

            def c_(name, j):
                return self.col("%s_%d" % (name, l), j, 1)

            for c in range(NCH):
                t0 = c * CH
                src = xsrc.rearrange("(j p) t -> p j t", p=128)[:, :, t0:t0 + CH]
                P.dma("sp", lambda e, src=src: e.dma_start(out=x[:], in_=src), x.b, writes=[x.b])
                self.rmsnorm(x, CH, sq, G1, S1, h[:, :, 1:CH + 1], h.b)
                P.op("pool", lambda e: e.tensor_tensor(out=xx[:], in0=h[:, :, 0:CH], in1=h[:, :, 1:CH + 1], op=ALU.subtract),
                     reads=[h.b], writes=[xx.b])
                for i in range(6):
                    tm = tmpm[i % 2]
                    mu = self.col("mu%d_%d" % (i, l), 0, 8)
                    P.op("dve", lambda e, tm=tm, mu=mu: e.tensor_tensor(out=tm[:], in0=xx[:], in1=mu.unsqueeze(2).to_broadcast([128, 8, CH]), op=ALU.mult),
                         reads=[xx.b, self.cols.b], writes=[tm.b])
                    P.op("pool", lambda e, tm=tm, i=i: e.tensor_tensor(out=xm[i][:], in0=tm[:], in1=h[:, :, 1:CH + 1], op=ALU.add),
                         reads=[tm.b, h.b], writes=[xm[i].b])
                P.op("pool", lambda e: e.tensor_copy(out=h[:, :, 0:1], in_=h[:, :, CH:CH + 1]), reads=[h.b], writes=[h.b])
                if l == 1:
                    P.dma("sp", lambda e, t0=t0: e.dma_start(out=VFt[:], in_=self.VF[t0:t0 + CH, :]), VFt.b, writes=[VFt.b])

                psl = self.psum_next()
                for kc in range(8):
                    P.op("pe", lambda e, kc=kc: e.matmul(psl[0:64, 0:128], lhsT=w1b[:, kc, :], rhs=xm[1][:, kc, :], start=(kc == 0), stop=(kc == 7)),
                         reads=[w1b.b, xm[1].b], writes=[psl.b])
                for kc in range(8):
                    P.op("pe", lambda e, kc=kc: e.matmul(psl[0:64, 128:256], lhsT=a1b[:, kc, :], rhs=xm[4][:, kc, :], start=(kc == 0), stop=(kc == 7)),
                         reads=[a1b.b, xm[4].b], writes=[psl.b])
                for kc in range(8):
                    P.op("pe", lambda e, kc=kc: e.matmul(psl[:, 256:384], lhsT=g1b[:, kc, 0:128], rhs=xm[5][:, kc, :], start=(kc == 0), stop=(kc == 7)),
                         reads=[g1b.b, xm[5].b], writes=[psl.b])
                for kc in range(8):
                    P.op("pe", lambda e, kc=kc: e.matmul(psl[0:32, 384:512], lhsT=g1b[:, kc, 128:160], rhs=xm[5][:, kc, :], start=(kc == 0), stop=(kc == 7)),
                         reads=[g1b.b, xm[5].b], writes=[psl.b])
                P.op("act", lambda e: e.activation(out=lwt[:], in_=psl[0:64, 0:128], func=AF.Tanh), reads=[psl.b], writes=[lwt.b])
                P.op("act", lambda e: e.copy(out=lat[:], in_=psl[0:64, 128:256]), reads=[psl.b], writes=[lat.b])
                P.op("act", lambda e: e.activation(out=lgt[:, 0, :], in_=psl[:, 256:384], func=AF.Sigmoid), reads=[psl.b], writes=[lgt.b])
                P.op("act", lambda e: e.activation(out=lgt[0:32, 1, :], in_=psl[0:32, 384:512], func=AF.Sigmoid), reads=[psl.b], writes=[lgt.b])
                if l == 1:
                    psv = self.psum_next()
                    for kc in range(8):
                        P.op("pe", lambda e, kc=kc: e.matmul(psv[0:32, 0:128], lhsT=v1b[:, kc, :], rhs=xm[3][:, kc, :], start=(kc == 0), stop=(kc == 7)),
                             reads=[v1b.b, xm[3].b], writes=[psv.b])
                    P.op("act", lambda e: e.copy(out=lvt[:], in_=psv[0:32, 0:128]), reads=[psv.b], writes=[lvt.b])
                for half in range(2):
                    ps = self.psum_next()
                    for kc in range(8):
                        P.op("pe", lambda e, ps=ps, kc=kc, half=half: e.matmul(ps[:, :], lhsT=xm[3][:, kc, :], rhs=Wv[:, kc, half * 512:(half + 1) * 512],
                                                                              start=(kc == 0), stop=(kc == 7)), reads=[xm[3].b, Wv.b], writes=[ps.b])
                    P.op("act", lambda e, ps=ps, half=half: e.copy(out=V[:, half * 512:(half + 1) * 512], in_=ps[:, :]), reads=[ps.b], writes=[V.b])
                if l == 1:
                    for half in range(2):
                        ps = self.psum_next()
                        hs = slice(half * 512, (half + 1) * 512)
                        P.op("pe", lambda e, ps=ps, hs=hs: e.matmul(ps[:, :], lhsT=lvt[:], rhs=v2b[0:32, 0, hs], start=True, stop=True),
                             reads=[lvt.b, v2b.b], writes=[ps.b])
                        P.op("dve", lambda e, ps=ps, hs=hs: e.tensor_tensor(out=sgv[:], in0=ps[:, :], in1=v0r[:, hs], op=ALU.add),
                             reads=[ps.b, v0r.b], writes=[sgv.b])
                        P.op("act", lambda e: e.activation(out=sgv[:], in_=sgv[:], func=AF.Sigmoid), reads=[sgv.b], writes=[sgv.b])
                        P.op("pool", lambda e, hs=hs: e.tensor_tensor(out=VFt[:, hs], in0=VFt[:, hs], in1=V[:, hs], op=ALU.subtract),
                             reads=[VFt.b, V.b], writes=[VFt.b])
                        P.op("pool", lambda e, hs=hs: e.tensor_tensor(out=VFt[:, hs], in0=VFt[:, hs], in1=sgv[:], op=ALU.mult),
                             reads=[VFt.b, sgv.b], writes=[VFt.b])
                        P.op("pool", lambda e, hs=hs: e.tensor_tensor(out=V[:, hs], in0=V[:, hs], in1=VFt[:, hs], op=ALU.add),
                             reads=[VFt.b, V.b], writes=[V.b])
                else:
                    P.dma("act", lambda e, t0=t0: e.dma_start(out=self.VF[t0:t0 + CH, :], in_=V[:]), V.b, reads=[V.b])
                if "rw_v" in self.dbg:
                    P.dma("act", lambda e, t0=t0: e.dma_start(out=self.dbg["rw_v"][t0:t0 + CH, :], in_=V[:]), V.b, reads=[V.b])

                if self.cfg.get("t_vb", True):
                    P.op("pool", lambda e: e.tensor_copy(out=Vb[:], in_=V[:]), reads=[V.b], writes=[Vb.b])

                def pair_gen(j, S):
                    CUT = self.cfg.get("rw_cut", 99)
                    EB = self.cfg.get("eng_b", "dve")

                    def smul(eng, out_ap, in_ap, sc_ap, rd, wr):
                        if eng == "act":
                            P.op("act", lambda e: e.activation(out=out_ap, in_=in_ap, func=AF.Copy, scale=sc_ap), reads=rd, writes=wr)
                        else:
                            P.op(eng, lambda e: e.tensor_scalar(out=out_ap, in0=in_ap, scalar1=sc_ap, scalar2=None, op0=ALU.mult), reads=rd, writes=wr)
                    js = slice(j * 128, (j + 1) * 128)
                    r_, k_, sg, a_, kk, tx, L_, eL = S["r"], S["k"], S["sg"], S["a"], S["kk"], S["x"], S["L"], S["eL"]
                    rh, bts, kts, WT, U_, AR, F3, T3 = S["rh"], S["bts"], S["kts"], S["WT"], S["U"], S["AR"], S["F3"], S["T3"]
                    SC = [S["SC0"], S["SC1"]]; Ym = [S["Ym0"], S["Ym1"]]; Tt = [S["Tt0"], S["Tt1"]]; AV = [S["AV0"], S["AV1"]]
                    ZY = [[S["ZY0_0"], S["ZY0_1"]], [S["ZY1_0"], S["ZY1_1"]]]
                    psA = self.psum_next()
                    for kc in range(8):
                        P.op("pe", lambda e: e.matmul(psA[:, 0:128], lhsT=Wr[:, kc, js], rhs=xm[0][:, kc, :], start=(kc == 0), stop=(kc == 7)),
                             reads=[Wr.b, xm[0].b], writes=[psA.b])
                    for kc in range(8):
                        P.op("pe", lambda e: e.matmul(psA[:, 128:256], lhsT=Wk[:, kc, js], rhs=xm[2][:, kc, :], start=(kc == 0), stop=(kc == 7)),
                             reads=[Wk.b, xm[2].b], writes=[psA.b])
                    P.op("pe", lambda e: e.matmul(psA[:, 256:384], lhsT=w2b[:, 0, js], rhs=lwt[:], start=True, stop=True), reads=[w2b.b, lwt.b], writes=[psA.b])
                    P.op("pe", lambda e: e.matmul(psA[:, 384:512], lhsT=a2b[:, 0, js], rhs=lat[:], start=True, stop=True), reads=[a2b.b, lat.b], writes=[psA.b])
                    P.op("act", lambda e: e.copy(out=r_[:], in_=psA[:, 0:128]), reads=[psA.b], writes=[r_.b])
                    P.op("act", lambda e: e.copy(out=k_[:], in_=psA[:, 128:256]), reads=[psA.b], writes=[k_.b])
                    P.op("act", lambda e: e.activation(out=sg[:], in_=psA[:, 256:384], func=AF.Sigmoid, bias=c_("w0", j), scale=1.0),
                         reads=[psA.b, self.cols.b], writes=[sg.b])
                    P.op("act", lambda e: e.activation(out=a_[:], in_=psA[:, 384:512], func=AF.Sigmoid, bias=c_("a0", j), scale=1.0),
                         reads=[psA.b, self.cols.b], writes=[a_.b])
                    yield
                    if CUT <= 1:
                        return
                    P.op("dve", lambda e: e.tensor_scalar(out=kk[:], in0=k_[:], scalar1=c_("k_k", j), scalar2=None, op0=ALU.mult),
                         reads=[k_.b, self.cols.b], writes=[kk.b])
                    P.op("pool", lambda e: e.tensor_tensor(out=tx[:], in0=kk[:], in1=kk[:], op=ALU.mult), reads=[kk.b], writes=[tx.b])
                    ps = self.psum_next()
                    P.op("pe", lambda e: e.matmul(ps[:, 0:128], lhsT=bd[:], rhs=tx[:], start=True, stop=True), reads=[bd.b, tx.b], writes=[ps.b])
                    P.op("act", lambda e: e.activation(out=tx[:], in_=ps[:, 0:128], func=AF.Sqrt), reads=[ps.b], writes=[tx.b])
                    yield
                    if CUT <= 2:
                        return
                    P.op("dve", lambda e: e.tensor_scalar(out=tx[:], in0=tx[:], scalar1=1e-12, scalar2=None, op0=ALU.max), reads=[tx.b], writes=[tx.b])
                    P.op("dve", lambda e: e.reciprocal(out=tx[:], in_=tx[:]), reads=[tx.b], writes=[tx.b])
                    P.op("pool", lambda e: e.tensor_tensor(out=kk[:], in0=kk[:], in1=tx[:], op=ALU.mult), reads=[kk.b, tx.b], writes=[kk.b])
                    P.op("dve", lambda e: e.tensor_scalar(out=tx[:], in0=a_[:], scalar1=c_("k_a", j), scalar2=omka[:, j:j + 1], op0=ALU.mult, op1=ALU.add),
                         reads=[a_.b, self.cols.b, omka.b], writes=[tx.b])
                    P.op("pool", lambda e: e.tensor_tensor(out=k_[:], in0=k_[:], in1=tx[:], op=ALU.mult), reads=[k_.b, tx.b], writes=[k_.b])
                    P.op("pool", lambda e: e.tensor_tensor(out=a_[:], in0=kk[:], in1=a_[:], op=ALU.mult), reads=[kk.b, a_.b], writes=[a_.b])
                    P.op("dve", lambda e: e.scalar_tensor_tensor(out=tx[:], in0=r_[:], scalar=c_("r_k", j), in1=k_[:], op0=ALU.mult, op1=ALU.mult),
                         reads=[r_.b, k_.b, self.cols.b], writes=[tx.b])
                    P.op("pe", lambda e: e.matmul(psB[:, 0:16], lhsT=tx[:], rhs=ind[:, j * 16:(j + 1) * 16], start=(j == 0), stop=(j == 7)),
                         reads=[tx.b, ind.b], writes=[psB.b])
                    yield
                    if CUT <= 3:
                        return
                    P.op("dve", lambda e: e.tensor_tensor_scan(out=L_[:], data0=onesT[:], data1=sg[:], initial=0.0, op0=ALU.mult, op1=ALU.add),
                         reads=[onesT.b, sg.b], writes=[L_.b])
                    P.op("pool", lambda e: e.tensor_tensor(out=sg[:], in0=L_[:], in1=sg[:], op=ALU.subtract), reads=[L_.b, sg.b], writes=[sg.b])
                    P.op("act", lambda e: e.activation(out=eL[:], in_=L_[:], func=AF.Exp, scale=-C0), reads=[L_.b], writes=[eL.b])
                    P.op("act", lambda e: e.activation(out=sg[:], in_=sg[:], func=AF.Exp, scale=-C0), reads=[sg.b], writes=[sg.b])
                    P.op("act", lambda e: e.activation(out=L_[:], in_=L_[:], func=AF.Exp, scale=C0), reads=[L_.b], writes=[L_.b])
                    yield
                    if CUT <= 4:
                        return
                    enL = L_
                    eE = sg
                    P.op("pool", lambda e: e.tensor_tensor(out=r_[:], in0=r_[:], in1=eL[:], op=ALU.mult), reads=[r_.b, eL.b], writes=[r_.b])
                    P.op("pool", lambda e: e.tensor_copy(out=rh[:], in_=r_[:]), reads=[r_.b], writes=[rh.b])
                    P.op("dve", lambda e: e.scalar_tensor_tensor(out=tx[:], in0=kk[:], scalar=-1.0, in1=eE[:], op0=ALU.mult, op1=ALU.mult),
                         reads=[kk.b, eE.b], writes=[tx.b])
                    P.op("pool", lambda e: e.tensor_copy(out=F3[:, 0, :], in_=tx[:]), reads=[tx.b], writes=[F3.b])
                    P.op("dve", lambda e: e.tensor_scalar(out=AR[:, 0, :], in0=tx[:], scalar1=enL[:, 63:64], scalar2=None, op0=ALU.mult),
                         reads=[tx.b, enL.b], writes=[AR.b])
                    P.op("dve", lambda e: e.tensor_scalar(out=AR[:, 1, :], in0=r_[:], scalar1=enL[:, 63:64], scalar2=None, op0=ALU.mult),
                         reads=[r_.b, enL.b], writes=[AR.b])
                    P.op("pool", lambda e: e.tensor_tensor(out=a_[:], in0=a_[:], in1=enL[:], op=ALU.mult), reads=[a_.b, enL.b], writes=[a_.b])
                    P.op("pool", lambda e: e.tensor_tensor(out=k_[:], in0=k_[:], in1=enL[:], op=ALU.mult), reads=[k_.b, enL.b], writes=[k_.b])
                    yield
                    if CUT <= 5:
                        return
                    smul(EB, bts[:], a_[:], eL[:, 63:64], [a_.b, eL.b], [bts.b])
                    smul(EB, kts[:], k_[:], eL[:, 63:64], [k_.b, eL.b], [kts.b])
                    smul(EB, F3[:, 1, :], a_[:], eL[:, 127:128], [a_.b, eL.b], [F3.b])
                    smul(EB, F3[:, 2, :], k_[:], eL[:, 127:128], [k_.b, eL.b], [F3.b])
                    yield
                    if CUT <= 6:
                        return
                    for q in range(3):
                        P.op("pe", lambda e: e.transpose(out=self.psbf[:, q * 128:(q + 1) * 128], in_=F3[:, q, :], identity=self.identb[:]),
                             reads=[F3.b, self.identb.b], writes=[self.psbf.b])
                    P.op("act", lambda e: e.copy(out=T3[:].rearrange("p a t -> p (a t)"), in_=self.psbf[:, 0:384]), reads=[self.psbf.b], writes=[T3.b])
                    yield
                    if CUT <= 7:
                        return
                    for par in range(2):
                        rows = slice(64 * par, 64 * par + 64)
                        pss = self.psum_next()
                        arv = AR[rows, :, :].rearrange("p a t -> p (a t)")
                        P.op("pe", lambda e: e.matmul(pss[:, 0:256], lhsT=bts[rows, :], rhs=arv, start=True, stop=True),
                             reads=[bts.b, AR.b], writes=[pss.b])
                        P.op("pe", lambda e: e.matmul(pss[:, 256:512], lhsT=kts[rows, :], rhs=arv, start=True, stop=True),
                             reads=[kts.b, AR.b], writes=[pss.b])
                        P.op("dve", lambda e: e.tensor_tensor(out=SC[par][:].rearrange("p a t -> p (a t)"), in0=pss[:, :], in1=mask4[:], op=ALU.mult),
                             reads=[pss.b, mask4.b], writes=[SC[par].b])
                        ps3 = self.psum_next()
                        P.op("pe", lambda e: e.matmul(ps3[:, 0:128], lhsT=AR[rows, 0, :], rhs=bts[rows, :], start=True, stop=True),
                             reads=[bts.b, AR.b], writes=[ps3.b])
                        P.op("dve", lambda e: e.tensor_tensor(out=Ym[par][:], in0=ps3[:, 0:128], in1=maskL[:], op=ALU.mult),
                             reads=[ps3.b, maskL.b], writes=[Ym[par].b])
                        P.op("pool", lambda e: e.tensor_tensor(out=Tt[par][:], in0=SC[par][:, 0, :], in1=self.identb[:], op=ALU.add),
                             reads=[SC[par].b, self.identb.b], writes=[Tt[par].b])
                        yield
                    cur = [(SC[0][:, 0, :], SC[0].b, Ym[0][:], Ym[0].b), (SC[1][:, 0, :], SC[1].b, Ym[1][:], Ym[1].b)]
                    for step in range(1, 7):
                        for par in range(2):
                            Zap, Zb, Yap, Yb = cur[par]
                            zy = ZY[par][step % 2]
                            psz = self.psum_next()
                            if step < 6:
                                P.op("pe", lambda e: e.matmul(psz[:, 0:128], lhsT=Yap, rhs=Zap, start=True, stop=True), reads=[Zb, Yb], writes=[psz.b])
                            P.op("pe", lambda e: e.matmul(psz[:, 128:256], lhsT=Zap, rhs=Yap, start=True, stop=True), reads=[Zb, Yb], writes=[psz.b])
                            ev = "act"
                            if step < 6:
                                self.copy(ev, zy[:].rearrange("p a t -> p (a t)"), psz[:, 0:256], [psz.b], [zy.b])
                            else:
                                self.copy(ev, zy[:, 1, :], psz[:, 128:256], [psz.b], [zy.b])
                            cur[par] = (zy[:, 0, :], zy.b, zy[:, 1, :], zy.b)
                            yield
                            tt = Tt[par]
                            psp = self.psum_next()
                            P.op("pe", lambda e: e.matmul(psp[:, 0:128], lhsT=zy[:, 1, :], rhs=tt[:], start=True, stop=True),
                                 reads=[zy.b, tt.b], writes=[psp.b])
                            P.op("dve", lambda e: e.tensor_tensor(out=tt[:], in0=psp[:, 0:128], in1=tt[:], op=ALU.add),
                                 reads=[psp.b, tt.b], writes=[tt.b])
                            yield
                    for par in range(2):
                        rows = slice(64 * par, 64 * par + 64)
                        hc = slice((2 * j + par) * 64, (2 * j + par + 1) * 64)
                        psw = self.psum_next()
                        P.op("pe", lambda e: e.matmul(psw[:, 0:128], lhsT=T3[:, 0, :], rhs=Tt[par][:], start=True, stop=True),
                             reads=[T3.b, Tt[par].b], writes=[psw.b])
                        P.op("pe", lambda e: e.matmul(psw[:, 128:192], lhsT=SC[par][:, 2, :], rhs=Vb[:, hc], start=True, stop=True),
                             reads=[SC[par].b, Vb.b], writes=[psw.b])
                        P.op("act", lambda e: e.copy(out=WT[rows, :], in_=psw[rows, 0:128]), reads=[psw.b], writes=[WT.b])
                        P.op("act", lambda e: e.copy(out=AV[par][:], in_=psw[:, 128:192]), reads=[psw.b], writes=[AV[par].b])
                        yield
                    psu = self.psum_next()
                    P.op("pe", lambda e: e.matmul(psu[:, 0:128], lhsT=WT[:], rhs=Hb[:, j, :], start=True, stop=True),
                         reads=[WT.b, Hb.b], writes=[psu.b])
                    for par in range(2):
                        cs = slice(64 * par, 64 * par + 64)
                        P.op("pe", lambda e: e.matmul(psu[:, cs], lhsT=Tt[par][:], rhs=AV[par][:], start=False, stop=(par == 1), skip_group_check=True),
                             reads=[Tt[par].b, AV[par].b], writes=[psu.b])
                    P.op("act", lambda e: e.copy(out=U_[:], in_=psu[:, 0:128]), reads=[psu.b], writes=[U_.b])
                    yield
                    if CUT <= 8:
                        return
                    psy = self.psum_next()
                    P.op("pe", lambda e: e.matmul(psy[:, 0:128], lhsT=rh[:], rhs=Hb[:, j, :], start=True, stop=True),
                         reads=[rh.b, Hb.b], writes=[psy.b])
                    for par in range(2):
                        hc = slice((2 * j + par) * 64, (2 * j + par + 1) * 64)
                        cs = slice(64 * par, 64 * par + 64)
                        P.op("pe", lambda e: e.matmul(psy[:, cs], lhsT=SC[par][:, 1, :], rhs=U_[:, cs], start=False, stop=False, skip_group_check=True),
                             reads=[SC[par].b, U_.b], writes=[psy.b])
                        P.op("pe", lambda e: e.matmul(psy[:, cs], lhsT=SC[par][:, 3, :], rhs=Vb[:, hc], start=False, stop=(par == 1), skip_group_check=True),
                             reads=[SC[par].b, Vb.b], writes=[psy.b])
                    P.op("act", lambda e: e.copy(out=Yt[:, js], in_=psy[:, 0:128]), reads=[psy.b], writes=[Yt.b])
                    psh = self.psum_next()
                    P.op("pe", lambda e: e.matmul(psh[:, 0:128], lhsT=T3[:, 2, :], rhs=Vb[:, js], start=True, stop=False),
                         reads=[T3.b, Vb.b], writes=[psh.b])
                    P.op("pe", lambda e: e.matmul(psh[:, 0:128], lhsT=T3[:, 1, :], rhs=U_[:], start=False, stop=True),
                         reads=[T3.b, U_.b], writes=[psh.b])
                    for par in range(2):
                        rows = slice(64 * par, 64 * par + 64)
                        P.op("dve", lambda e: e.scalar_tensor_tensor(out=Hbd[rows, j, rows], in0=Hbd[rows, j, rows], scalar=eL[rows, 127:128],
                                                                     in1=psh[rows, rows], op0=ALU.mult, op1=ALU.add),
                             reads=[Hbd.b, eL.b, psh.b], writes=[Hbd.b])
                        P.op("pool", lambda e: e.tensor_copy(out=Hb[rows, j, rows], in_=Hbd[rows, j, rows]), reads=[Hbd.b], writes=[Hb.b])
                    yield
                    if CUT <= 9:
                        return

                STAG = self.cfg.get("stagger", 0)
                pending = list(range(8))
                free_slots = list(range(NSLOT))
                active = []
                tick = 0
                next_start = 0
                while pending or active:
                    if pending and free_slots and tick >= next_start:
                        jn = pending.pop(0)
                        sl = free_slots.pop(0)
                        active.append((pair_gen(jn, slots[sl]), sl))
                        next_start = tick + STAG
                    for item in list(active):
                        try:
                            next(item[0])
                        except StopIteration:
                            active.remove(item)
                            free_slots.append(item[1])
                    tick += 1

                if "rw_o" in self.dbg:
                    P.dma("act", lambda e, t0=t0: e.dma_start(out=self.dbg["rw_o"][t0:t0 + CH, :], in_=Yt[:]), Yt.b, reads=[Yt.b])
                Y3 = Yt[:].rearrange("p (h n) -> p h n", n=64)
                S1t = tmpm[0][:].rearrange("p a t -> p (a t)")
                S1b = tmpm[0].b
                S2t = tmpm[1][:].rearrange("p a t -> p (a t)")
                S2b = tmpm[1].b
                P.op("act", lambda e: e.copy(out=rkb[:], in_=psB[:, 0:16]), reads=[psB.b], writes=[rkb.b])
                P.op("dve", lambda e: e.tensor_reduce(out=st[:, 0, :], in_=Y3, axis=AX.X, op=ALU.add), reads=[Yt.b], writes=[st.b])
                P.op("pool", lambda e: e.tensor_tensor(out=S1t, in0=Yt[:], in1=Yt[:], op=ALU.mult), reads=[Yt.b], writes=[S1b])
                P.op("dve", lambda e: e.tensor_reduce(out=st[:, 1, :], in_=S1t.rearrange("p (h n) -> p h n", n=64), axis=AX.X, op=ALU.add),
                     reads=[S1b], writes=[st.b])
                P.op("dve", lambda e: e.tensor_scalar(out=st[:, 2, :], in0=st[:, 0, :], scalar1=1.0 / 64, scalar2=None, op0=ALU.mult), reads=[st.b], writes=[st.b])
                P.op("dve", lambda e: e.tensor_tensor(out=st[:, 3, :], in0=st[:, 2, :], in1=st[:, 2, :], op=ALU.mult), reads=[st.b], writes=[st.b])
                P.op("dve", lambda e: e.scalar_tensor_tensor(out=st[:, 4, :], in0=st[:, 1, :], scalar=1.0 / 64, in1=st[:, 3, :], op0=ALU.mult, op1=ALU.subtract),
                     reads=[st.b], writes=[st.b])
                P.op("act", lambda e: e.activation(out=st[:, 5, :], in_=st[:, 4, :], func=AF.Sqrt, bias=eps2[:], scale=1.0), reads=[st.b, eps2.b], writes=[st.b])
                P.op("dve", lambda e: e.reciprocal(out=st[:, 5, :], in_=st[:, 5, :]), reads=[st.b], writes=[st.b])
                P.op("pool", lambda e: e.tensor_tensor(out=S1t.rearrange("p (h n) -> p h n", n=64), in0=Y3, in1=st[:, 2, :].unsqueeze(2).to_broadcast([128, 16, 64]), op=ALU.subtract),
                     reads=[Yt.b, st.b], writes=[S1b])
                P.op("dve", lambda e: e.tensor_tensor(out=S1t.rearrange("p (h n) -> p h n", n=64), in0=S1t.rearrange("p (h n) -> p h n", n=64),
                                                      in1=st[:, 5, :].unsqueeze(2).to_broadcast([128, 16, 64]), op=ALU.mult),
                     reads=[S1b, st.b], writes=[S1b])
                P.op("pool", lambda e: e.tensor_tensor(out=S1t, in0=S1t, in1=lnw[:], op=ALU.mult), reads=[S1b, lnw.b], writes=[S1b])
                P.op("dve", lambda e: e.tensor_tensor(out=S1t, in0=S1t, in1=lnb[:], op=ALU.add), reads=[S1b, lnb.b], writes=[S1b])
                P.op("pool", lambda e: e.tensor_tensor(out=S2t.rearrange("p (h n) -> p h n", n=64), in0=V[:].rearrange("p (h n) -> p h n", n=64),
                                                       in1=rkb[:].unsqueeze(2).to_broadcast([128, 16, 64]), op=ALU.mult),
                     reads=[V.b, rkb.b], writes=[S2b])
                P.op("dve", lambda e: e.tensor_tensor(out=S1t, in0=S1t, in1=S2t, op=ALU.add), reads=[S1b, S2b], writes=[S1b])
                for half in range(2):
                    ps = self.psum_next()
                    hs = slice(half * 512, (half + 1) * 512)
                    P.op("pe", lambda e: e.matmul(ps[:, :], lhsT=lgt[:, 0, :], rhs=g2b[:, 0, hs], start=True, stop=False),
                         reads=[lgt.b, g2b.b], writes=[ps.b])
                    P.op("pe", lambda e: e.matmul(ps[:, :], lhsT=lgt[0:32, 1, :], rhs=g2b[0:32, 1, hs], start=False, stop=True),
                         reads=[lgt.b, g2b.b], writes=[ps.b])
                    P.op("dve", lambda e: e.tensor_tensor(out=ogb[:, hs], in0=S1t[:, hs], in1=ps[:, :], op=ALU.mult), reads=[S1b, ps.b], writes=[ogb.b])
                for jj in range(8):
                    P.op("pe", lambda e, jj=jj: e.transpose(out=self.psbf[:, jj * 128:(jj + 1) * 128], in_=ogb[:, jj * 128:(jj + 1) * 128], identity=self.identb[:]),
                         reads=[ogb.b, self.identb.b], writes=[self.psbf.b])
                P.op("act", lambda e: e.copy(out=ogT[:].rearrange("p a t -> p (a t)"), in_=self.psbf[:, :]), reads=[self.psbf.b], writes=[ogT.b])
                for half in range(2):
                    ps = self.psum_next()
                    for jj in range(4):
                        nj = half * 4 + jj
                        for kc in range(8):
                            P.op("pe", lambda e, ps=ps, jj=jj, nj=nj, kc=kc: e.matmul(ps[:, jj * 128:(jj + 1) * 128], lhsT=Wo[:, kc, nj * 128:(nj + 1) * 128], rhs=ogT[:, kc, :],
                                                                                    start=(kc == 0), stop=(kc == 7)), reads=[Wo.b, ogT.b], writes=[ps.b])
                    for jj in range(4):
                        nj = half * 4 + jj
                        P.op("dve", lambda e, ps=ps, jj=jj, nj=nj: e.scalar_tensor_tensor(out=x[:, nj, :], in0=ps[:, jj * 128:(jj + 1) * 128], scalar=g1c[:, nj:nj + 1],
                                                                                         in1=x[:, nj, :], op0=ALU.mult, op1=ALU.add),
                             reads=[ps.b, x.b, self.modc.b], writes=[x.b])
                dst = self.X.rearrange("(j p) t -> p j t", p=128)[:, :, t0:t0 + CH]
                P.dma("sp", lambda e, dst=dst: e.dma_start(out=dst, in_=x[:]), x.b, reads=[x.b])
            self.end_phase()

    def rope_proj(self, es, W, hb, ncol0, dst_blk, CTt, STt, tmpA, tmpB):
        P = self.P
        roper = self.cst["roper"]
        tmpAs, tmpBs = tmpA, tmpB
        for hh in range(8):
            tmpA = tmpAs[hh % len(tmpAs)]
            tmpB = tmpBs[hh % len(tmpBs)]
            ps = self.psum_next()
            for kc in range(8):
                P.op("pe", lambda e: e.matmul(ps[:, :], lhsT=W[:, kc, ncol0 + hh * 128:ncol0 + (hh + 1) * 128], rhs=hb[:, kc, :],
                                              start=(kc == 0), stop=(kc == 7)), reads=[W.b, hb.b], writes=[ps.b])
            P.op("act", lambda e: e.copy(out=tmpA[:], in_=ps[:, :]), reads=[ps.b], writes=[tmpA.b])
            ps2 = self.psum_next()
            P.op("pe", lambda e: e.matmul(ps2[:, :], lhsT=roper[:], rhs=tmpA[:], start=True, stop=True), reads=[roper.b, tmpA.b], writes=[ps2.b])
            P.op("dve", lambda e: e.tensor_tensor(out=tmpB[:], in0=ps2[:, :], in1=STt[:], op=ALU.mult), reads=[ps2.b, STt.b], writes=[tmpB.b])
            P.op("pool", lambda e: e.tensor_tensor(out=tmpA[:], in0=tmpA[:], in1=CTt[:], op=ALU.mult), reads=[tmpA.b, CTt.b], writes=[tmpA.b])
            P.op("pool", lambda e: e.tensor_tensor(out=dst_blk[:, hh, :], in0=tmpA[:], in1=tmpB[:], op=ALU.add), reads=[tmpA.b, tmpB.b], writes=[dst_blk.b])

    def phase_kv(self, xsrc):
        P, nc, inp = self.P, self.nc, self.inp
        TB = 512
        with ExitStack() as es:
            self._stg = None
            Wkv = self.tile(es, "Wkv", [128, 8, 2 * D], BF16)
            s3 = inp["w_kv"].rearrange("(kc p) n -> p kc n", p=128)
            pieces = []
            for kc in range(8):
                for hf in range(2):
                    pieces.append((Wkv[:, kc:kc + 1, hf * D:(hf + 1) * D], s3[:, kc:kc + 1, hf * D:(hf + 1) * D], 128, 1, D))
            self.load_cast(es, pieces, Wkv.b)
            xs_ = [self.tile(es, "kx%d" % i, [128, 8, TB], dma=True) for i in range(2)]
            sq = self.tile(es, "ksq", [128, 8, TB])
            self.rstd = self.tile(es, "krstd", [128, TB])
            hbs_ = [self.tile(es, "khb%d" % i, [128, 8, TB], BF16) for i in range(2)]
            CTt = self.tile(es, "kCT", [128, TB], dma=True)
            STt = self.tile(es, "kST", [128, TB], dma=True)
            tmpA = [self.tile(es, "ktA%d" % i, [128, TB]) for i in range(3)]
            tmpB = [self.tile(es, "ktB%d" % i, [128, TB]) for i in range(3)]
            Kblks = [self.tile(es, "Kblk%d" % i, [128, 8, TB], BF16, dma=True) for i in range(2)]
            Vblks = [self.tile(es, "Vblk%d" % i, [128, 4, D], BF16, dma=True) for i in range(2)]
            G = self.col("kv_norm", 0, 8)
            for nb in range(T // TB):
                t0 = nb * TB
                x, hb, Kblk, Vblk = xs_[nb % 2], hbs_[nb % 2], Kblks[nb % 2], Vblks[nb % 2]
                src = xsrc.rearrange("(j p) t -> p j t", p=128)[:, :, t0:t0 + TB]
                P.dma("sp", lambda e: e.dma_start(out=x[:], in_=src), x.b, writes=[x.b])
                P.dma("act", lambda e: e.dma_start(out=CTt[:], in_=inp["ropec"][:, t0:t0 + TB]), CTt.b, writes=[CTt.b])
                P.dma("act", lambda e: e.dma_start(out=STt[:], in_=inp["ropes"][:, t0:t0 + TB]), STt.b, writes=[STt.b])
                self.rmsnorm(x, TB, sq, G, None, hb[:], hb.b)
                self.rope_proj(es, Wkv, hb, 0, Kblk, CTt, STt, tmpA, tmpB)
                dst = self.KT.rearrange("(j p) t -> p j t", p=128)[:, :, t0:t0 + TB]
                P.dma("sp", lambda e: e.dma_start(out=dst, in_=Kblk[:]), Kblk.b, reads=[Kblk.b])
                for tl in range(4):
                    for half in range(2):
                        ps = self.psum_next()
                        for kc in range(8):
                            P.op("pe", lambda e: e.matmul(ps[:, :], lhsT=hb[:, kc, tl * 128:(tl + 1) * 128], rhs=Wkv[:, kc, D + half * 512:D + (half + 1) * 512],
                                                          start=(kc == 0), stop=(kc == 7)), reads=[hb.b, Wkv.b], writes=[ps.b])
                        self.copy(("act", "dve")[half], Vblk[:, tl, half * 512:(half + 1) * 512], ps[:, :], [ps.b], [Vblk.b])
                dstv = self.VS[t0:t0 + TB, :].rearrange("(a p) e -> p a e", p=128)
                P.dma("sp", lambda e: e.dma_start(out=dstv, in_=Vblk[:]), Vblk.b, reads=[Vblk.b])
            self.end_phase()

    def phase_attn(self, l, xsrc):
        P, nc, inp = self.P, self.nc, self.inp
        jl = l - 2
        TB = 512
        lam_init = 0.8 - 0.6 * math.exp(-0.3 * l)
        G1 = self.der[:, l, 0, :]
        S1 = self.modcol(l, 0)
        g1c = self.modcol(l, 2)
        with ExitStack() as es:
            self._stg = None
            Wq = self.tile(es, "Wq", [128, 8, D], BF16)
            s3 = inp["b_w_q"][jl].rearrange("(kc p) n -> p kc n", p=128)
            self.load_cast(es, [(Wq[:, kc:kc + 1, :], s3[:, kc:kc + 1, :], 128, 1, D) for kc in range(8)], Wq.b)
            xs_ = [self.tile(es, "qx%d" % i, [128, 8, TB], dma=True) for i in range(2)]
            sq = self.tile(es, "qsq", [128, 8, TB])
            self.rstd = self.tile(es, "qrstd", [128, TB])
            hbs_ = [self.tile(es, "qhb%d" % i, [128, 8, TB], BF16) for i in range(2)]
            CTt = self.tile(es, "qCT", [128, TB], dma=True)
            STt = self.tile(es, "qST", [128, TB], dma=True)
            tmpA = [self.tile(es, "qtA%d" % i, [128, TB]) for i in range(3)]
            tmpB = [self.tile(es, "qtB%d" % i, [128, TB]) for i in range(3)]
            Qblks = [self.tile(es, "Qblk%d" % i, [128, 8, TB], BF16, dma=True) for i in range(2)]
            for nb in range(T // TB):
                t0 = nb * TB
                x, hb, Qblk = xs_[nb % 2], hbs_[nb % 2], Qblks[nb % 2]
                src = xsrc.rearrange("(j p) t -> p j t", p=128)[:, :, t0:t0 + TB]
                P.dma("sp", lambda e: e.dma_start(out=x[:], in_=src), x.b, writes=[x.b])
                P.dma("act", lambda e: e.dma_start(out=CTt[:], in_=inp["ropec"][:, t0:t0 + TB]), CTt.b, writes=[CTt.b])
                P.dma("act", lambda e: e.dma_start(out=STt[:], in_=inp["ropes"][:, t0:t0 + TB]), STt.b, writes=[STt.b])
                self.rmsnorm(x, TB, sq, G1, S1, hb[:], hb.b)
                self.rope_proj(es, Wq, hb, 0, Qblk, CTt, STt, tmpA, tmpB)
                dst = self.QT.rearrange("(j p) t -> p j t", p=128)[:, :, t0:t0 + TB]
                P.dma("sp", lambda e: e.dma_start(out=dst, in_=Qblk[:]), Qblk.b, reads=[Qblk.b])
            self.end_phase()

        with ExitStack() as es:
            NH = self.cfg.get("nheads", 8)
            NQB = self.cfg.get("nqb", T // TB)
            lamv = self.tile(es, "lamv", [128, 256], dma=True)
            o_l = 7 * D + jl * 256
            P.dma("sp", lambda e: e.dma_start(out=lamv[:], in_=inp["rows"][o_l:o_l + 256].partition_broadcast(128)), lamv.b, writes=[lamv.b])
            subw = self.tile(es, "subw", [128, 128], dma=True)
            o_s = 5 * D + jl * D
            P.dma("sp", lambda e: e.dma_start(out=subw[:], in_=inp["rows"][o_s:o_s + 128].partition_broadcast(128)), subw.b, writes=[subw.b])
            P.op("dve", lambda e: e.tensor_scalar(out=subw[:], in0=subw[:], scalar1=(1.0 - lam_init), scalar2=None, op0=ALU.mult), reads=[subw.b], writes=[subw.b])
            lt = self.tile(es, "lt", [128, 2, 64])
            ls = self.tile(es, "ls", [128, 4])
            P.op("dve", lambda e: e.tensor_tensor(out=lt[:, 0, :], in0=lamv[:, 0:64], in1=lamv[:, 64:128], op=ALU.mult), reads=[lamv.b], writes=[lt.b])
            P.op("dve", lambda e: e.tensor_tensor(out=lt[:, 1, :], in0=lamv[:, 128:192], in1=lamv[:, 192:256], op=ALU.mult), reads=[lamv.b], writes=[lt.b])
            P.op("dve", lambda e: e.tensor_reduce(out=ls[:, 0:2], in_=lt[:], axis=AX.X, op=ALU.add), reads=[lt.b], writes=[ls.b])
            P.op("act", lambda e: e.activation(out=ls[:, 0:2], in_=ls[:, 0:2], func=AF.Exp), reads=[ls.b], writes=[ls.b])
            P.op("dve", lambda e: e.tensor_tensor(out=ls[:, 2:3], in0=ls[:, 1:2], in1=ls[:, 0:1], op=ALU.subtract), reads=[ls.b], writes=[ls.b])
            P.op("dve", lambda e: e.tensor_scalar(out=ls[:, 3:4], in0=ls[:, 2:3], scalar1=-lam_init, scalar2=None, op0=ALU.add), reads=[ls.b], writes=[ls.b])
            neglam = ls[:, 3:4]
            cmaskb = self.tile(es, "cmaskb", [128, 128], BF16)
            self.copy("pool", cmaskb[:], self.cst["mask4"][:, 128:256], [self.cst["mask4"].b], [cmaskb.b])
            eps1 = self.eps

            KTh = [self.tile(es, "KTh%d" % i, [128, T], BF16, dma=True) for i in range(2)]
            QTh = [self.tile(es, "QTh%d" % i, [128, T], BF16, dma=True) for i in range(2)]
            Vh = [self.tile(es, "Vh%d" % i, [128, 32, 129], BF16, dma=True) for i in range(2)]
            YTh = [self.tile(es, "YTh%d" % i, [128, T], BF16, dma=True) for i in range(2)]
            for i in range(2):
                P.op("pool", lambda e: e.memset(Vh[i][:, :, 128:129], 1.0), writes=[Vh[i].b])
            ET = [self.tile(es, "ET%d" % i, [128, 512], BF16) for i in range(4)]
            Oc = [self.tile(es, "Oc%d" % i, [128, 4, 129]) for i in range(2)]
            rz = self.tile(es, "rz", [128, 2, 4])
            y = self.tile(es, "ay", [128, 4, 128])
            ysq = self.tile(es, "aysq", [128, 4, 128])
            ss = self.tile(es, "ass", [128, 4])
            ynb = self.tile(es, "aynb", [128, 4, 128], BF16)
            if "at_o" in self.dbg:
                self.dbg_tile = self.tile(es, "dbgt", [128, 4, 128], dma=True)
            eti = 0
            for hh in range(NH):
                kt_, qt_, vh_, yt_ = KTh[hh % 2], QTh[hh % 2], Vh[hh % 2], YTh[hh % 2]
                hs = slice(hh * 128, (hh + 1) * 128)
                P.dma("sp", lambda e: e.dma_start(out=kt_[:], in_=self.KT[hs, :]), kt_.b, writes=[kt_.b])
                P.dma("act", lambda e: e.dma_start(out=qt_[:], in_=self.QT[hs, :]), qt_.b, writes=[qt_.b])
                P.dma("sp", lambda e: e.dma_start(out=vh_[:, :, 0:128], in_=self.VS.rearrange("(kt p) e -> p kt e", p=128)[:, :, hs]), vh_.b, writes=[vh_.b])
                sbanks = [self.ps[0], self.ps[1], self.ps[6]]
                tasks = []
                for qb in range(NQB):
                    for cc in range(2):
                        for kt in range(4 * qb + 4):
                            tasks.append((qb, cc, kt))

                def score(ti):
                    qb, cc, kt = tasks[ti]
                    rows = slice(64 * cc, 64 * cc + 64)
                    c0 = max(kt - 4 * qb, 0) * 128
                    pS = sbanks[ti % 3]
                    P.op("pe", lambda e: e.matmul(pS[:, c0:512], lhsT=kt_[rows, kt * 128:(kt + 1) * 128], rhs=qt_[rows, qb * 512 + c0:(qb + 1) * 512],
                                                  start=True, stop=True), reads=[kt_.b, qt_.b], writes=[pS.b])

                def combine(qb):
                    qs = slice(qb * 512, (qb + 1) * 512)
                    P.op("dve", lambda e: e.reciprocal(out=rz[:, 0, :], in_=Oc[0][:, :, 128]), reads=[Oc[0].b], writes=[rz.b])
                    P.op("dve", lambda e: e.reciprocal(out=rz[:, 1, :], in_=Oc[1][:, :, 128]), reads=[Oc[1].b], writes=[rz.b])
                    P.op("dve", lambda e: e.tensor_scalar(out=rz[:, 1, :], in0=rz[:, 1, :], scalar1=neglam, scalar2=None, op0=ALU.mult), reads=[rz.b, ls.b], writes=[rz.b])
                    P.op("pool", lambda e: e.tensor_tensor(out=y[:], in0=Oc[0][:, :, 0:128], in1=rz[:, 0, :].unsqueeze(2).to_broadcast([128, 4, 128]), op=ALU.mult),
                         reads=[Oc[0].b, rz.b], writes=[y.b])
                    P.op("pool", lambda e: e.tensor_tensor(out=ysq[:], in0=Oc[1][:, :, 0:128], in1=rz[:, 1, :].unsqueeze(2).to_broadcast([128, 4, 128]), op=ALU.mult),
                         reads=[Oc[1].b, rz.b], writes=[ysq.b])
                    P.op("dve", lambda e: e.tensor_tensor(out=y[:], in0=y[:], in1=ysq[:], op=ALU.add), reads=[y.b, ysq.b], writes=[y.b])
                    if "at_o" in self.dbg:
                        dtl = self.dbg_tile
                        self.copy("dve", dtl[:], y[:], [y.b], [dtl.b])
                        dd = self.dbg["at_o"][qb * 512:(qb + 1) * 512, hs].rearrange("(a p) e -> p a e", p=128)
                        P.dma("act", lambda e: e.dma_start(out=dd, in_=dtl[:]), dtl.b, reads=[dtl.b])
                    P.op("pool", lambda e: e.tensor_tensor(out=ysq[:], in0=y[:], in1=y[:], op=ALU.mult), reads=[y.b], writes=[ysq.b])
                    P.op("dve", lambda e: e.tensor_reduce(out=ss[:], in_=ysq[:], axis=AX.X, op=ALU.add), reads=[ysq.b], writes=[ss.b])
                    P.op("act", lambda e: e.activation(out=ss[:], in_=ss[:], func=AF.Sqrt, bias=eps1[:], scale=1.0 / 128), reads=[ss.b, eps1.b], writes=[ss.b])
                    P.op("dve", lambda e: e.reciprocal(out=ss[:], in_=ss[:]), reads=[ss.b], writes=[ss.b])
                    P.op("pool", lambda e: e.tensor_tensor(out=y[:], in0=y[:], in1=ss[:].unsqueeze(2).to_broadcast([128, 4, 128]), op=ALU.mult),
                         reads=[y.b, ss.b], writes=[y.b])
                    P.op("dve", lambda e: e.tensor_tensor(out=ynb[:], in0=y[:], in1=subw[:].unsqueeze(1).to_broadcast([128, 4, 128]), op=ALU.mult),
                         reads=[y.b, subw.b], writes=[ynb.b])
                    for qt in range(4):
                        P.op("pe", lambda e: e.transpose(out=self.psbf[:, qt * 128:(qt + 1) * 128], in_=ynb[:, qt, :], identity=self.identb[:]),
                             reads=[ynb.b, self.identb.b], writes=[self.psbf.b])
                    P.op("act", lambda e: e.copy(out=yt_[:, qs], in_=self.psbf[:, 0:512]), reads=[self.psbf.b], writes=[yt_.b])

                LOOK = 2
                for ti in range(min(LOOK, len(tasks))):
                    score(ti)
                for ti in range(len(tasks)):
                    if ti + LOOK < len(tasks):
                        score(ti + LOOK)
                    qb, cc, kt = tasks[ti]
                    r = kt - 4 * qb
                    c0 = max(r, 0) * 128
                    pS = sbanks[ti % 3]
                    pO = [self.ps[2 + 2 * cc], self.ps[3 + 2 * cc]]
                    et = ET[ti % 4]
                    P.op("act", lambda e: e.activation(out=et[:, c0:512], in_=pS[:, c0:512], func=AF.Exp, scale=0.125), reads=[pS.b], writes=[et.b])
                    if r >= 0:
                        P.op("pool", lambda e: e.tensor_tensor(out=et[:, c0:c0 + 128], in0=et[:, c0:c0 + 128], in1=cmaskb[:], op=ALU.mult),
                             reads=[et.b, cmaskb.b], writes=[et.b])
                    for qt in range(max(r, 0), 4):
                        po = pO[qt // 2]
                        oc = (qt % 2) * 129
                        P.op("pe", lambda e: e.matmul(po[:, oc:oc + 129], lhsT=et[:, qt * 128:(qt + 1) * 128], rhs=vh_[:, kt, :],
                                                      start=(kt == 0 and qt % 2 == 0), stop=(kt == 4 * qb + qt), skip_group_check=True),
                             reads=[et.b, vh_.b], writes=[po.b])
                    if kt == 4 * qb + 3:
                        for i2 in range(2):
                            self.copy(("act", "dve")[i2], Oc[cc][:, 2 * i2:2 * i2 + 2, :].rearrange("p a e -> p (a e)"), pO[i2][:, 0:258], [pO[i2].b], [Oc[cc].b])
                        if cc == 1:
                            combine(qb)
                P.dma("sp", lambda e: e.dma_start(out=self.YT[hs, :], in_=yt_[:]), yt_.b, reads=[yt_.b])
            self.end_phase()

        with ExitStack() as es:
            self._stg = None
            Wo = self.tile(es, "aWo", [128, 8, D], BF16)
            s3 = inp["b_w_o"][jl].rearrange("(kc p) n -> p kc n", p=128)
            self.load_cast(es, [(Wo[:, kc:kc + 1, :], s3[:, kc:kc + 1, :], 128, 1, D) for kc in range(8)], Wo.b)
            xs = [self.tile(es, "cx%d" % i, [128, 8, TB], dma=True) for i in range(2)]
            ys = [self.tile(es, "cy%d" % i, [128, 8, TB], BF16, dma=True) for i in range(2)]
            for nb in range(T // TB):
                t0 = nb * TB
                x = xs[nb % 2]
                yb = ys[nb % 2]
                src = xsrc.rearrange("(j p) t -> p j t", p=128)[:, :, t0:t0 + TB]
                P.dma("sp", lambda e: e.dma_start(out=x[:], in_=src), x.b, writes=[x.b])
                srcy = self.YT.rearrange("(j p) t -> p j t", p=128)[:, :, t0:t0 + TB]
                P.dma("act", lambda e: e.dma_start(out=yb[:], in_=srcy), yb.b, writes=[yb.b])
                for nj in range(8):
                    ps = self.psum_next()
                    for kc in range(8):
                        P.op("pe", lambda e: e.matmul(ps[:, :], lhsT=Wo[:, kc, nj * 128:(nj + 1) * 128], rhs=yb[:, kc, :], start=(kc == 0), stop=(kc == 7)),
                             reads=[Wo.b, yb.b], writes=[ps.b])
                    P.op("dve", lambda e: e.scalar_tensor_tensor(out=x[:, nj, :], in0=ps[:, :], scalar=g1c[:, nj:nj + 1], in1=x[:, nj, :], op0=ALU.mult, op1=ALU.add),
                         reads=[ps.b, x.b, self.modc.b], writes=[x.b])
                dst = self.X.rearrange("(j p) t -> p j t", p=128)[:, :, t0:t0 + TB]
                P.dma("sp", lambda e: e.dma_start(out=dst, in_=x[:]), x.b, reads=[x.b])
            self.end_phase()


_CONSTS = None


def prepare_inputs(inputs):
    global _CONSTS
    if _CONSTS is None:
        _CONSTS = make_consts()
    f = lambda a: np.ascontiguousarray(np.asarray(a, np.float32))
    vecs = {}
    for l in range(4):
        vecs["ada_b%d" % l] = inputs["ada_b"][l]
        vecs["norm1_%d" % l] = inputs["norm1"][l]
        vecs["norm2_%d" % l] = inputs["norm2"][l]
        for i in range(3):
            vecs["cw%d_%d" % (i, l)] = inputs["ffn_conv_w"][l][i]
        vecs["cb_%d" % l] = inputs["ffn_conv_b"][l]
    vecs["final_norm"] = inputs["final_norm"]
    vecs["kv_norm"] = inputs["kv_norm"]
    for l in range(2):
        for i in range(6):
            vecs["mu%d_%d" % (i, l)] = inputs["a_mu"][l][i]
        vecs["w0_%d" % l] = inputs["a_w0"][l]
        vecs["a0_%d" % l] = inputs["a_a0"][l]
        vecs["k_k_%d" % l] = inputs["a_k_k"][l]
        vecs["k_a_%d" % l] = inputs["a_k_a"][l]
        vecs["r_k_%d" % l] = np.asarray(inputs["a_r_k"][l]).reshape(-1)
    cols = CP.pack(vecs)
    rows = np.concatenate([
        f(inputs["a_ln_w"][0]), f(inputs["a_ln_b"][0]), f(inputs["a_ln_w"][1]), f(inputs["a_ln_b"][1]),
        f(inputs["a_v0"][0]),
        np.tile(f(inputs["b_subln"][0]), 8), np.tile(f(inputs["b_subln"][1]), 8),
        f(inputs["b_lam"][0]).reshape(-1), f(inputs["b_lam"][1]).reshape(-1)])
    shared = dict(_CONSTS)
    shared["cols"] = cols
    shared["rows"] = rows
    for k in ("ada_w", "a_w_rkv", "a_w1", "a_w2", "a_a1", "a_a2", "a_v1", "a_v2", "a_g1", "a_g2", "a_w_o",
              "w_kv", "b_w_q", "b_w_o", "ffn_w_up", "ffn_w_down"):
        shared[k] = f(inputs[k])
    x = np.asarray(inputs["x"], np.float32)
    c = np.asarray(inputs["c"], np.float32)
    in_maps = []
    for b in range(NCORES):
        m = dict(shared)
        m["xT"] = np.ascontiguousarray(x[b].T)
        m["ccol"] = np.ascontiguousarray(c[b].reshape(8, 128).T)
        in_maps.append(m)
    return in_maps


_NC_CACHE = {}


def run(inputs, cfg, key="full", ncores=NCORES):
    if key not in _NC_CACHE:
        _NC_CACHE[key] = Builder(cfg).build()
    nc = _NC_CACHE[key]
    in_maps = prepare_inputs(inputs)[:ncores]
    res = run_bass_kernel_spmd(nc, in_maps, core_ids=list(range(ncores)))
    return res


def kernel(**inputs):
    cfg = {"layers": [0, 1, 2, 3]}
    res = run(inputs, cfg)
    out = np.stack([np.ascontiguousarray(r["outT"].T) for r in res.results], axis=0)
    return out.astype(np.float32)
```

```python
import math
import numpy as np
import concourse.bass as bass
import concourse.mybir as mybir
from concourse.bass_utils import run_bass_kernel_spmd
from contextlib import ExitStack
import types

F32 = mybir.dt.float32
BF16 = mybir.dt.bfloat16
AF = mybir.ActivationFunctionType
ALU = mybir.AluOpType
AX = mybir.AxisListType

D = 1024
T = 4096
NJ = 8
DFF = 2816
F2 = 5632
NF = 44
NG = 22
C0 = math.exp(-0.5)
NCORES = 8

ENGS = ["pe", "act", "dve", "pool", "sp"]


def freeze(fn):
    if fn.__closure__ is None:
        return fn
    cells = []
    for c in fn.__closure__:
        try:
            cells.append(types.CellType(c.cell_contents))
        except ValueError:
            cells.append(c)
    return types.FunctionType(fn.__code__, fn.__globals__, fn.__name__, fn.__defaults__, tuple(cells))


class Buf:
    __slots__ = ("name", "lw", "rd", "dsem", "excl")

    def __init__(self, name):
        self.name = name
        self.lw = None
        self.rd = {}
        self.dsem = None
        self.excl = False


class Tl:
    __slots__ = ("ap", "b")

    def __init__(self, ap, b):
        self.ap = ap
        self.b = b

    def __getitem__(self, k):
        return self.ap[k]


class Prog:
    def __init__(self, nc, es, n_dma_sems=40):
        self.nc = nc
        self.es = es
        self.q = {e: [] for e in ENGS}
        self.cnt = {e: 0 for e in ENGS}
        self.sems = {}
        self.semkey = 0
        self.esem = {e: self._newsem("c_" + e) for e in ENGS}
        self.seen = {e: {} for e in ENGS}
        self.bar = self._newsem("bar")
        self.nbar = 0
        self.dma_pool = [self._newsem("d%d" % i) for i in range(n_dma_sems)]
        self.dma_cnt = {k: 0 for k in self.dma_pool}
        self.dma_free = list(self.dma_pool)
        self.ninstr = 0

    def _newsem(self, name):
        s = self.es.enter_context(self.nc.semaphore(name))
        self.semkey += 1
        self.sems[self.semkey] = s
        return self.semkey

    def buf(self, name):
        return Buf(name)

    def dma_buf(self, name):
        b = Buf(name)
        b.dsem = self.dma_free.pop(0)
        return b

    def release(self, bufs):
        for b in bufs:
            if b.dsem is not None:
                self.dma_free.append(b.dsem)
                b.dsem = None

    def _waits(self, e, reads, writes, is_dma=False):
        need = {}
        seen = self.seen[e]

        def add(ev, raw):
            key, val, src = ev
            if src == e and not is_dma and (e in ("pe", "sp") or not raw):
                return
            if seen.get(key, 0) >= val:
                return
            if need.get(key, 0) < val:
                need[key] = val

        for b in reads:
            if b.lw is not None:
                add(b.lw, True)
            if b.excl:
                for src, ev in b.rd.items():
                    if src != e:
                        add(ev, False)
        for b in writes:
            if b.lw is not None:
                add(b.lw, False)
            for ev in b.rd.values():
                add(ev, False)
        out = []
        for key, val in need.items():
            seen[key] = val
            out.append((self.sems[key], val))
        return out

    def _emit(self, e, fn, waits, sem, inc):
        fn = freeze(fn)

        attach = (inc == 1 and e in ("act", "dve", "pool") and len(waits) > 0)

        def run(eng, fn=fn, waits=waits, sem=sem, inc=inc, attach=attach):
            for (s, v) in (waits[:-1] if attach else waits):
                eng.wait_ge(s, v)
            ins = fn(eng)
            if attach:
                ins._wait_ge(waits[-1][0], waits[-1][1])
            ins.then_inc(sem, inc)
        self.q[e].append(run)
        self.ninstr += 1 + len(waits)

    def op(self, e, fn, reads=(), writes=()):
        waits = self._waits(e, reads, writes)
        self.cnt[e] += 1
        key = self.esem[e]
        self._emit(e, fn, waits, self.sems[key], 1)
        ev = (key, self.cnt[e], e)
        for b in writes:
            b.lw = ev
            b.rd = {}
        for b in reads:
            if b not in writes:
                b.rd[e] = ev

    def dma(self, e, fn, owner, reads=(), writes=()):
        assert owner.dsem is not None, owner.name
        waits = self._waits(e, reads, writes, is_dma=True)
        key = owner.dsem
        self.dma_cnt[key] += 16
        self._emit(e, fn, waits, self.sems[key], 16)
        src = "dma%d" % key
        ev = (key, self.dma_cnt[key], src)
        for b in writes:
            b.lw = ev
            b.rd = {}
        for b in reads:
            if b not in writes:
                b.rd[src] = ev

    def barrier(self):
        g = "sp"
        gw = []
        for e in ENGS:
            if e == g or self.cnt[e] == 0:
                continue
            key = self.esem[e]
            if self.seen[g].get(key, 0) < self.cnt[e]:
                gw.append((self.sems[key], self.cnt[e]))
        for key in self.dma_pool:
            val = self.dma_cnt[key]
            if val > 0 and self.seen[g].get(key, 0) < val:
                gw.append((self.sems[key], val))
        self.nbar += 1
        bsem = self.sems[self.bar]

        def run_g(eng, gw=gw, bsem=bsem):
            for (s, v) in gw:
                eng.wait_ge(s, v)
            eng.sem_inc(bsem, 1)
        self.q[g].append(run_g)
        for e in ENGS:
            if e != g:
                self.q[e].append(lambda eng, sem=bsem, val=self.nbar: eng.wait_ge(sem, val))
        for e in ENGS:
            for e2 in ENGS:
                self.seen[e][self.esem[e2]] = self.cnt[e2]
            for key in self.dma_pool:
                self.seen[e][key] = self.dma_cnt[key]
        for e in ENGS:
            if self.cnt[e] > 12000:
                self.esem[e] = self._newsem("c_%s_%d" % (e, self.nbar))
                self.cnt[e] = 0

    def finish(self):
        nc = self.nc
        self.barrier()
        with nc.Block() as block:
            @block.tensor
            def _(eng):
                for f in self.q["pe"]:
                    f(eng)

            @block.scalar
            def _(eng):
                for f in self.q["act"]:
                    f(eng)

            @block.vector
            def _(eng):
                for f in self.q["dve"]:
                    f(eng)

            @block.gpsimd
            def _(eng):
                for f in self.q["pool"]:
                    f(eng)

            @block.sync
            def _(eng):
                for f in self.q["sp"]:
                    f(eng)


class ColPack:
    def __init__(self):
        self.off = {}
        self.n = 0
        self.items = []

    def add(self, name, length):
        assert length % 128 == 0
        self.off[name] = self.n
        self.n += length // 128
        self.items.append((name, length))

    def pack(self, vecs):
        arr = np.zeros((128, self.n), np.float32)
        for name, length in self.items:
            v = np.asarray(vecs[name], np.float32).reshape(length // 128, 128)
            arr[:, self.off[name]:self.off[name] + length // 128] = v.T
        return arr


def make_colpack():
    cp = ColPack()
    for l in range(4):
        cp.add("ada_b%d" % l, 6 * D)
        cp.add("norm1_%d" % l, D)
        cp.add("norm2_%d" % l, D)
        for i in range(3):
            cp.add("cw%d_%d" % (i, l), F2)
        cp.add("cb_%d" % l, F2)
    cp.add("final_norm", D)
    cp.add("kv_norm", D)
    for l in range(2):
        for i in range(6):
            cp.add("mu%d_%d" % (i, l), D)
        for nm in ("w0", "a0", "k_k", "k_a", "r_k"):
            cp.add("%s_%d" % (nm, l), D)
    return cp


CP = make_colpack()


def make_consts():
    c = {}
    c["ident"] = np.eye(128, dtype=np.float32)
    c["ones"] = np.ones((128, 128), np.float32)
    bd = np.zeros((128, 128), np.float32)
    bd[:64, :64] = 1
    bd[64:, 64:] = 1
    c["bd"] = bd
    ind = np.zeros((128, 8, 16), np.float32)
    for p in range(128):
        for j in range(8):
            ind[p, j, 2 * j + p // 64] = 1
    c["ind"] = ind.reshape(128, 128)
    s = np.arange(128)[:, None]
    t = np.arange(128)[None, :]
    strict = (t > s).astype(np.float32)
    incl = (t >= s).astype(np.float32)
    c["mask4"] = np.concatenate([strict, incl, strict, incl], axis=1)
    c["maskL"] = (t < s).astype(np.float32)
    pos = np.arange(T, dtype=np.float64)
    inv = 500000.0 ** (-np.arange(0, 16, 2, dtype=np.float64) / 16)
    ct = np.ones((128, T), np.float64)
    st = np.zeros((128, T), np.float64)
    rm = np.zeros((128, 128), np.float32)
    for cc in range(2):
        for d in range(16):
            p = cc * 64 + d
            ang = pos * inv[d % 8]
            ct[p] = np.cos(ang)
            st[p] = np.sin(ang)
            if d < 8:
                rm[p + 8, p] = -1.0
            else:
                rm[p - 8, p] = 1.0
    c["ropec"] = ct.astype(np.float32)
    c["ropes"] = st.astype(np.float32)
    c["roper"] = rm
    return c


class Builder:
    def __init__(self, cfg):
        self.cfg = cfg
        self.nc = bass.Bass("TRN2", target_bir_lowering=False)
        self.uid = 0
        self.rr = 0

    def dram_in(self, name, shape, dt=F32):
        return self.nc.dram_tensor(name, list(shape), dt, kind="ExternalInput").ap()

    def tile(self, es, name, shape, dt=F32, dma=False):
        self.uid += 1
        t = es.enter_context(self.nc.sbuf_tensor("%s_%d" % (name, self.uid), list(shape), dt))
        b = self.P.dma_buf(name) if dma else self.P.buf(name)
        if dma:
            self.phase_dma_bufs.append(b)
        return Tl(t, b)

    def eng_rr(self, engs=("dve", "pool", "act")):
        self.rr += 1
        return engs[self.rr % len(engs)]

    def copy(self, e, out_ap, in_ap, reads, writes):
        P = self.P
        if e == "act":
            P.op("act", lambda g: g.copy(out=out_ap, in_=in_ap), reads=reads, writes=writes)
        else:
            P.op(e, lambda g: g.tensor_copy(out=out_ap, in_=in_ap), reads=reads, writes=writes)

    def psum_next(self):
        self.psi = (self.psi + 1) % 6
        return self.ps[self.psi]

    def build(self):
        nc = self.nc
        cfg = self.cfg
        inp = {}
        inp["xT"] = self.dram_in("xT", [D, T])
        inp["ccol"] = self.dram_in("ccol", [128, 8])
        inp["cols"] = self.dram_in("cols", [128, CP.n])
        for k in ("ident", "ones", "bd", "ind", "maskL", "roper"):
            inp[k] = self.dram_in(k, [128, 128])
        inp["mask4"] = self.dram_in("mask4", [128, 512])
        inp["ropec"] = self.dram_in("ropec", [128, T])
        inp["ropes"] = self.dram_in("ropes", [128, T])
        inp["ada_w"] = self.dram_in("ada_w", [4, D, 6 * D])
        inp["a_w_rkv"] = self.dram_in("a_w_rkv", [2, 3, D, D])
        inp["a_w1"] = self.dram_in("a_w1", [2, D, 64])
        inp["a_w2"] = self.dram_in("a_w2", [2, 64, D])
        inp["a_a1"] = self.dram_in("a_a1", [2, D, 64])
        inp["a_a2"] = self.dram_in("a_a2", [2, 64, D])
        inp["a_v1"] = self.dram_in("a_v1", [1, D, 32])
        inp["a_v2"] = self.dram_in("a_v2", [1, 32, D])
        inp["a_g1"] = self.dram_in("a_g1", [2, D, 160])
        inp["a_g2"] = self.dram_in("a_g2", [2, 160, D])
        inp["a_w_o"] = self.dram_in("a_w_o", [2, D, D])
        inp["rows"] = self.dram_in("rows", [7 * D + 512])
        inp["w_kv"] = self.dram_in("w_kv", [D, 2 * D])
        inp["b_w_q"] = self.dram_in("b_w_q", [2, D, D])
        inp["b_w_o"] = self.dram_in("b_w_o", [2, D, D])
        inp["ffn_w_up"] = self.dram_in("ffn_w_up", [4, D, F2])
        inp["ffn_w_down"] = self.dram_in("ffn_w_down", [4, DFF, D])
        self.inp = inp
        self.outT = nc.dram_tensor("outT", [D, T], F32, kind="ExternalOutput").ap()
        self.X = nc.dram_tensor("Xs", [D, T], F32).ap()
        self.VF = nc.dram_tensor("VFs", [T, D], F32).ap()
        self.KT = nc.dram_tensor("KTs", [D, T], BF16).ap()
        self.VS = nc.dram_tensor("VSs", [T, D], BF16).ap()
        self.QT = nc.dram_tensor("QTs", [D, T], BF16).ap()
        self.YT = nc.dram_tensor("YTs", [D, T], BF16).ap()
        self.dbg = {}
        for name, shape in cfg.get("dbg", {}).items():
            self.dbg[name] = nc.dram_tensor("dbg_" + name, list(shape), F32, kind="ExternalOutput").ap()

        with ExitStack() as es:
            self.P = P = Prog(nc, es)
            self.phase_dma_bufs = []
            self.ps = []
            for i in range(7):
                t = es.enter_context(nc.psum_tensor("psb%d" % i, [128, 512], F32))
                self.ps.append(Tl(t, P.buf("psb%d" % i)))
                self.ps[-1].b.excl = True
            t = es.enter_context(nc.psum_tensor("psbf", [128, 1024], BF16))
            self.psbf = Tl(t, P.buf("psbf"))
            self.psbf.b.excl = True
            self.psi = 0
            g = es
            self.cols = self.tile(g, "cols", [128, CP.n], dma=True)
            self.ccol = self.tile(g, "ccol", [128, 8], dma=True)
            self.cst = {}
            for k in ("ident", "ones", "bd", "ind", "maskL", "roper"):
                self.cst[k] = self.tile(g, k, [128, 128], dma=True)
            self.cst["mask4"] = self.tile(g, "mask4", [128, 512], dma=True)
            P.dma("sp", lambda e: e.dma_start(out=self.cols[:], in_=inp["cols"]), self.cols.b, writes=[self.cols.b])
            P.dma("sp", lambda e: e.dma_start(out=self.ccol[:], in_=inp["ccol"]), self.ccol.b, writes=[self.ccol.b])
            for k, tl in self.cst.items():
                P.dma("act", lambda e, tl=tl, k=k: e.dma_start(out=tl[:], in_=inp[k]), tl.b, writes=[tl.b])
            self.eps = self.tile(g, "eps", [128, 1])
            P.op("pool", lambda e: e.memset(self.eps[:], 1e-6), writes=[self.eps.b])
            self.identb = self.tile(g, "identb", [128, 128], BF16)
            self.copy("pool", self.identb[:], self.cst["ident"][:], [self.cst["ident"].b], [self.identb.b])
            self.modc = self.tile(g, "modc", [128, 192])
            self.der = self.tile(g, "der", [128, 4, 2, 8])

            self.phase_mod()
            xsrc = inp["xT"]
            for l in cfg["layers"]:
                if cfg.get("mixer", True):
                    if l < 2:
                        self.phase_rwkv(l, xsrc)
                    else:
                        if l == 2 or cfg.get("force_kv", False):
                            self.phase_kv(xsrc)
                        self.phase_attn(l, xsrc)
                    xsrc = self.X
                fuse_final = (l == cfg["layers"][-1]) and cfg.get("ffn", True) and cfg.get("final", True) and cfg.get("fuse_final", True)
                if cfg.get("ffn", True):
                    self.phase_ffn(l, xsrc, fuse_final)
                    xsrc = self.X
            if not fuse_final:
                self.phase_final(xsrc, cfg.get("final", True))
            P.finish()
        return nc

    def col(self, name, j0=0, n=None):
        o = CP.off[name] + j0
        if n is None:
            n = 1
        return self.cols[:, o:o + n]

    def end_phase(self):
        self.P.barrier()
        self.P.release(self.phase_dma_bufs)
        self.phase_dma_bufs = []

    def phase_mod(self):
        P, nc, inp = self.P, self.nc, self.inp
        with ExitStack() as es:
            cact = self.tile(es, "cact", [128, 8])
            P.op("act", lambda e: e.activation(out=cact[:], in_=self.ccol[:], func=AF.Silu),
                 reads=[self.ccol.b], writes=[cact.b])
            A = [self.tile(es, "adaA%d" % i, [128, 8, 768], dma=True) for i in range(4)]
            psm = self.ps[0]
            it = 0
            for l in range(4):
                for blk in range(8):
                    a = A[it % 4]
                    it += 1
                    src = inp["ada_w"][l].rearrange("(kc p) n -> p kc n", p=128)[:, :, blk * 768:(blk + 1) * 768]
                    P.dma("sp" if it % 2 else "act", lambda e, a=a, src=src: e.dma_start(out=a[:], in_=src), a.b, writes=[a.b])
                    for n_ in range(6):
                        colidx = l * 48 + blk * 6 + n_
                        for kc in range(8):
                            P.op("pe", lambda e, a=a, kc=kc, n_=n_, colidx=colidx: e.matmul(
                                psm[:, colidx:colidx + 1], lhsT=a[:, kc, n_ * 128:(n_ + 1) * 128], rhs=cact[:, kc:kc + 1],
                                start=(kc == 0), stop=(kc == 7)), reads=[a.b, cact.b], writes=[psm.b])
            for l in range(4):
                o = CP.off["ada_b%d" % l]
                P.op("dve", lambda e, l=l, o=o: e.tensor_tensor(out=self.modc[:, l * 48:(l + 1) * 48], in0=psm[:, l * 48:(l + 1) * 48],
                                                                in1=self.cols[:, o:o + 48], op=ALU.add),
                     reads=[psm.b, self.cols.b], writes=[self.modc.b])
            for l in range(4):
                for which in range(2):
                    sc = self.modc[:, l * 48 + which * 24 + 8: l * 48 + which * 24 + 16]
                    nm = self.col("norm%d_%d" % (which + 1, l), 0, 8)
                    P.op("dve", lambda e, l=l, which=which, sc=sc, nm=nm: e.scalar_tensor_tensor(
                        out=self.der[:, l, which, :], in0=sc, scalar=1.0, in1=nm, op0=ALU.add, op1=ALU.mult),
                        reads=[self.modc.b, self.cols.b], writes=[self.der.b])
            if "modc" in self.dbg:
                dt = self.tile(es, "dbgm", [128, 192], dma=True)
                self.copy("dve", dt[:], self.modc[:], [self.modc.b], [dt.b])
                P.dma("sp", lambda e: e.dma_start(out=self.dbg["modc"], in_=dt[:]), dt.b, reads=[dt.b])
            self.end_phase()

    def modcol(self, l, i, j0=0, n=8):
        o = l * 48 + i * 8 + j0
        return self.modc[:, o:o + n]

    def rmsnorm(self, x, N, sq, G, S, out_ap, out_b, extra_reads=()):
        P = self.P
        ones = self.cst["ones"]
        P.op("act", lambda e: e.activation(out=sq[:], in_=x[:], func=AF.Square), reads=[x.b], writes=[sq.b])
        ps = self.psum_next()
        for j in range(8):
            P.op("pe", lambda e, j=j: e.matmul(ps[:, 0:N], lhsT=ones[:], rhs=sq[:, j, :], start=(j == 0), stop=(j == 7)),
                 reads=[ones.b, sq.b], writes=[ps.b])
        rs = self.rstd
        P.op("act", lambda e: e.activation(out=rs[:, 0:N], in_=ps[:, 0:N], func=AF.Sqrt, bias=self.eps[:], scale=1.0 / D),
             reads=[ps.b, self.eps.b], writes=[rs.b])
        P.op("dve", lambda e: e.reciprocal(out=rs[:, 0:N], in_=rs[:, 0:N]), reads=[rs.b], writes=[rs.b])
        P.op("dve", lambda e: e.tensor_tensor(out=sq[:], in0=x[:], in1=rs[:, 0:N].unsqueeze(1).to_broadcast([128, 8, N]), op=ALU.mult),
             reads=[x.b, rs.b], writes=[sq.b])
        if S is None:
            P.op("pool", lambda e: e.tensor_tensor(out=out_ap, in0=sq[:], in1=G.unsqueeze(2).to_broadcast([128, 8, N]), op=ALU.mult),
                 reads=[sq.b, self.cols.b, self.der.b] + list(extra_reads), writes=[out_b])
        else:
            P.op("pool", lambda e: e.tensor_tensor(out=sq[:], in0=sq[:], in1=G.unsqueeze(2).to_broadcast([128, 8, N]), op=ALU.mult),
                 reads=[sq.b, self.cols.b, self.der.b], writes=[sq.b])
            P.op("pool", lambda e: e.tensor_tensor(out=out_ap, in0=sq[:], in1=S.unsqueeze(2).to_broadcast([128, 8, N]), op=ALU.add),
                 reads=[sq.b, self.modc.b] + list(extra_reads), writes=[out_b])

    def load_cast(self, es_stage, pieces, wb):
        P = self.P
        if not hasattr(self, "_stg") or self._stg is None:
            self._stg = [self.tile(es_stage, "stg%d" % i, [128, 1024], dma=True) for i in range(getattr(self, "_stg_n", 2))]
            self._stgi = 0
        for (dst, src, p, a, b) in pieces:
            assert a * b <= 1024
            st = self._stg[self._stgi % len(self._stg)]
            self._stgi += 1
            sv = st[0:p, 0:a * b].rearrange("p (a b) -> p a b", a=a)
            q = ("sp", "act")[self._stgi % 2]
            P.dma(q, lambda e, sv=sv, src=src: e.dma_start(out=sv, in_=src), st.b, writes=[st.b])
            self.copy(self.eng_rr(("pool", "dve", "act")), dst, sv, [st.b], [wb])

    def phase_ffn(self, l, xsrc, fuse_final=False):
        P, nc, inp = self.P, self.nc, self.inp
        TB = 256
        NB = T // TB
        with ExitStack() as es:
            self._stg = None
            self._stg_n = 6
            ses = ExitStack()
            wup = self.tile(es, "wup", [128, 8, F2], BF16)
            wdn = self.tile(es, "wdn", [128, NG, D], BF16)
            pieces = []
            srcu = inp["ffn_w_up"][l].rearrange("(kc p) n -> p kc n", p=128)
            for kc in range(8):
                for (n0, n1) in ((0, 1024), (1024, 2048), (2048, 3072), (3072, 4096), (4096, 5120), (5120, F2)):
                    pieces.append((wup[:, kc:kc + 1, n0:n1], srcu[:, kc:kc + 1, n0:n1], 128, 1, n1 - n0))
            self.load_cast(ses, pieces, wup.b)
            srcd = inp["ffn_w_down"][l].rearrange("(kc p) n -> p kc n", p=128)
            pieces = [(wdn[:, i:i + 1, :], srcd[:, i:i + 1, :], 128, 1, D) for i in range(0, NG)]
            self.load_cast(ses, pieces, wdn.b)
            P.barrier()
            ses.close()
            self._stg = None
            self._stg_n = 2

            xs = [self.tile(es, "fx%d" % i, [128, 8, TB], dma=True) for i in range(2)]
            sq = self.tile(es, "fsq", [128, 8, TB])
            self.rstd = self.tile(es, "frstd", [128, TB])
            h2s = [self.tile(es, "fh2_%d" % i, [128, 8, TB + 2], BF16) for i in range(2)]
            for i in range(2):
                P.op("pool", lambda e: e.memset(h2s[i][:], 0.0), writes=[h2s[i].b])
            NCV = 6
            cv = [self.tile(es, "fcv%d" % i, [128, TB]) for i in range(NCV)]
            sgl = [self.tile(es, "fsg%d" % i, [128, TB]) for i in range(3)]
            hm = self.tile(es, "fhm", [128, NG, TB], BF16)
            G2 = self.der[:, l, 1, :]
            S2 = self.modcol(l, 3)
            g2c = self.modcol(l, 5)
            cw = [CP.off["cw%d_%d" % (i, l)] for i in range(3)]
            cb = CP.off["cb_%d" % l]
            xr = lambda ap: ap.rearrange("(j p) t -> p j t", p=128)

            def norm(nb):
                x = xs[nb % 2]
                h2 = h2s[nb % 2]
                t0 = nb * TB
                P.dma("sp", lambda e: e.dma_start(out=x[:], in_=xr(xsrc)[:, :, t0:t0 + TB]), x.b, writes=[x.b])
                if nb > 0:
                    hp = h2s[(nb - 1) % 2]
                    P.op("pool", lambda e: e.tensor_copy(out=h2[:, :, 0:2], in_=hp[:, :, TB:TB + 2]), reads=[hp.b], writes=[h2.b])
                self.rmsnorm(x, TB, sq, G2, S2, h2[:, :, 2:TB + 2], h2.b)

            def up(nb):
                h2 = h2s[nb % 2]
                ui = 0
                for i in range(NG):
                    for which in range(2):
                        n = i + which * NG
                        ps = self.psum_next()
                        for kc in range(8):
                            P.op("pe", lambda e: e.matmul(ps[:, 0:TB + 2], lhsT=wup[:, kc, n * 128:(n + 1) * 128], rhs=h2[:, kc, :],
                                                          start=(kc == 0), stop=(kc == 7)), reads=[wup.b, h2.b], writes=[ps.b])
                        c = cv[ui % NCV]
                        ui += 1
                        P.op("act", lambda e: e.activation(out=c[:], in_=ps[:, 2:TB + 2], func=AF.Identity,
                                                           bias=self.cols[:, cb + n:cb + n + 1], scale=self.cols[:, cw[2] + n:cw[2] + n + 1]),
                             reads=[ps.b, self.cols.b], writes=[c.b])
                        P.op("dve", lambda e: e.scalar_tensor_tensor(out=c[:], in0=ps[:, 1:TB + 1], scalar=self.cols[:, cw[1] + n:cw[1] + n + 1],
                                                                     in1=c[:], op0=ALU.mult, op1=ALU.add), reads=[ps.b, c.b, self.cols.b], writes=[c.b])
                        P.op("dve", lambda e: e.scalar_tensor_tensor(out=c[:], in0=ps[:, 0:TB], scalar=self.cols[:, cw[0] + n:cw[0] + n + 1],
                                                                     in1=c[:], op0=ALU.mult, op1=ALU.add), reads=[ps.b, c.b, self.cols.b], writes=[c.b])
                        s_ = sgl[i % 3]
                        if which == 0:
                            P.op("act", lambda e: e.activation(out=s_[:], in_=c[:], func=AF.Silu), reads=[c.b], writes=[s_.b])
                        else:
                            P.op("pool", lambda e: e.tensor_tensor(out=hm[:, i, :], in0=s_[:], in1=c[:], op=ALU.mult),
                                 reads=[s_.b, c.b], writes=[hm.b])

            def down(nb):
                x = xs[nb % 2]
                t0 = nb * TB
                for nj in range(8):
                    ps = self.psum_next()
                    for i in range(NG):
                        P.op("pe", lambda e: e.matmul(ps[:, 0:TB], lhsT=wdn[:, i, nj * 128:(nj + 1) * 128], rhs=hm[:, i, :],
                                                      start=(i == 0), stop=(i == NG - 1)), reads=[wdn.b, hm.b], writes=[ps.b])
                    P.op("dve", lambda e: e.scalar_tensor_tensor(out=x[:, nj, :], in0=ps[:, 0:TB], scalar=g2c[:, nj:nj + 1],
                                                                 in1=x[:, nj, :], op0=ALU.mult, op1=ALU.add),
                         reads=[ps.b, x.b, self.modc.b], writes=[x.b])
                if fuse_final:
                    self.rmsnorm(x, TB, sq, self.col("final_norm", 0, 8), None, x[:], x.b)
                    P.dma("sp", lambda e: e.dma_start(out=xr(self.outT)[:, :, t0:t0 + TB], in_=x[:]), x.b, reads=[x.b])
                else:
                    P.dma("sp", lambda e: e.dma_start(out=xr(self.X)[:, :, t0:t0 + TB], in_=x[:]), x.b, reads=[x.b])

            norm(0)
            for nb in range(NB):
                up(nb)
                if nb + 1 < NB:
                    norm(nb + 1)
                down(nb)
            self.end_phase()

    def phase_final(self, xsrc, do_norm):
        P = self.P
        TB = 512
        with ExitStack() as es:
            xs = [self.tile(es, "nx%d" % i, [128, 8, TB], dma=True) for i in range(2)]
            sq = self.tile(es, "nsq", [128, 8, TB])
            self.rstd = self.tile(es, "nrstd", [128, TB])
            G = self.col("final_norm", 0, 8)
            for nb in range(T // TB):
                x = xs[nb % 2]
                t0 = nb * TB
                src = xsrc.rearrange("(j p) t -> p j t", p=128)[:, :, t0:t0 + TB]
                P.dma("sp", lambda e, x=x, src=src: e.dma_start(out=x[:], in_=src), x.b, writes=[x.b])
                if do_norm:
                    self.rmsnorm(x, TB, sq, G, None, x[:], x.b)
                dst = self.outT.rearrange("(j p) t -> p j t", p=128)[:, :, t0:t0 + TB]
                P.dma("act", lambda e, x=x, dst=dst: e.dma_start(out=dst, in_=x[:]), x.b, reads=[x.b])
            self.end_phase()

    def phase_rwkv(self, l, xsrc):
        P, nc, inp = self.P, self.nc, self.inp
        CH = 128
        NCH = self.cfg.get("nch", T // CH)
        ident = self.cst["ident"]
        with ExitStack() as es:
            self._stg = None
            reuse_stage = self.cfg.get("stage_reuse", True)
            self._stg_n = 6 if reuse_stage else 2
            ses = ExitStack() if reuse_stage else es
            Wr = self.tile(es, "Wr", [128, 8, D], BF16)
            Wk = self.tile(es, "Wk", [128, 8, D], BF16)
            Wv = self.tile(es, "Wv", [128, 8, D], BF16)
            Wo = self.tile(es, "Wo", [128, 8, D], BF16)
            w1b = self.tile(es, "w1b", [128, 8, 64], BF16)
            a1b = self.tile(es, "a1b", [128, 8, 64], BF16)
            g1b = self.tile(es, "g1b", [128, 8, 160], BF16)
            w2b = self.tile(es, "w2b", [64, 1, D], BF16)
            a2b = self.tile(es, "a2b", [64, 1, D], BF16)
            g2b = self.tile(es, "g2b", [128, 2, D], BF16)
            if l == 1:
                v1b = self.tile(es, "v1b", [128, 8, 32], BF16)
                v2b = self.tile(es, "v2b", [32, 1, D], BF16)
                v0r = self.tile(es, "v0r", [128, D], dma=True)
            lnw = self.tile(es, "lnw", [128, D], dma=True)
            lnb = self.tile(es, "lnb", [128, D], dma=True)
            omka = self.tile(es, "omka", [128, 8])
            eps2 = self.tile(es, "eps2", [128, 1])
            onesT = self.tile(es, "onesT", [128, 128])
            for W, src in ((Wr, inp["a_w_rkv"][l, 0]), (Wk, inp["a_w_rkv"][l, 1]), (Wv, inp["a_w_rkv"][l, 2]), (Wo, inp["a_w_o"][l])):
                s3 = src.rearrange("(kc p) n -> p kc n", p=128)
                self.load_cast(ses, [(W[:, kc:kc + 1, :], s3[:, kc:kc + 1, :], 128, 1, D) for kc in range(8)], W.b)
            self.load_cast(ses, [(w1b[:], inp["a_w1"][l].rearrange("(kc p) n -> p kc n", p=128), 128, 8, 64)], w1b.b)
            self.load_cast(ses, [(a1b[:], inp["a_a1"][l].rearrange("(kc p) n -> p kc n", p=128), 128, 8, 64)], a1b.b)
            sg1 = inp["a_g1"][l].rearrange("(kc p) n -> p kc n", p=128)
            self.load_cast(ses, [(g1b[:, 0:4, :], sg1[:, 0:4, :], 128, 4, 160), (g1b[:, 4:8, :], sg1[:, 4:8, :], 128, 4, 160)], g1b.b)
            self.load_cast(ses, [(w2b[:], inp["a_w2"][l].rearrange("(o p) n -> p o n", o=1), 64, 1, D)], w2b.b)
            self.load_cast(ses, [(a2b[:], inp["a_a2"][l].rearrange("(o p) n -> p o n", o=1), 64, 1, D)], a2b.b)
            self.load_cast(ses, [(g2b[:, 0:1, :], inp["a_g2"][l][0:128, :].rearrange("(o p) n -> p o n", o=1), 128, 1, D),
                                (g2b[0:32, 1:2, :], inp["a_g2"][l][128:160, :].rearrange("(o p) n -> p o n", o=1), 32, 1, D)], g2b.b)
            if l == 1:
                self.load_cast(ses, [(v1b[:], inp["a_v1"][0].rearrange("(kc p) n -> p kc n", p=128), 128, 8, 32)], v1b.b)
                self.load_cast(ses, [(v2b[:], inp["a_v2"][0].rearrange("(o p) n -> p o n", o=1), 32, 1, D)], v2b.b)
                P.dma("sp", lambda e: e.dma_start(out=v0r[:], in_=inp["rows"][4 * D:5 * D].partition_broadcast(128)), v0r.b, writes=[v0r.b])
            P.dma("sp", lambda e: e.dma_start(out=lnw[:], in_=inp["rows"][(2 * l) * D:(2 * l + 1) * D].partition_broadcast(128)), lnw.b, writes=[lnw.b])
            P.dma("sp", lambda e: e.dma_start(out=lnb[:], in_=inp["rows"][(2 * l + 1) * D:(2 * l + 2) * D].partition_broadcast(128)), lnb.b, writes=[lnb.b])
            P.op("dve", lambda e: e.tensor_scalar(out=omka[:], in0=self.col("k_a_%d" % l, 0, 8), scalar1=-1.0, scalar2=1.0, op0=ALU.mult, op1=ALU.add),
                 reads=[self.cols.b], writes=[omka.b])
            P.op("pool", lambda e: e.memset(eps2[:], 64e-5), writes=[eps2.b])
            P.op("pool", lambda e: e.memset(onesT[:], 1.0), writes=[onesT.b])

            if reuse_stage:
                P.barrier()
                ses.close()
                self._stg = None
            self._stg_n = 2
            Hbd = self.tile(es, "Hbd", [128, 8, 128])
            Hb = self.tile(es, "Hb", [128, 8, 128], BF16)
            if self.cfg.get("t_hb", True):
                P.op("pool", lambda e: e.memset(Hb[:], 0.0), writes=[Hb.b])
            Vb = self.tile(es, "Vb", [128, D], BF16)
            P.op("pool", lambda e: e.memset(Hbd[:], 0.0), writes=[Hbd.b])
            h = self.tile(es, "h", [128, 8, 129])
            P.op("pool", lambda e: e.memset(h[:], 0.0), writes=[h.b])
            x = self.tile(es, "rx", [128, 8, CH], dma=True)
            self.rstd = self.tile(es, "rrstd", [128, CH])
            xx = self.tile(es, "xx", [128, 8, CH])
            tmpm = [self.tile(es, "tmpm%d" % i, [128, 8, CH], dma=(i == 0)) for i in range(2)]
            sq = tmpm[1] if self.cfg.get("t_sq", True) else self.tile(es, "rsq", [128, 8, CH])
            xm = [self.tile(es, "xm%d" % i, [128, 8, CH], BF16) for i in range(6)]
            lwt = self.tile(es, "lwt", [64, CH], BF16)
            lat = self.tile(es, "lat", [64, CH], BF16)
            lgt = self.tile(es, "lgt", [128, 2, CH], BF16)
            V = self.tile(es, "V", [128, D], dma=True)
            Yt = self.tile(es, "Yt", [128, D], dma=True)
            if l == 1:
                lvt = self.tile(es, "lvt", [32, CH], BF16)
                VFt = Tl(tmpm[0][:].rearrange("p a t -> p (a t)"), tmpm[0].b)
                sgv = Tl(tmpm[1][:].rearrange("p a t -> p (a t)")[:, 0:512], tmpm[1].b)
            ogb = self.tile(es, "ogb", [128, D], BF16)
            ogT = self.tile(es, "ogT", [128, 8, CH], BF16)
            st = self.tile(es, "stat", [128, 6, 16])
            rkb = self.tile(es, "rkb", [128, 16])

            NSLOT = self.cfg.get("nslot%d" % l, 4)

            def mkslot(si):
                S = {}

                def pt(name, shape=(128, 128), dt=F32):
                    S[name] = self.tile(es, "s%d_%s" % (si, name), list(shape), dt)
                for nm in ("r", "k", "sg", "a", "kk", "x", "L", "eL"):
                    pt(nm)
                for nm in ("rh", "bts", "kts", "WT", "U"):
                    pt(nm, (128, 128), BF16)
                pt("AR", (128, 2, 128), BF16)
                pt("F3", (128, 3, 128), BF16)
                pt("T3", (128, 3, 128), BF16)
                for i in range(2):
                    pt("SC%d" % i, (128, 4, 128), BF16)
                    pt("Ym%d" % i, (128, 128), BF16)
                    pt("ZY%d_0" % i, (128, 2, 128), BF16)
                    pt("ZY%d_1" % i, (128, 2, 128), BF16)
                    pt("Tt%d" % i, (128, 128), BF16)
                    pt("AV%d" % i, (128, 64), BF16)
                return S
            slots = [mkslot(i) for i in range(NSLOT)]
            mask4 = self.cst["mask4"]; maskL = self.cst["maskL"]; bd = self.cst["bd"]; ind = self.cst["ind"]
            G1 = self.der[:, l, 0, :]
            S1 = self.modcol(l, 0)
            g1c = self.modcol(l, 2)
            psB = self.ps[6]

            def c_(name, j):
                return self.col("%s_%d" % (name, l), j, 1)

            for c in range(NCH):
                t0 = c * CH
                src = xsrc.rearrange("(j p) t -> p j t", p=128)[:, :, t0:t0 + CH]
                P.dma("sp", lambda e, src=src: e.dma_start(out=x[:], in_=src), x.b, writes=[x.b])
                self.rmsnorm(x, CH, sq, G1, S1, h[:, :, 1:CH + 1], h.b)
                P.op("pool", lambda e: e.tensor_tensor(out=xx[:], in0=h[:, :, 0:CH], in1=h[:, :, 1:CH + 1], op=ALU.subtract),
                     reads=[h.b], writes=[xx.b])
                for i in range(6):
                    tm = tmpm[i % 2]
                    mu = self.col("mu%d_%d" % (i, l), 0, 8)
                    P.op("dve", lambda e, tm=tm, mu=mu: e.tensor_tensor(out=tm[:], in0=xx[:], in1=mu.unsqueeze(2).to_broadcast([128, 8, CH]), op=ALU.mult),
                         reads=[xx.b, self.cols.b], writes=[tm.b])
                    P.op("pool", lambda e, tm=tm, i=i: e.tensor_tensor(out=xm[i][:], in0=tm[:], in1=h[:, :, 1:CH + 1], op=ALU.add),
                         reads=[tm.b, h.b], writes=[xm[i].b])
                P.op("pool", lambda e: e.tensor_copy(out=h[:, :, 0:1], in_=h[:, :, CH:CH + 1]), reads=[h.b], writes=[h.b])
                if l == 1:
                    P.dma("sp", lambda e, t0=t0: e.dma_start(out=VFt[:], in_=self.VF[t0:t0 + CH, :]), VFt.b, writes=[VFt.b])

                psl = self.psum_next()
                for kc in range(8):
                    P.op("pe", lambda e, kc=kc: e.matmul(psl[0:64, 0:128], lhsT=w1b[:, kc, :], rhs=xm[1][:, kc, :], start=(kc == 0), stop=(kc == 7)),
                         reads=[w1b.b, xm[1].b], writes=[psl.b])
                for kc in range(8):
                    P.op("pe", lambda e, kc=kc: e.matmul(psl[0:64, 128:256], lhsT=a1b[:, kc, :], rhs=xm[4][:, kc, :], start=(kc == 0), stop=(kc == 7)),
                         reads=[a1b.b, xm[4].b], writes=[psl.b])
                for kc in range(8):
                    P.op("pe", lambda e, kc=kc: e.matmul(psl[:, 256:384], lhsT=g1b[:, kc, 0:128], rhs=xm[5][:, kc, :], start=(kc == 0), stop=(kc == 7)),
                         reads=[g1b.b, xm[5].b], writes=[psl.b])
                for kc in range(8):
                    P.op("pe", lambda e, kc=kc: e.matmul(psl[0:32, 384:512], lhsT=g1b[:, kc, 128:160], rhs=xm[5][:, kc, :], start=(kc == 0), stop=(kc == 7)),
                         reads=[g1b.b, xm[5].b], writes=[psl.b])
                P.op("act", lambda e: e.activation(out=lwt[:], in_=psl[0:64, 0:128], func=AF.Tanh), reads=[psl.b], writes=[lwt.b])
                P.op("act", lambda e: e.copy(out=lat[:], in_=psl[0:64, 128:256]), reads=[psl.b], writes=[lat.b])
                P.op("act", lambda e: e.activation(out=lgt[:, 0, :], in_=psl[:, 256:384], func=AF.Sigmoid), reads=[psl.b], writes=[lgt.b])
                P.op("act", lambda e: e.activation(out=lgt[0:32, 1, :], in_=psl[0:32, 384:512], func=AF.Sigmoid), reads=[psl.b], writes=[lgt.b])
                if l == 1:
                    psv = self.psum_next()
                    for kc in range(8):
                        P.op("pe", lambda e, kc=kc: e.matmul(psv[0:32, 0:128], lhsT=v1b[:, kc, :], rhs=xm[3][:, kc, :], start=(kc == 0), stop=(kc == 7)),
                             reads=[v1b.b, xm[3].b], writes=[psv.b])
                    P.op("act", lambda e: e.copy(out=lvt[:], in_=psv[0:32, 0:128]), reads=[psv.b], writes=[lvt.b])
                for half in range(2):
                    ps = self.psum_next()
                    for kc in range(8):
                        P.op("pe", lambda e, ps=ps, kc=kc, half=half: e.matmul(ps[:, :], lhsT=xm[3][:, kc, :], rhs=Wv[:, kc, half * 512:(half + 1) * 512],
                                                                              start=(kc == 0), stop=(kc == 7)), reads=[xm[3].b, Wv.b], writes=[ps.b])
                    P.op("act", lambda e, ps=ps, half=half: e.copy(out=V[:, half * 512:(half + 1) * 512], in_=ps[:, :]), reads=[ps.b], writes=[V.b])
                if l == 1:
                    for half in range(2):
                        ps = self.psum_next()
                        hs = slice(half * 512, (half + 1) * 512)
                        P.op("pe", lambda e, ps=ps, hs=hs: e.matmul(ps[:, :], lhsT=lvt[:], rhs=v2b[0:32, 0, hs], start=True, stop=True),
                             reads=[lvt.b, v2b.b], writes=[ps.b])
                        P.op("dve", lambda e, ps=ps, hs=hs: e.tensor_tensor(out=sgv[:], in0=ps[:, :], in1=v0r[:, hs], op=ALU.add),
                             reads=[ps.b, v0r.b], writes=[sgv.b])
                        P.op("act", lambda e: e.activation(out=sgv[:], in_=sgv[:], func=AF.Sigmoid), reads=[sgv.b], writes=[sgv.b])
                        P.op("pool", lambda e, hs=hs: e.tensor_tensor(out=VFt[:, hs], in0=VFt[:, hs], in1=V[:, hs], op=ALU.subtract),
                             reads=[VFt.b, V.b], writes=[VFt.b])
                        P.op("pool", lambda e, hs=hs: e.tensor_tensor(out=VFt[:, hs], in0=VFt[:, hs], in1=sgv[:], op=ALU.mult),
                             reads=[VFt.b, sgv.b], writes=[VFt.b])
                        P.op("pool", lambda e, hs=hs: e.tensor_tensor(out=V[:, hs], in0=V[:, hs], in1=VFt[:, hs], op=ALU.add),
                             reads=[VFt.b, V.b], writes=[V.b])
                else:
                    P.dma("act", lambda e, t0=t0: e.dma_start(out=self.VF[t0:t0 + CH, :], in_=V[:]), V.b, reads=[V.b])
                if "rw_v" in self.dbg:
                    P.dma("act", lambda e, t0=t0: e.dma_start(out=self.dbg["rw_v"][t0:t0 + CH, :], in_=V[:]), V.b, reads=[V.b])

                if self.cfg.get("t_vb", True):
                    P.op("pool", lambda e: e.tensor_copy(out=Vb[:], in_=V[:]), reads=[V.b], writes=[Vb.b])

                def pair_gen(j, S):
                    CUT = self.cfg.get("rw_cut", 99)
                    EB = self.cfg.get("eng_b", "dve")

                    def smul(eng, out_ap, in_ap, sc_ap, rd, wr):
                        if eng == "act":
                            P.op("act", lambda e: e.activation(out=out_ap, in_=in_ap, func=AF.Copy, scale=sc_ap), reads=rd, writes=wr)
                        else:
                            P.op(eng, lambda e: e.tensor_scalar(out=out_ap, in0=in_ap, scalar1=sc_ap, scalar2=None, op0=ALU.mult), reads=rd, writes=wr)
                    js = slice(j * 128, (j + 1) * 128)
                    r_, k_, sg, a_, kk, tx, L_, eL = S["r"], S["k"], S["sg"], S["a"], S["kk"], S["x"], S["L"], S["eL"]
                    rh, bts, kts, WT, U_, AR, F3, T3 = S["rh"], S["bts"], S["kts"], S["WT"], S["U"], S["AR"], S["F3"], S["T3"]
                    SC = [S["SC0"], S["SC1"]]; Ym = [S["Ym0"], S["Ym1"]]; Tt = [S["Tt0"], S["Tt1"]]; AV = [S["AV0"], S["AV1"]]
                    ZY = [[S["ZY0_0"], S["ZY0_1"]], [S["ZY1_0"], S["ZY1_1"]]]
                    psA = self.psum_next()
                    for kc in range(8):
                        P.op("pe", lambda e: e.matmul(psA[:, 0:128], lhsT=Wr[:, kc, js], rhs=xm[0][:, kc, :], start=(kc == 0), stop=(kc == 7)),
                             reads=[Wr.b, xm[0].b], writes=[psA.b])
                    for kc in range(8):
                        P.op("pe", lambda e: e.matmul(psA[:, 128:256], lhsT=Wk[:, kc, js], rhs=xm[2][:, kc, :], start=(kc == 0), stop=(kc == 7)),
                             reads=[Wk.b, xm[2].b], writes=[psA.b])
                    P.op("pe", lambda e: e.matmul(psA[:, 256:384], lhsT=w2b[:, 0, js], rhs=lwt[:], start=True, stop=True), reads=[w2b.b, lwt.b], writes=[psA.b])
                    P.op("pe", lambda e: e.matmul(psA[:, 384:512], lhsT=a2b[:, 0, js], rhs=lat[:], start=True, stop=True), reads=[a2b.b, lat.b], writes=[psA.b])
                    P.op("act", lambda e: e.copy(out=r_[:], in_=psA[:, 0:128]), reads=[psA.b], writes=[r_.b])
                    P.op("act", lambda e: e.copy(out=k_[:], in_=psA[:, 128:256]), reads=[psA.b], writes=[k_.b])
                    P.op("act", lambda e: e.activation(out=sg[:], in_=psA[:, 256:384], func=AF.Sigmoid, bias=c_("w0", j), scale=1.0),
                         reads=[psA.b, self.cols.b], writes=[sg.b])
                    P.op("act", lambda e: e.activation(out=a_[:], in_=psA[:, 384:512], func=AF.Sigmoid, bias=c_("a0", j), scale=1.0),
                         reads=[psA.b, self.cols.b], writes=[a_.b])
                    yield
                    if CUT <= 1:
                        return
                    P.op("dve", lambda e: e.tensor_scalar(out=kk[:], in0=k_[:], scalar1=c_("k_k", j), scalar2=None, op0=ALU.mult),
                         reads=[k_.b, self.cols.b], writes=[kk.b])
                    P.op("pool", lambda e: e.tensor_tensor(out=tx[:], in0=kk[:], in1=kk[:], op=ALU.mult), reads=[kk.b], writes=[tx.b])
                    ps = self.psum_next()
                    P.op("pe", lambda e: e.matmul(ps[:, 0:128], lhsT=bd[:], rhs=tx[:], start=True, stop=True), reads=[bd.b, tx.b], writes=[ps.b])
                    P.op("act", lambda e: e.activation(out=tx[:], in_=ps[:, 0:128], func=AF.Sqrt), reads=[ps.b], writes=[tx.b])
                    yield
                    if CUT <= 2:
                        return
                    P.op("dve", lambda e: e.tensor_scalar(out=tx[:], in0=tx[:], scalar1=1e-12, scalar2=None, op0=ALU.max), reads=[tx.b], writes=[tx.b])
                    P.op("dve", lambda e: e.reciprocal(out=tx[:], in_=tx[:]), reads=[tx.b], writes=[tx.b])
                    P.op("pool", lambda e: e.tensor_tensor(out=kk[:], in0=kk[:], in1=tx[:], op=ALU.mult), reads=[kk.b, tx.b], writes=[kk.b])
                    P.op("dve", lambda e: e.tensor_scalar(out=tx[:], in0=a_[:], scalar1=c_("k_a", j), scalar2=omka[:, j:j + 1], op0=ALU.mult, op1=ALU.add),
                         reads=[a_.b, self.cols.b, omka.b], writes=[tx.b])
                    P.op("pool", lambda e: e.tensor_tensor(out=k_[:], in0=k_[:], in1=tx[:], op=ALU.mult), reads=[k_.b, tx.b], writes=[k_.b])
                    P.op("pool", lambda e: e.tensor_tensor(out=a_[:], in0=kk[:], in1=a_[:], op=ALU.mult), reads=[kk.b, a_.b], writes=[a_.b])
                    P.op("dve", lambda e: e.scalar_tensor_tensor(out=tx[:], in0=r_[:], scalar=c_("r_k", j), in1=k_[:], op0=ALU.mult, op1=ALU.mult),
                         reads=[r_.b, k_.b, self.cols.b], writes=[tx.b])
                    P.op("pe", lambda e: e.matmul(psB[:, 0:16], lhsT=tx[:], rhs=ind[:, j * 16:(j + 1) * 16], start=(j == 0), stop=(j == 7)),
                         reads=[tx.b, ind.b], writes=[psB.b])
                    yield
                    if CUT <= 3:
                        return
                    P.op("dve", lambda e: e.tensor_tensor_scan(out=L_[:], data0=onesT[:], data1=sg[:], initial=0.0, op0=ALU.mult, op1=ALU.add),
                         reads=[onesT.b, sg.b], writes=[L_.b])
                    P.op("pool", lambda e: e.tensor_tensor(out=sg[:], in0=L_[:], in1=sg[:], op=ALU.subtract), reads=[L_.b, sg.b], writes=[sg.b])
                    P.op("act", lambda e: e.activation(out=eL[:], in_=L_[:], func=AF.Exp, scale=-C0), reads=[L_.b], writes=[eL.b])
                    P.op("act", lambda e: e.activation(out=sg[:], in_=sg[:], func=AF.Exp, scale=-C0), reads=[sg.b], writes=[sg.b])
                    P.op("act", lambda e: e.activation(out=L_[:], in_=L_[:], func=AF.Exp, scale=C0), reads=[L_.b], writes=[L_.b])
                    yield
                    if CUT <= 4:
                        return
                    enL = L_
                    eE = sg
                    P.op("pool", lambda e: e.tensor_tensor(out=r_[:], in0=r_[:], in1=eL[:], op=ALU.mult), reads=[r_.b, eL.b], writes=[r_.b])
                    P.op("pool", lambda e: e.tensor_copy(out=rh[:], in_=r_[:]), reads=[r_.b], writes=[rh.b])
                    P.op("dve", lambda e: e.scalar_tensor_tensor(out=tx[:], in0=kk[:], scalar=-1.0, in1=eE[:], op0=ALU.mult, op1=ALU.mult),
                         reads=[kk.b, eE.b], writes=[tx.b])
                    P.op("pool", lambda e: e.tensor_copy(out=F3[:, 0, :], in_=tx[:]), reads=[tx.b], writes=[F3.b])
                    P.op("dve", lambda e: e.tensor_scalar(out=AR[:, 0, :], in0=tx[:], scalar1=enL[:, 63:64], scalar2=None, op0=ALU.mult),
                         reads=[tx.b, enL.b], writes=[AR.b])
                    P.op("dve", lambda e: e.tensor_scalar(out=AR[:, 1, :], in0=r_[:], scalar1=enL[:, 63:64], scalar2=None, op0=ALU.mult),
                         reads=[r_.b, enL.b], writes=[AR.b])
                    P.op("pool", lambda e: e.tensor_tensor(out=a_[:], in0=a_[:], in1=enL[:], op=ALU.mult), reads=[a_.b, enL.b], writes=[a_.b])
                    P.op("pool", lambda e: e.tensor_tensor(out=k_[:], in0=k_[:], in1=enL[:], op=ALU.mult), reads=[k_.b, enL.b], writes=[k_.b])
                    yield
                    if CUT <= 5:
                        return
                    smul(EB, bts[:], a_[:], eL[:, 63:64], [a_.b, eL.b], [bts.b])
                    smul(EB, kts[:], k_[:], eL[:, 63:64], [k_.b, eL.b], [kts.b])
                    smul(EB, F3[:, 1, :], a_[:], eL[:, 127:128], [a_.b, eL.b], [F3.b])
                    smul(EB, F3[:, 2, :], k_[:], eL[:, 127:128], [k_.b, eL.b], [F3.b])
                    yield
                    if CUT <= 6:
                        return
                    for q in range(3):
                        P.op("pe", lambda e: e.transpose(out=self.psbf[:, q * 128:(q + 1) * 128], in_=F3[:, q, :], identity=self.identb[:]),
                             reads=[F3.b, self.identb.b], writes=[self.psbf.b])
                    P.op("act", lambda e: e.copy(out=T3[:].rearrange("p a t -> p (a t)"), in_=self.psbf[:, 0:384]), reads=[self.psbf.b], writes=[T3.b])
                    yield
                    if CUT <= 7:
                        return
                    for par in range(2):
                        rows = slice(64 * par, 64 * par + 64)
                        pss = self.psum_next()
                        arv = AR[rows, :, :].rearrange("p a t -> p (a t)")
                        P.op("pe", lambda e: e.matmul(pss[:, 0:256], lhsT=bts[rows, :], rhs=arv, start=True, stop=True),
                             reads=[bts.b, AR.b], writes=[pss.b])
                        P.op("pe", lambda e: e.matmul(pss[:, 256:512], lhsT=kts[rows, :], rhs=arv, start=True, stop=True),
                             reads=[kts.b, AR.b], writes=[pss.b])
                        P.op("dve", lambda e: e.tensor_tensor(out=SC[par][:].rearrange("p a t -> p (a t)"), in0=pss[:, :], in1=mask4[:], op=ALU.mult),
                             reads=[pss.b, mask4.b], writes=[SC[par].b])
                        ps3 = self.psum_next()
                        P.op("pe", lambda e: e.matmul(ps3[:, 0:128], lhsT=AR[rows, 0, :], rhs=bts[rows, :], start=True, stop=True),
                             reads=[bts.b, AR.b], writes=[ps3.b])
                        P.op("dve", lambda e: e.tensor_tensor(out=Ym[par][:], in0=ps3[:, 0:128], in1=maskL[:], op=ALU.mult),
                             reads=[ps3.b, maskL.b], writes=[Ym[par].b])
                        P.op("pool", lambda e: e.tensor_tensor(out=Tt[par][:], in0=SC[par][:, 0, :], in1=self.identb[:], op=ALU.add),
                             reads=[SC[par].b, self.identb.b], writes=[Tt[par].b])
                        yield
                    cur = [(SC[0][:, 0, :], SC[0].b, Ym[0][:], Ym[0].b), (SC[1][:, 0, :], SC[1].b, Ym[1][:], Ym[1].b)]
                    for step in range(1, 7):
                        for par in range(2):
                            Zap, Zb, Yap, Yb = cur[par]
                            zy = ZY[par][step % 2]
                            psz = self.psum_next()
                            if step < 6:
                                P.op("pe", lambda e: e.matmul(psz[:, 0:128], lhsT=Yap, rhs=Zap, start=True, stop=True), reads=[Zb, Yb], writes=[psz.b])
                            P.op("pe", lambda e: e.matmul(psz[:, 128:256], lhsT=Zap, rhs=Yap, start=True, stop=True), reads=[Zb, Yb], writes=[psz.b])
                            ev = "act"
                            if step < 6:
                                self.copy(ev, zy[:].rearrange("p a t -> p (a t)"), psz[:, 0:256], [psz.b], [zy.b])
                            else:
                                self.copy(ev, zy[:, 1, :], psz[:, 128:256], [psz.b], [zy.b])
                            cur[par] = (zy[:, 0, :], zy.b, zy[:, 1, :], zy.b)
                            yield
                            tt = Tt[par]
                            psp = self.psum_next()
                            P.op("pe", lambda e: e.matmul(psp[:, 0:128], lhsT=zy[:, 1, :], rhs=tt[:], start=True, stop=True),
                                 reads=[zy.b, tt.b], writes=[psp.b])
                            P.op("dve", lambda e: e.tensor_tensor(out=tt[:], in0=psp[:, 0:128], in1=tt[:], op=ALU.add),
                                 reads=[psp.b, tt.b], writes=[tt.b])
                            yield
                    for par in range(2):
                        rows = slice(64 * par, 64 * par + 64)
                        hc = slice((2 * j + par) * 64, (2 * j + par + 1) * 64)
                        psw = self.psum_next()
                        P.op("pe", lambda e: e.matmul(psw[:, 0:128], lhsT=T3[:, 0, :], rhs=Tt[par][:], start=True, stop=True),
                             reads=[T3.b, Tt[par].b], writes=[psw.b])
                        P.op("pe", lambda e: e.matmul(psw[:, 128:192], lhsT=SC[par][:, 2, :], rhs=Vb[:, hc], start=True, stop=True),
                             reads=[SC[par].b, Vb.b], writes=[psw.b])
                        P.op("act", lambda e: e.copy(out=WT[rows, :], in_=psw[rows, 0:128]), reads=[psw.b], writes=[WT.b])
                        P.op("act", lambda e: e.copy(out=AV[par][:], in_=psw[:, 128:192]), reads=[psw.b], writes=[AV[par].b])
                        yield
                    psu = self.psum_next()
                    P.op("pe", lambda e: e.matmul(psu[:, 0:128], lhsT=WT[:], rhs=Hb[:, j, :], start=True, stop=True),
                         reads=[WT.b, Hb.b], writes=[psu.b])
                    for par in range(2):
                        cs = slice(64 * par, 64 * par + 64)
                        P.op("pe", lambda e: e.matmul(psu[:, cs], lhsT=Tt[par][:], rhs=AV[par][:], start=False, stop=(par == 1), skip_group_check=True),
                             reads=[Tt[par].b, AV[par].b], writes=[psu.b])
                    P.op("act", lambda e: e.copy(out=U_[:], in_=psu[:, 0:128]), reads=[psu.b], writes=[U_.b])
                    yield
                    if CUT <= 8:
                        return
                    psy = self.psum_next()
                    P.op("pe", lambda e: e.matmul(psy[:, 0:128], lhsT=rh[:], rhs=Hb[:, j, :], start=True, stop=True),
                         reads=[rh.b, Hb.b], writes=[psy.b])
                    for par in range(2):
                        hc = slice((2 * j + par) * 64, (2 * j + par + 1) * 64)
                        cs = slice(64 * par, 64 * par + 64)
                        P.op("pe", lambda e: e.matmul(psy[:, cs], lhsT=SC[par][:, 1, :], rhs=U_[:, cs], start=False, stop=False, skip_group_check=True),
                             reads=[SC[par].b, U_.b], writes=[psy.b])
                        P.op("pe", lambda e: e.matmul(psy[:, cs], lhsT=SC[par][:, 3, :], rhs=Vb[:, hc], start=False, stop=(par == 1), skip_group_check=True),
                             reads=[SC[par].b, Vb.b], writes=[psy.b])
                    P.op("act", lambda e: e.copy(out=Yt[:, js], in_=psy[:, 0:128]), reads=[psy.b], writes=[Yt.b])
                    psh = self.psum_next()
                    P.op("pe", lambda e: e.matmul(psh[:, 0:128], lhsT=T3[:, 2, :], rhs=Vb[:, js], start=True, stop=False),
                         reads=[T3.b, Vb.b], writes=[psh.b])
                    P.op("pe", lambda e: e.matmul(psh[:, 0:128], lhsT=T3[:, 1, :], rhs=U_[:], start=False, stop=True),
                         reads=[T3.b, U_.b], writes=[psh.b])
                    for par in range(2):
                        rows = slice(64 * par, 64 * par + 64)
                        P.op("dve", lambda e: e.scalar_tensor_tensor(out=Hbd[rows, j, rows], in0=Hbd[rows, j, rows], scalar=eL[rows, 127:128],
                                                                     in1=psh[rows, rows], op0=ALU.mult, op1=ALU.add),
                             reads=[Hbd.b, eL.b, psh.b], writes=[Hbd.b])
                        P.op("pool", lambda e: e.tensor_copy(out=Hb[rows, j, rows], in_=Hbd[rows, j, rows]), reads=[Hbd.b], writes=[Hb.b])
                    yield
                    if CUT <= 9:
                        return

                STAG = self.cfg.get("stagger", 0)
                pending = list(range(8))
                free_slots = list(range(NSLOT))
                active = []
                tick = 0
                next_start = 0
                while pending or active:
                    if pending and free_slots and tick >= next_start:
                        jn = pending.pop(0)
                        sl = free_slots.pop(0)
                        active.append((pair_gen(jn, slots[sl]), sl))
                        next_start = tick + STAG
                    for item in list(active):
                        try:
                            next(item[0])
                        except StopIteration:
                            active.remove(item)
                            free_slots.append(item[1])
                    tick += 1

                if "rw_o" in self.dbg:
                    P.dma("act", lambda e, t0=t0: e.dma_start(out=self.dbg["rw_o"][t0:t0 + CH, :], in_=Yt[:]), Yt.b, reads=[Yt.b])
                Y3 = Yt[:].rearrange("p (h n) -> p h n", n=64)
                S1t = tmpm[0][:].rearrange("p a t -> p (a t)")
                S1b = tmpm[0].b
                S2t = tmpm[1][:].rearrange("p a t -> p (a t)")
                S2b = tmpm[1].b
                P.op("act", lambda e: e.copy(out=rkb[:], in_=psB[:, 0:16]), reads=[psB.b], writes=[rkb.b])
                P.op("dve", lambda e: e.tensor_reduce(out=st[:, 0, :], in_=Y3, axis=AX.X, op=ALU.add), reads=[Yt.b], writes=[st.b])
                P.op("pool", lambda e: e.tensor_tensor(out=S1t, in0=Yt[:], in1=Yt[:], op=ALU.mult), reads=[Yt.b], writes=[S1b])
                P.op("dve", lambda e: e.tensor_reduce(out=st[:, 1, :], in_=S1t.rearrange("p (h n) -> p h n", n=64), axis=AX.X, op=ALU.add),
                     reads=[S1b], writes=[st.b])
                P.op("dve", lambda e: e.tensor_scalar(out=st[:, 2, :], in0=st[:, 0, :], scalar1=1.0 / 64, scalar2=None, op0=ALU.mult), reads=[st.b], writes=[st.b])
                P.op("dve", lambda e: e.tensor_tensor(out=st[:, 3, :], in0=st[:, 2, :], in1=st[:, 2, :], op=ALU.mult), reads=[st.b], writes=[st.b])
                P.op("dve", lambda e: e.scalar_tensor_tensor(out=st[:, 4, :], in0=st[:, 1, :], scalar=1.0 / 64, in1=st[:, 3, :], op0=ALU.mult, op1=ALU.subtract),
                     reads=[st.b], writes=[st.b])
                P.op("act", lambda e: e.activation(out=st[:, 5, :], in_=st[:, 4, :], func=AF.Sqrt, bias=eps2[:], scale=1.0), reads=[st.b, eps2.b], writes=[st.b])
                P.op("dve", lambda e: e.reciprocal(out=st[:, 5, :], in_=st[:, 5, :]), reads=[st.b], writes=[st.b])
                P.op("pool", lambda e: e.tensor_tensor(out=S1t.rearrange("p (h n) -> p h n", n=64), in0=Y3, in1=st[:, 2, :].unsqueeze(2).to_broadcast([128, 16, 64]), op=ALU.subtract),
                     reads=[Yt.b, st.b], writes=[S1b])
                P.op("dve", lambda e: e.tensor_tensor(out=S1t.rearrange("p (h n) -> p h n", n=64), in0=S1t.rearrange("p (h n) -> p h n", n=64),
                                                      in1=st[:, 5, :].unsqueeze(2).to_broadcast([128, 16, 64]), op=ALU.mult),
                     reads=[S1b, st.b], writes=[S1b])
                P.op("pool", lambda e: e.tensor_tensor(out=S1t, in0=S1t, in1=lnw[:], op=ALU.mult), reads=[S1b, lnw.b], writes=[S1b])
                P.op("dve", lambda e: e.tensor_tensor(out=S1t, in0=S1t, in1=lnb[:], op=ALU.add), reads=[S1b, lnb.b], writes=[S1b])
                P.op("pool", lambda e: e.tensor_tensor(out=S2t.rearrange("p (h n) -> p h n", n=64), in0=V[:].rearrange("p (h n) -> p h n", n=64),
                                                       in1=rkb[:].unsqueeze(2).to_broadcast([128, 16, 64]), op=ALU.mult),
                     reads=[V.b, rkb.b], writes=[S2b])
                P.op("dve", lambda e: e.tensor_tensor(out=S1t, in0=S1t, in1=S2t, op=ALU.add), reads=[S1b, S2b], writes=[S1b])
                for half in range(2):
                    ps = self.psum_next()
                    hs = slice(half * 512, (half + 1) * 512)
                    P.op("pe", lambda e: e.matmul(ps[:, :], lhsT=lgt[:, 0, :], rhs=g2b[:, 0, hs], start=True, stop=False),
                         reads=[lgt.b, g2b.b], writes=[ps.b])
                    P.op("pe", lambda e: e.matmul(ps[:, :], lhsT=lgt[0:32, 1, :], rhs=g2b[0:32, 1, hs], start=False, stop=True),
                         reads=[lgt.b, g2b.b], writes=[ps.b])
                    P.op("dve", lambda e: e.tensor_tensor(out=ogb[:, hs], in0=S1t[:, hs], in1=ps[:, :], op=ALU.mult), reads=[S1b, ps.b], writes=[ogb.b])
                for jj in range(8):
                    P.op("pe", lambda e, jj=jj: e.transpose(out=self.psbf[:, jj * 128:(jj + 1) * 128], in_=ogb[:, jj * 128:(jj + 1) * 128], identity=self.identb[:]),
                         reads=[ogb.b, self.identb.b], writes=[self.psbf.b])
                P.op("act", lambda e: e.copy(out=ogT[:].rearrange("p a t -> p (a t)"), in_=self.psbf[:, :]), reads=[self.psbf.b], writes=[ogT.b])
                for half in range(2):
                    ps = self.psum_next()
                    for jj in range(4):
                        nj = half * 4 + jj
                        for kc in range(8):
                            P.op("pe", lambda e, ps=ps, jj=jj, nj=nj, kc=kc: e.matmul(ps[:, jj * 128:(jj + 1) * 128], lhsT=Wo[:, kc, nj * 128:(nj + 1) * 128], rhs=ogT[:, kc, :],
                                                                                    start=(kc == 0), stop=(kc == 7)), reads=[Wo.b, ogT.b], writes=[ps.b])
                    for jj in range(4):
                        nj = half * 4 + jj
                        P.op("dve", lambda e, ps=ps, jj=jj, nj=nj: e.scalar_tensor_tensor(out=x[:, nj, :], in0=ps[:, jj * 128:(jj + 1) * 128], scalar=g1c[:, nj:nj + 1],
                                                                                         in1=x[:, nj, :], op0=ALU.mult, op1=ALU.add),
                             reads=[ps.b, x.b, self.modc.b], writes=[x.b])
                dst = self.X.rearrange("(j p) t -> p j t", p=128)[:, :, t0:t0 + CH]
                P.dma("sp", lambda e, dst=dst: e.dma_start(out=dst, in_=x[:]), x.b, reads=[x.b])
            self.end_phase()

    def rope_proj(self, es, W, hb, ncol0, dst_blk, CTt, STt, tmpA, tmpB):
        P = self.P
        roper = self.cst["roper"]
        tmpAs, tmpBs = tmpA, tmpB
        for hh in range(8):
            tmpA = tmpAs[hh % len(tmpAs)]
            tmpB = tmpBs[hh % len(tmpBs)]
            ps = self.psum_next()
            for kc in range(8):
                P.op("pe", lambda e: e.matmul(ps[:, :], lhsT=W[:, kc, ncol0 + hh * 128:ncol0 + (hh + 1) * 128], rhs=hb[:, kc, :],
                                              start=(kc == 0), stop=(kc == 7)), reads=[W.b, hb.b], writes=[ps.b])
            P.op("act", lambda e: e.copy(out=tmpA[:], in_=ps[:, :]), reads=[ps.b], writes=[tmpA.b])
            ps2 = self.psum_next()
            P.op("pe", lambda e: e.matmul(ps2[:, :], lhsT=roper[:], rhs=tmpA[:], start=True, stop=True), reads=[roper.b, tmpA.b], writes=[ps2.b])
            P.op("dve", lambda e: e.tensor_tensor(out=tmpB[:], in0=ps2[:, :], in1=STt[:], op=ALU.mult), reads=[ps2.b, STt.b], writes=[tmpB.b])
            P.op("pool", lambda e: e.tensor_tensor(out=tmpA[:], in0=tmpA[:], in1=CTt[:], op=ALU.mult), reads=[tmpA.b, CTt.b], writes=[tmpA.b])
            P.op("pool", lambda e: e.tensor_tensor(out=dst_blk[:, hh, :], in0=tmpA[:], in1=tmpB[:], op=ALU.add), reads=[tmpA.b, tmpB.b], writes=[dst_blk.b])

    def phase_kv(self, xsrc):
        P, nc, inp = self.P, self.nc, self.inp
        TB = 512
        with ExitStack() as es:
            self._stg = None
            Wkv = self.tile(es, "Wkv", [128, 8, 2 * D], BF16)
            s3 = inp["w_kv"].rearrange("(kc p) n -> p kc n", p=128)
            pieces = []
            for kc in range(8):
                for hf in range(2):
                    pieces.append((Wkv[:, kc:kc + 1, hf * D:(hf + 1) * D], s3[:, kc:kc + 1, hf * D:(hf + 1) * D], 128, 1, D))
            self.load_cast(es, pieces, Wkv.b)
            xs_ = [self.tile(es, "kx%d" % i, [128, 8, TB], dma=True) for i in range(2)]
            sq = self.tile(es, "ksq", [128, 8, TB])
            self.rstd = self.tile(es, "krstd", [128, TB])
            hbs_ = [self.tile(es, "khb%d" % i, [128, 8, TB], BF16) for i in range(2)]
            CTt = self.tile(es, "kCT", [128, TB], dma=True)
            STt = self.tile(es, "kST", [128, TB], dma=True)
            tmpA = [self.tile(es, "ktA%d" % i, [128, TB]) for i in range(3)]
            tmpB = [self.tile(es, "ktB%d" % i, [128, TB]) for i in range(3)]
            Kblks = [self.tile(es, "Kblk%d" % i, [128, 8, TB], BF16, dma=True) for i in range(2)]
            Vblks = [self.tile(es, "Vblk%d" % i, [128, 4, D], BF16, dma=True) for i in range(2)]
            G = self.col("kv_norm", 0, 8)
            for nb in range(T // TB):
                t0 = nb * TB
                x, hb, Kblk, Vblk = xs_[nb % 2], hbs_[nb % 2], Kblks[nb % 2], Vblks[nb % 2]
                src = xsrc.rearrange("(j p) t -> p j t", p=128)[:, :, t0:t0 + TB]
                P.dma("sp", lambda e: e.dma_start(out=x[:], in_=src), x.b, writes=[x.b])
                P.dma("act", lambda e: e.dma_start(out=CTt[:], in_=inp["ropec"][:, t0:t0 + TB]), CTt.b, writes=[CTt.b])
                P.dma("act", lambda e: e.dma_start(out=STt[:], in_=inp["ropes"][:, t0:t0 + TB]), STt.b, writes=[STt.b])
                self.rmsnorm(x, TB, sq, G, None, hb[:], hb.b)
                self.rope_proj(es, Wkv, hb, 0, Kblk, CTt, STt, tmpA, tmpB)
                dst = self.KT.rearrange("(j p) t -> p j t", p=128)[:, :, t0:t0 + TB]
                P.dma("sp", lambda e: e.dma_start(out=dst, in_=Kblk[:]), Kblk.b, reads=[Kblk.b])
                for tl in range(4):
                    for half in range(2):
                        ps = self.psum_next()
                        for kc in range(8):
                            P.op("pe", lambda e: e.matmul(ps[:, :], lhsT=hb[:, kc, tl * 128:(tl + 1) * 128], rhs=Wkv[:, kc, D + half * 512:D + (half + 1) * 512],
                                                          start=(kc == 0), stop=(kc == 7)), reads=[hb.b, Wkv.b], writes=[ps.b])
                        self.copy(("act", "dve")[half], Vblk[:, tl, half * 512:(half + 1) * 512], ps[:, :], [ps.b], [Vblk.b])
                dstv = self.VS[t0:t0 + TB, :].rearrange("(a p) e -> p a e", p=128)
                P.dma("sp", lambda e: e.dma_start(out=dstv, in_=Vblk[:]), Vblk.b, reads=[Vblk.b])
            self.end_phase()

    def phase_attn(self, l, xsrc):
        P, nc, inp = self.P, self.nc, self.inp
        jl = l - 2
        TB = 512
        lam_init = 0.8 - 0.6 * math.exp(-0.3 * l)
        G1 = self.der[:, l, 0, :]
        S1 = self.modcol(l, 0)
        g1c = self.modcol(l, 2)
        with ExitStack() as es:
            self._stg = None
            Wq = self.tile(es, "Wq", [128, 8, D], BF16)
            s3 = inp["b_w_q"][jl].rearrange("(kc p) n -> p kc n", p=128)
            self.load_cast(es, [(Wq[:, kc:kc + 1, :], s3[:, kc:kc + 1, :], 128, 1, D) for kc in range(8)], Wq.b)
            xs_ = [self.tile(es, "qx%d" % i, [128, 8, TB], dma=True) for i in range(2)]
            sq = self.tile(es, "qsq", [128, 8, TB])
            self.rstd = self.tile(es, "qrstd", [128, TB])
            hbs_ = [self.tile(es, "qhb%d" % i, [128, 8, TB], BF16) for i in range(2)]
            CTt = self.tile(es, "qCT", [128, TB], dma=True)
            STt = self.tile(es, "qST", [128, TB], dma=True)
            tmpA = [self.tile(es, "qtA%d" % i, [128, TB]) for i in range(3)]
            tmpB = [self.tile(es, "qtB%d" % i, [128, TB]) for i in range(3)]
            Qblks = [self.tile(es, "Qblk%d" % i, [128, 8, TB], BF16, dma=True) for i in range(2)]
            for nb in range(T // TB):
                t0 = nb * TB
                x, hb, Qblk = xs_[nb % 2], hbs_[nb % 2], Qblks[nb % 2]
                src = xsrc.rearrange("(j p) t -> p j t", p=128)[:, :, t0:t0 + TB]
                P.dma("sp", lambda e: e.dma_start(out=x[:], in_=src), x.b, writes=[x.b])
                P.dma("act", lambda e: e.dma_start(out=CTt[:], in_=inp["ropec"][:, t0:t0 + TB]), CTt.b, writes=[CTt.b])
                P.dma("act", lambda e: e.dma_start(out=STt[:], in_=inp["ropes"][:, t0:t0 + TB]), STt.b, writes=[STt.b])
                self.rmsnorm(x, TB, sq, G1, S1, hb[:], hb.b)
                self.rope_proj(es, Wq, hb, 0, Qblk, CTt, STt, tmpA, tmpB)
                dst = self.QT.rearrange("(j p) t -> p j t", p=128)[:, :, t0:t0 + TB]
                P.dma("sp", lambda e: e.dma_start(out=dst, in_=Qblk[:]), Qblk.b, reads=[Qblk.b])
            self.end_phase()

        with ExitStack() as es:
            NH = self.cfg.get("nheads", 8)
            NQB = self.cfg.get("nqb", T // TB)
            lamv = self.tile(es, "lamv", [128, 256], dma=True)
            o_l = 7 * D + jl * 256
            P.dma("sp", lambda e: e.dma_start(out=lamv[:], in_=inp["rows"][o_l:o_l + 256].partition_broadcast(128)), lamv.b, writes=[lamv.b])
            subw = self.tile(es, "subw", [128, 128], dma=True)
            o_s = 5 * D + jl * D
            P.dma("sp", lambda e: e.dma_start(out=subw[:], in_=inp["rows"][o_s:o_s + 128].partition_broadcast(128)), subw.b, writes=[subw.b])
            P.op("dve", lambda e: e.tensor_scalar(out=subw[:], in0=subw[:], scalar1=(1.0 - lam_init), scalar2=None, op0=ALU.mult), reads=[subw.b], writes=[subw.b])
            lt = self.tile(es, "lt", [128, 2, 64])
            ls = self.tile(es, "ls", [128, 4])
            P.op("dve", lambda e: e.tensor_tensor(out=lt[:, 0, :], in0=lamv[:, 0:64], in1=lamv[:, 64:128], op=ALU.mult), reads=[lamv.b], writes=[lt.b])
            P.op("dve", lambda e: e.tensor_tensor(out=lt[:, 1, :], in0=lamv[:, 128:192], in1=lamv[:, 192:256], op=ALU.mult), reads=[lamv.b], writes=[lt.b])
            P.op("dve", lambda e: e.tensor_reduce(out=ls[:, 0:2], in_=lt[:], axis=AX.X, op=ALU.add), reads=[lt.b], writes=[ls.b])
            P.op("act", lambda e: e.activation(out=ls[:, 0:2], in_=ls[:, 0:2], func=AF.Exp), reads=[ls.b], writes=[ls.b])
            P.op("dve", lambda e: e.tensor_tensor(out=ls[:, 2:3], in0=ls[:, 1:2], in1=ls[:, 0:1], op=ALU.subtract), reads=[ls.b], writes=[ls.b])
            P.op("dve", lambda e: e.tensor_scalar(out=ls[:, 3:4], in0=ls[:, 2:3], scalar1=-lam_init, scalar2=None, op0=ALU.add), reads=[ls.b], writes=[ls.b])
            neglam = ls[:, 3:4]
            cmaskb = self.tile(es, "cmaskb", [128, 128], BF16)
            self.copy("pool", cmaskb[:], self.cst["mask4"][:, 128:256], [self.cst["mask4"].b], [cmaskb.b])
            eps1 = self.eps

            KTh = [self.tile(es, "KTh%d" % i, [128, T], BF16, dma=True) for i in range(2)]
            QTh = [self.tile(es, "QTh%d" % i, [128, T], BF16, dma=True) for i in range(2)]
            Vh = [self.tile(es, "Vh%d" % i, [128, 32, 129], BF16, dma=True) for i in range(2)]
            YTh = [self.tile(es, "YTh%d" % i, [128, T], BF16, dma=True) for i in range(2)]
            for i in range(2):
                P.op("pool", lambda e: e.memset(Vh[i][:, :, 128:129], 1.0), writes=[Vh[i].b])
            ET = [self.tile(es, "ET%d" % i, [128, 512], BF16) for i in range(4)]
            Oc = [self.tile(es, "Oc%d" % i, [128, 4, 129]) for i in range(2)]
            rz = self.tile(es, "rz", [128, 2, 4])
            y = self.tile(es, "ay", [128, 4, 128])
            ysq = self.tile(es, "aysq", [128, 4, 128])
            ss = self.tile(es, "ass", [128, 4])
            ynb = self.tile(es, "aynb", [128, 4, 128], BF16)
            if "at_o" in self.dbg:
                self.dbg_tile = self.tile(es, "dbgt", [128, 4, 128], dma=True)
            eti = 0
            for hh in range(NH):
                kt_, qt_, vh_, yt_ = KTh[hh % 2], QTh[hh % 2], Vh[hh % 2], YTh[hh % 2]
                hs = slice(hh * 128, (hh + 1) * 128)
                P.dma("sp", lambda e: e.dma_start(out=kt_[:], in_=self.KT[hs, :]), kt_.b, writes=[kt_.b])
                P.dma("act", lambda e: e.dma_start(out=qt_[:], in_=self.QT[hs, :]), qt_.b, writes=[qt_.b])
                P.dma("sp", lambda e: e.dma_start(out=vh_[:, :, 0:128], in_=self.VS.rearrange("(kt p) e -> p kt e", p=128)[:, :, hs]), vh_.b, writes=[vh_.b])
                sbanks = [self.ps[0], self.ps[1], self.ps[6]]
                tasks = []
                for qb in range(NQB):
                    for cc in range(2):
                        for kt in range(4 * qb + 4):
                            tasks.append((qb, cc, kt))

                def score(ti):
                    qb, cc, kt = tasks[ti]
                    rows = slice(64 * cc, 64 * cc + 64)
                    c0 = max(kt - 4 * qb, 0) * 128
                    pS = sbanks[ti % 3]
                    P.op("pe", lambda e: e.matmul(pS[:, c0:512], lhsT=kt_[rows, kt * 128:(kt + 1) * 128], rhs=qt_[rows, qb * 512 + c0:(qb + 1) * 512],
                                                  start=True, stop=True), reads=[kt_.b, qt_.b], writes=[pS.b])

                def combine(qb):
                    qs = slice(qb * 512, (qb + 1) * 512)
                    P.op("dve", lambda e: e.reciprocal(out=rz[:, 0, :], in_=Oc[0][:, :, 128]), reads=[Oc[0].b], writes=[rz.b])
                    P.op("dve", lambda e: e.reciprocal(out=rz[:, 1, :], in_=Oc[1][:, :, 128]), reads=[Oc[1].b], writes=[rz.b])
                    P.op("dve", lambda e: e.tensor_scalar(out=rz[:, 1, :], in0=rz[:, 1, :], scalar1=neglam, scalar2=None, op0=ALU.mult), reads=[rz.b, ls.b], writes=[rz.b])
                    P.op("pool", lambda e: e.tensor_tensor(out=y[:], in0=Oc[0][:, :, 0:128], in1=rz[:, 0, :].unsqueeze(2).to_broadcast([128, 4, 128]), op=ALU.mult),
                         reads=[Oc[0].b, rz.b], writes=[y.b])
                    P.op("pool", lambda e: e.tensor_tensor(out=ysq[:], in0=Oc[1][:, :, 0:128], in1=rz[:, 1, :].unsqueeze(2).to_broadcast([128, 4, 128]), op=ALU.mult),
                         reads=[Oc[1].b, rz.b], writes=[ysq.b])
                    P.op("dve", lambda e: e.tensor_tensor(out=y[:], in0=y[:], in1=ysq[:], op=ALU.add), reads=[y.b, ysq.b], writes=[y.b])
                    if "at_o" in self.dbg:
                        dtl = self.dbg_tile
                        self.copy("dve", dtl[:], y[:], [y.b], [dtl.b])
                        dd = self.dbg["at_o"][qb * 512:(qb + 1) * 512, hs].rearrange("(a p) e -> p a e", p=128)
                        P.dma("act", lambda e: e.dma_start(out=dd, in_=dtl[:]), dtl.b, reads=[dtl.b])
                    P.op("pool", lambda e: e.tensor_tensor(out=ysq[:], in0=y[:], in1=y[:], op=ALU.mult), reads=[y.b], writes=[ysq.b])
                    P.op("dve", lambda e: e.tensor_reduce(out=ss[:], in_=ysq[:], axis=AX.X, op=ALU.add), reads=[ysq.b], writes=[ss.b])
                    P.op("act", lambda e: e.activation(out=ss[:], in_=ss[:], func=AF.Sqrt, bias=eps1[:], scale=1.0 / 128), reads=[ss.b, eps1.b], writes=[ss.b])
                    P.op("dve", lambda e: e.reciprocal(out=ss[:], in_=ss[:]), reads=[ss.b], writes=[ss.b])
                    P.op("pool", lambda e: e.tensor_tensor(out=y[:], in0=y[:], in1=ss[:].unsqueeze(2).to_broadcast([128, 4, 128]), op=ALU.mult),
                         reads=[y.b, ss.b], writes=[y.b])
                    P.op("dve", lambda e: e.tensor_tensor(out=ynb[:], in0=y[:], in1=subw[:].unsqueeze(1).to_broadcast([128, 4, 128]), op=ALU.mult),
                         reads=[y.b, subw.b], writes=[ynb.b])
                    for qt in range(4):
                        P.op("pe", lambda e: e.transpose(out=self.psbf[:, qt * 128:(qt + 1) * 128], in_=ynb[:, qt, :], identity=self.identb[:]),
                             reads=[ynb.b, self.identb.b], writes=[self.psbf.b])
                    P.op("act", lambda e: e.copy(out=yt_[:, qs], in_=self.psbf[:, 0:512]), reads=[self.psbf.b], writes=[yt_.b])

                LOOK = 2
                for ti in range(min(LOOK, len(tasks))):
                    score(ti)
                for ti in range(len(tasks)):
                    if ti + LOOK < len(tasks):
                        score(ti + LOOK)
                    qb, cc, kt = tasks[ti]
                    r = kt - 4 * qb
                    c0 = max(r, 0) * 128
                    pS = sbanks[ti % 3]
                    pO = [self.ps[2 + 2 * cc], self.ps[3 + 2 * cc]]
                    et = ET[ti % 4]
                    P.op("act", lambda e: e.activation(out=et[:, c0:512], in_=pS[:, c0:512], func=AF.Exp, scale=0.125), reads=[pS.b], writes=[et.b])
                    if r >= 0:
                        P.op("pool", lambda e: e.tensor_tensor(out=et[:, c0:c0 + 128], in0=et[:, c0:c0 + 128], in1=cmaskb[:], op=ALU.mult),
                             reads=[et.b, cmaskb.b], writes=[et.b])
                    for qt in range(max(r, 0), 4):
                        po = pO[qt // 2]
                        oc = (qt % 2) * 129
                        P.op("pe", lambda e: e.matmul(po[:, oc:oc + 129], lhsT=et[:, qt * 128:(qt + 1) * 128], rhs=vh_[:, kt, :],
                                                      start=(kt == 0 and qt % 2 == 0), stop=(kt == 4 * qb + qt), skip_group_check=True),
                             reads=[et.b, vh_.b], writes=[po.b])
                    if kt == 4 * qb + 3:
                        for i2 in range(2):
                            self.copy(("act", "dve")[i2], Oc[cc][:, 2 * i2:2 * i2 + 2, :].rearrange("p a e -> p (a e)"), pO[i2][:, 0:258], [pO[i2].b], [Oc[cc].b])
                        if cc == 1:
                            combine(qb)
                P.dma("sp", lambda e: e.dma_start(out=self.YT[hs, :], in_=yt_[:]), yt_.b, reads=[yt_.b])
            self.end_phase()

        with ExitStack() as es:
            self._stg = None
            Wo = self.tile(es, "aWo", [128, 8, D], BF16)
            s3 = inp["b_w_o"][jl].rearrange("(kc p) n -> p kc n", p=128)
            self.load_cast(es, [(Wo[:, kc:kc + 1, :], s3[:, kc:kc + 1, :], 128, 1, D) for kc in range(8)], Wo.b)
            xs = [self.tile(es, "cx%d" % i, [128, 8, TB], dma=True) for i in range(2)]
            ys = [self.tile(es, "cy%d" % i, [128, 8, TB], BF16, dma=True) for i in range(2)]
            for nb in range(T // TB):
                t0 = nb * TB
                x = xs[nb % 2]
                yb = ys[nb % 2]
                src = xsrc.rearrange("(j p) t -> p j t", p=128)[:, :, t0:t0 + TB]
                P.dma("sp", lambda e: e.dma_start(out=x[:], in_=src), x.b, writes=[x.b])
                srcy = self.YT.rearrange("(j p) t -> p j t", p=128)[:, :, t0:t0 + TB]
                P.dma("act", lambda e: e.dma_start(out=yb[:], in_=srcy), yb.b, writes=[yb.b])
                for nj in range(8):
                    ps = self.psum_next()
                    for kc in range(8):
                        P.op("pe", lambda e: e.matmul(ps[:, :], lhsT=Wo[:, kc, nj * 128:(nj + 1) * 128], rhs=yb[:, kc, :], start=(kc == 0), stop=(kc == 7)),
                             reads=[Wo.b, yb.b], writes=[ps.b])
                    P.op("dve", lambda e: e.scalar_tensor_tensor(out=x[:, nj, :], in0=ps[:, :], scalar=g1c[:, nj:nj + 1], in1=x[:, nj, :], op0=ALU.mult, op1=ALU.add),
                         reads=[ps.b, x.b, self.modc.b], writes=[x.b])
                dst = self.X.rearrange("(j p) t -> p j t", p=128)[:, :, t0:t0 + TB]
                P.dma("sp", lambda e: e.dma_start(out=dst, in_=x[:]), x.b, reads=[x.b])
            self.end_phase()


_CONSTS = None


def prepare_inputs(inputs):
    global _CONSTS
    if _CONSTS is None:
        _CONSTS = make_consts()
    f = lambda a: np.ascontiguousarray(np.asarray(a, np.float32))
    vecs = {}
    for l in range(4):
        vecs["ada_b%d" % l] = inputs["ada_b"][l]
        vecs["norm1_%d" % l] = inputs["norm1"][l]
        vecs["norm2_%d" % l] = inputs["norm2"][l]
        for i in range(3):
            vecs["cw%d_%d" % (i, l)] = inputs["ffn_conv_w"][l][i]
        vecs["cb_%d" % l] = inputs["ffn_conv_b"][l]
    vecs["final_norm"] = inputs["final_norm"]
    vecs["kv_norm"] = inputs["kv_norm"]
    for l in range(2):
        for i in range(6):
            vecs["mu%d_%d" % (i, l)] = inputs["a_mu"][l][i]
        vecs["w0_%d" % l] = inputs["a_w0"][l]
        vecs["a0_%d" % l] = inputs["a_a0"][l]
        vecs["k_k_%d" % l] = inputs["a_k_k"][l]
        vecs["k_a_%d" % l] = inputs["a_k_a"][l]
        vecs["r_k_%d" % l] = np.asarray(inputs["a_r_k"][l]).reshape(-1)
    cols = CP.pack(vecs)
    rows = np.concatenate([
        f(inputs["a_ln_w"][0]), f(inputs["a_ln_b"][0]), f(inputs["a_ln_w"][1]), f(inputs["a_ln_b"][1]),
        f(inputs["a_v0"][0]),
        np.tile(f(inputs["b_subln"][0]), 8), np.tile(f(inputs["b_subln"][1]), 8),
        f(inputs["b_lam"][0]).reshape(-1), f(inputs["b_lam"][1]).reshape(-1)])
    shared = dict(_CONSTS)
    shared["cols"] = cols
    shared["rows"] = rows
    for k in ("ada_w", "a_w_rkv", "a_w1", "a_w2", "a_a1", "a_a2", "a_v1", "a_v2", "a_g1", "a_g2", "a_w_o",
              "w_kv", "b_w_q", "b_w_o", "ffn_w_up", "ffn_w_down"):
        shared[k] = f(inputs[k])
    x = np.asarray(inputs["x"], np.float32)
    c = np.asarray(inputs["c"], np.float32)
    in_maps = []
    for b in range(NCORES):
        m = dict(shared)
        m["xT"] = np.ascontiguousarray(x[b].T)
        m["ccol"] = np.ascontiguousarray(c[b].reshape(8, 128).T)
        in_maps.append(m)
    return in_maps


_NC_CACHE = {}


def run(inputs, cfg, key="full", ncores=NCORES):
    if key not in _NC_CACHE:
        _NC_CACHE[key] = Builder(cfg).build()
    nc = _NC_CACHE[key]
    in_maps = prepare_inputs(inputs)[:ncores]
    res = run_bass_kernel_spmd(nc, in_maps, core_ids=list(range(ncores)))
    return res


def kernel(**inputs):
    cfg = {"layers": [0, 1, 2, 3]}
    res = run(inputs, cfg)
    out = np.stack([np.ascontiguousarray(r["outT"].T) for r in res.results], axis=0)
    return out.astype(np.float32)
```

```python
import math
import numpy as np
import concourse.bass as bass
import concourse.mybir as mybir
from concourse.bass_utils import run_bass_kernel_spmd
from contextlib import ExitStack
import types

F32 = mybir.dt.float32
BF16 = mybir.dt.bfloat16
AF = mybir.ActivationFunctionType
ALU = mybir.AluOpType
AX = mybir.AxisListType

D = 1024
T = 4096
NJ = 8
DFF = 2816
F2 = 5632
NF = 44
NG = 22
C0 = math.exp(-0.5)
NCORES = 8

ENGS = ["pe", "act", "dve", "pool", "sp"]


def freeze(fn):
    if fn.__closure__ is None:
        return fn
    cells = []
    for c in fn.__closure__:
        try:
            cells.append(types.CellType(c.cell_contents))
        except ValueError:
            cells.append(c)
    return types.FunctionType(fn.__code__, fn.__globals__, fn.__name__, fn.__defaults__, tuple(cells))


class Buf:
    __slots__ = ("name", "lw", "rd", "dsem", "excl")

    def __init__(self, name):
        self.name = name
        self.lw = None
        self.rd = {}
        self.dsem = None
        self.excl = False


class Tl:
    __slots__ = ("ap", "b")

    def __init__(self, ap, b):
        self.ap = ap
        self.b = b

    def __getitem__(self, k):
        return self.ap[k]


class Prog:
    def __init__(self, nc, es, n_dma_sems=40):
        self.nc = nc
        self.es = es
        self.q = {e: [] for e in ENGS}
        self.cnt = {e: 0 for e in ENGS}
        self.sems = {}
        self.semkey = 0
        self.esem = {e: self._newsem("c_" + e) for e in ENGS}
        self.seen = {e: {} for e in ENGS}
        self.bar = self._newsem("bar")
        self.nbar = 0
        self.dma_pool = [self._newsem("d%d" % i) for i in range(n_dma_sems)]
        self.dma_cnt = {k: 0 for k in self.dma_pool}
        self.dma_free = list(self.dma_pool)
        self.ninstr = 0

    def _newsem(self, name):
        s = self.es.enter_context(self.nc.semaphore(name))
        self.semkey += 1
        self.sems[self.semkey] = s
        return self.semkey

    def buf(self, name):
        return Buf(name)

    def dma_buf(self, name):
        b = Buf(name)
        b.dsem = self.dma_free.pop(0)
        return b

    def release(self, bufs):
        for b in bufs:
            if b.dsem is not None:
                self.dma_free.append(b.dsem)
                b.dsem = None

    def _waits(self, e, reads, writes, is_dma=False):
        need = {}
        seen = self.seen[e]

        def add(ev, raw):
            key, val, src = ev
            if src == e and not is_dma and (e in ("pe", "sp") or not raw):
                return
            if seen.get(key, 0) >= val:
                return
            if need.get(key, 0) < val:
                need[key] = val

        for b in reads:
            if b.lw is not None:
                add(b.lw, True)
            if b.excl:
                for src, ev in b.rd.items():
                    if src != e:
                        add(ev, False)
        for b in writes:
            if b.lw is not None:
                add(b.lw, False)
            for ev in b.rd.values():
                add(ev, False)
        out = []
        for key, val in need.items():
            seen[key] = val
            out.append((self.sems[key], val))
        return out

    def _emit(self, e, fn, waits, sem, inc):
        fn = freeze(fn)

        attach = (inc == 1 and e in ("act", "dve", "pool") and len(waits) > 0)

        def run(eng, fn=fn, waits=waits, sem=sem, inc=inc, attach=attach):
            for (s, v) in (waits[:-1] if attach else waits):
                eng.wait_ge(s, v)
            ins = fn(eng)
            if attach:
                ins._wait_ge(waits[-1][0], waits[-1][1])
            ins.then_inc(sem, inc)
        self.q[e].append(run)
        self.ninstr += 1 + len(waits)

    def op(self, e, fn, reads=(), writes=()):
        waits = self._waits(e, reads, writes)
        self.cnt[e] += 1
        key = self.esem[e]
        self._emit(e, fn, waits, self.sems[key], 1)
        ev = (key, self.cnt[e], e)
        for b in writes:
            b.lw = ev
            b.rd = {}
        for b in reads:
            if b not in writes:
                b.rd[e] = ev

    def dma(self, e, fn, owner, reads=(), writes=()):
        assert owner.dsem is not None, owner.name
        waits = self._waits(e, reads, writes, is_dma=True)
        key = owner.dsem
        self.dma_cnt[key] += 16
        self._emit(e, fn, waits, self.sems[key], 16)
        src = "dma%d" % key
        ev = (key, self.dma_cnt[key], src)
        for b in writes:
            b.lw = ev
            b.rd = {}
        for b in reads:
            if b not in writes:
                b.rd[src] = ev

    def barrier(self):
        g = "sp"
        gw = []
        for e in ENGS:
            if e == g or self.cnt[e] == 0:
                continue
            key = self.esem[e]
            if self.seen[g].get(key, 0) < self.cnt[e]:
                gw.append((self.sems[key], self.cnt[e]))
        for key in self.dma_pool:
            val = self.dma_cnt[key]
            if val > 0 and self.seen[g].get(key, 0) < val:
                gw.append((self.sems[key], val))
        self.nbar += 1
        bsem = self.sems[self.bar]

        def run_g(eng, gw=gw, bsem=bsem):
            for (s, v) in gw:
                eng.wait_ge(s, v)
            eng.sem_inc(bsem, 1)
        self.q[g].append(run_g)
        for e in ENGS:
            if e != g:
                self.q[e].append(lambda eng, sem=bsem, val=self.nbar: eng.wait_ge(sem, val))
        for e in ENGS:
            for e2 in ENGS:
                self.seen[e][self.esem[e2]] = self.cnt[e2]
            for key in self.dma_pool:
                self.seen[e][key] = self.dma_cnt[key]
        for e in ENGS:
            if self.cnt[e] > 12000:
                self.esem[e] = self._newsem("c_%s_%d" % (e, self.nbar))
                self.cnt[e] = 0

    def finish(self):
        nc = self.nc
        self.barrier()
        with nc.Block() as block:
            @block.tensor
            def _(eng):
                for f in self.q["pe"]:
                    f(eng)

            @block.scalar
            def _(eng):
                for f in self.q["act"]:
                    f(eng)

            @block.vector
            def _(eng):
                for f in self.q["dve"]:
                    f(eng)

            @block.gpsimd
            def _(eng):
                for f in self.q["pool"]:
                    f(eng)

            @block.sync
            def _(eng):
                for f in self.q["sp"]:
                    f(eng)


class ColPack:
    def __init__(self):
        self.off = {}
        self.n = 0
        self.items = []

    def add(self, name, length):
        assert length % 128 == 0
        self.off[name] = self.n
        self.n += length // 128
        self.items.append((name, length))

    def pack(self, vecs):
        arr = np.zeros((128, self.n), np.float32)
        for name, length in self.items:
            v = np.asarray(vecs[name], np.float32).reshape(length // 128, 128)
            arr[:, self.off[name]:self.off[name] + length // 128] = v.T
        return arr


def make_colpack():
    cp = ColPack()
    for l in range(4):
        cp.add("ada_b%d" % l, 6 * D)
        cp.add("norm1_%d" % l, D)
        cp.add("norm2_%d" % l, D)
        for i in range(3):
            cp.add("cw%d_%d" % (i, l), F2)
        cp.add("cb_%d" % l, F2)
    cp.add("final_norm", D)
    cp.add("kv_norm", D)
    for l in range(2):
        for i in range(6):
            cp.add("mu%d_%d" % (i, l), D)
        for nm in ("w0", "a0", "k_k", "k_a", "r_k"):
            cp.add("%s_%d" % (nm, l), D)
    return cp


CP = make_colpack()


def make_consts():
    c = {}
    c["ident"] = np.eye(128, dtype=np.float32)
    c["ones"] = np.ones((128, 128), np.float32)
    bd = np.zeros((128, 128), np.float32)
    bd[:64, :64] = 1
    bd[64:, 64:] = 1
    c["bd"] = bd
    ind = np.zeros((128, 8, 16), np.float32)
    for p in range(128):
        for j in range(8):
            ind[p, j, 2 * j + p // 64] = 1
    c["ind"] = ind.reshape(128, 128)
    s = np.arange(128)[:, None]
    t = np.arange(128)[None, :]
    strict = (t > s).astype(np.float32)
    incl = (t >= s).astype(np.float32)
    c["mask4"] = np.concatenate([strict, incl, strict, incl], axis=1)
    c["maskL"] = (t < s).astype(np.float32)
    pos = np.arange(T, dtype=np.float64)
    inv = 500000.0 ** (-np.arange(0, 16, 2, dtype=np.float64) / 16)
    ct = np.ones((128, T), np.float64)
    st = np.zeros((128, T), np.float64)
    rm = np.zeros((128, 128), np.float32)
    for cc in range(2):
        for d in range(16):
            p = cc * 64 + d
            ang = pos * inv[d % 8]
            ct[p] = np.cos(ang)
            st[p] = np.sin(ang)
            if d < 8:
                rm[p + 8, p] = -1.0
            else:
                rm[p - 8, p] = 1.0
    c["ropec"] = ct.astype(np.float32)
    c["ropes"] = st.astype(np.float32)
    c["roper"] = rm
    return c


class Builder:
    def __init__(self, cfg):
        self.cfg = cfg
        self.nc = bass.Bass("TRN2", target_bir_lowering=False)
        self.uid = 0
        self.rr = 0

    def dram_in(self, name, shape, dt=F32):
        return self.nc.dram_tensor(name, list(shape), dt, kind="ExternalInput").ap()

    def tile(self, es, name, shape, dt=F32, dma=False):
        self.uid += 1
        t = es.enter_context(self.nc.sbuf_tensor("%s_%d" % (name, self.uid), list(shape), dt))
        b = self.P.dma_buf(name) if dma else self.P.buf(name)
        if dma:
            self.phase_dma_bufs.append(b)
        return Tl(t, b)

    def eng_rr(self, engs=("dve", "pool", "act")):
        self.rr += 1
        return engs[self.rr % len(engs)]

    def copy(self, e, out_ap, in_ap, reads, writes):
        P = self.P
        if e == "act":
            P.op("act", lambda g: g.copy(out=out_ap, in_=in_ap), reads=reads, writes=writes)
        else:
            P.op(e, lambda g: g.tensor_copy(out=out_ap, in_=in_ap), reads=reads, writes=writes)

    def psum_next(self):
        self.psi = (self.psi + 1) % 6
        return self.ps[self.psi]

    def build(self):
        nc = self.nc
        cfg = self.cfg
        inp = {}
        inp["xT"] = self.dram_in("xT", [D, T])
        inp["ccol"] = self.dram_in("ccol", [128, 8])
        inp["cols"] = self.dram_in("cols", [128, CP.n])
        for k in ("ident", "ones", "bd", "ind", "maskL", "roper"):
            inp[k] = self.dram_in(k, [128, 128])
        inp["mask4"] = self.dram_in("mask4", [128, 512])
        inp["ropec"] = self.dram_in("ropec", [128, T])
        inp["ropes"] = self.dram_in("ropes", [128, T])
        inp["ada_w"] = self.dram_in("ada_w", [4, D, 6 * D])
        inp["a_w_rkv"] = self.dram_in("a_w_rkv", [2, 3, D, D])
        inp["a_w1"] = self.dram_in("a_w1", [2, D, 64])
        inp["a_w2"] = self.dram_in("a_w2", [2, 64, D])
        inp["a_a1"] = self.dram_in("a_a1", [2, D, 64])
        inp["a_a2"] = self.dram_in("a_a2", [2, 64, D])
        inp["a_v1"] = self.dram_in("a_v1", [1, D, 32])
        inp["a_v2"] = self.dram_in("a_v2", [1, 32, D])
        inp["a_g1"] = self.dram_in("a_g1", [2, D, 160])
        inp["a_g2"] = self.dram_in("a_g2", [2, 160, D])
        inp["a_w_o"] = self.dram_in("a_w_o", [2, D, D])
        inp["rows"] = self.dram_in("rows", [7 * D + 512])
        inp["w_kv"] = self.dram_in("w_kv", [D, 2 * D])
        inp["b_w_q"] = self.dram_in("b_w_q", [2, D, D])
        inp["b_w_o"] = self.dram_in("b_w_o", [2, D, D])
        inp["ffn_w_up"] = self.dram_in("ffn_w_up", [4, D, F2])
        inp["ffn_w_down"] = self.dram_in("ffn_w_down", [4, DFF, D])
        self.inp = inp
        self.outT = nc.dram_tensor("outT", [D, T], F32, kind="ExternalOutput").ap()
        self.X = nc.dram_tensor("Xs", [D, T], F32).ap()
        self.VF = nc.dram_tensor("VFs", [T, D], F32).ap()
        self.KT = nc.dram_tensor("KTs", [D, T], BF16).ap()
        self.VS = nc.dram_tensor("VSs", [T, D], BF16).ap()
        self.QT = nc.dram_tensor("QTs", [D, T], BF16).ap()
        self.YT = nc.dram_tensor("YTs", [D, T], BF16).ap()
        self.dbg = {}
        for name, shape in cfg.get("dbg", {}).items():
            self.dbg[name] = nc.dram_tensor("dbg_" + name, list(shape), F32, kind="ExternalOutput").ap()

        with ExitStack() as es:
            self.P = P = Prog(nc, es)
            self.phase_dma_bufs = []
            self.ps = []
            for i in range(7):
                t = es.enter_context(nc.psum_tensor("psb%d" % i, [128, 512], F32))
                self.ps.append(Tl(t, P.buf("psb%d" % i)))
                self.ps[-1].b.excl = True
            t = es.enter_context(nc.psum_tensor("psbf", [128, 1024], BF16))
            self.psbf = Tl(t, P.buf("psbf"))
            self.psbf.b.excl = True
            self.psi = 0
            g = es
            self.cols = self.tile(g, "cols", [128, CP.n], dma=True)
            self.ccol = self.tile(g, "ccol", [128, 8], dma=True)
            self.cst = {}
            for k in ("ident", "ones", "bd", "ind", "maskL", "roper"):
                self.cst[k] = self.tile(g, k, [128, 128], dma=True)
            self.cst["mask4"] = self.tile(g, "mask4", [128, 512], dma=True)
            P.dma("sp", lambda e: e.dma_start(out=self.cols[:], in_=inp["cols"]), self.cols.b, writes=[self.cols.b])
            P.dma("sp", lambda e: e.dma_start(out=self.ccol[:], in_=inp["ccol"]), self.ccol.b, writes=[self.ccol.b])
            for k, tl in self.cst.items():
                P.dma("act", lambda e, tl=tl, k=k: e.dma_start(out=tl[:], in_=inp[k]), tl.b, writes=[tl.b])
            self.eps = self.tile(g, "eps", [128, 1])
            P.op("pool", lambda e: e.memset(self.eps[:], 1e-6), writes=[self.eps.b])
            self.identb = self.tile(g, "identb", [128, 128], BF16)
            self.copy("pool", self.identb[:], self.cst["ident"][:], [self.cst["ident"].b], [self.identb.b])
            self.modc = self.tile(g, "modc", [128, 192])
            self.der = self.tile(g, "der", [128, 4, 2, 8])

            self.phase_mod()
            xsrc = inp["xT"]
            for l in cfg["layers"]:
                if cfg.get("mixer", True):
                    if l < 2:
                        self.phase_rwkv(l, xsrc)
                    else:
                        if l == 2 or cfg.get("force_kv", False):
                            self.phase_kv(xsrc)
                        self.phase_attn(l, xsrc)
                    xsrc = self.X
                fuse_final = (l == cfg["layers"][-1]) and cfg.get("ffn", True) and cfg.get("final", True) and cfg.get("fuse_final", True)
                if cfg.get("ffn", True):
                    self.phase_ffn(l, xsrc, fuse_final)
                    xsrc = self.X
            if not fuse_final:
                self.phase_final(xsrc, cfg.get("final", True))
            P.finish()
        return nc

    def col(self, name, j0=0, n=None):
        o = CP.off[name] + j0
        if n is None:
            n = 1
        return self.cols[:, o:o + n]

    def end_phase(self):
        self.P.barrier()
        self.P.release(self.phase_dma_bufs)
        self.phase_dma_bufs = []

    def phase_mod(self):
        P, nc, inp = self.P, self.nc, self.inp
        with ExitStack() as es:
            cact = self.tile(es, "cact", [128, 8])
            P.op("act", lambda e: e.activation(out=cact[:], in_=self.ccol[:], func=AF.Silu),
                 reads=[self.ccol.b], writes=[cact.b])
            A = [self.tile(es, "adaA%d" % i, [128, 8, 768], dma=True) for i in range(4)]
            psm = self.ps[0]
            it = 0
            for l in range(4):
                for blk in range(8):
                    a = A[it % 4]
                    it += 1
                    src = inp["ada_w"][l].rearrange("(kc p) n -> p kc n", p=128)[:, :, blk * 768:(blk + 1) * 768]
                    P.dma("sp" if it % 2 else "act", lambda e, a=a, src=src: e.dma_start(out=a[:], in_=src), a.b, writes=[a.b])
                    for n_ in range(6):
                        colidx = l * 48 + blk * 6 + n_
                        for kc in range(8):
                            P.op("pe", lambda e, a=a, kc=kc, n_=n_, colidx=colidx: e.matmul(
                                psm[:, colidx:colidx + 1], lhsT=a[:, kc, n_ * 128:(n_ + 1) * 128], rhs=cact[:, kc:kc + 1],
                                start=(kc == 0), stop=(kc == 7)), reads=[a.b, cact.b], writes=[psm.b])
            for l in range(4):
                o = CP.off["ada_b%d" % l]
                P.op("dve", lambda e, l=l, o=o: e.tensor_tensor(out=self.modc[:, l * 48:(l + 1) * 48], in0=psm[:, l * 48:(l + 1) * 48],
                                                                in1=self.cols[:, o:o + 48], op=ALU.add),
                     reads=[psm.b, self.cols.b], writes=[self.modc.b])
            for l in range(4):
                for which in range(2):
                    sc = self.modc[:, l * 48 + which * 24 + 8: l * 48 + which * 24 + 16]
                    nm = self.col("norm%d_%d" % (which + 1, l), 0, 8)
                    P.op("dve", lambda e, l=l, which=which, sc=sc, nm=nm: e.scalar_tensor_tensor(
                        out=self.der[:, l, which, :], in0=sc, scalar=1.0, in1=nm, op0=ALU.add, op1=ALU.mult),
                        reads=[self.modc.b, self.cols.b], writes=[self.der.b])
            if "modc" in self.dbg:
                dt = self.tile(es, "dbgm", [128, 192], dma=True)
                self.copy("dve", dt[:], self.modc[:], [self.modc.b], [dt.b])
                P.dma("sp", lambda e: e.dma_start(out=self.dbg["modc"], in_=dt[:]), dt.b, reads=[dt.b])
            self.end_phase()

    def modcol(self, l, i, j0=0, n=8):
        o = l * 48 + i * 8 + j0
        return self.modc[:, o:o + n]

    def rmsnorm(self, x, N, sq, G, S, out_ap, out_b, extra_reads=()):
        P = self.P
        ones = self.cst["ones"]
        P.op("act", lambda e: e.activation(out=sq[:], in_=x[:], func=AF.Square), reads=[x.b], writes=[sq.b])
        ps = self.psum_next()
        for j in range(8):
            P.op("pe", lambda e, j=j: e.matmul(ps[:, 0:N], lhsT=ones[:], rhs=sq[:, j, :], start=(j == 0), stop=(j == 7)),
                 reads=[ones.b, sq.b], writes=[ps.b])
        rs = self.rstd
        P.op("act", lambda e: e.activation(out=rs[:, 0:N], in_=ps[:, 0:N], func=AF.Sqrt, bias=self.eps[:], scale=1.0 / D),
             reads=[ps.b, self.eps.b], writes=[rs.b])
        P.op("dve", lambda e: e.reciprocal(out=rs[:, 0:N], in_=rs[:, 0:N]), reads=[rs.b], writes=[rs.b])
        P.op("dve", lambda e: e.tensor_tensor(out=sq[:], in0=x[:], in1=rs[:, 0:N].unsqueeze(1).to_broadcast([128, 8, N]), op=ALU.mult),
             reads=[x.b, rs.b], writes=[sq.b])
        if S is None:
            P.op("pool", lambda e: e.tensor_tensor(out=out_ap, in0=sq[:], in1=G.unsqueeze(2).to_broadcast([128, 8, N]), op=ALU.mult),
                 reads=[sq.b, self.cols.b, self.der.b] + list(extra_reads), writes=[out_b])
        else:
            P.op("pool", lambda e: e.tensor_tensor(out=sq[:], in0=sq[:], in1=G.unsqueeze(2).to_broadcast([128, 8, N]), op=ALU.mult),
                 reads=[sq.b, self.cols.b, self.der.b], writes=[sq.b])
            P.op("pool", lambda e: e.tensor_tensor(out=out_ap, in0=sq[:], in1=S.unsqueeze(2).to_broadcast([128, 8, N]), op=ALU.add),
                 reads=[sq.b, self.modc.b] + list(extra_reads), writes=[out_b])

    def load_cast(self, es_stage, pieces, wb):
        P = self.P
        if not hasattr(self, "_stg") or self._stg is None:
            self._stg = [self.tile(es_stage, "stg%d" % i, [128, 1024], dma=True) for i in range(getattr(self, "_stg_n", 2))]
            self._stgi = 0
        for (dst, src, p, a, b) in pieces:
            assert a * b <= 1024
            st = self._stg[self._stgi % len(self._stg)]
            self._stgi += 1
            sv = st[0:p, 0:a * b].rearrange("p (a b) -> p a b", a=a)
            q = ("sp", "act")[self._stgi % 2]
            P.dma(q, lambda e, sv=sv, src=src: e.dma_start(out=sv, in_=src), st.b, writes=[st.b])
            self.copy(self.eng_rr(("pool", "dve", "act")), dst, sv, [st.b], [wb])

    def phase_ffn(self, l, xsrc, fuse_final=False):
        P, nc, inp = self.P, self.nc, self.inp
        TB = 256
        NB = T // TB
        with ExitStack() as es:
            self._stg = None
            self._stg_n = 6
            ses = ExitStack()
            wup = self.tile(es, "wup", [128, 8, F2], BF16)
            wdn = self.tile(es, "wdn", [128, NG, D], BF16)
            pieces = []
            srcu = inp["ffn_w_up"][l].rearrange("(kc p) n -> p kc n", p=128)
            for kc in range(8):
                for (n0, n1) in ((0, 1024), (1024, 2048), (2048, 3072), (3072, 4096), (4096, 5120), (5120, F2)):
                    pieces.append((wup[:, kc:kc + 1, n0:n1], srcu[:, kc:kc + 1, n0:n1], 128, 1, n1 - n0))
            self.load_cast(ses, pieces, wup.b)
            srcd = inp["ffn_w_down"][l].rearrange("(kc p) n -> p kc n", p=128)
            pieces = [(wdn[:, i:i + 1, :], srcd[:, i:i + 1, :], 128, 1, D) for i in range(0, NG)]
            self.load_cast(ses, pieces, wdn.b)
            P.barrier()
            ses.close()
            self._stg = None
            self._stg_n = 2

            xs = [self.tile(es, "fx%d" % i, [128, 8, TB], dma=True) for i in range(2)]
            sq = self.tile(es, "fsq", [128, 8, TB])
            self.rstd = self.tile(es, "frstd", [128, TB])
            h2s = [self.tile(es, "fh2_%d" % i, [128, 8, TB + 2], BF16) for i in range(2)]
            for i in range(2):
                P.op("pool", lambda e: e.memset(h2s[i][:], 0.0), writes=[h2s[i].b])
            NCV = 6
            cv = [self.tile(es, "fcv%d" % i, [128, TB]) for i in range(NCV)]
            sgl = [self.tile(es, "fsg%d" % i, [128, TB]) for i in range(3)]
            hm = self.tile(es, "fhm", [128, NG, TB], BF16)
            G2 = self.der[:, l, 1, :]
            S2 = self.modcol(l, 3)
            g2c = self.modcol(l, 5)
            cw = [CP.off["cw%d_%d" % (i, l)] for i in range(3)]
            cb = CP.off["cb_%d" % l]
            xr = lambda ap: ap.rearrange("(j p) t -> p j t", p=128)

            def norm(nb):
                x = xs[nb % 2]
                h2 = h2s[nb % 2]
                t0 = nb * TB
                P.dma("sp", lambda e: e.dma_start(out=x[:], in_=xr(xsrc)[:, :, t0:t0 + TB]), x.b, writes=[x.b])
                if nb > 0:
                    hp = h2s[(nb - 1) % 2]
                    P.op("pool", lambda e: e.tensor_copy(out=h2[:, :, 0:2], in_=hp[:, :, TB:TB + 2]), reads=[hp.b], writes=[h2.b])
                self.rmsnorm(x, TB, sq, G2, S2, h2[:, :, 2:TB + 2], h2.b)

            def up(nb):
                h2 = h2s[nb % 2]
                ui = 0
                for i in range(NG):
                    for which in range(2):
                        n = i + which * NG
                        ps = self.psum_next()
                        for kc in range(8):
                            P.op("pe", lambda e: e.matmul(ps[:, 0:TB + 2], lhsT=wup[:, kc, n * 128:(n + 1) * 128], rhs=h2[:, kc, :],
                                                          start=(kc == 0), stop=(kc == 7)), reads=[wup.b, h2.b], writes=[ps.b])
                        c = cv[ui % NCV]
                        ui += 1
                        P.op("act", lambda e: e.activation(out=c[:], in_=ps[:, 2:TB + 2], func=AF.Identity,
                                                           bias=self.cols[:, cb + n:cb + n + 1], scale=self.cols[:, cw[2] + n:cw[2] + n + 1]),
                             reads=[ps.b, self.cols.b], writes=[c.b])
                        P.op("dve", lambda e: e.scalar_tensor_tensor(out=c[:], in0=ps[:, 1:TB + 1], scalar=self.cols[:, cw[1] + n:cw[1] + n + 1],
                                                                     in1=c[:], op0=ALU.mult, op1=ALU.add), reads=[ps.b, c.b, self.cols.b], writes=[c.b])
                        P.op("dve", lambda e: e.scalar_tensor_tensor(out=c[:], in0=ps[:, 0:TB], scalar=self.cols[:, cw[0] + n:cw[0] + n + 1],
                                                                     in1=c[:], op0=ALU.mult, op1=ALU.add), reads=[ps.b, c.b, self.cols.b], writes=[c.b])
                        s_ = sgl[i % 3]
                        if which == 0:
                            P.op("act", lambda e: e.activation(out=s_[:], in_=c[:], func=AF.Silu), reads=[c.b], writes=[s_.b])
                        else:
                            P.op("pool", lambda e: e.tensor_tensor(out=hm[:, i, :], in0=s_[:], in1=c[:], op=ALU.mult),
                                 reads=[s_.b, c.b], writes=[hm.b])

            def down(nb):
                x = xs[nb % 2]
                t0 = nb * TB
                for nj in range(8):
                    ps = self.psum_next()
                    for i in range(NG):
                        P.op("pe", lambda e: e.matmul(ps[:, 0:TB], lhsT=wdn[:, i, nj * 128:(nj + 1) * 128], rhs=hm[:, i, :],
                                                      start=(i == 0), stop=(i == NG - 1)), reads=[wdn.b, hm.b], writes=[ps.b])
                    P.op("dve", lambda e: e.scalar_tensor_tensor(out=x[:, nj, :], in0=ps[:, 0:TB], scalar=g2c[:, nj:nj + 1],
                                                                 in1=x[:, nj, :], op0=ALU.mult, op1=ALU.add),
                         reads=[ps.b, x.b, self.modc.b], writes=[x.b])
                if fuse_final:
                    self.rmsnorm(x, TB, sq, self.col("final_norm", 0, 8), None, x[:], x.b)
                    P.dma("sp", lambda e: e.dma_start(out=xr(self.outT)[:, :, t0:t0 + TB], in_=x[:]), x.b, reads=[x.b])
                else:
                    P.dma("sp", lambda e: e.dma_start(out=xr(self.X)[:, :, t0:t0 + TB], in_=x[:]), x.b, reads=[x.b])

            norm(0)
            for nb in range(NB):
                up(nb)
                if nb + 1 < NB:
                    norm(nb + 1)
                down(nb)
            self.end_phase()

    def phase_final(self, xsrc, do_norm):
        P = self.P
        TB = 512
        with ExitStack() as es:
            xs = [self.tile(es, "nx%d" % i, [128, 8, TB], dma=True) for i in range(2)]
            sq = self.tile(es, "nsq", [128, 8, TB])
            self.rstd = self.tile(es, "nrstd", [128, TB])
            G = self.col("final_norm", 0, 8)
            for nb in range(T // TB):
                x = xs[nb % 2]
                t0 = nb * TB
                src = xsrc.rearrange("(j p) t -> p j t", p=128)[:, :, t0:t0 + TB]
                P.dma("sp", lambda e, x=x, src=src: e.dma_start(out=x[:], in_=src), x.b, writes=[x.b])
                if do_norm:
                    self.rmsnorm(x, TB, sq, G, None, x[:], x.b)
                dst = self.outT.rearrange("(j p) t -> p j t", p=128)[:, :, t0:t0 + TB]
                P.dma("act", lambda e, x=x, dst=dst: e.dma_start(out=dst, in_=x[:]), x.b, reads=[x.b])
            self.end_phase()

    def phase_rwkv(self, l, xsrc):
        P, nc, inp = self.P, self.nc, self.inp
        CH = 128
        NCH = self.cfg.get("nch", T // CH)
        ident = self.cst["ident"]
        with ExitStack() as es:
            self._stg = None
            reuse_stage = self.cfg.get("stage_reuse", True)
            self._stg_n = 6 if reuse_stage else 2
            ses = ExitStack() if reuse_stage else es
            Wr = self.tile(es, "Wr", [128, 8, D], BF16)
            Wk = self.tile(es, "Wk", [128, 8, D], BF16)
            Wv = self.tile(es, "Wv", [128, 8, D], BF16)
            Wo = self.tile(es, "Wo", [128, 8, D], BF16)
            w1b = self.tile(es, "w1b", [128, 8, 64], BF16)
            a1b = self.tile(es, "a1b", [128, 8, 64], BF16)
            g1b = self.tile(es, "g1b", [128, 8, 160], BF16)
            w2b = self.tile(es, "w2b", [64, 1, D], BF16)
            a2b = self.tile(es, "a2b", [64, 1, D], BF16)
            g2b = self.tile(es, "g2b", [128, 2, D], BF16)
            if l == 1:
                v1b = self.tile(es, "v1b", [128, 8, 32], BF16)
                v2b = self.tile(es, "v2b", [32, 1, D], BF16)
                v0r = self.tile(es, "v0r", [128, D], dma=True)
            lnw = self.tile(es, "lnw", [128, D], dma=True)
            lnb = self.tile(es, "lnb", [128, D], dma=True)
            omka = self.tile(es, "omka", [128, 8])
            eps2 = self.tile(es, "eps2", [128, 1])
            onesT = self.tile(es, "onesT", [128, 128])
            for W, src in ((Wr, inp["a_w_rkv"][l, 0]), (Wk, inp["a_w_rkv"][l, 1]), (Wv, inp["a_w_rkv"][l, 2]), (Wo, inp["a_w_o"][l])):
                s3 = src.rearrange("(kc p) n -> p kc n", p=128)
                self.load_cast(ses, [(W[:, kc:kc + 1, :], s3[:, kc:kc + 1, :], 128, 1, D) for kc in range(8)], W.b)
            self.load_cast(ses, [(w1b[:], inp["a_w1"][l].rearrange("(kc p) n -> p kc n", p=128), 128, 8, 64)], w1b.b)
            self.load_cast(ses, [(a1b[:], inp["a_a1"][l].rearrange("(kc p) n -> p kc n", p=128), 128, 8, 64)], a1b.b)
            sg1 = inp["a_g1"][l].rearrange("(kc p) n -> p kc n", p=128)
            self.load_cast(ses, [(g1b[:, 0:4, :], sg1[:, 0:4, :], 128, 4, 160), (g1b[:, 4:8, :], sg1[:, 4:8, :], 128, 4, 160)], g1b.b)
            self.load_cast(ses, [(w2b[:], inp["a_w2"][l].rearrange("(o p) n -> p o n", o=1), 64, 1, D)], w2b.b)
            self.load_cast(ses, [(a2b[:], inp["a_a2"][l].rearrange("(o p) n -> p o n", o=1), 64, 1, D)], a2b.b)
            self.load_cast(ses, [(g2b[:, 0:1, :], inp["a_g2"][l][0:128, :].rearrange("(o p) n -> p o n", o=1), 128, 1, D),
                                (g2b[0:32, 1:2, :], inp["a_g2"][l][128:160, :].rearrange("(o p) n -> p o n", o=1), 32, 1, D)], g2b.b)
            if l == 1:
                self.load_cast(ses, [(v1b[:], inp["a_v1"][0].rearrange("(kc p) n -> p kc n", p=128), 128, 8, 32)], v1b.b)
                self.load_cast(ses, [(v2b[:], inp["a_v2"][0].rearrange("(o p) n -> p o n", o=1), 32, 1, D)], v2b.b)
                P.dma("sp", lambda e: e.dma_start(out=v0r[:], in_=inp["rows"][4 * D:5 * D].partition_broadcast(128)), v0r.b, writes=[v0r.b])
            P.dma("sp", lambda e: e.dma_start(out=lnw[:], in_=inp["rows"][(2 * l) * D:(2 * l + 1) * D].partition_broadcast(128)), lnw.b, writes=[lnw.b])
            P.dma("sp", lambda e: e.dma_start(out=lnb[:], in_=inp["rows"][(2 * l + 1) * D:(2 * l + 2) * D].partition_broadcast(128)), lnb.b, writes=[lnb.b])
            P.op("dve", lambda e: e.tensor_scalar(out=omka[:], in0=self.col("k_a_%d" % l, 0, 8), scalar1=-1.0, scalar2=1.0, op0=ALU.mult, op1=ALU.add),
                 reads=[self.cols.b], writes=[omka.b])
            P.op("pool", lambda e: e.memset(eps2[:], 64e-5), writes=[eps2.b])
            P.op("pool", lambda e: e.memset(onesT[:], 1.0), writes=[onesT.b])

            if reuse_stage:
                P.barrier()
                ses.close()
                self._stg = None
            self._stg_n = 2
            Hbd = self.tile(es, "Hbd", [128, 8, 128])
            Hb = self.tile(es, "Hb", [128, 8, 128], BF16)
            if self.cfg.get("t_hb", True):
                P.op("pool", lambda e: e.memset(Hb[:], 0.0), writes=[Hb.b])
            Vb = self.tile(es, "Vb", [128, D], BF16)
            P.op("pool", lambda e: e.memset(Hbd[:], 0.0), writes=[Hbd.b])
            h = self.tile(es, "h", [128, 8, 129])
            P.op("pool", lambda e: e.memset(h[:], 0.0), writes=[h.b])
            x = self.tile(es, "rx", [128, 8, CH], dma=True)
            self.rstd = self.tile(es, "rrstd", [128, CH])
            xx = self.tile(es, "xx", [128, 8, CH])
            tmpm = [self.tile(es, "tmpm%d" % i, [128, 8, CH], dma=(i == 0)) for i in range(2)]
            sq = tmpm[1] if self.cfg.get("t_sq", True) else self.tile(es, "rsq", [128, 8, CH])
            xm = [self.tile(es, "xm%d" % i, [128, 8, CH], BF16) for i in range(6)]
            lwt = self.tile(es, "lwt", [64, CH], BF16)
            lat = self.tile(es, "lat", [64, CH], BF16)
            lgt = self.tile(es, "lgt", [128, 2, CH], BF16)
            V = self.tile(es, "V", [128, D], dma=True)
            Yt = self.tile(es, "Yt", [128, D], dma=True)
            if l == 1:
                lvt = self.tile(es, "lvt", [32, CH], BF16)
                VFt = Tl(tmpm[0][:].rearrange("p a t -> p (a t)"), tmpm[0].b)
                sgv = Tl(tmpm[1][:].rearrange("p a t -> p (a t)")[:, 0:512], tmpm[1].b)
            ogb = self.tile(es, "ogb", [128, D], BF16)
            ogT = self.tile(es, "ogT", [128, 8, CH], BF16)
            st = self.tile(es, "stat", [128, 6, 16])
            rkb = self.tile(es, "rkb", [128, 16])

            NSLOT = self.cfg.get("nslot%d" % l, 4)

            def mkslot(si):
                S = {}

                def pt(name, shape=(128, 128), dt=F32):
                    S[name] = self.tile(es, "s%d_%s" % (si, name), list(shape), dt)
                for nm in ("r", "k", "sg", "a", "kk", "x", "L", "eL"):
                    pt(nm)
                for nm in ("rh", "bts", "kts", "WT", "U"):
                    pt(nm, (128, 128), BF16)
                pt("AR", (128, 2, 128), BF16)
                pt("F3", (128, 3, 128), BF16)
                pt("T3", (128, 3, 128), BF16)
                for i in range(2):
                    pt("SC%d" % i, (128, 4, 128), BF16)
                    pt("Ym%d" % i, (128, 128), BF16)
                    pt("ZY%d_0" % i, (128, 2, 128), BF16)
                    pt("ZY%d_1" % i, (128, 2, 128), BF16)
                    pt("Tt%d" % i, (128, 128), BF16)
                    pt("AV%d" % i, (128, 64), BF16)
                return S
            slots = [mkslot(i) for i in range(NSLOT)]
            mask4 = self.cst["mask4"]; maskL = self.cst["maskL"]; bd = self.cst["bd"]; ind = self.cst["ind"]
            G1 = self.der[:, l, 0, :]
            S1 = self.modcol(l, 0)
            g1c = self.modcol(l, 2)
            psB = self.ps[6]

            def c_(name, j):
                return self.col("%s_%d" % (name, l), j, 1)

            for c in range(NCH):
                t0 = c * CH
                src = xsrc.rearrange("(j p) t -> p j t", p=128)[:, :, t0:t0 + CH]
                P.dma("sp", lambda e, src=src: e.dma_start(out=x[:], in_=src), x.b, writes=[x.b])
                self.rmsnorm(x, CH, sq, G1, S1, h[:, :, 1:CH + 1], h.b)
                P.op("pool", lambda e: e.tensor_tensor(out=xx[:], in0=h[:, :, 0:CH], in1=h[:, :, 1:CH + 1], op=ALU.subtract),
                     reads=[h.b], writes=[xx.b])
                for i in range(6):
                    tm = tmpm[i % 2]
                    mu = self.col("mu%d_%d" % (i, l), 0, 8)
                    P.op("dve", lambda e, tm=tm, mu=mu: e.tensor_tensor(out=tm[:], in0=xx[:], in1=mu.unsqueeze(2).to_broadcast([128, 8, CH]), op=ALU.mult),
                         reads=[xx.b, self.cols.b], writes=[tm.b])
                    P.op("pool", lambda e, tm=tm, i=i: e.tensor_tensor(out=xm[i][:], in0=tm[:], in1=h[:, :, 1:CH + 1], op=ALU.add),
                         reads=[tm.b, h.b], writes=[xm[i].b])
                P.op("pool", lambda e: e.tensor_copy(out=h[:, :, 0:1], in_=h[:, :, CH:CH + 1]), reads=[h.b], writes=[h.b])
                if l == 1:
                    P.dma("sp", lambda e, t0=t0: e.dma_start(out=VFt[:], in_=self.VF[t0:t0 + CH, :]), VFt.b, writes=[VFt.b])

                psl = self.psum_next()
                for kc in range(8):
                    P.op("pe", lambda e, kc=kc: e.matmul(psl[0:64, 0:128], lhsT=w1b[:, kc, :], rhs=xm[1][:, kc, :], start=(kc == 0), stop=(kc == 7)),
                         reads=[w1b.b, xm[1].b], writes=[psl.b])
                for kc in range(8):
                    P.op("pe", lambda e, kc=kc: e.matmul(psl[0:64, 128:256], lhsT=a1b[:, kc, :], rhs=xm[4][:, kc, :], start=(kc == 0), stop=(kc == 7)),
                         reads=[a1b.b, xm[4].b], writes=[psl.b])
                for kc in range(8):
                    P.op("pe", lambda e, kc=kc: e.matmul(psl[:, 256:384], lhsT=g1b[:, kc, 0:128], rhs=xm[5][:, kc, :], start=(kc == 0), stop=(kc == 7)),
                         reads=[g1b.b, xm[5].b], writes=[psl.b])
                for kc in range(8):
                    P.op("pe", lambda e, kc=kc: e.matmul(psl[0:32, 384:512], lhsT=g1b[:, kc, 128:160], rhs=xm[5][:, kc, :], start=(kc == 0), stop=(kc == 7)),
                         reads=[g1b.b, xm[5].b], writes=[psl.b])
                P.op("act", lambda e: e.activation(out=lwt[:], in_=psl[0:64, 0:128], func=AF.Tanh), reads=[psl.b], writes=[lwt.b])
                P.op("act", lambda e: e.copy(out=lat[:], in_=psl[0:64, 128:256]), reads=[psl.b], writes=[lat.b])
                P.op("act", lambda e: e.activation(out=lgt[:, 0, :], in_=psl[:, 256:384], func=AF.Sigmoid), reads=[psl.b], writes=[lgt.b])
                P.op("act", lambda e: e.activation(out=lgt[0:32, 1, :], in_=psl[0:32, 384:512], func=AF.Sigmoid), reads=[psl.b], writes=[lgt.b])
                if l == 1:
                    psv = self.psum_next()
                    for kc in range(8):
                        P.op("pe", lambda e, kc=kc: e.matmul(psv[0:32, 0:128], lhsT=v1b[:, kc, :], rhs=xm[3][:, kc, :], start=(kc == 0), stop=(kc == 7)),
                             reads=[v1b.b, xm[3].b], writes=[psv.b])
                    P.op("act", lambda e: e.copy(out=lvt[:], in_=psv[0:32, 0:128]), reads=[psv.b], writes=[lvt.b])
                for half in range(2):
                    ps = self.psum_next()
                    for kc in range(8):
                        P.op("pe", lambda e, ps=ps, kc=kc, half=half: e.matmul(ps[:, :], lhsT=xm[3][:, kc, :], rhs=Wv[:, kc, half * 512:(half + 1) * 512],
                                                                              start=(kc == 0), stop=(kc == 7)), reads=[xm[3].b, Wv.b], writes=[ps.b])
                    P.op("act", lambda e, ps=ps, half=half: e.copy(out=V[:, half * 512:(half + 1) * 512], in_=ps[:, :]), reads=[ps.b], writes=[V.b])
                if l == 1:
                    for half in range(2):
                        ps = self.psum_next()
                        hs = slice(half * 512, (half + 1) * 512)
                        P.op("pe", lambda e, ps=ps, hs=hs: e.matmul(ps[:, :], lhsT=lvt[:], rhs=v2b[0:32, 0, hs], start=True, stop=True),
                             reads=[lvt.b, v2b.b], writes=[ps.b])
                        P.op("dve", lambda e, ps=ps, hs=hs: e.tensor_tensor(out=sgv[:], in0=ps[:, :], in1=v0r[:, hs], op=ALU.add),
                             reads=[ps.b, v0r.b], writes=[sgv.b])
                        P.op("act", lambda e: e.activation(out=sgv[:], in_=sgv[:], func=AF.Sigmoid), reads=[sgv.b], writes=[sgv.b])
                        P.op("pool", lambda e, hs=hs: e.tensor_tensor(out=VFt[:, hs], in0=VFt[:, hs], in1=V[:, hs], op=ALU.subtract),
                             reads=[VFt.b, V.b], writes=[VFt.b])
                        P.op("pool", lambda e, hs=hs: e.tensor_tensor(out=VFt[:, hs], in0=VFt[:, hs], in1=sgv[:], op=ALU.mult),
                             reads=[VFt.b, sgv.b], writes=[VFt.b])
                        P.op("pool", lambda e, hs=hs: e.tensor_tensor(out=V[:, hs], in0=V[:, hs], in1=VFt[:, hs], op=ALU.add),
                             reads=[VFt.b, V.b], writes=[V.b])
                else:
                    P.dma("act", lambda e, t0=t0: e.dma_start(out=self.VF[t0:t0 + CH, :], in_=V[:]), V.b, reads=[V.b])
                if "rw_v" in self.dbg:
                    P.dma("act", lambda e, t0=t0: e.dma_start(out=self.dbg["rw_v"][t0:t0 + CH, :], in_=V[:]), V.b, reads=[V.b])

                if self.cfg.get("t_vb", True):
                    P.op("pool", lambda e: e.tensor_copy(out=Vb[:], in_=V[:]), reads=[V.b], writes=[Vb.b])

                def pair_gen(j, S):
                    CUT = self.cfg.get("rw_cut", 99)
                    EB = self.cfg.get("eng_b", "dve")

                    def smul(eng, out_ap, in_ap, sc_ap, rd, wr):
                        if eng == "act":
                            P.op("act", lambda e: e.activation(out=out_ap, in_=in_ap, func=AF.Copy, scale=sc_ap), reads=rd, writes=wr)
                        else:
                            P.op(eng, lambda e: e.tensor_scalar(out=out_ap, in0=in_ap, scalar1=sc_ap, scalar2=None, op0=ALU.mult), reads=rd, writes=wr)
                    js = slice(j * 128, (j + 1) * 128)
                    r_, k_, sg, a_, kk, tx, L_, eL = S["r"], S["k"], S["sg"], S["a"], S["kk"], S["x"], S["L"], S["eL"]
                    rh, bts, kts, WT, U_, AR, F3, T3 = S["rh"], S["bts"], S["kts"], S["WT"], S["U"], S["AR"], S["F3"], S["T3"]
                    SC = [S["SC0"], S["SC1"]]; Ym = [S["Ym0"], S["Ym1"]]; Tt = [S["Tt0"], S["Tt1"]]; AV = [S["AV0"], S["AV1"]]
                    ZY = [[S["ZY0_0"], S["ZY0_1"]], [S["ZY1_0"], S["ZY1_1"]]]
                    psA = self.psum_next()
                    for kc in range(8):
                        P.op("pe", lambda e: e.matmul(psA[:, 0:128], lhsT=Wr[:, kc, js], rhs=xm[0][:, kc, :], start=(kc == 0), stop=(kc == 7)),
                             reads=[Wr.b, xm[0].b], writes=[psA.b])
                    for kc in range(8):
                        P.op("pe", lambda e: e.matmul(psA[:, 128:256], lhsT=Wk[:, kc, js], rhs=xm[2][:, kc, :], start=(kc == 0), stop=(kc == 7)),
                             reads=[Wk.b, xm[2].b], writes=[psA.b])
                    P.op("pe", lambda e: e.matmul(psA[:, 256:384], lhsT=w2b[:, 0, js], rhs=lwt[:], start=True, stop=True), reads=[w2b.b, lwt.b], writes=[psA.b])
                    P.op("pe", lambda e: e.matmul(psA[:, 384:512], lhsT=a2b[:, 0, js], rhs=lat[:], start=True, stop=True), reads=[a2b.b, lat.b], writes=[psA.b])
                    P.op("act", lambda e: e.copy(out=r_[:], in_=psA[:, 0:128]), reads=[psA.b], writes=[r_.b])
                    P.op("act", lambda e: e.copy(out=k_[:], in_=psA[:, 128:256]), reads=[psA.b], writes=[k_.b])
                    P.op("act", lambda e: e.activation(out=sg[:], in_=psA[:, 256:384], func=AF.Sigmoid, bias=c_("w0", j), scale=1.0),
                         reads=[psA.b, self.cols.b], writes=[sg.b])
                    P.op("act", lambda e: e.activation(out=a_[:], in_=psA[:, 384:512], func=AF.Sigmoid, bias=c_("a0", j), scale=1.0),
                         reads=[psA.b, self.cols.b], writes=[a_.b])
                    yield
                    if CUT <= 1:
                        return
                    P.op("dve", lambda e: e.tensor_scalar(out=kk[:], in0=k_[:], scalar1=c_("k_k", j), scalar2=None, op0=ALU.mult),
                         reads=[k_.b, self.cols.b], writes=[kk.b])
                    P.op("pool", lambda e: e.tensor_tensor(out=tx[:], in0=kk[:], in1=kk[:], op=ALU.mult), reads=[kk.b], writes=[tx.b])
                    ps = self.psum_next()
                    P.op("pe", lambda e: e.matmul(ps[:, 0:128], lhsT=bd[:], rhs=tx[:], start=True, stop=True), reads=[bd.b, tx.b], writes=[ps.b])
                    P.op("act", lambda e: e.activation(out=tx[:], in_=ps[:, 0:128], func=AF.Sqrt), reads=[ps.b], writes=[tx.b])
                    yield
                    if CUT <= 2:
                        return
                    P.op("dve", lambda e: e.tensor_scalar(out=tx[:], in0=tx[:], scalar1=1e-12, scalar2=None, op0=ALU.max), reads=[tx.b], writes=[tx.b])
                    P.op("dve", lambda e: e.reciprocal(out=tx[:], in_=tx[:]), reads=[tx.b], writes=[tx.b])
                    P.op("pool", lambda e: e.tensor_tensor(out=kk[:], in0=kk[:], in1=tx[:], op=ALU.mult), reads=[kk.b, tx.b], writes=[kk.b])
                    P.op("dve", lambda e: e.tensor_scalar(out=tx[:], in0=a_[:], scalar1=c_("k_a", j), scalar2=omka[:, j:j + 1], op0=ALU.mult, op1=ALU.add),
                         reads=[a_.b, self.cols.b, omka.b], writes=[tx.b])
                    P.op("pool", lambda e: e.tensor_tensor(out=k_[:], in0=k_[:], in1=tx[:], op=ALU.mult), reads=[k_.b, tx.b], writes=[k_.b])
                    P.op("pool", lambda e: e.tensor_tensor(out=a_[:], in0=kk[:], in1=a_[:], op=ALU.mult), reads=[kk.b, a_.b], writes=[a_.b])
                    P.op("dve", lambda e: e.scalar_tensor_tensor(out=tx[:], in0=r_[:], scalar=c_("r_k", j), in1=k_[:], op0=ALU.mult, op1=ALU.mult),
                         reads=[r_.b, k_.b, self.cols.b], writes=[tx.b])
                    P.op("pe", lambda e: e.matmul(psB[:, 0:16], lhsT=tx[:], rhs=ind[:, j * 16:(j + 1) * 16], start=(j == 0), stop=(j == 7)),
                         reads=[tx.b, ind.b], writes=[psB.b])
                    yield
                    if CUT <= 3:
                        return
                    P.op("dve", lambda e: e.tensor_tensor_scan(out=L_[:], data0=onesT[:], data1=sg[:], initial=0.0, op0=ALU.mult, op1=ALU.add),
                         reads=[onesT.b, sg.b], writes=[L_.b])
                    P.op("pool", lambda e: e.tensor_tensor(out=sg[:], in0=L_[:], in1=sg[:], op=ALU.subtract), reads=[L_.b, sg.b], writes=[sg.b])
                    P.op("act", lambda e: e.activation(out=eL[:], in_=L_[:], func=AF.Exp, scale=-C0), reads=[L_.b], writes=[eL.b])
                    P.op("act", lambda e: e.activation(out=sg[:], in_=sg[:], func=AF.Exp, scale=-C0), reads=[sg.b], writes=[sg.b])
                    P.op("act", lambda e: e.activation(out=L_[:], in_=L_[:], func=AF.Exp, scale=C0), reads=[L_.b], writes=[L_.b])
                    yield
                    if CUT <= 4:
                        return
                    enL = L_
                    eE = sg
                    P.op("pool", lambda e: e.tensor_tensor(out=r_[:], in0=r_[:], in1=eL[:], op=ALU.mult), reads=[r_.b, eL.b], writes=[r_.b])
                    P.op("pool", lambda e: e.tensor_copy(out=rh[:], in_=r_[:]), reads=[r_.b], writes=[rh.b])
                    P.op("dve", lambda e: e.scalar_tensor_tensor(out=tx[:], in0=kk[:], scalar=-1.0, in1=eE[:], op0=ALU.mult, op1=ALU.mult),
                         reads=[kk.b, eE.b], writes=[tx.b])
                    P.op("pool", lambda e: e.tensor_copy(out=F3[:, 0, :], in_=tx[:]), reads=[tx.b], writes=[F3.b])
                    P.op("dve", lambda e: e.tensor_scalar(out=AR[:, 0, :], in0=tx[:], scalar1=enL[:, 63:64], scalar2=None, op0=ALU.mult),
                         reads=[tx.b, enL.b], writes=[AR.b])
                    P.op("dve", lambda e: e.tensor_scalar(out=AR[:, 1, :], in0=r_[:], scalar1=enL[:, 63:64], scalar2=None, op0=ALU.mult),
                         reads=[r_.b, enL.b], writes=[AR.b])
                    P.op("pool", lambda e: e.tensor_tensor(out=a_[:], in0=a_[:], in1=enL[:], op=ALU.mult), reads=[a_.b, enL.b], writes=[a_.b])
                    P.op("pool", lambda e: e.tensor_tensor(out=k_[:], in0=k_[:], in1=enL[:], op=ALU.mult), reads=[k_.b, enL.b], writes=[k_.b])
                    yield
                    if CUT <= 5:
                        return
                    smul(EB, bts[:], a_[:], eL[:, 63:64], [a_.b, eL.b], [bts.b])
                    smul(EB, kts[:], k_[:], eL[:, 63:64], [k_.b, eL.b], [kts.b])
                    smul(EB, F3[:, 1, :], a_[:], eL[:, 127:128], [a_.b, eL.b], [F3.b])
                    smul(EB, F3[:, 2, :], k_[:], eL[:, 127:128], [k_.b, eL.b], [F3.b])
                    yield
                    if CUT <= 6:
                        return
                    for q in range(3):
                        P.op("pe", lambda e: e.transpose(out=self.psbf[:, q * 128:(q + 1) * 128], in_=F3[:, q, :], identity=self.identb[:]),
                             reads=[F3.b, self.identb.b], writes=[self.psbf.b])
                    P.op("act", lambda e: e.copy(out=T3[:].rearrange("p a t -> p (a t)"), in_=self.psbf[:, 0:384]), reads=[self.psbf.b], writes=[T3.b])
                    yield
                    if CUT <= 7:
                        return
                    for par in range(2):
                        rows = slice(64 * par, 64 * par + 64)
                        pss = self.psum_next()
                        arv = AR[rows, :, :].rearrange("p a t -> p (a t)")
                        P.op("pe", lambda e: e.matmul(pss[:, 0:256], lhsT=bts[rows, :], rhs=arv, start=True, stop=True),
                             reads=[bts.b, AR.b], writes=[pss.b])
                        P.op("pe", lambda e: e.matmul(pss[:, 256:512], lhsT=kts[rows, :], rhs=arv, start=True, stop=True),
                             reads=[kts.b, AR.b], writes=[pss.b])
                        P.op("dve", lambda e: e.tensor_tensor(out=SC[par][:].rearrange("p a t -> p (a t)"), in0=pss[:, :], in1=mask4[:], op=ALU.mult),
                             reads=[pss.b, mask4.b], writes=[SC[par].b])
                        ps3 = self.psum_next()
                        P.op("pe", lambda e: e.matmul(ps3[:, 0:128], lhsT=AR[rows, 0, :], rhs=bts[rows, :], start=True, stop=True),
                             reads=[bts.b, AR.b], writes=[ps3.b])
                        P.op("dve", lambda e: e.tensor_tensor(out=Ym[par][:], in0=ps3[:, 0:128], in1=maskL[:], op=ALU.mult),
                             reads=[ps3.b, maskL.b], writes=[Ym[par].b])
                        P.op("pool", lambda e: e.tensor_tensor(out=Tt[par][:], in0=SC[par][:, 0, :], in1=self.identb[:], op=ALU.add),
                             reads=[SC[par].b, self.identb.b], writes=[Tt[par].b])
                        yield
                    cur = [(SC[0][:, 0, :], SC[0].b, Ym[0][:], Ym[0].b), (SC[1][:, 0, :], SC[1].b, Ym[1][:], Ym[1].b)]
                    for step in range(1, 7):
                        for par in range(2):
                            Zap, Zb, Yap, Yb = cur[par]
                            zy = ZY[par][step % 2]
                            psz = self.psum_next()
                            if step < 6:
                                P.op("pe", lambda e: e.matmul(psz[:, 0:128], lhsT=Yap, rhs=Zap, start=True, stop=True), reads=[Zb, Yb], writes=[psz.b])
                            P.op("pe", lambda e: e.matmul(psz[:, 128:256], lhsT=Zap, rhs=Yap, start=True, stop=True), reads=[Zb, Yb], writes=[psz.b])
                            ev = "act"
                            if step < 6:
                                self.copy(ev, zy[:].rearrange("p a t -> p (a t)"), psz[:, 0:256], [psz.b], [zy.b])
                            else:
                                self.copy(ev, zy[:, 1, :], psz[:, 128:256], [psz.b], [zy.b])
                            cur[par] = (zy[:, 0, :], zy.b, zy[:, 1, :], zy.b)
                            yield
                            tt = Tt[par]
                            psp = self.psum_next()
                            P.op("pe", lambda e: e.matmul(psp[:, 0:128], lhsT=zy[:, 1, :], rhs=tt[:], start=True, stop=True),
                                 reads=[zy.b, tt.b], writes=[psp.b])
                            P.op("dve", lambda e: e.tensor_tensor(out=tt[:], in0=psp[:, 0:128], in1=tt[:], op=ALU.add),
                                 reads=[psp.b, tt.b], writes=[tt.b])
                            yield
                    for par in range(2):
                        rows = slice(64 * par, 64 * par + 64)
                        hc = slice((2 * j + par) * 64, (2 * j + par + 1) * 64)
                        psw = self.psum_next()
                        P.op("pe", lambda e: e.matmul(psw[:, 0:128], lhsT=T3[:, 0, :], rhs=Tt[par][:], start=True, stop=True),
                             reads=[T3.b, Tt[par].b], writes=[psw.b])
                        P.op("pe", lambda e: e.matmul(psw[:, 128:192], lhsT=SC[par][:, 2, :], rhs=Vb[:, hc], start=True, stop=True),
                             reads=[SC[par].b, Vb.b], writes=[psw.b])
                        P.op("act", lambda e: e.copy(out=WT[rows, :], in_=psw[rows, 0:128]), reads=[psw.b], writes=[WT.b])
                        P.op("act", lambda e: e.copy(out=AV[par][:], in_=psw[:, 128:192]), reads=[psw.b], writes=[AV[par].b])
                        yield
                    psu = self.psum_next()
                    P.op("pe", lambda e: e.matmul(psu[:, 0:128], lhsT=WT[:], rhs=Hb[:, j, :], start=True, stop=True),
                         reads=[WT.b, Hb.b], writes=[psu.b])
                    for par in range(2):
                        cs = slice(64 * par, 64 * par + 64)
                        P.op("pe", lambda e: e.matmul(psu[:, cs], lhsT=Tt[par][:], rhs=AV[par][:], start=False, stop=(par == 1), skip_group_check=True),
                             reads=[Tt[par].b, AV[par].b], writes=[psu.b])
                    P.op("act", lambda e: e.copy(out=U_[:], in_=psu[:, 0:128]), reads=[psu.b], writes=[U_.b])
                    yield
                    if CUT <= 8:
                        return
                    psy = self.psum_next()
                    P.op("pe", lambda e: e.matmul(psy[:, 0:128], lhsT=rh[:], rhs=Hb[:, j, :], start=True, stop=True),
                         reads=[rh.b, Hb.b], writes=[psy.b])
                    for par in range(2):
                        hc = slice((2 * j + par) * 64, (2 * j + par + 1) * 64)
                        cs = slice(64 * par, 64 * par + 64)
                        P.op("pe", lambda e: e.matmul(psy[:, cs], lhsT=SC[par][:, 1, :], rhs=U_[:, cs], start=False, stop=False, skip_group_check=True),
                             reads=[SC[par].b, U_.b], writes=[psy.b])
                        P.op("pe", lambda e: e.matmul(psy[:, cs], lhsT=SC[par][:, 3, :], rhs=Vb[:, hc], start=False, stop=(par == 1), skip_group_check=True),
                             reads=[SC[par].b, Vb.b], writes=[psy.b])
                    P.op("act", lambda e: e.copy(out=Yt[:, js], in_=psy[:, 0:128]), reads=[psy.b], writes=[Yt.b])
                    psh = self.psum_next()
                    P.op("pe", lambda e: e.matmul(psh[:, 0:128], lhsT=T3[:, 2, :], rhs=Vb[:, js], start=True, stop=False),
                         reads=[T3.b, Vb.b], writes=[psh.b])
                    P.op("pe", lambda e: e.matmul(psh[:, 0:128], lhsT=T3[:, 1, :], rhs=U_[:], start=False, stop=True),
                         reads=[T3.b, U_.b], writes=[psh.b])
                    for par in range(2):
                        rows = slice(64 * par, 64 * par + 64)
                        P.op("dve", lambda e: e.scalar_tensor_tensor(out=Hbd[rows, j, rows], in0=Hbd[rows, j, rows], scalar=eL[rows, 127:128],
                                                                     in1=psh[rows, rows], op0=ALU.mult, op1=ALU.add),
                             reads=[Hbd.b, eL.b, psh.b], writes=[Hbd.b])
                        P.op("pool", lambda e: e.tensor_copy(out=Hb[rows, j, rows], in_=Hbd[rows, j, rows]), reads=[Hbd.b], writes=[Hb.b])
                    yield
                    if CUT <= 9:
                        return

                STAG = self.cfg.get("stagger", 0)
                pending = list(range(8))
                free_slots = list(range(NSLOT))
                active = []
                tick = 0
                next_start = 0
                while pending or active:
                    if pending and free_slots and tick >= next_start:
                        jn = pending.pop(0)
                        sl = free_slots.pop(0)
                        active.append((pair_gen(jn, slots[sl]), sl))
                        next_start = tick + STAG
                    for item in list(active):
                        try:
                            next(item[0])
                        except StopIteration:
                            active.remove(item)
                            free_slots.append(item[1])
                    tick += 1

                if "rw_o" in self.dbg:
                    P.dma("act", lambda e, t0=t0: e.dma_start(out=self.dbg["rw_o"][t0:t0 + CH, :], in_=Yt[:]), Yt.b, reads=[Yt.b])
                Y3 = Yt[:].rearrange("p (h n) -> p h n", n=64)
                S1t = tmpm[0][:].rearrange("p a t -> p (a t)")
                S1b = tmpm[0].b
                S2t = tmpm[1][:].rearrange("p a t -> p (a t)")
                S2b = tmpm[1].b
                P.op("act", lambda e: e.copy(out=rkb[:], in_=psB[:, 0:16]), reads=[psB.b], writes=[rkb.b])
                P.op("dve", lambda e: e.tensor_reduce(out=st[:, 0, :], in_=Y3, axis=AX.X, op=ALU.add), reads=[Yt.b], writes=[st.b])
                P.op("pool", lambda e: e.tensor_tensor(out=S1t, in0=Yt[:], in1=Yt[:], op=ALU.mult), reads=[Yt.b], writes=[S1b])
                P.op("dve", lambda e: e.tensor_reduce(out=st[:, 1, :], in_=S1t.rearrange("p (h n) -> p h n", n=64), axis=AX.X, op=ALU.add),
                     reads=[S1b], writes=[st.b])
                P.op("dve", lambda e: e.tensor_scalar(out=st[:, 2, :], in0=st[:, 0, :], scalar1=1.0 / 64, scalar2=None, op0=ALU.mult), reads=[st.b], writes=[st.b])
                P.op("dve", lambda e: e.tensor_tensor(out=st[:, 3, :], in0=st[:, 2, :], in1=st[:, 2, :], op=ALU.mult), reads=[st.b], writes=[st.b])
                P.op("dve", lambda e: e.scalar_tensor_tensor(out=st[:, 4, :], in0=st[:, 1, :], scalar=1.0 / 64, in1=st[:, 3, :], op0=ALU.mult, op1=ALU.subtract),
                     reads=[st.b], writes=[st.b])
                P.op("act", lambda e: e.activation(out=st[:, 5, :], in_=st[:, 4, :], func=AF.Sqrt, bias=eps2[:], scale=1.0), reads=[st.b, eps2.b], writes=[st.b])
                P.op("dve", lambda e: e.reciprocal(out=st[:, 5, :], in_=st[:, 5, :]), reads=[st.b], writes=[st.b])
                P.op("pool", lambda e: e.tensor_tensor(out=S1t.rearrange("p (h n) -> p h n", n=64), in0=Y3, in1=st[:, 2, :].unsqueeze(2).to_broadcast([128, 16, 64]), op=ALU.subtract),
                     reads=[Yt.b, st.b], writes=[S1b])
                P.op("dve", lambda e: e.tensor_tensor(out=S1t.rearrange("p (h n) -> p h n", n=64), in0=S1t.rearrange("p (h n) -> p h n", n=64),
                                                      in1=st[:, 5, :].unsqueeze(2).to_broadcast([128, 16, 64]), op=ALU.mult),
                     reads=[S1b, st.b], writes=[S1b])
                P.op("pool", lambda e: e.tensor_tensor(out=S1t, in0=S1t, in1=lnw[:], op=ALU.mult), reads=[S1b, lnw.b], writes=[S1b])
                P.op("dve", lambda e: e.tensor_tensor(out=S1t, in0=S1t, in1=lnb[:], op=ALU.add), reads=[S1b, lnb.b], writes=[S1b])
                P.op("pool", lambda e: e.tensor_tensor(out=S2t.rearrange("p (h n) -> p h n", n=64), in0=V[:].rearrange("p (h n) -> p h n", n=64),
                                                       in1=rkb[:].unsqueeze(2).to_broadcast([128, 16, 64]), op=ALU.mult),
                     reads=[V.b, rkb.b], writes=[S2b])
                P.op("dve", lambda e: e.tensor_tensor(out=S1t, in0=S1t, in1=S2t, op=ALU.add), reads=[S1b, S2b], writes=[S1b])
                for half in range(2):
                    ps = self.psum_next()
                    hs = slice(half * 512, (half + 1) * 512)
                    P.op("pe", lambda e: e.matmul(ps[:, :], lhsT=lgt[:, 0, :], rhs=g2b[:, 0, hs], start=True, stop=False),
                         reads=[lgt.b, g2b.b], writes=[ps.b])
                    P.op("pe", lambda e: e.matmul(ps[:, :], lhsT=lgt[0:32, 1, :], rhs=g2b[0:32, 1, hs], start=False, stop=True),
                         reads=[lgt.b, g2b.b], writes=[ps.b])
                    P.op("dve", lambda e: e.tensor_tensor(out=ogb[:, hs], in0=S1t[:, hs], in1=ps[:, :], op=ALU.mult), reads=[S1b, ps.b], writes=[ogb.b])
                for jj in range(8):
                    P.op("pe", lambda e, jj=jj: e.transpose(out=self.psbf[:, jj * 128:(jj + 1) * 128], in_=ogb[:, jj * 128:(jj + 1) * 128], identity=self.identb[:]),
                         reads=[ogb.b, self.identb.b], writes=[self.psbf.b])
                P.op("act", lambda e: e.copy(out=ogT[:].rearrange("p a t -> p (a t)"), in_=self.psbf[:, :]), reads=[self.psbf.b], writes=[ogT.b])
                for half in range(2):
                    ps = self.psum_next()
                    for jj in range(4):
                        nj = half * 4 + jj
                        for kc in range(8):
                            P.op("pe", lambda e, ps=ps, jj=jj, nj=nj, kc=kc: e.matmul(ps[:, jj * 128:(jj + 1) * 128], lhsT=Wo[:, kc, nj * 128:(nj + 1) * 128], rhs=ogT[:, kc, :],
                                                                                    start=(kc == 0), stop=(kc == 7)), reads=[Wo.b, ogT.b], writes=[ps.b])
                    for jj in range(4):
                        nj = half * 4 + jj
                        P.op("dve", lambda e, ps=ps, jj=jj, nj=nj: e.scalar_tensor_tensor(out=x[:, nj, :], in0=ps[:, jj * 128:(jj + 1) * 128], scalar=g1c[:, nj:nj + 1],
                                                                                         in1=x[:, nj, :], op0=ALU.mult, op1=ALU.add),
                             reads=[ps.b, x.b, self.modc.b], writes=[x.b])
                dst = self.X.rearrange("(j p) t -> p j t", p=128)[:, :, t0:t0 + CH]
                P.dma("sp", lambda e, dst=dst: e.dma_start(out=dst, in_=x[:]), x.b, reads=[x.b])
            self.end_phase()

    def rope_proj(self, es, W, hb, ncol0, dst_blk, CTt, STt, tmpA, tmpB):
        P = self.P
        roper = self.cst["roper"]
        tmpAs, tmpBs = tmpA, tmpB
        for hh in range(8):
            tmpA = tmpAs[hh % len(tmpAs)]
            tmpB = tmpBs[hh % len(tmpBs)]
            ps = self.psum_next()
            for kc in range(8):
                P.op("pe", lambda e: e.matmul(ps[:, :], lhsT=W[:, kc, ncol0 + hh * 128:ncol0 + (hh + 1) * 128], rhs=hb[:, kc, :],
                                              start=(kc == 0), stop=(kc == 7)), reads=[W.b, hb.b], writes=[ps.b])
            P.op("act", lambda e: e.copy(out=tmpA[:], in_=ps[:, :]), reads=[ps.b], writes=[tmpA.b])
            ps2 = self.psum_next()
            P.op("pe", lambda e: e.matmul(ps2[:, :], lhsT=roper[:], rhs=tmpA[:], start=True, stop=True), reads=[roper.b, tmpA.b], writes=[ps2.b])
            P.op("dve", lambda e: e.tensor_tensor(out=tmpB[:], in0=ps2[:, :], in1=STt[:], op=ALU.mult), reads=[ps2.b, STt.b], writes=[tmpB.b])
            P.op("pool", lambda e: e.tensor_tensor(out=tmpA[:], in0=tmpA[:], in1=CTt[:], op=ALU.mult), reads=[tmpA.b, CTt.b], writes=[tmpA.b])
            P.op("pool", lambda e: e.tensor_tensor(out=dst_blk[:, hh, :], in0=tmpA[:], in1=tmpB[:], op=ALU.add), reads=[tmpA.b, tmpB.b], writes=[dst_blk.b])

    def phase_kv(self, xsrc):
        P, nc, inp = self.P, self.nc, self.inp
        TB = 512
        with ExitStack() as es:
            self._stg = None
            self._stg_n = 6
            Wkv = self.tile(es, "Wkv", [128, 8, 2 * D], BF16)
            s3 = inp["w_kv"].rearrange("(kc p) n -> p kc n", p=128)
            pieces = []
            for kc in range(8):
                for hf in range(2):
                    pieces.append((Wkv[:, kc:kc + 1, hf * D:(hf + 1) * D], s3[:, kc:kc + 1, hf * D:(hf + 1) * D], 128, 1, D))
            self.load_cast(es, pieces, Wkv.b)
            xs_ = [self.tile(es, "kx%d" % i, [128, 8, TB], dma=True) for i in range(2)]
            sq = self.tile(es, "ksq", [128, 8, TB])
            self.rstd = self.tile(es, "krstd", [128, TB])
            hbs_ = [self.tile(es, "khb%d" % i, [128, 8, TB], BF16) for i in range(2)]
            CTt = self.tile(es, "kCT", [128, TB], dma=True)
            STt = self.tile(es, "kST", [128, TB], dma=True)
            tmpA = [self.tile(es, "ktA%d" % i, [128, TB]) for i in range(3)]
            tmpB = [self.tile(es, "ktB%d" % i, [128, TB]) for i in range(3)]
            Kblks = [self.tile(es, "Kblk%d" % i, [128, 8, TB], BF16, dma=True) for i in range(2)]
            Vblks = [self.tile(es, "Vblk%d" % i, [128, 4, D], BF16, dma=True) for i in range(2)]
            G = self.col("kv_norm", 0, 8)
            for nb in range(T // TB):
                t0 = nb * TB
                x, hb, Kblk, Vblk = xs_[nb % 2], hbs_[nb % 2], Kblks[nb % 2], Vblks[nb % 2]
                src = xsrc.rearrange("(j p) t -> p j t", p=128)[:, :, t0:t0 + TB]
                P.dma("sp", lambda e: e.dma_start(out=x[:], in_=src), x.b, writes=[x.b])
                P.dma("act", lambda e: e.dma_start(out=CTt[:], in_=inp["ropec"][:, t0:t0 + TB]), CTt.b, writes=[CTt.b])
                P.dma("act", lambda e: e.dma_start(out=STt[:], in_=inp["ropes"][:, t0:t0 + TB]), STt.b, writes=[STt.b])
                self.rmsnorm(x, TB, sq, G, None, hb[:], hb.b)
                self.rope_proj(es, Wkv, hb, 0, Kblk, CTt, STt, tmpA, tmpB)
                dst = self.KT.rearrange("(j p) t -> p j t", p=128)[:, :, t0:t0 + TB]
                P.dma("sp", lambda e: e.dma_start(out=dst, in_=Kblk[:]), Kblk.b, reads=[Kblk.b])
                for tl in range(4):
                    for half in range(2):
                        ps = self.psum_next()
                        for kc in range(8):
                            P.op("pe", lambda e: e.matmul(ps[:, :], lhsT=hb[:, kc, tl * 128:(tl + 1) * 128], rhs=Wkv[:, kc, D + half * 512:D + (half + 1) * 512],
                                                          start=(kc == 0), stop=(kc == 7)), reads=[hb.b, Wkv.b], writes=[ps.b])
                        self.copy(("act", "dve")[half], Vblk[:, tl, half * 512:(half + 1) * 512], ps[:, :], [ps.b], [Vblk.b])
                dstv = self.VS[t0:t0 + TB, :].rearrange("(a p) e -> p a e", p=128)
                P.dma("sp", lambda e: e.dma_start(out=dstv, in_=Vblk[:]), Vblk.b, reads=[Vblk.b])
            self.end_phase()

    def phase_attn(self, l, xsrc):
        P, nc, inp = self.P, self.nc, self.inp
        jl = l - 2
        TB = 512
        lam_init = 0.8 - 0.6 * math.exp(-0.3 * l)
        G1 = self.der[:, l, 0, :]
        S1 = self.modcol(l, 0)
        g1c = self.modcol(l, 2)
        with ExitStack() as es:
            self._stg = None
            self._stg_n = 6
            Wq = self.tile(es, "Wq", [128, 8, D], BF16)
            s3 = inp["b_w_q"][jl].rearrange("(kc p) n -> p kc n", p=128)
            self.load_cast(es, [(Wq[:, kc:kc + 1, :], s3[:, kc:kc + 1, :], 128, 1, D) for kc in range(8)], Wq.b)
            xs_ = [self.tile(es, "qx%d" % i, [128, 8, TB], dma=True) for i in range(2)]
            sq = self.tile(es, "qsq", [128, 8, TB])
            self.rstd = self.tile(es, "qrstd", [128, TB])
            hbs_ = [self.tile(es, "qhb%d" % i, [128, 8, TB], BF16) for i in range(2)]
            CTt = self.tile(es, "qCT", [128, TB], dma=True)
            STt = self.tile(es, "qST", [128, TB], dma=True)
            tmpA = [self.tile(es, "qtA%d" % i, [128, TB]) for i in range(3)]
            tmpB = [self.tile(es, "qtB%d" % i, [128, TB]) for i in range(3)]
            Qblks = [self.tile(es, "Qblk%d" % i, [128, 8, TB], BF16, dma=True) for i in range(2)]
            for nb in range(T // TB):
                t0 = nb * TB
                x, hb, Qblk = xs_[nb % 2], hbs_[nb % 2], Qblks[nb % 2]
                src = xsrc.rearrange("(j p) t -> p j t", p=128)[:, :, t0:t0 + TB]
                P.dma("sp", lambda e: e.dma_start(out=x[:], in_=src), x.b, writes=[x.b])
                P.dma("act", lambda e: e.dma_start(out=CTt[:], in_=inp["ropec"][:, t0:t0 + TB]), CTt.b, writes=[CTt.b])
                P.dma("act", lambda e: e.dma_start(out=STt[:], in_=inp["ropes"][:, t0:t0 + TB]), STt.b, writes=[STt.b])
                self.rmsnorm(x, TB, sq, G1, S1, hb[:], hb.b)
                self.rope_proj(es, Wq, hb, 0, Qblk, CTt, STt, tmpA, tmpB)
                dst = self.QT.rearrange("(j p) t -> p j t", p=128)[:, :, t0:t0 + TB]
                P.dma("sp", lambda e: e.dma_start(out=dst, in_=Qblk[:]), Qblk.b, reads=[Qblk.b])
            self.end_phase()

        with ExitStack() as es:
            NH = self.cfg.get("nheads", 8)
            NQB = self.cfg.get("nqb", T // TB)
            lamv = self.tile(es, "lamv", [128, 256], dma=True)
            o_l = 7 * D + jl * 256
            P.dma("sp", lambda e: e.dma_start(out=lamv[:], in_=inp["rows"][o_l:o_l + 256].partition_broadcast(128)), lamv.b, writes=[lamv.b])
            subw = self.tile(es, "subw", [128, 128], dma=True)
            o_s = 5 * D + jl * D
            P.dma("sp", lambda e: e.dma_start(out=subw[:], in_=inp["rows"][o_s:o_s + 128].partition_broadcast(128)), subw.b, writes=[subw.b])
            P.op("dve", lambda e: e.tensor_scalar(out=subw[:], in0=subw[:], scalar1=(1.0 - lam_init), scalar2=None, op0=ALU.mult), reads=[subw.b], writes=[subw.b])
            lt = self.tile(es, "lt", [128, 2, 64])
            ls = self.tile(es, "ls", [128, 4])
            P.op("dve", lambda e: e.tensor_tensor(out=lt[:, 0, :], in0=lamv[:, 0:64], in1=lamv[:, 64:128], op=ALU.mult), reads=[lamv.b], writes=[lt.b])
            P.op("dve", lambda e: e.tensor_tensor(out=lt[:, 1, :], in0=lamv[:, 128:192], in1=lamv[:, 192:256], op=ALU.mult), reads=[lamv.b], writes=[lt.b])
            P.op("dve", lambda e: e.tensor_reduce(out=ls[:, 0:2], in_=lt[:], axis=AX.X, op=ALU.add), reads=[lt.b], writes=[ls.b])
            P.op("act", lambda e: e.activation(out=ls[:, 0:2], in_=ls[:, 0:2], func=AF.Exp), reads=[ls.b], writes=[ls.b])
            P.op("dve", lambda e: e.tensor_tensor(out=ls[:, 2:3], in0=ls[:, 1:2], in1=ls[:, 0:1], op=ALU.subtract), reads=[ls.b], writes=[ls.b])
            P.op("dve", lambda e: e.tensor_scalar(out=ls[:, 3:4], in0=ls[:, 2:3], scalar1=-lam_init, scalar2=None, op0=ALU.add), reads=[ls.b], writes=[ls.b])
            neglam = ls[:, 3:4]
            cmaskb = self.tile(es, "cmaskb", [128, 128], BF16)
            self.copy("pool", cmaskb[:], self.cst["mask4"][:, 128:256], [self.cst["mask4"].b], [cmaskb.b])
            eps1 = self.eps

            KTh = [self.tile(es, "KTh%d" % i, [128, T], BF16, dma=True) for i in range(2)]
            QTh = [self.tile(es, "QTh%d" % i, [128, T], BF16, dma=True) for i in range(2)]
            Vh = [self.tile(es, "Vh%d" % i, [128, 32, 129], BF16, dma=True) for i in range(2)]
            YTh = [self.tile(es, "YTh%d" % i, [128, T], BF16, dma=True) for i in range(2)]
            for i in range(2):
                P.op("pool", lambda e: e.memset(Vh[i][:, :, 128:129], 1.0), writes=[Vh[i].b])
            ET = [self.tile(es, "ET%d" % i, [128, 512], BF16) for i in range(4)]
            Oc = [self.tile(es, "Oc%d" % i, [128, 4, 129]) for i in range(2)]
            rz = self.tile(es, "rz", [128, 2, 4])
            y = self.tile(es, "ay", [128, 4, 128])
            ysq = self.tile(es, "aysq", [128, 4, 128])
            ss = self.tile(es, "ass", [128, 4])
            ynb = self.tile(es, "aynb", [128, 4, 128], BF16)
            if "at_o" in self.dbg:
                self.dbg_tile = self.tile(es, "dbgt", [128, 4, 128], dma=True)
            eti = 0
            for hh in range(NH):
                kt_, qt_, vh_, yt_ = KTh[hh % 2], QTh[hh % 2], Vh[hh % 2], YTh[hh % 2]
                hs = slice(hh * 128, (hh + 1) * 128)
                P.dma("sp", lambda e: e.dma_start(out=kt_[:], in_=self.KT[hs, :]), kt_.b, writes=[kt_.b])
                P.dma("act", lambda e: e.dma_start(out=qt_[:], in_=self.QT[hs, :]), qt_.b, writes=[qt_.b])
                P.dma("sp", lambda e: e.dma_start(out=vh_[:, :, 0:128], in_=self.VS.rearrange("(kt p) e -> p kt e", p=128)[:, :, hs]), vh_.b, writes=[vh_.b])
                sbanks = [self.ps[0], self.ps[1], self.ps[6]]
                tasks = []
                for qb in range(NQB):
                    for cc in range(2):
                        for kt in range(4 * qb + 4):
                            tasks.append((qb, cc, kt))

                def score(ti):
                    qb, cc, kt = tasks[ti]
                    rows = slice(64 * cc, 64 * cc + 64)
                    c0 = max(kt - 4 * qb, 0) * 128
                    pS = sbanks[ti % 3]
                    P.op("pe", lambda e: e.matmul(pS[:, c0:512], lhsT=kt_[rows, kt * 128:(kt + 1) * 128], rhs=qt_[rows, qb * 512 + c0:(qb + 1) * 512],
                                                  start=True, stop=True), reads=[kt_.b, qt_.b], writes=[pS.b])

                def combine(qb):
                    qs = slice(qb * 512, (qb + 1) * 512)
                    P.op("dve", lambda e: e.reciprocal(out=rz[:, 0, :], in_=Oc[0][:, :, 128]), reads=[Oc[0].b], writes=[rz.b])
                    P.op("dve", lambda e: e.reciprocal(out=rz[:, 1, :], in_=Oc[1][:, :, 128]), reads=[Oc[1].b], writes=[rz.b])
                    P.op("dve", lambda e: e.tensor_scalar(out=rz[:, 1, :], in0=rz[:, 1, :], scalar1=neglam, scalar2=None, op0=ALU.mult), reads=[rz.b, ls.b], writes=[rz.b])
                    P.op("pool", lambda e: e.tensor_tensor(out=y[:], in0=Oc[0][:, :, 0:128], in1=rz[:, 0, :].unsqueeze(2).to_broadcast([128, 4, 128]), op=ALU.mult),
                         reads=[Oc[0].b, rz.b], writes=[y.b])
                    P.op("pool", lambda e: e.tensor_tensor(out=ysq[:], in0=Oc[1][:, :, 0:128], in1=rz[:, 1, :].unsqueeze(2).to_broadcast([128, 4, 128]), op=ALU.mult),
                         reads=[Oc[1].b, rz.b], writes=[ysq.b])
                    P.op("dve", lambda e: e.tensor_tensor(out=y[:], in0=y[:], in1=ysq[:], op=ALU.add), reads=[y.b, ysq.b], writes=[y.b])
                    if "at_o" in self.dbg:
                        dtl = self.dbg_tile
                        self.copy("dve", dtl[:], y[:], [y.b], [dtl.b])
                        dd = self.dbg["at_o"][qb * 512:(qb + 1) * 512, hs].rearrange("(a p) e -> p a e", p=128)
                        P.dma("act", lambda e: e.dma_start(out=dd, in_=dtl[:]), dtl.b, reads=[dtl.b])
                    P.op("pool", lambda e: e.tensor_tensor(out=ysq[:], in0=y[:], in1=y[:], op=ALU.mult), reads=[y.b], writes=[ysq.b])
                    P.op("dve", lambda e: e.tensor_reduce(out=ss[:], in_=ysq[:], axis=AX.X, op=ALU.add), reads=[ysq.b], writes=[ss.b])
                    P.op("act", lambda e: e.activation(out=ss[:], in_=ss[:], func=AF.Sqrt, bias=eps1[:], scale=1.0 / 128), reads=[ss.b, eps1.b], writes=[ss.b])
                    P.op("dve", lambda e: e.reciprocal(out=ss[:], in_=ss[:]), reads=[ss.b], writes=[ss.b])
                    P.op("pool", lambda e: e.tensor_tensor(out=y[:], in0=y[:], in1=ss[:].unsqueeze(2).to_broadcast([128, 4, 128]), op=ALU.mult),
                         reads=[y.b, ss.b], writes=[y.b])
                    P.op("dve", lambda e: e.tensor_tensor(out=ynb[:], in0=y[:], in1=subw[:].unsqueeze(1).to_broadcast([128, 4, 128]), op=ALU.mult),
                         reads=[y.b, subw.b], writes=[ynb.b])
                    for qt in range(4):
                        P.op("pe", lambda e: e.transpose(out=self.psbf[:, qt * 128:(qt + 1) * 128], in_=ynb[:, qt, :], identity=self.identb[:]),
                             reads=[ynb.b, self.identb.b], writes=[self.psbf.b])
                    P.op("act", lambda e: e.copy(out=yt_[:, qs], in_=self.psbf[:, 0:512]), reads=[self.psbf.b], writes=[yt_.b])

                LOOK = 2
                for ti in range(min(LOOK, len(tasks))):
                    score(ti)
                for ti in range(len(tasks)):
                    if ti + LOOK < len(tasks):
                        score(ti + LOOK)
                    qb, cc, kt = tasks[ti]
                    r = kt - 4 * qb
                    c0 = max(r, 0) * 128
                    pS = sbanks[ti % 3]
                    pO = [self.ps[2 + 2 * cc], self.ps[3 + 2 * cc]]
                    et = ET[ti % 4]
                    P.op("act", lambda e: e.activation(out=et[:, c0:512], in_=pS[:, c0:512], func=AF.Exp, scale=0.125), reads=[pS.b], writes=[et.b])
                    if r >= 0:
                        P.op("pool", lambda e: e.tensor_tensor(out=et[:, c0:c0 + 128], in0=et[:, c0:c0 + 128], in1=cmaskb[:], op=ALU.mult),
                             reads=[et.b, cmaskb.b], writes=[et.b])
                    for qt in range(max(r, 0), 4):
                        po = pO[qt // 2]
                        oc = (qt % 2) * 129
                        P.op("pe", lambda e: e.matmul(po[:, oc:oc + 129], lhsT=et[:, qt * 128:(qt + 1) * 128], rhs=vh_[:, kt, :],
                                                      start=(kt == 0 and qt % 2 == 0), stop=(kt == 4 * qb + qt), skip_group_check=True),
                             reads=[et.b, vh_.b], writes=[po.b])
                    if kt == 4 * qb + 3:
                        for i2 in range(2):
                            self.copy(("act", "dve")[i2], Oc[cc][:, 2 * i2:2 * i2 + 2, :].rearrange("p a e -> p (a e)"), pO[i2][:, 0:258], [pO[i2].b], [Oc[cc].b])
                        if cc == 1:
                            combine(qb)
                P.dma("sp", lambda e: e.dma_start(out=self.YT[hs, :], in_=yt_[:]), yt_.b, reads=[yt_.b])
            self.end_phase()

        with ExitStack() as es:
            self._stg = None
            self._stg_n = 6
            Wo = self.tile(es, "aWo", [128, 8, D], BF16)
            s3 = inp["b_w_o"][jl].rearrange("(kc p) n -> p kc n", p=128)
            self.load_cast(es, [(Wo[:, kc:kc + 1, :], s3[:, kc:kc + 1, :], 128, 1, D) for kc in range(8)], Wo.b)
            xs = [self.tile(es, "cx%d" % i, [128, 8, TB], dma=True) for i in range(2)]
            ys = [self.tile(es, "cy%d" % i, [128, 8, TB], BF16, dma=True) for i in range(2)]
            for nb in range(T // TB):
                t0 = nb * TB
                x = xs[nb % 2]
                yb = ys[nb % 2]
                src = xsrc.rearrange("(j p) t -> p j t", p=128)[:, :, t0:t0 + TB]
                P.dma("sp", lambda e: e.dma_start(out=x[:], in_=src), x.b, writes=[x.b])
                srcy = self.YT.rearrange("(j p) t -> p j t", p=128)[:, :, t0:t0 + TB]
                P.dma("act", lambda e: e.dma_start(out=yb[:], in_=srcy), yb.b, writes=[yb.b])
                for nj in range(8):
                    ps = self.psum_next()
                    for kc in range(8):
                        P.op("pe", lambda e: e.matmul(ps[:, :], lhsT=Wo[:, kc, nj * 128:(nj + 1) * 128], rhs=yb[:, kc, :], start=(kc == 0), stop=(kc == 7)),
                             reads=[Wo.b, yb.b], writes=[ps.b])
                    P.op("dve", lambda e: e.scalar_tensor_tensor(out=x[:, nj, :], in0=ps[:, :], scalar=g1c[:, nj:nj + 1], in1=x[:, nj, :], op0=ALU.mult, op1=ALU.add),
                         reads=[ps.b, x.b, self.modc.b], writes=[x.b])
                dst = self.X.rearrange("(j p) t -> p j t", p=128)[:, :, t0:t0 + TB]
                P.dma("sp", lambda e: e.dma_start(out=dst, in_=x[:]), x.b, reads=[x.b])
            self.end_phase()


_CONSTS = None


def prepare_inputs(inputs):
    global _CONSTS
    if _CONSTS is None:
        _CONSTS = make_consts()
    f = lambda a: np.ascontiguousarray(np.asarray(a, np.float32))
    vecs = {}
    for l in range(4):
        vecs["ada_b%d" % l] = inputs["ada_b"][l]
        vecs["norm1_%d" % l] = inputs["norm1"][l]
        vecs["norm2_%d" % l] = inputs["norm2"][l]
        for i in range(3):
            vecs["cw%d_%d" % (i, l)] = inputs["ffn_conv_w"][l][i]
        vecs["cb_%d" % l] = inputs["ffn_conv_b"][l]
    vecs["final_norm"] = inputs["final_norm"]
    vecs["kv_norm"] = inputs["kv_norm"]
    for l in range(2):
        for i in range(6):
            vecs["mu%d_%d" % (i, l)] = inputs["a_mu"][l][i]
        vecs["w0_%d" % l] = inputs["a_w0"][l]
        vecs["a0_%d" % l] = inputs["a_a0"][l]
        vecs["k_k_%d" % l] = inputs["a_k_k"][l]
        vecs["k_a_%d" % l] = inputs["a_k_a"][l]
        vecs["r_k_%d" % l] = np.asarray(inputs["a_r_k"][l]).reshape(-1)
    cols = CP.pack(vecs)
    rows = np.concatenate([
        f(inputs["a_ln_w"][0]), f(inputs["a_ln_b"][0]), f(inputs["a_ln_w"][1]), f(inputs["a_ln_b"][1]),
        f(inputs["a_v0"][0]),
        np.tile(f(inputs["b_subln"][0]), 8), np.tile(f(inputs["b_subln"][1]), 8),
        f(inputs["b_lam"][0]).reshape(-1), f(inputs["b_lam"][1]).reshape(-1)])
    shared = dict(_CONSTS)
    shared["cols"] = cols
    shared["rows"] = rows
    for k in ("ada_w", "a_w_rkv", "a_w1", "a_w2", "a_a1", "a_a2", "a_v1", "a_v2", "a_g1", "a_g2", "a_w_o",
              "w_kv", "b_w_q", "b_w_o", "ffn_w_up", "ffn_w_down"):
        shared[k] = f(inputs[k])
    x = np.asarray(inputs["x"], np.float32)
    c = np.asarray(inputs["c"], np.float32)
    in_maps = []
    for b in range(NCORES):
        m = dict(shared)
        m["xT"] = np.ascontiguousarray(x[b].T)
        m["ccol"] = np.ascontiguousarray(c[b].reshape(8, 128).T)
        in_maps.append(m)
    return in_maps


_NC_CACHE = {}


def run(inputs, cfg, key="full", ncores=NCORES):
    if key not in _NC_CACHE:
        _NC_CACHE[key] = Builder(cfg).build()
    nc = _NC_CACHE[key]
    in_maps = prepare_inputs(inputs)[:ncores]
    res = run_bass_kernel_spmd(nc, in_maps, core_ids=list(range(ncores)))
    return res


def kernel(**inputs):
    cfg = {"layers": [0, 1, 2, 3]}
    res = run(inputs, cfg)
    out = np.stack([np.ascontiguousarray(r["outT"].T) for r in res.results], axis=0)
    return out.astype(np.float32)
```

```python
import math
import numpy as np
import concourse.bass as bass
import concourse.mybir as mybir
from concourse.bass_utils import run_bass_kernel_spmd
from contextlib import ExitStack
import types

F32 = mybir.dt.float32
BF16 = mybir.dt.bfloat16
AF = mybir.ActivationFunctionType
ALU = mybir.AluOpType
AX = mybir.AxisListType

D = 1024
T = 4096
NJ = 8
DFF = 2816
F2 = 5632
NF = 44
NG = 22
C0 = math.exp(-0.5)
NCORES = 8

ENGS = ["pe", "act", "dve", "pool", "sp"]


def freeze(fn):
    if fn.__closure__ is None:
        return fn
    cells = []
    for c in fn.__closure__:
        try:
            cells.append(types.CellType(c.cell_contents))
        except ValueError:
            cells.append(c)
    return types.FunctionType(fn.__code__, fn.__globals__, fn.__name__, fn.__defaults__, tuple(cells))


class Buf:
    __slots__ = ("name", "lw", "rd", "dsem", "excl")

    def __init__(self, name):
        self.name = name
        self.lw = None
        self.rd = {}
        self.dsem = None
        self.excl = False


class Tl:
    __slots__ = ("ap", "b")

    def __init__(self, ap, b):
        self.ap = ap
        self.b = b

    def __getitem__(self, k):
        return self.ap[k]


class Prog:
    def __init__(self, nc, es, n_dma_sems=40):
        self.nc = nc
        self.es = es
        self.q = {e: [] for e in ENGS}
        self.cnt = {e: 0 for e in ENGS}
        self.sems = {}
        self.semkey = 0
        self.esem = {e: self._newsem("c_" + e) for e in ENGS}
        self.seen = {e: {} for e in ENGS}
        self.bar = self._newsem("bar")
        self.nbar = 0
        self.dma_pool = [self._newsem("d%d" % i) for i in range(n_dma_sems)]
        self.dma_cnt = {k: 0 for k in self.dma_pool}
        self.dma_free = list(self.dma_pool)
        self.ninstr = 0

    def _newsem(self, name):
        s = self.es.enter_context(self.nc.semaphore(name))
        self.semkey += 1
        self.sems[self.semkey] = s
        return self.semkey

    def buf(self, name):
        return Buf(name)

    def dma_buf(self, name):
        b = Buf(name)
        b.dsem = self.dma_free.pop(0)
        return b

    def release(self, bufs):
        for b in bufs:
            if b.dsem is not None:
                self.dma_free.append(b.dsem)
                b.dsem = None

    def _waits(self, e, reads, writes, is_dma=False):
        need = {}
        seen = self.seen[e]

        def add(ev, raw):
            key, val, src = ev
            if src == e and not is_dma and (e in ("pe", "sp") or not raw):
                return
            if seen.get(key, 0) >= val:
                return
            if need.get(key, 0) < val:
                need[key] = val

        for b in reads:
            if b.lw is not None:
                add(b.lw, True)
            if b.excl:
                for src, ev in b.rd.items():
                    if src != e:
                        add(ev, False)
        for b in writes:
            if b.lw is not None:
                add(b.lw, False)
            for ev in b.rd.values():
                add(ev, False)
        out = []
        for key, val in need.items():
            seen[key] = val
            out.append((self.sems[key], val))
        return out

    def _emit(self, e, fn, waits, sem, inc):
        fn = freeze(fn)

        attach = (inc == 1 and e in ("act", "dve", "pool") and len(waits) > 0)

        def run(eng, fn=fn, waits=waits, sem=sem, inc=inc, attach=attach):
            for (s, v) in (waits[:-1] if attach else waits):
                eng.wait_ge(s, v)
            ins = fn(eng)
            if attach:
                ins._wait_ge(waits[-1][0], waits[-1][1])
            ins.then_inc(sem, inc)
        self.q[e].append(run)
        self.ninstr += 1 + len(waits)

    def op(self, e, fn, reads=(), writes=()):
        waits = self._waits(e, reads, writes)
        self.cnt[e] += 1
        key = self.esem[e]
        self._emit(e, fn, waits, self.sems[key], 1)
        ev = (key, self.cnt[e], e)
        for b in writes:
            b.lw = ev
            b.rd = {}
        for b in reads:
            if b not in writes:
                b.rd[e] = ev

    def dma(self, e, fn, owner, reads=(), writes=()):
        assert owner.dsem is not None, owner.name
        waits = self._waits(e, reads, writes, is_dma=True)
        key = owner.dsem
        self.dma_cnt[key] += 16
        self._emit(e, fn, waits, self.sems[key], 16)
        src = "dma%d" % key
        ev = (key, self.dma_cnt[key], src)
        for b in writes:
            b.lw = ev
            b.rd = {}
        for b in reads:
            if b not in writes:
                b.rd[src] = ev

    def barrier(self):
        g = "sp"
        gw = []
        for e in ENGS:
            if e == g or self.cnt[e] == 0:
                continue
            key = self.esem[e]
            if self.seen[g].get(key, 0) < self.cnt[e]:
                gw.append((self.sems[key], self.cnt[e]))
        for key in self.dma_pool:
            val = self.dma_cnt[key]
            if val > 0 and self.seen[g].get(key, 0) < val:
                gw.append((self.sems[key], val))
        self.nbar += 1
        bsem = self.sems[self.bar]

        def run_g(eng, gw=gw, bsem=bsem):
            for (s, v) in gw:
                eng.wait_ge(s, v)
            eng.sem_inc(bsem, 1)
        self.q[g].append(run_g)
        for e in ENGS:
            if e != g:
                self.q[e].append(lambda eng, sem=bsem, val=self.nbar: eng.wait_ge(sem, val))
        for e in ENGS:
            for e2 in ENGS:
                self.seen[e][self.esem[e2]] = self.cnt[e2]
            for key in self.dma_pool:
                self.seen[e][key] = self.dma_cnt[key]
        for e in ENGS:
            if self.cnt[e] > 12000:
                self.esem[e] = self._newsem("c_%s_%d" % (e, self.nbar))
                self.cnt[e] = 0

    def finish(self):
        nc = self.nc
        self.barrier()
        with nc.Block() as block:
            @block.tensor
            def _(eng):
                for f in self.q["pe"]:
                    f(eng)

            @block.scalar
            def _(eng):
                for f in self.q["act"]:
                    f(eng)

            @block.vector
            def _(eng):
                for f in self.q["dve"]:
                    f(eng)

            @block.gpsimd
            def _(eng):
                for f in self.q["pool"]:
                    f(eng)

            @block.sync
            def _(eng):
                for f in self.q["sp"]:
                    f(eng)


class ColPack:
    def __init__(self):
        self.off = {}
        self.n = 0
        self.items = []

    def add(self, name, length):
        assert length % 128 == 0
        self.off[name] = self.n
        self.n += length // 128
        self.items.append((name, length))

    def pack(self, vecs):
        arr = np.zeros((128, self.n), np.float32)
        for name, length in self.items:
            v = np.asarray(vecs[name], np.float32).reshape(length // 128, 128)
            arr[:, self.off[name]:self.off[name] + length // 128] = v.T
        return arr


def make_colpack():
    cp = ColPack()
    for l in range(4):
        cp.add("ada_b%d" % l, 6 * D)
        cp.add("norm1_%d" % l, D)
        cp.add("norm2_%d" % l, D)
        for i in range(3):
            cp.add("cw%d_%d" % (i, l), F2)
        cp.add("cb_%d" % l, F2)
    cp.add("final_norm", D)
    cp.add("kv_norm", D)
    for l in range(2):
        for i in range(6):
            cp.add("mu%d_%d" % (i, l), D)
        for nm in ("w0", "a0", "k_k", "k_a", "r_k"):
            cp.add("%s_%d" % (nm, l), D)
    return cp


CP = make_colpack()


def make_consts():
    c = {}
    c["ident"] = np.eye(128, dtype=np.float32)
    c["ones"] = np.ones((128, 128), np.float32)
    bd = np.zeros((128, 128), np.float32)
    bd[:64, :64] = 1
    bd[64:, 64:] = 1
    c["bd"] = bd
    ind = np.zeros((128, 8, 16), np.float32)
    for p in range(128):
        for j in range(8):
            ind[p, j, 2 * j + p // 64] = 1
    c["ind"] = ind.reshape(128, 128)
    s = np.arange(128)[:, None]
    t = np.arange(128)[None, :]
    strict = (t > s).astype(np.float32)
    incl = (t >= s).astype(np.float32)
    c["mask4"] = np.concatenate([strict, incl, strict, incl], axis=1)
    c["maskL"] = (t < s).astype(np.float32)
    pos = np.arange(T, dtype=np.float64)
    inv = 500000.0 ** (-np.arange(0, 16, 2, dtype=np.float64) / 16)
    ct = np.ones((128, T), np.float64)
    st = np.zeros((128, T), np.float64)
    rm = np.zeros((128, 128), np.float32)
    for cc in range(2):
        for d in range(16):
            p = cc * 64 + d
            ang = pos * inv[d % 8]
            ct[p] = np.cos(ang)
            st[p] = np.sin(ang)
            if d < 8:
                rm[p + 8, p] = -1.0
            else:
                rm[p - 8, p] = 1.0
    c["ropec"] = ct.astype(np.float32)
    c["ropes"] = st.astype(np.float32)
    c["roper"] = rm
    return c


class Builder:
    def __init__(self, cfg):
        self.cfg = cfg
        self.nc = bass.Bass("TRN2", target_bir_lowering=False)
        self.uid = 0
        self.rr = 0

    def dram_in(self, name, shape, dt=F32):
        return self.nc.dram_tensor(name, list(shape), dt, kind="ExternalInput").ap()

    def tile(self, es, name, shape, dt=F32, dma=False):
        self.uid += 1
        t = es.enter_context(self.nc.sbuf_tensor("%s_%d" % (name, self.uid), list(shape), dt))
        b = self.P.dma_buf(name) if dma else self.P.buf(name)
        if dma:
            self.phase_dma_bufs.append(b)
        return Tl(t, b)

    def eng_rr(self, engs=("dve", "pool", "act")):
        self.rr += 1
        return engs[self.rr % len(engs)]

    def copy(self, e, out_ap, in_ap, reads, writes):
        P = self.P
        if e == "act":
            P.op("act", lambda g: g.copy(out=out_ap, in_=in_ap), reads=reads, writes=writes)
        else:
            P.op(e, lambda g: g.tensor_copy(out=out_ap, in_=in_ap), reads=reads, writes=writes)

    def psum_next(self):
        self.psi = (self.psi + 1) % 6
        return self.ps[self.psi]

    def build(self):
        nc = self.nc
        cfg = self.cfg
        inp = {}
        inp["xT"] = self.dram_in("xT", [D, T])
        inp["ccol"] = self.dram_in("ccol", [128, 8])
        inp["cols"] = self.dram_in("cols", [128, CP.n])
        for k in ("ident", "ones", "bd", "ind", "maskL", "roper"):
            inp[k] = self.dram_in(k, [128, 128])
        inp["mask4"] = self.dram_in("mask4", [128, 512])
        inp["ropec"] = self.dram_in("ropec", [128, T])
        inp["ropes"] = self.dram_in("ropes", [128, T])
        inp["ada_w"] = self.dram_in("ada_w", [4, D, 6 * D])
        inp["a_w_rkv"] = self.dram_in("a_w_rkv", [2, 3, D, D])
        inp["a_w1"] = self.dram_in("a_w1", [2, D, 64])
        inp["a_w2"] = self.dram_in("a_w2", [2, 64, D])
        inp["a_a1"] = self.dram_in("a_a1", [2, D, 64])
        inp["a_a2"] = self.dram_in("a_a2", [2, 64, D])
        inp["a_v1"] = self.dram_in("a_v1", [1, D, 32])
        inp["a_v2"] = self.dram_in("a_v2", [1, 32, D])
        inp["a_g1"] = self.dram_in("a_g1", [2, D, 160])
        inp["a_g2"] = self.dram_in("a_g2", [2, 160, D])
        inp["a_w_o"] = self.dram_in("a_w_o", [2, D, D])
        inp["rows"] = self.dram_in("rows", [7 * D + 512])
        inp["w_kv"] = self.dram_in("w_kv", [D, 2 * D])
        inp["b_w_q"] = self.dram_in("b_w_q", [2, D, D])
        inp["b_w_o"] = self.dram_in("b_w_o", [2, D, D])
        inp["ffn_w_up"] = self.dram_in("ffn_w_up", [4, D, F2])
        inp["ffn_w_down"] = self.dram_in("ffn_w_down", [4, DFF, D])
        self.inp = inp
        self.outT = nc.dram_tensor("outT", [D, T], F32, kind="ExternalOutput").ap()
        self.X = nc.dram_tensor("Xs", [D, T], F32).ap()
        self.VF = nc.dram_tensor("VFs", [T, D], F32).ap()
        self.KT = nc.dram_tensor("KTs", [D, T], BF16).ap()
        self.VS = nc.dram_tensor("VSs", [T, D], BF16).ap()
        self.QT = nc.dram_tensor("QTs", [D, T], BF16).ap()
        self.YT = nc.dram_tensor("YTs", [D, T], BF16).ap()
        self.dbg = {}
        for name, shape in cfg.get("dbg", {}).items():
            self.dbg[name] = nc.dram_tensor("dbg_" + name, list(shape), F32, kind="ExternalOutput").ap()

        with ExitStack() as es:
            self.P = P = Prog(nc, es)
            self.phase_dma_bufs = []
            self.ps = []
            for i in range(7):
                t = es.enter_context(nc.psum_tensor("psb%d" % i, [128, 512], F32))
                self.ps.append(Tl(t, P.buf("psb%d" % i)))
                self.ps[-1].b.excl = True
            t = es.enter_context(nc.psum_tensor("psbf", [128, 1024], BF16))
            self.psbf = Tl(t, P.buf("psbf"))
            self.psbf.b.excl = True
            self.psi = 0
            g = es
            self.cols = self.tile(g, "cols", [128, CP.n], dma=True)
            self.ccol = self.tile(g, "ccol", [128, 8], dma=True)
            self.cst = {}
            for k in ("ident", "ones", "bd", "ind", "maskL", "roper"):
                self.cst[k] = self.tile(g, k, [128, 128], dma=True)
            self.cst["mask4"] = self.tile(g, "mask4", [128, 512], dma=True)
            P.dma("sp", lambda e: e.dma_start(out=self.cols[:], in_=inp["cols"]), self.cols.b, writes=[self.cols.b])
            P.dma("sp", lambda e: e.dma_start(out=self.ccol[:], in_=inp["ccol"]), self.ccol.b, writes=[self.ccol.b])
            for k, tl in self.cst.items():
                P.dma("act", lambda e, tl=tl, k=k: e.dma_start(out=tl[:], in_=inp[k]), tl.b, writes=[tl.b])
            self.eps = self.tile(g, "eps", [128, 1])
            P.op("pool", lambda e: e.memset(self.eps[:], 1e-6), writes=[self.eps.b])
            self.identb = self.tile(g, "identb", [128, 128], BF16)
            self.copy("pool", self.identb[:], self.cst["ident"][:], [self.cst["ident"].b], [self.identb.b])
            self.modc = self.tile(g, "modc", [128, 192])
            self.der = self.tile(g, "der", [128, 4, 2, 8])

            self.phase_mod()
            xsrc = inp["xT"]
            for l in cfg["layers"]:
                if cfg.get("mixer", True):
                    if l < 2:
                        self.phase_rwkv(l, xsrc)
                    else:
                        if l == 2 or cfg.get("force_kv", False):
                            self.phase_kv(xsrc)
                        self.phase_attn(l, xsrc)
                    xsrc = self.X
                fuse_final = (l == cfg["layers"][-1]) and cfg.get("ffn", True) and cfg.get("final", True) and cfg.get("fuse_final", True)
                if cfg.get("ffn", True):
                    self.phase_ffn(l, xsrc, fuse_final)
                    xsrc = self.X
            if not fuse_final:
                self.phase_final(xsrc, cfg.get("final", True))
            P.finish()
        return nc

    def col(self, name, j0=0, n=None):
        o = CP.off[name] + j0
        if n is None:
            n = 1
        return self.cols[:, o:o + n]

    def end_phase(self):
        self.P.barrier()
        self.P.release(self.phase_dma_bufs)
        self.phase_dma_bufs = []

    def phase_mod(self):
        P, nc, inp = self.P, self.nc, self.inp
        with ExitStack() as es:
            cact = self.tile(es, "cact", [128, 8])
            P.op("act", lambda e: e.activation(out=cact[:], in_=self.ccol[:], func=AF.Silu),
                 reads=[self.ccol.b], writes=[cact.b])
            A = [self.tile(es, "adaA%d" % i, [128, 8, 768], dma=True) for i in range(4)]
            psm = self.ps[0]
            it = 0
            for l in range(4):
                for blk in range(8):
                    a = A[it % 4]
                    it += 1
                    src = inp["ada_w"][l].rearrange("(kc p) n -> p kc n", p=128)[:, :, blk * 768:(blk + 1) * 768]
                    P.dma("sp" if it % 2 else "act", lambda e, a=a, src=src: e.dma_start(out=a[:], in_=src), a.b, writes=[a.b])
                    for n_ in range(6):
                        colidx = l * 48 + blk * 6 + n_
                        for kc in range(8):
                            P.op("pe", lambda e, a=a, kc=kc, n_=n_, colidx=colidx: e.matmul(
                                psm[:, colidx:colidx + 1], lhsT=a[:, kc, n_ * 128:(n_ + 1) * 128], rhs=cact[:, kc:kc + 1],
                                start=(kc == 0), stop=(kc == 7)), reads=[a.b, cact.b], writes=[psm.b])
            for l in range(4):
                o = CP.off["ada_b%d" % l]
                P.op("dve", lambda e, l=l, o=o: e.tensor_tensor(out=self.modc[:, l * 48:(l + 1) * 48], in0=psm[:, l * 48:(l + 1) * 48],
                                                                in1=self.cols[:, o:o + 48], op=ALU.add),
                     reads=[psm.b, self.cols.b], writes=[self.modc.b])
            for l in range(4):
                for which in range(2):
                    sc = self.modc[:, l * 48 + which * 24 + 8: l * 48 + which * 24 + 16]
                    nm = self.col("norm%d_%d" % (which + 1, l), 0, 8)
                    P.op("dve", lambda e, l=l, which=which, sc=sc, nm=nm: e.scalar_tensor_tensor(
                        out=self.der[:, l, which, :], in0=sc, scalar=1.0, in1=nm, op0=ALU.add, op1=ALU.mult),
                        reads=[self.modc.b, self.cols.b], writes=[self.der.b])
            if "modc" in self.dbg:
                dt = self.tile(es, "dbgm", [128, 192], dma=True)
                self.copy("dve", dt[:], self.modc[:], [self.modc.b], [dt.b])
                P.dma("sp", lambda e: e.dma_start(out=self.dbg["modc"], in_=dt[:]), dt.b, reads=[dt.b])
            self.end_phase()

    def modcol(self, l, i, j0=0, n=8):
        o = l * 48 + i * 8 + j0
        return self.modc[:, o:o + n]

    def rmsnorm(self, x, N, sq, G, S, out_ap, out_b, extra_reads=()):
        P = self.P
        ones = self.cst["ones"]
        P.op("act", lambda e: e.activation(out=sq[:], in_=x[:], func=AF.Square), reads=[x.b], writes=[sq.b])
        ps = self.psum_next()
        for j in range(8):
            P.op("pe", lambda e, j=j: e.matmul(ps[:, 0:N], lhsT=ones[:], rhs=sq[:, j, :], start=(j == 0), stop=(j == 7)),
                 reads=[ones.b, sq.b], writes=[ps.b])
        rs = self.rstd
        P.op("act", lambda e: e.activation(out=rs[:, 0:N], in_=ps[:, 0:N], func=AF.Sqrt, bias=self.eps[:], scale=1.0 / D),
             reads=[ps.b, self.eps.b], writes=[rs.b])
        P.op("dve", lambda e: e.reciprocal(out=rs[:, 0:N], in_=rs[:, 0:N]), reads=[rs.b], writes=[rs.b])
        P.op("dve", lambda e: e.tensor_tensor(out=sq[:], in0=x[:], in1=rs[:, 0:N].unsqueeze(1).to_broadcast([128, 8, N]), op=ALU.mult),
             reads=[x.b, rs.b], writes=[sq.b])
        if S is None:
            P.op("pool", lambda e: e.tensor_tensor(out=out_ap, in0=sq[:], in1=G.unsqueeze(2).to_broadcast([128, 8, N]), op=ALU.mult),
                 reads=[sq.b, self.cols.b, self.der.b] + list(extra_reads), writes=[out_b])
        else:
            P.op("pool", lambda e: e.tensor_tensor(out=sq[:], in0=sq[:], in1=G.unsqueeze(2).to_broadcast([128, 8, N]), op=ALU.mult),
                 reads=[sq.b, self.cols.b, self.der.b], writes=[sq.b])
            P.op("pool", lambda e: e.tensor_tensor(out=out_ap, in0=sq[:], in1=S.unsqueeze(2).to_broadcast([128, 8, N]), op=ALU.add),
                 reads=[sq.b, self.modc.b] + list(extra_reads), writes=[out_b])

    def load_cast(self, es_stage, pieces, wb):
        P = self.P
        if not hasattr(self, "_stg") or self._stg is None:
            self._stg = [self.tile(es_stage, "stg%d" % i, [128, 1024], dma=True) for i in range(getattr(self, "_stg_n", 2))]
            self._stgi = 0
        for (dst, src, p, a, b) in pieces:
            assert a * b <= 1024
            st = self._stg[self._stgi % len(self._stg)]
            self._stgi += 1
            sv = st[0:p, 0:a * b].rearrange("p (a b) -> p a b", a=a)
            q = ("sp", "act")[self._stgi % 2]
            P.dma(q, lambda e, sv=sv, src=src: e.dma_start(out=sv, in_=src), st.b, writes=[st.b])
            self.copy(self.eng_rr(("pool", "dve", "act")), dst, sv, [st.b], [wb])

    def phase_ffn(self, l, xsrc, fuse_final=False):
        P, nc, inp = self.P, self.nc, self.inp
        TB = 256
        NB = T // TB
        with ExitStack() as es:
            self._stg = None
            self._stg_n = 6
            ses = ExitStack()
            wup = self.tile(es, "wup", [128, 8, F2], BF16)
            wdn = self.tile(es, "wdn", [128, NG, D], BF16)
            pieces = []
            srcu = inp["ffn_w_up"][l].rearrange("(kc p) n -> p kc n", p=128)
            for kc in range(8):
                for (n0, n1) in ((0, 1024), (1024, 2048), (2048, 3072), (3072, 4096), (4096, 5120), (5120, F2)):
                    pieces.append((wup[:, kc:kc + 1, n0:n1], srcu[:, kc:kc + 1, n0:n1], 128, 1, n1 - n0))
            self.load_cast(ses, pieces, wup.b)
            srcd = inp["ffn_w_down"][l].rearrange("(kc p) n -> p kc n", p=128)
            pieces = [(wdn[:, i:i + 1, :], srcd[:, i:i + 1, :], 128, 1, D) for i in range(0, NG)]
            self.load_cast(ses, pieces, wdn.b)
            P.barrier()
            ses.close()
            self._stg = None
            self._stg_n = 2

            xs = [self.tile(es, "fx%d" % i, [128, 8, TB], dma=True) for i in range(2)]
            sq = self.tile(es, "fsq", [128, 8, TB])
            self.rstd = self.tile(es, "frstd", [128, TB])
            h2s = [self.tile(es, "fh2_%d" % i, [128, 8, TB + 2], BF16) for i in range(2)]
            for i in range(2):
                P.op("pool", lambda e: e.memset(h2s[i][:], 0.0), writes=[h2s[i].b])
            NCV = 6
            cv = [self.tile(es, "fcv%d" % i, [128, TB]) for i in range(NCV)]
            sgl = [self.tile(es, "fsg%d" % i, [128, TB]) for i in range(3)]
            hm = self.tile(es, "fhm", [128, NG, TB], BF16)
            G2 = self.der[:, l, 1, :]
            S2 = self.modcol(l, 3)
            g2c = self.modcol(l, 5)
            cw = [CP.off["cw%d_%d" % (i, l)] for i in range(3)]
            cb = CP.off["cb_%d" % l]
            xr = lambda ap: ap.rearrange("(j p) t -> p j t", p=128)

            def norm(nb):
                x = xs[nb % 2]
                h2 = h2s[nb % 2]
                t0 = nb * TB
                P.dma("sp", lambda e: e.dma_start(out=x[:], in_=xr(xsrc)[:, :, t0:t0 + TB]), x.b, writes=[x.b])
                if nb > 0:
                    hp = h2s[(nb - 1) % 2]
                    P.op("pool", lambda e: e.tensor_copy(out=h2[:, :, 0:2], in_=hp[:, :, TB:TB + 2]), reads=[hp.b], writes=[h2.b])
                self.rmsnorm(x, TB, sq, G2, S2, h2[:, :, 2:TB + 2], h2.b)

            def up(nb):
                h2 = h2s[nb % 2]
                ui = 0
                for i in range(NG):
                    for which in range(2):
                        n = i + which * NG
                        ps = self.psum_next()
                        for kc in range(8):
                            P.op("pe", lambda e: e.matmul(ps[:, 0:TB + 2], lhsT=wup[:, kc, n * 128:(n + 1) * 128], rhs=h2[:, kc, :],
                                                          start=(kc == 0), stop=(kc == 7)), reads=[wup.b, h2.b], writes=[ps.b])
                        c = cv[ui % NCV]
                        ui += 1
                        P.op("act", lambda e: e.activation(out=c[:], in_=ps[:, 2:TB + 2], func=AF.Identity,
                                                           bias=self.cols[:, cb + n:cb + n + 1], scale=self.cols[:, cw[2] + n:cw[2] + n + 1]),
                             reads=[ps.b, self.cols.b], writes=[c.b])
                        if which == 1:
                            sg_ = sgl[i % 3]
                            P.op("act", lambda e: e.activation(out=sg_[:], in_=cg[:], func=AF.Silu), reads=[cg.b], writes=[sg_.b])
                        P.op("dve", lambda e: e.scalar_tensor_tensor(out=c[:], in0=ps[:, 1:TB + 1], scalar=self.cols[:, cw[1] + n:cw[1] + n + 1],
                                                                     in1=c[:], op0=ALU.mult, op1=ALU.add), reads=[ps.b, c.b, self.cols.b], writes=[c.b])
                        P.op("dve", lambda e: e.scalar_tensor_tensor(out=c[:], in0=ps[:, 0:TB], scalar=self.cols[:, cw[0] + n:cw[0] + n + 1],
                                                                     in1=c[:], op0=ALU.mult, op1=ALU.add), reads=[ps.b, c.b, self.cols.b], writes=[c.b])
                        s_ = sgl[i % 3]
                        if which == 0:
                            cg = c
                        else:
                            P.op("pool", lambda e: e.tensor_tensor(out=hm[:, i, :], in0=s_[:], in1=c[:], op=ALU.mult),
                                 reads=[s_.b, c.b], writes=[hm.b])

            def down(nb):
                x = xs[nb % 2]
                t0 = nb * TB
                for nj in range(8):
                    ps = self.psum_next()
                    for i in range(NG):
                        P.op("pe", lambda e: e.matmul(ps[:, 0:TB], lhsT=wdn[:, i, nj * 128:(nj + 1) * 128], rhs=hm[:, i, :],
                                                      start=(i == 0), stop=(i == NG - 1)), reads=[wdn.b, hm.b], writes=[ps.b])
                    P.op("dve", lambda e: e.scalar_tensor_tensor(out=x[:, nj, :], in0=ps[:, 0:TB], scalar=g2c[:, nj:nj + 1],
                                                                 in1=x[:, nj, :], op0=ALU.mult, op1=ALU.add),
                         reads=[ps.b, x.b, self.modc.b], writes=[x.b])
                if fuse_final:
                    self.rmsnorm(x, TB, sq, self.col("final_norm", 0, 8), None, x[:], x.b)
                    P.dma("sp", lambda e: e.dma_start(out=xr(self.outT)[:, :, t0:t0 + TB], in_=x[:]), x.b, reads=[x.b])
                else:
                    P.dma("sp", lambda e: e.dma_start(out=xr(self.X)[:, :, t0:t0 + TB], in_=x[:]), x.b, reads=[x.b])

            norm(0)
            for nb in range(NB):
                up(nb)
                if nb + 1 < NB:
                    norm(nb + 1)
                down(nb)
            self.end_phase()

    def phase_final(self, xsrc, do_norm):
        P = self.P
        TB = 512
        with ExitStack() as es:
            xs = [self.tile(es, "nx%d" % i, [128, 8, TB], dma=True) for i in range(2)]
            sq = self.tile(es, "nsq", [128, 8, TB])
            self.rstd = self.tile(es, "nrstd", [128, TB])
            G = self.col("final_norm", 0, 8)
            for nb in range(T // TB):
                x = xs[nb % 2]
                t0 = nb * TB
                src = xsrc.rearrange("(j p) t -> p j t", p=128)[:, :, t0:t0 + TB]
                P.dma("sp", lambda e, x=x, src=src: e.dma_start(out=x[:], in_=src), x.b, writes=[x.b])
                if do_norm:
                    self.rmsnorm(x, TB, sq, G, None, x[:], x.b)
                dst = self.outT.rearrange("(j p) t -> p j t", p=128)[:, :, t0:t0 + TB]
                P.dma("act", lambda e, x=x, dst=dst: e.dma_start(out=dst, in_=x[:]), x.b, reads=[x.b])
            self.end_phase()

    def phase_rwkv(self, l, xsrc):
        P, nc, inp = self.P, self.nc, self.inp
        CH = 128
        NCH = self.cfg.get("nch", T // CH)
        ident = self.cst["ident"]
        with ExitStack() as es:
            self._stg = None
            reuse_stage = self.cfg.get("stage_reuse", True)
            self._stg_n = 6 if reuse_stage else 2
            ses = ExitStack() if reuse_stage else es
            Wr = self.tile(es, "Wr", [128, 8, D], BF16)
            Wk = self.tile(es, "Wk", [128, 8, D], BF16)
            Wv = self.tile(es, "Wv", [128, 8, D], BF16)
            Wo = self.tile(es, "Wo", [128, 8, D], BF16)
            w1b = self.tile(es, "w1b", [128, 8, 64], BF16)
            a1b = self.tile(es, "a1b", [128, 8, 64], BF16)
            g1b = self.tile(es, "g1b", [128, 8, 160], BF16)
            w2b = self.tile(es, "w2b", [64, 1, D], BF16)
            a2b = self.tile(es, "a2b", [64, 1, D], BF16)
            g2b = self.tile(es, "g2b", [128, 2, D], BF16)
            if l == 1:
                v1b = self.tile(es, "v1b", [128, 8, 32], BF16)
                v2b = self.tile(es, "v2b", [32, 1, D], BF16)
                v0r = self.tile(es, "v0r", [128, D], dma=True)
            lnw = self.tile(es, "lnw", [128, D], dma=True)
            lnb = self.tile(es, "lnb", [128, D], dma=True)
            omka = self.tile(es, "omka", [128, 8])
            eps2 = self.tile(es, "eps2", [128, 1])
            onesT = self.tile(es, "onesT", [128, 128])
            for W, src in ((Wr, inp["a_w_rkv"][l, 0]), (Wk, inp["a_w_rkv"][l, 1]), (Wv, inp["a_w_rkv"][l, 2]), (Wo, inp["a_w_o"][l])):
                s3 = src.rearrange("(kc p) n -> p kc n", p=128)
                self.load_cast(ses, [(W[:, kc:kc + 1, :], s3[:, kc:kc + 1, :], 128, 1, D) for kc in range(8)], W.b)
            self.load_cast(ses, [(w1b[:], inp["a_w1"][l].rearrange("(kc p) n -> p kc n", p=128), 128, 8, 64)], w1b.b)
            self.load_cast(ses, [(a1b[:], inp["a_a1"][l].rearrange("(kc p) n -> p kc n", p=128), 128, 8, 64)], a1b.b)
            sg1 = inp["a_g1"][l].rearrange("(kc p) n -> p kc n", p=128)
            self.load_cast(ses, [(g1b[:, 0:4, :], sg1[:, 0:4, :], 128, 4, 160), (g1b[:, 4:8, :], sg1[:, 4:8, :], 128, 4, 160)], g1b.b)
            self.load_cast(ses, [(w2b[:], inp["a_w2"][l].rearrange("(o p) n -> p o n", o=1), 64, 1, D)], w2b.b)
            self.load_cast(ses, [(a2b[:], inp["a_a2"][l].rearrange("(o p) n -> p o n", o=1), 64, 1, D)], a2b.b)
            self.load_cast(ses, [(g2b[:, 0:1, :], inp["a_g2"][l][0:128, :].rearrange("(o p) n -> p o n", o=1), 128, 1, D),
                                (g2b[0:32, 1:2, :], inp["a_g2"][l][128:160, :].rearrange("(o p) n -> p o n", o=1), 32, 1, D)], g2b.b)
            if l == 1:
                self.load_cast(ses, [(v1b[:], inp["a_v1"][0].rearrange("(kc p) n -> p kc n", p=128), 128, 8, 32)], v1b.b)
                self.load_cast(ses, [(v2b[:], inp["a_v2"][0].rearrange("(o p) n -> p o n", o=1), 32, 1, D)], v2b.b)
                P.dma("sp", lambda e: e.dma_start(out=v0r[:], in_=inp["rows"][4 * D:5 * D].partition_broadcast(128)), v0r.b, writes=[v0r.b])
            P.dma("sp", lambda e: e.dma_start(out=lnw[:], in_=inp["rows"][(2 * l) * D:(2 * l + 1) * D].partition_broadcast(128)), lnw.b, writes=[lnw.b])
            P.dma("sp", lambda e: e.dma_start(out=lnb[:], in_=inp["rows"][(2 * l + 1) * D:(2 * l + 2) * D].partition_broadcast(128)), lnb.b, writes=[lnb.b])
            P.op("dve", lambda e: e.tensor_scalar(out=omka[:], in0=self.col("k_a_%d" % l, 0, 8), scalar1=-1.0, scalar2=1.0, op0=ALU.mult, op1=ALU.add),
                 reads=[self.cols.b], writes=[omka.b])
            P.op("pool", lambda e: e.memset(eps2[:], 64e-5), writes=[eps2.b])
            P.op("pool", lambda e: e.memset(onesT[:], 1.0), writes=[onesT.b])

            if reuse_stage:
                P.barrier()
                ses.close()
                self._stg = None
            self._stg_n = 2
            Hbd = self.tile(es, "Hbd", [128, 8, 128])
            Hb = self.tile(es, "Hb", [128, 8, 128], BF16)
            if self.cfg.get("t_hb", True):
                P.op("pool", lambda e: e.memset(Hb[:], 0.0), writes=[Hb.b])
            Vb = self.tile(es, "Vb", [128, D], BF16)
            P.op("pool", lambda e: e.memset(Hbd[:], 0.0), writes=[Hbd.b])
            h = self.tile(es, "h", [128, 8, 129])
            P.op("pool", lambda e: e.memset(h[:], 0.0), writes=[h.b])
            x = self.tile(es, "rx", [128, 8, CH], dma=True)
            self.rstd = self.tile(es, "rrstd", [128, CH])
            xx = self.tile(es, "xx", [128, 8, CH])
            tmpm = [self.tile(es, "tmpm%d" % i, [128, 8, CH], dma=(i == 0)) for i in range(2)]
            sq = tmpm[1] if self.cfg.get("t_sq", True) else self.tile(es, "rsq", [128, 8, CH])
            xm = [self.tile(es, "xm%d" % i, [128, 8, CH], BF16) for i in range(6)]
            lwt = self.tile(es, "lwt", [64, CH], BF16)
            lat = self.tile(es, "lat", [64, CH], BF16)
            lgt = self.tile(es, "lgt", [128, 2, CH], BF16)
            V = self.tile(es, "V", [128, D], dma=True)
            Yt = self.tile(es, "Yt", [128, D], dma=True)
            if l == 1:
                lvt = self.tile(es, "lvt", [32, CH], BF16)
                VFt = Tl(tmpm[0][:].rearrange("p a t -> p (a t)"), tmpm[0].b)
                sgv = Tl(tmpm[1][:].rearrange("p a t -> p (a t)")[:, 0:512], tmpm[1].b)
            ogb = self.tile(es, "ogb", [128, D], BF16)
            ogT = self.tile(es, "ogT", [128, 8, CH], BF16)
            st = self.tile(es, "stat", [128, 6, 16])
            rkb = self.tile(es, "rkb", [128, 16])

            NSLOT = self.cfg.get("nslot%d" % l, 4)

            def mkslot(si):
                S = {}

                def pt(name, shape=(128, 128), dt=F32):
                    S[name] = self.tile(es, "s%d_%s" % (si, name), list(shape), dt)
                for nm in ("r", "k", "sg", "a", "kk", "x", "L", "eL"):
                    pt(nm)
                for nm in ("rh", "bts", "kts", "WT", "U"):
                    pt(nm, (128, 128), BF16)
                pt("AR", (128, 2, 128), BF16)
                pt("F3", (128, 3, 128), BF16)
                pt("T3", (128, 3, 128), BF16)
                for i in range(2):
                    pt("SC%d" % i, (128, 4, 128), BF16)
                    pt("Ym%d" % i, (128, 128), BF16)
                    pt("ZY%d_0" % i, (128, 2, 128), BF16)
                    pt("ZY%d_1" % i, (128, 2, 128), BF16)
                    pt("Tt%d" % i, (128, 128), BF16)
                    pt("AV%d" % i, (128, 64), BF16)
                return S
            slots = [mkslot(i) for i in range(NSLOT)]
            mask4 = self.cst["mask4"]; maskL = self.cst["maskL"]; bd = self.cst["bd"]; ind = self.cst["ind"]
            G1 = self.der[:, l, 0, :]
            S1 = self.modcol(l, 0)
            g1c = self.modcol(l, 2)
            psB = self.ps[6]

            def c_(name, j):
                return self.col("%s_%d" % (name, l), j, 1)

            for c in range(NCH):
                t0 = c * CH
                src = xsrc.rearrange("(j p) t -> p j t", p=128)[:, :, t0:t0 + CH]
                P.dma("sp", lambda e, src=src: e.dma_start(out=x[:], in_=src), x.b, writes=[x.b])
                self.rmsnorm(x, CH, sq, G1, S1, h[:, :, 1:CH + 1], h.b)
                P.op("pool", lambda e: e.tensor_tensor(out=xx[:], in0=h[:, :, 0:CH], in1=h[:, :, 1:CH + 1], op=ALU.subtract),
                     reads=[h.b], writes=[xx.b])
                for i in range(6):
                    tm = tmpm[i % 2]
                    mu = self.col("mu%d_%d" % (i, l), 0, 8)
                    P.op("dve", lambda e, tm=tm, mu=mu: e.tensor_tensor(out=tm[:], in0=xx[:], in1=mu.unsqueeze(2).to_broadcast([128, 8, CH]), op=ALU.mult),
                         reads=[xx.b, self.cols.b], writes=[tm.b])
                    P.op("pool", lambda e, tm=tm, i=i: e.tensor_tensor(out=xm[i][:], in0=tm[:], in1=h[:, :, 1:CH + 1], op=ALU.add),
                         reads=[tm.b, h.b], writes=[xm[i].b])
                P.op("pool", lambda e: e.tensor_copy(out=h[:, :, 0:1], in_=h[:, :, CH:CH + 1]), reads=[h.b], writes=[h.b])
                if l == 1:
                    P.dma("sp", lambda e, t0=t0: e.dma_start(out=VFt[:], in_=self.VF[t0:t0 + CH, :]), VFt.b, writes=[VFt.b])

                psl = self.psum_next()
                for kc in range(8):
                    P.op("pe", lambda e, kc=kc: e.matmul(psl[0:64, 0:128], lhsT=w1b[:, kc, :], rhs=xm[1][:, kc, :], start=(kc == 0), stop=(kc == 7)),
                         reads=[w1b.b, xm[1].b], writes=[psl.b])
                for kc in range(8):
                    P.op("pe", lambda e, kc=kc: e.matmul(psl[0:64, 128:256], lhsT=a1b[:, kc, :], rhs=xm[4][:, kc, :], start=(kc == 0), stop=(kc == 7)),
                         reads=[a1b.b, xm[4].b], writes=[psl.b])
                for kc in range(8):
                    P.op("pe", lambda e, kc=kc: e.matmul(psl[:, 256:384], lhsT=g1b[:, kc, 0:128], rhs=xm[5][:, kc, :], start=(kc == 0), stop=(kc == 7)),
                         reads=[g1b.b, xm[5].b], writes=[psl.b])
                for kc in range(8):
                    P.op("pe", lambda e, kc=kc: e.matmul(psl[0:32, 384:512], lhsT=g1b[:, kc, 128:160], rhs=xm[5][:, kc, :], start=(kc == 0), stop=(kc == 7)),
                         reads=[g1b.b, xm[5].b], writes=[psl.b])
                P.op("act", lambda e: e.activation(out=lwt[:], in_=psl[0:64, 0:128], func=AF.Tanh), reads=[psl.b], writes=[lwt.b])
                P.op("act", lambda e: e.copy(out=lat[:], in_=psl[0:64, 128:256]), reads=[psl.b], writes=[lat.b])
                P.op("act", lambda e: e.activation(out=lgt[:, 0, :], in_=psl[:, 256:384], func=AF.Sigmoid), reads=[psl.b], writes=[lgt.b])
                P.op("act", lambda e: e.activation(out=lgt[0:32, 1, :], in_=psl[0:32, 384:512], func=AF.Sigmoid), reads=[psl.b], writes=[lgt.b])
                if l == 1:
                    psv = self.psum_next()
                    for kc in range(8):
                        P.op("pe", lambda e, kc=kc: e.matmul(psv[0:32, 0:128], lhsT=v1b[:, kc, :], rhs=xm[3][:, kc, :], start=(kc == 0), stop=(kc == 7)),
                             reads=[v1b.b, xm[3].b], writes=[psv.b])
                    P.op("act", lambda e: e.copy(out=lvt[:], in_=psv[0:32, 0:128]), reads=[psv.b], writes=[lvt.b])
                for half in range(2):
                    ps = self.psum_next()
                    for kc in range(8):
                        P.op("pe", lambda e, ps=ps, kc=kc, half=half: e.matmul(ps[:, :], lhsT=xm[3][:, kc, :], rhs=Wv[:, kc, half * 512:(half + 1) * 512],
                                                                              start=(kc == 0), stop=(kc == 7)), reads=[xm[3].b, Wv.b], writes=[ps.b])
                    P.op("act", lambda e, ps=ps, half=half: e.copy(out=V[:, half * 512:(half + 1) * 512], in_=ps[:, :]), reads=[ps.b], writes=[V.b])
                if l == 1:
                    for half in range(2):
                        ps = self.psum_next()
                        hs = slice(half * 512, (half + 1) * 512)
                        P.op("pe", lambda e, ps=ps, hs=hs: e.matmul(ps[:, :], lhsT=lvt[:], rhs=v2b[0:32, 0, hs], start=True, stop=True),
                             reads=[lvt.b, v2b.b], writes=[ps.b])
                        P.op("dve", lambda e, ps=ps, hs=hs: e.tensor_tensor(out=sgv[:], in0=ps[:, :], in1=v0r[:, hs], op=ALU.add),
                             reads=[ps.b, v0r.b], writes=[sgv.b])
                        P.op("act", lambda e: e.activation(out=sgv[:], in_=sgv[:], func=AF.Sigmoid), reads=[sgv.b], writes=[sgv.b])
                        P.op("pool", lambda e, hs=hs: e.tensor_tensor(out=VFt[:, hs], in0=VFt[:, hs], in1=V[:, hs], op=ALU.subtract),
                             reads=[VFt.b, V.b], writes=[VFt.b])
                        P.op("pool", lambda e, hs=hs: e.tensor_tensor(out=VFt[:, hs], in0=VFt[:, hs], in1=sgv[:], op=ALU.mult),
                             reads=[VFt.b, sgv.b], writes=[VFt.b])
                        P.op("pool", lambda e, hs=hs: e.tensor_tensor(out=V[:, hs], in0=V[:, hs], in1=VFt[:, hs], op=ALU.add),
                             reads=[VFt.b, V.b], writes=[V.b])
                else:
                    P.dma("act", lambda e, t0=t0: e.dma_start(out=self.VF[t0:t0 + CH, :], in_=V[:]), V.b, reads=[V.b])
                if "rw_v" in self.dbg:
                    P.dma("act", lambda e, t0=t0: e.dma_start(out=self.dbg["rw_v"][t0:t0 + CH, :], in_=V[:]), V.b, reads=[V.b])

                if self.cfg.get("t_vb", True):
                    P.op("pool", lambda e: e.tensor_copy(out=Vb[:], in_=V[:]), reads=[V.b], writes=[Vb.b])

                def pair_gen(j, S):
                    CUT = self.cfg.get("rw_cut", 99)
                    EB = self.cfg.get("eng_b", "dve")

                    def smul(eng, out_ap, in_ap, sc_ap, rd, wr):
                        if eng == "act":
                            P.op("act", lambda e: e.activation(out=out_ap, in_=in_ap, func=AF.Copy, scale=sc_ap), reads=rd, writes=wr)
                        else:
                            P.op(eng, lambda e: e.tensor_scalar(out=out_ap, in0=in_ap, scalar1=sc_ap, scalar2=None, op0=ALU.mult), reads=rd, writes=wr)
                    js = slice(j * 128, (j + 1) * 128)
                    r_, k_, sg, a_, kk, tx, L_, eL = S["r"], S["k"], S["sg"], S["a"], S["kk"], S["x"], S["L"], S["eL"]
                    rh, bts, kts, WT, U_, AR, F3, T3 = S["rh"], S["bts"], S["kts"], S["WT"], S["U"], S["AR"], S["F3"], S["T3"]
                    SC = [S["SC0"], S["SC1"]]; Ym = [S["Ym0"], S["Ym1"]]; Tt = [S["Tt0"], S["Tt1"]]; AV = [S["AV0"], S["AV1"]]
                    ZY = [[S["ZY0_0"], S["ZY0_1"]], [S["ZY1_0"], S["ZY1_1"]]]
                    psA = self.psum_next()
                    for kc in range(8):
                        P.op("pe", lambda e: e.matmul(psA[:, 0:128], lhsT=Wr[:, kc, js], rhs=xm[0][:, kc, :], start=(kc == 0), stop=(kc == 7)),
                             reads=[Wr.b, xm[0].b], writes=[psA.b])
                    for kc in range(8):
                        P.op("pe", lambda e: e.matmul(psA[:, 128:256], lhsT=Wk[:, kc, js], rhs=xm[2][:, kc, :], start=(kc == 0), stop=(kc == 7)),
                             reads=[Wk.b, xm[2].b], writes=[psA.b])
                    P.op("pe", lambda e: e.matmul(psA[:, 256:384], lhsT=w2b[:, 0, js], rhs=lwt[:], start=True, stop=True), reads=[w2b.b, lwt.b], writes=[psA.b])
                    P.op("pe", lambda e: e.matmul(psA[:, 384:512], lhsT=a2b[:, 0, js], rhs=lat[:], start=True, stop=True), reads=[a2b.b, lat.b], writes=[psA.b])
                    P.op("act", lambda e: e.copy(out=r_[:], in_=psA[:, 0:128]), reads=[psA.b], writes=[r_.b])
                    P.op("act", lambda e: e.copy(out=k_[:], in_=psA[:, 128:256]), reads=[psA.b], writes=[k_.b])
                    P.op("act", lambda e: e.activation(out=sg[:], in_=psA[:, 256:384], func=AF.Sigmoid, bias=c_("w0", j), scale=1.0),
                         reads=[psA.b, self.cols.b], writes=[sg.b])
                    P.op("act", lambda e: e.activation(out=a_[:], in_=psA[:, 384:512], func=AF.Sigmoid, bias=c_("a0", j), scale=1.0),
                         reads=[psA.b, self.cols.b], writes=[a_.b])
                    yield
                    if CUT <= 1:
                        return
                    P.op("dve", lambda e: e.tensor_scalar(out=kk[:], in0=k_[:], scalar1=c_("k_k", j), scalar2=None, op0=ALU.mult),
                         reads=[k_.b, self.cols.b], writes=[kk.b])
                    P.op("pool", lambda e: e.tensor_tensor(out=tx[:], in0=kk[:], in1=kk[:], op=ALU.mult), reads=[kk.b], writes=[tx.b])
                    ps = self.psum_next()
                    P.op("pe", lambda e: e.matmul(ps[:, 0:128], lhsT=bd[:], rhs=tx[:], start=True, stop=True), reads=[bd.b, tx.b], writes=[ps.b])
                    P.op("act", lambda e: e.activation(out=tx[:], in_=ps[:, 0:128], func=AF.Sqrt), reads=[ps.b], writes=[tx.b])
                    yield
                    if CUT <= 2:
                        return
                    P.op("dve", lambda e: e.tensor_scalar(out=tx[:], in0=tx[:], scalar1=1e-12, scalar2=None, op0=ALU.max), reads=[tx.b], writes=[tx.b])
                    P.op("dve", lambda e: e.reciprocal(out=tx[:], in_=tx[:]), reads=[tx.b], writes=[tx.b])
                    P.op("pool", lambda e: e.tensor_tensor(out=kk[:], in0=kk[:], in1=tx[:], op=ALU.mult), reads=[kk.b, tx.b], writes=[kk.b])
                    P.op("dve", lambda e: e.tensor_scalar(out=tx[:], in0=a_[:], scalar1=c_("k_a", j), scalar2=omka[:, j:j + 1], op0=ALU.mult, op1=ALU.add),
                         reads=[a_.b, self.cols.b, omka.b], writes=[tx.b])
                    P.op("pool", lambda e: e.tensor_tensor(out=k_[:], in0=k_[:], in1=tx[:], op=ALU.mult), reads=[k_.b, tx.b], writes=[k_.b])
                    P.op("pool", lambda e: e.tensor_tensor(out=a_[:], in0=kk[:], in1=a_[:], op=ALU.mult), reads=[kk.b, a_.b], writes=[a_.b])
                    P.op("dve", lambda e: e.scalar_tensor_tensor(out=tx[:], in0=r_[:], scalar=c_("r_k", j), in1=k_[:], op0=ALU.mult, op1=ALU.mult),
                         reads=[r_.b, k_.b, self.cols.b], writes=[tx.b])
                    P.op("pe", lambda e: e.matmul(psB[:, 0:16], lhsT=tx[:], rhs=ind[:, j * 16:(j + 1) * 16], start=(j == 0), stop=(j == 7)),
                         reads=[tx.b, ind.b], writes=[psB.b])
                    yield
                    if CUT <= 3:
                        return
                    P.op("dve", lambda e: e.tensor_tensor_scan(out=L_[:], data0=onesT[:], data1=sg[:], initial=0.0, op0=ALU.mult, op1=ALU.add),
                         reads=[onesT.b, sg.b], writes=[L_.b])
                    P.op("pool", lambda e: e.tensor_tensor(out=sg[:], in0=L_[:], in1=sg[:], op=ALU.subtract), reads=[L_.b, sg.b], writes=[sg.b])
                    P.op("act", lambda e: e.activation(out=eL[:], in_=L_[:], func=AF.Exp, scale=-C0), reads=[L_.b], writes=[eL.b])
                    P.op("act", lambda e: e.activation(out=sg[:], in_=sg[:], func=AF.Exp, scale=-C0), reads=[sg.b], writes=[sg.b])
                    P.op("act", lambda e: e.activation(out=L_[:], in_=L_[:], func=AF.Exp, scale=C0), reads=[L_.b], writes=[L_.b])
                    yield
                    if CUT <= 4:
                        return
                    enL = L_
                    eE = sg
                    P.op("pool", lambda e: e.tensor_tensor(out=r_[:], in0=r_[:], in1=eL[:], op=ALU.mult), reads=[r_.b, eL.b], writes=[r_.b])
                    P.op("pool", lambda e: e.tensor_copy(out=rh[:], in_=r_[:]), reads=[r_.b], writes=[rh.b])
                    P.op("dve", lambda e: e.scalar_tensor_tensor(out=tx[:], in0=kk[:], scalar=-1.0, in1=eE[:], op0=ALU.mult, op1=ALU.mult),
                         reads=[kk.b, eE.b], writes=[tx.b])
                    P.op("pool", lambda e: e.tensor_copy(out=F3[:, 0, :], in_=tx[:]), reads=[tx.b], writes=[F3.b])
                    P.op("dve", lambda e: e.tensor_scalar(out=AR[:, 0, :], in0=tx[:], scalar1=enL[:, 63:64], scalar2=None, op0=ALU.mult),
                         reads=[tx.b, enL.b], writes=[AR.b])
                    P.op("dve", lambda e: e.tensor_scalar(out=AR[:, 1, :], in0=r_[:], scalar1=enL[:, 63:64], scalar2=None, op0=ALU.mult),
                         reads=[r_.b, enL.b], writes=[AR.b])
                    P.op("pool", lambda e: e.tensor_tensor(out=a_[:], in0=a_[:], in1=enL[:], op=ALU.mult), reads=[a_.b, enL.b], writes=[a_.b])
                    P.op("pool", lambda e: e.tensor_tensor(out=k_[:], in0=k_[:], in1=enL[:], op=ALU.mult), reads=[k_.b, enL.b], writes=[k_.b])
                    yield
                    if CUT <= 5:
                        return
                    smul(EB, bts[:], a_[:], eL[:, 63:64], [a_.b, eL.b], [bts.b])
                    smul(EB, kts[:], k_[:], eL[:, 63:64], [k_.b, eL.b], [kts.b])
                    smul(EB, F3[:, 1, :], a_[:], eL[:, 127:128], [a_.b, eL.b], [F3.b])
                    smul(EB, F3[:, 2, :], k_[:], eL[:, 127:128], [k_.b, eL.b], [F3.b])
                    yield
                    if CUT <= 6:
                        return
                    for q in range(3):
                        P.op("pe", lambda e: e.transpose(out=self.psbf[:, q * 128:(q + 1) * 128], in_=F3[:, q, :], identity=self.identb[:]),
                             reads=[F3.b, self.identb.b], writes=[self.psbf.b])
                    P.op("act", lambda e: e.copy(out=T3[:].rearrange("p a t -> p (a t)"), in_=self.psbf[:, 0:384]), reads=[self.psbf.b], writes=[T3.b])
                    yield
                    if CUT <= 7:
                        return
                    for par in range(2):
                        rows = slice(64 * par, 64 * par + 64)
                        pss = self.psum_next()
                        arv = AR[rows, :, :].rearrange("p a t -> p (a t)")
                        P.op("pe", lambda e: e.matmul(pss[:, 0:256], lhsT=bts[rows, :], rhs=arv, start=True, stop=True),
                             reads=[bts.b, AR.b], writes=[pss.b])
                        P.op("pe", lambda e: e.matmul(pss[:, 256:512], lhsT=kts[rows, :], rhs=arv, start=True, stop=True),
                             reads=[kts.b, AR.b], writes=[pss.b])
                        P.op("dve", lambda e: e.tensor_tensor(out=SC[par][:].rearrange("p a t -> p (a t)"), in0=pss[:, :], in1=mask4[:], op=ALU.mult),
                             reads=[pss.b, mask4.b], writes=[SC[par].b])
                        ps3 = self.psum_next()
                        P.op("pe", lambda e: e.matmul(ps3[:, 0:128], lhsT=AR[rows, 0, :], rhs=bts[rows, :], start=True, stop=True),
                             reads=[bts.b, AR.b], writes=[ps3.b])
                        P.op("dve", lambda e: e.tensor_tensor(out=Ym[par][:], in0=ps3[:, 0:128], in1=maskL[:], op=ALU.mult),
                             reads=[ps3.b, maskL.b], writes=[Ym[par].b])
                        P.op("pool", lambda e: e.tensor_tensor(out=Tt[par][:], in0=SC[par][:, 0, :], in1=self.identb[:], op=ALU.add),
                             reads=[SC[par].b, self.identb.b], writes=[Tt[par].b])
                        yield
                    cur = [(SC[0][:, 0, :], SC[0].b, Ym[0][:], Ym[0].b), (SC[1][:, 0, :], SC[1].b, Ym[1][:], Ym[1].b)]
                    for step in range(1, 7):
                        for par in range(2):
                            Zap, Zb, Yap, Yb = cur[par]
                            zy = ZY[par][step % 2]
                            psz = self.psum_next()
                            if step < 6:
                                P.op("pe", lambda e: e.matmul(psz[:, 0:128], lhsT=Yap, rhs=Zap, start=True, stop=True), reads=[Zb, Yb], writes=[psz.b])
                            P.op("pe", lambda e: e.matmul(psz[:, 128:256], lhsT=Zap, rhs=Yap, start=True, stop=True), reads=[Zb, Yb], writes=[psz.b])
                            ev = "act"
                            if step < 6:
                                self.copy(ev, zy[:].rearrange("p a t -> p (a t)"), psz[:, 0:256], [psz.b], [zy.b])
                            else:
                                self.copy(ev, zy[:, 1, :], psz[:, 128:256], [psz.b], [zy.b])
                            cur[par] = (zy[:, 0, :], zy.b, zy[:, 1, :], zy.b)
                            yield
                            tt = Tt[par]
                            psp = self.psum_next()
                            P.op("pe", lambda e: e.matmul(psp[:, 0:128], lhsT=zy[:, 1, :], rhs=tt[:], start=True, stop=True),
                                 reads=[zy.b, tt.b], writes=[psp.b])
                            P.op("dve", lambda e: e.tensor_tensor(out=tt[:], in0=psp[:, 0:128], in1=tt[:], op=ALU.add),
                                 reads=[psp.b, tt.b], writes=[tt.b])
                            yield
                    for par in range(2):
                        rows = slice(64 * par, 64 * par + 64)
                        hc = slice((2 * j + par) * 64, (2 * j + par + 1) * 64)
                        psw = self.psum_next()
                        P.op("pe", lambda e: e.matmul(psw[:, 0:128], lhsT=T3[:, 0, :], rhs=Tt[par][:], start=True, stop=True),
                             reads=[T3.b, Tt[par].b], writes=[psw.b])
                        P.op("pe", lambda e: e.matmul(psw[:, 128:192], lhsT=SC[par][:, 2, :], rhs=Vb[:, hc], start=True, stop=True),
                             reads=[SC[par].b, Vb.b], writes=[psw.b])
                        P.op("act", lambda e: e.copy(out=WT[rows, :], in_=psw[rows, 0:128]), reads=[psw.b], writes=[WT.b])
                        P.op("act", lambda e: e.copy(out=AV[par][:], in_=psw[:, 128:192]), reads=[psw.b], writes=[AV[par].b])
                        yield
                    psu = self.psum_next()
                    P.op("pe", lambda e: e.matmul(psu[:, 0:128], lhsT=WT[:], rhs=Hb[:, j, :], start=True, stop=True),
                         reads=[WT.b, Hb.b], writes=[psu.b])
                    for par in range(2):
                        cs = slice(64 * par, 64 * par + 64)
                        P.op("pe", lambda e: e.matmul(psu[:, cs], lhsT=Tt[par][:], rhs=AV[par][:], start=False, stop=(par == 1), skip_group_check=True),
                             reads=[Tt[par].b, AV[par].b], writes=[psu.b])
                    P.op("act", lambda e: e.copy(out=U_[:], in_=psu[:, 0:128]), reads=[psu.b], writes=[U_.b])
                    yield
                    if CUT <= 8:
                        return
                    psy = self.psum_next()
                    P.op("pe", lambda e: e.matmul(psy[:, 0:128], lhsT=rh[:], rhs=Hb[:, j, :], start=True, stop=True),
                         reads=[rh.b, Hb.b], writes=[psy.b])
                    for par in range(2):
                        hc = slice((2 * j + par) * 64, (2 * j + par + 1) * 64)
                        cs = slice(64 * par, 64 * par + 64)
                        P.op("pe", lambda e: e.matmul(psy[:, cs], lhsT=SC[par][:, 1, :], rhs=U_[:, cs], start=False, stop=False, skip_group_check=True),
                             reads=[SC[par].b, U_.b], writes=[psy.b])
                        P.op("pe", lambda e: e.matmul(psy[:, cs], lhsT=SC[par][:, 3, :], rhs=Vb[:, hc], start=False, stop=(par == 1), skip_group_check=True),
                             reads=[SC[par].b, Vb.b], writes=[psy.b])
                    P.op("act", lambda e: e.copy(out=Yt[:, js], in_=psy[:, 0:128]), reads=[psy.b], writes=[Yt.b])
                    psh = self.psum_next()
                    P.op("pe", lambda e: e.matmul(psh[:, 0:128], lhsT=T3[:, 2, :], rhs=Vb[:, js], start=True, stop=False),
                         reads=[T3.b, Vb.b], writes=[psh.b])
                    P.op("pe", lambda e: e.matmul(psh[:, 0:128], lhsT=T3[:, 1, :], rhs=U_[:], start=False, stop=True),
                         reads=[T3.b, U_.b], writes=[psh.b])
                    for par in range(2):
                        rows = slice(64 * par, 64 * par + 64)
                        P.op("dve", lambda e: e.scalar_tensor_tensor(out=Hbd[rows, j, rows], in0=Hbd[rows, j, rows], scalar=eL[rows, 127:128],
                                                                     in1=psh[rows, rows], op0=ALU.mult, op1=ALU.add),
                             reads=[Hbd.b, eL.b, psh.b], writes=[Hbd.b])
                        P.op("pool", lambda e: e.tensor_copy(out=Hb[rows, j, rows], in_=Hbd[rows, j, rows]), reads=[Hbd.b], writes=[Hb.b])
                    yield
                    if CUT <= 9:
                        return

                STAG = self.cfg.get("stagger", 0)
                pending = list(range(8))
                free_slots = list(range(NSLOT))
                active = []
                tick = 0
                next_start = 0
                while pending or active:
                    if pending and free_slots and tick >= next_start:
                        jn = pending.pop(0)
                        sl = free_slots.pop(0)
                        active.append((pair_gen(jn, slots[sl]), sl))
                        next_start = tick + STAG
                    for item in list(active):
                        try:
                            next(item[0])
                        except StopIteration:
                            active.remove(item)
                            free_slots.append(item[1])
                    tick += 1

                if "rw_o" in self.dbg:
                    P.dma("act", lambda e, t0=t0: e.dma_start(out=self.dbg["rw_o"][t0:t0 + CH, :], in_=Yt[:]), Yt.b, reads=[Yt.b])
                Y3 = Yt[:].rearrange("p (h n) -> p h n", n=64)
                S1t = tmpm[0][:].rearrange("p a t -> p (a t)")
                S1b = tmpm[0].b
                S2t = tmpm[1][:].rearrange("p a t -> p (a t)")
                S2b = tmpm[1].b
                P.op("act", lambda e: e.copy(out=rkb[:], in_=psB[:, 0:16]), reads=[psB.b], writes=[rkb.b])
                P.op("dve", lambda e: e.tensor_reduce(out=st[:, 0, :], in_=Y3, axis=AX.X, op=ALU.add), reads=[Yt.b], writes=[st.b])
                P.op("pool", lambda e: e.tensor_tensor(out=S1t, in0=Yt[:], in1=Yt[:], op=ALU.mult), reads=[Yt.b], writes=[S1b])
                P.op("dve", lambda e: e.tensor_reduce(out=st[:, 1, :], in_=S1t.rearrange("p (h n) -> p h n", n=64), axis=AX.X, op=ALU.add),
                     reads=[S1b], writes=[st.b])
                P.op("dve", lambda e: e.tensor_scalar(out=st[:, 2, :], in0=st[:, 0, :], scalar1=1.0 / 64, scalar2=None, op0=ALU.mult), reads=[st.b], writes=[st.b])
                P.op("dve", lambda e: e.tensor_tensor(out=st[:, 3, :], in0=st[:, 2, :], in1=st[:, 2, :], op=ALU.mult), reads=[st.b], writes=[st.b])
                P.op("dve", lambda e: e.scalar_tensor_tensor(out=st[:, 4, :], in0=st[:, 1, :], scalar=1.0 / 64, in1=st[:, 3, :], op0=ALU.mult, op1=ALU.subtract),
                     reads=[st.b], writes=[st.b])
                P.op("act", lambda e: e.activation(out=st[:, 5, :], in_=st[:, 4, :], func=AF.Sqrt, bias=eps2[:], scale=1.0), reads=[st.b, eps2.b], writes=[st.b])
                P.op("dve", lambda e: e.reciprocal(out=st[:, 5, :], in_=st[:, 5, :]), reads=[st.b], writes=[st.b])
                P.op("pool", lambda e: e.tensor_tensor(out=S1t.rearrange("p (h n) -> p h n", n=64), in0=Y3, in1=st[:, 2, :].unsqueeze(2).to_broadcast([128, 16, 64]), op=ALU.subtract),
                     reads=[Yt.b, st.b], writes=[S1b])
                P.op("dve", lambda e: e.tensor_tensor(out=S1t.rearrange("p (h n) -> p h n", n=64), in0=S1t.rearrange("p (h n) -> p h n", n=64),
                                                      in1=st[:, 5, :].unsqueeze(2).to_broadcast([128, 16, 64]), op=ALU.mult),
                     reads=[S1b, st.b], writes=[S1b])
                P.op("pool", lambda e: e.tensor_tensor(out=S1t, in0=S1t, in1=lnw[:], op=ALU.mult), reads=[S1b, lnw.b], writes=[S1b])
                P.op("dve", lambda e: e.tensor_tensor(out=S1t, in0=S1t, in1=lnb[:], op=ALU.add), reads=[S1b, lnb.b], writes=[S1b])
                P.op("pool", lambda e: e.tensor_tensor(out=S2t.rearrange("p (h n) -> p h n", n=64), in0=V[:].rearrange("p (h n) -> p h n", n=64),
                                                       in1=rkb[:].unsqueeze(2).to_broadcast([128, 16, 64]), op=ALU.mult),
                     reads=[V.b, rkb.b], writes=[S2b])
                P.op("dve", lambda e: e.tensor_tensor(out=S1t, in0=S1t, in1=S2t, op=ALU.add), reads=[S1b, S2b], writes=[S1b])
                for half in range(2):
                    ps = self.psum_next()
                    hs = slice(half * 512, (half + 1) * 512)
                    P.op("pe", lambda e: e.matmul(ps[:, :], lhsT=lgt[:, 0, :], rhs=g2b[:, 0, hs], start=True, stop=False),
                         reads=[lgt.b, g2b.b], writes=[ps.b])
                    P.op("pe", lambda e: e.matmul(ps[:, :], lhsT=lgt[0:32, 1, :], rhs=g2b[0:32, 1, hs], start=False, stop=True),
                         reads=[lgt.b, g2b.b], writes=[ps.b])
                    P.op("dve", lambda e: e.tensor_tensor(out=ogb[:, hs], in0=S1t[:, hs], in1=ps[:, :], op=ALU.mult), reads=[S1b, ps.b], writes=[ogb.b])
                for jj in range(8):
                    P.op("pe", lambda e, jj=jj: e.transpose(out=self.psbf[:, jj * 128:(jj + 1) * 128], in_=ogb[:, jj * 128:(jj + 1) * 128], identity=self.identb[:]),
                         reads=[ogb.b, self.identb.b], writes=[self.psbf.b])
                P.op("act", lambda e: e.copy(out=ogT[:].rearrange("p a t -> p (a t)"), in_=self.psbf[:, :]), reads=[self.psbf.b], writes=[ogT.b])
                for half in range(2):
                    ps = self.psum_next()
                    for jj in range(4):
                        nj = half * 4 + jj
                        for kc in range(8):
                            P.op("pe", lambda e, ps=ps, jj=jj, nj=nj, kc=kc: e.matmul(ps[:, jj * 128:(jj + 1) * 128], lhsT=Wo[:, kc, nj * 128:(nj + 1) * 128], rhs=ogT[:, kc, :],
                                                                                    start=(kc == 0), stop=(kc == 7)), reads=[Wo.b, ogT.b], writes=[ps.b])
                    for jj in range(4):
                        nj = half * 4 + jj
                        P.op("dve", lambda e, ps=ps, jj=jj, nj=nj: e.scalar_tensor_tensor(out=x[:, nj, :], in0=ps[:, jj * 128:(jj + 1) * 128], scalar=g1c[:, nj:nj + 1],
                                                                                         in1=x[:, nj, :], op0=ALU.mult, op1=ALU.add),
                             reads=[ps.b, x.b, self.modc.b], writes=[x.b])
                dst = self.X.rearrange("(j p) t -> p j t", p=128)[:, :, t0:t0 + CH]
                P.dma("sp", lambda e, dst=dst: e.dma_start(out=dst, in_=x[:]), x.b, reads=[x.b])
            self.end_phase()

    def rope_proj(self, es, W, hb, ncol0, dst_blk, CTt, STt, tmpA, tmpB):
        P = self.P
        roper = self.cst["roper"]
        tmpAs, tmpBs = tmpA, tmpB
        for hh in range(8):
            tmpA = tmpAs[hh % len(tmpAs)]
            tmpB = tmpBs[hh % len(tmpBs)]
            ps = self.psum_next()
            for kc in range(8):
                P.op("pe", lambda e: e.matmul(ps[:, :], lhsT=W[:, kc, ncol0 + hh * 128:ncol0 + (hh + 1) * 128], rhs=hb[:, kc, :],
                                              start=(kc == 0), stop=(kc == 7)), reads=[W.b, hb.b], writes=[ps.b])
            P.op("act", lambda e: e.copy(out=tmpA[:], in_=ps[:, :]), reads=[ps.b], writes=[tmpA.b])
            ps2 = self.psum_next()
            P.op("pe", lambda e: e.matmul(ps2[:, :], lhsT=roper[:], rhs=tmpA[:], start=True, stop=True), reads=[roper.b, tmpA.b], writes=[ps2.b])
            P.op("dve", lambda e: e.tensor_tensor(out=tmpB[:], in0=ps2[:, :], in1=STt[:], op=ALU.mult), reads=[ps2.b, STt.b], writes=[tmpB.b])
            P.op("pool", lambda e: e.tensor_tensor(out=tmpA[:], in0=tmpA[:], in1=CTt[:], op=ALU.mult), reads=[tmpA.b, CTt.b], writes=[tmpA.b])
            P.op("pool", lambda e: e.tensor_tensor(out=dst_blk[:, hh, :], in0=tmpA[:], in1=tmpB[:], op=ALU.add), reads=[tmpA.b, tmpB.b], writes=[dst_blk.b])

    def phase_kv(self, xsrc):
        P, nc, inp = self.P, self.nc, self.inp
        TB = 512
        with ExitStack() as es:
            self._stg = None
            Wkv = self.tile(es, "Wkv", [128, 8, 2 * D], BF16)
            s3 = inp["w_kv"].rearrange("(kc p) n -> p kc n", p=128)
            pieces = []
            for kc in range(8):
                for hf in range(2):
                    pieces.append((Wkv[:, kc:kc + 1, hf * D:(hf + 1) * D], s3[:, kc:kc + 1, hf * D:(hf + 1) * D], 128, 1, D))
            self.load_cast(es, pieces, Wkv.b)
            xs_ = [self.tile(es, "kx%d" % i, [128, 8, TB], dma=True) for i in range(2)]
            sq = self.tile(es, "ksq", [128, 8, TB])
            self.rstd = self.tile(es, "krstd", [128, TB])
            hbs_ = [self.tile(es, "khb%d" % i, [128, 8, TB], BF16) for i in range(2)]
            CTt = self.tile(es, "kCT", [128, TB], dma=True)
            STt = self.tile(es, "kST", [128, TB], dma=True)
            tmpA = [self.tile(es, "ktA%d" % i, [128, TB]) for i in range(3)]
            tmpB = [self.tile(es, "ktB%d" % i, [128, TB]) for i in range(3)]
            Kblks = [self.tile(es, "Kblk%d" % i, [128, 8, TB], BF16, dma=True) for i in range(2)]
            Vblks = [self.tile(es, "Vblk%d" % i, [128, 4, D], BF16, dma=True) for i in range(2)]
            G = self.col("kv_norm", 0, 8)
            for nb in range(T // TB):
                t0 = nb * TB
                x, hb, Kblk, Vblk = xs_[nb % 2], hbs_[nb % 2], Kblks[nb % 2], Vblks[nb % 2]
                src = xsrc.rearrange("(j p) t -> p j t", p=128)[:, :, t0:t0 + TB]
                P.dma("sp", lambda e: e.dma_start(out=x[:], in_=src), x.b, writes=[x.b])
                P.dma("act", lambda e: e.dma_start(out=CTt[:], in_=inp["ropec"][:, t0:t0 + TB]), CTt.b, writes=[CTt.b])
                P.dma("act", lambda e: e.dma_start(out=STt[:], in_=inp["ropes"][:, t0:t0 + TB]), STt.b, writes=[STt.b])
                self.rmsnorm(x, TB, sq, G, None, hb[:], hb.b)
                self.rope_proj(es, Wkv, hb, 0, Kblk, CTt, STt, tmpA, tmpB)
                dst = self.KT.rearrange("(j p) t -> p j t", p=128)[:, :, t0:t0 + TB]
                P.dma("sp", lambda e: e.dma_start(out=dst, in_=Kblk[:]), Kblk.b, reads=[Kblk.b])
                for tl in range(4):
                    for half in range(2):
                        ps = self.psum_next()
                        for kc in range(8):
                            P.op("pe", lambda e: e.matmul(ps[:, :], lhsT=hb[:, kc, tl * 128:(tl + 1) * 128], rhs=Wkv[:, kc, D + half * 512:D + (half + 1) * 512],
                                                          start=(kc == 0), stop=(kc == 7)), reads=[hb.b, Wkv.b], writes=[ps.b])
                        self.copy(("act", "dve")[half], Vblk[:, tl, half * 512:(half + 1) * 512], ps[:, :], [ps.b], [Vblk.b])
                dstv = self.VS[t0:t0 + TB, :].rearrange("(a p) e -> p a e", p=128)
                P.dma("sp", lambda e: e.dma_start(out=dstv, in_=Vblk[:]), Vblk.b, reads=[Vblk.b])
            self.end_phase()

    def phase_attn(self, l, xsrc):
        P, nc, inp = self.P, self.nc, self.inp
        jl = l - 2
        TB = 512
        lam_init = 0.8 - 0.6 * math.exp(-0.3 * l)
        G1 = self.der[:, l, 0, :]
        S1 = self.modcol(l, 0)
        g1c = self.modcol(l, 2)
        with ExitStack() as es:
            self._stg = None
            Wq = self.tile(es, "Wq", [128, 8, D], BF16)
            s3 = inp["b_w_q"][jl].rearrange("(kc p) n -> p kc n", p=128)
            self.load_cast(es, [(Wq[:, kc:kc + 1, :], s3[:, kc:kc + 1, :], 128, 1, D) for kc in range(8)], Wq.b)
            xs_ = [self.tile(es, "qx%d" % i, [128, 8, TB], dma=True) for i in range(2)]
            sq = self.tile(es, "qsq", [128, 8, TB])
            self.rstd = self.tile(es, "qrstd", [128, TB])
            hbs_ = [self.tile(es, "qhb%d" % i, [128, 8, TB], BF16) for i in range(2)]
            CTt = self.tile(es, "qCT", [128, TB], dma=True)
            STt = self.tile(es, "qST", [128, TB], dma=True)
            tmpA = [self.tile(es, "qtA%d" % i, [128, TB]) for i in range(3)]
            tmpB = [self.tile(es, "qtB%d" % i, [128, TB]) for i in range(3)]
            Qblks = [self.tile(es, "Qblk%d" % i, [128, 8, TB], BF16, dma=True) for i in range(2)]
            for nb in range(T // TB):
                t0 = nb * TB
                x, hb, Qblk = xs_[nb % 2], hbs_[nb % 2], Qblks[nb % 2]
                src = xsrc.rearrange("(j p) t -> p j t", p=128)[:, :, t0:t0 + TB]
                P.dma("sp", lambda e: e.dma_start(out=x[:], in_=src), x.b, writes=[x.b])
                P.dma("act", lambda e: e.dma_start(out=CTt[:], in_=inp["ropec"][:, t0:t0 + TB]), CTt.b, writes=[CTt.b])
                P.dma("act", lambda e: e.dma_start(out=STt[:], in_=inp["ropes"][:, t0:t0 + TB]), STt.b, writes=[STt.b])
                self.rmsnorm(x, TB, sq, G1, S1, hb[:], hb.b)
                self.rope_proj(es, Wq, hb, 0, Qblk, CTt, STt, tmpA, tmpB)
                dst = self.QT.rearrange("(j p) t -> p j t", p=128)[:, :, t0:t0 + TB]
                P.dma("sp", lambda e: e.dma_start(out=dst, in_=Qblk[:]), Qblk.b, reads=[Qblk.b])
            self.end_phase()

        with ExitStack() as es:
            NH = self.cfg.get("nheads", 8)
            NQB = self.cfg.get("nqb", T // TB)
            lamv = self.tile(es, "lamv", [128, 256], dma=True)
            o_l = 7 * D + jl * 256
            P.dma("sp", lambda e: e.dma_start(out=lamv[:], in_=inp["rows"][o_l:o_l + 256].partition_broadcast(128)), lamv.b, writes=[lamv.b])
            subw = self.tile(es, "subw", [128, 128], dma=True)
            o_s = 5 * D + jl * D
            P.dma("sp", lambda e: e.dma_start(out=subw[:], in_=inp["rows"][o_s:o_s + 128].partition_broadcast(128)), subw.b, writes=[subw.b])
            P.op("dve", lambda e: e.tensor_scalar(out=subw[:], in0=subw[:], scalar1=(1.0 - lam_init), scalar2=None, op0=ALU.mult), reads=[subw.b], writes=[subw.b])
            lt = self.tile(es, "lt", [128, 2, 64])
            ls = self.tile(es, "ls", [128, 4])
            P.op("dve", lambda e: e.tensor_tensor(out=lt[:, 0, :], in0=lamv[:, 0:64], in1=lamv[:, 64:128], op=ALU.mult), reads=[lamv.b], writes=[lt.b])
            P.op("dve", lambda e: e.tensor_tensor(out=lt[:, 1, :], in0=lamv[:, 128:192], in1=lamv[:, 192:256], op=ALU.mult), reads=[lamv.b], writes=[lt.b])
            P.op("dve", lambda e: e.tensor_reduce(out=ls[:, 0:2], in_=lt[:], axis=AX.X, op=ALU.add), reads=[lt.b], writes=[ls.b])
            P.op("act", lambda e: e.activation(out=ls[:, 0:2], in_=ls[:, 0:2], func=AF.Exp), reads=[ls.b], writes=[ls.b])
            P.op("dve", lambda e: e.tensor_tensor(out=ls[:, 2:3], in0=ls[:, 1:2], in1=ls[:, 0:1], op=ALU.subtract), reads=[ls.b], writes=[ls.b])
            P.op("dve", lambda e: e.tensor_scalar(out=ls[:, 3:4], in0=ls[:, 2:3], scalar1=-lam_init, scalar2=None, op0=ALU.add), reads=[ls.b], writes=[ls.b])
            neglam = ls[:, 3:4]
            cmaskb = self.tile(es, "cmaskb", [128, 128], BF16)
            self.copy("pool", cmaskb[:], self.cst["mask4"][:, 128:256], [self.cst["mask4"].b], [cmaskb.b])
            eps1 = self.eps

            KTh = [self.tile(es, "KTh%d" % i, [128, T], BF16, dma=True) for i in range(2)]
            QTh = [self.tile(es, "QTh%d" % i, [128, T], BF16, dma=True) for i in range(2)]
            Vh = [self.tile(es, "Vh%d" % i, [128, 32, 129], BF16, dma=True) for i in range(2)]
            YTh = [self.tile(es, "YTh%d" % i, [128, T], BF16, dma=True) for i in range(2)]
            for i in range(2):
                P.op("pool", lambda e: e.memset(Vh[i][:, :, 128:129], 1.0), writes=[Vh[i].b])
            ET = [self.tile(es, "ET%d" % i, [128, 512], BF16) for i in range(4)]
            Oc = [self.tile(es, "Oc%d" % i, [128, 4, 129]) for i in range(2)]
            rz = self.tile(es, "rz", [128, 2, 4])
            y = self.tile(es, "ay", [128, 4, 128])
            ysq = self.tile(es, "aysq", [128, 4, 128])
            ss = self.tile(es, "ass", [128, 4])
            ynb = self.tile(es, "aynb", [128, 4, 128], BF16)
            if "at_o" in self.dbg:
                self.dbg_tile = self.tile(es, "dbgt", [128, 4, 128], dma=True)
            eti = 0
            for hh in range(NH):
                kt_, qt_, vh_, yt_ = KTh[hh % 2], QTh[hh % 2], Vh[hh % 2], YTh[hh % 2]
                hs = slice(hh * 128, (hh + 1) * 128)
                P.dma("sp", lambda e: e.dma_start(out=kt_[:], in_=self.KT[hs, :]), kt_.b, writes=[kt_.b])
                P.dma("act", lambda e: e.dma_start(out=qt_[:], in_=self.QT[hs, :]), qt_.b, writes=[qt_.b])
                P.dma("sp", lambda e: e.dma_start(out=vh_[:, :, 0:128], in_=self.VS.rearrange("(kt p) e -> p kt e", p=128)[:, :, hs]), vh_.b, writes=[vh_.b])
                sbanks = [self.ps[0], self.ps[1], self.ps[6]]
                tasks = []
                for qb in range(NQB):
                    for cc in range(2):
                        for kt in range(4 * qb + 4):
                            tasks.append((qb, cc, kt))

                def score(ti):
                    qb, cc, kt = tasks[ti]
                    rows = slice(64 * cc, 64 * cc + 64)
                    c0 = max(kt - 4 * qb, 0) * 128
                    pS = sbanks[ti % 3]
                    P.op("pe", lambda e: e.matmul(pS[:, c0:512], lhsT=kt_[rows, kt * 128:(kt + 1) * 128], rhs=qt_[rows, qb * 512 + c0:(qb + 1) * 512],
                                                  start=True, stop=True), reads=[kt_.b, qt_.b], writes=[pS.b])

                def combine(qb):
                    qs = slice(qb * 512, (qb + 1) * 512)
                    P.op("dve", lambda e: e.reciprocal(out=rz[:, 0, :], in_=Oc[0][:, :, 128]), reads=[Oc[0].b], writes=[rz.b])
                    P.op("dve", lambda e: e.reciprocal(out=rz[:, 1, :], in_=Oc[1][:, :, 128]), reads=[Oc[1].b], writes=[rz.b])
                    P.op("dve", lambda e: e.tensor_scalar(out=rz[:, 1, :], in0=rz[:, 1, :], scalar1=neglam, scalar2=None, op0=ALU.mult), reads=[rz.b, ls.b], writes=[rz.b])
                    P.op("pool", lambda e: e.tensor_tensor(out=y[:], in0=Oc[0][:, :, 0:128], in1=rz[:, 0, :].unsqueeze(2).to_broadcast([128, 4, 128]), op=ALU.mult),
                         reads=[Oc[0].b, rz.b], writes=[y.b])
                    P.op("pool", lambda e: e.tensor_tensor(out=ysq[:], in0=Oc[1][:, :, 0:128], in1=rz[:, 1, :].unsqueeze(2).to_broadcast([128, 4, 128]), op=ALU.mult),
                         reads=[Oc[1].b, rz.b], writes=[ysq.b])
                    P.op("dve", lambda e: e.tensor_tensor(out=y[:], in0=y[:], in1=ysq[:], op=ALU.add), reads=[y.b, ysq.b], writes=[y.b])
                    if "at_o" in self.dbg:
                        dtl = self.dbg_tile
                        self.copy("dve", dtl[:], y[:], [y.b], [dtl.b])
                        dd = self.dbg["at_o"][qb * 512:(qb + 1) * 512, hs].rearrange("(a p) e -> p a e", p=128)
                        P.dma("act", lambda e: e.dma_start(out=dd, in_=dtl[:]), dtl.b, reads=[dtl.b])
                    P.op("pool", lambda e: e.tensor_tensor(out=ysq[:], in0=y[:], in1=y[:], op=ALU.mult), reads=[y.b], writes=[ysq.b])
                    P.op("dve", lambda e: e.tensor_reduce(out=ss[:], in_=ysq[:], axis=AX.X, op=ALU.add), reads=[ysq.b], writes=[ss.b])
                    P.op("act", lambda e: e.activation(out=ss[:], in_=ss[:], func=AF.Sqrt, bias=eps1[:], scale=1.0 / 128), reads=[ss.b, eps1.b], writes=[ss.b])
                    P.op("dve", lambda e: e.reciprocal(out=ss[:], in_=ss[:]), reads=[ss.b], writes=[ss.b])
                    P.op("pool", lambda e: e.tensor_tensor(out=y[:], in0=y[:], in1=ss[:].unsqueeze(2).to_broadcast([128, 4, 128]), op=ALU.mult),
                         reads=[y.b, ss.b], writes=[y.b])
                    P.op("dve", lambda e: e.tensor_tensor(out=ynb[:], in0=y[:], in1=subw[:].unsqueeze(1).to_broadcast([128, 4, 128]), op=ALU.mult),
                         reads=[y.b, subw.b], writes=[ynb.b])
                    for qt in range(4):
                        P.op("pe", lambda e: e.transpose(out=self.psbf[:, qt * 128:(qt + 1) * 128], in_=ynb[:, qt, :], identity=self.identb[:]),
                             reads=[ynb.b, self.identb.b], writes=[self.psbf.b])
                    P.op("act", lambda e: e.copy(out=yt_[:, qs], in_=self.psbf[:, 0:512]), reads=[self.psbf.b], writes=[yt_.b])

                LOOK = 2
                for ti in range(min(LOOK, len(tasks))):
                    score(ti)
                for ti in range(len(tasks)):
                    if ti + LOOK < len(tasks):
                        score(ti + LOOK)
                    qb, cc, kt = tasks[ti]
                    r = kt - 4 * qb
                    c0 = max(r, 0) * 128
                    pS = sbanks[ti % 3]
                    pO = [self.ps[2 + 2 * cc], self.ps[3 + 2 * cc]]
                    et = ET[ti % 4]
                    P.op("act", lambda e: e.activation(out=et[:, c0:512], in_=pS[:, c0:512], func=AF.Exp, scale=0.125), reads=[pS.b], writes=[et.b])
                    if r >= 0:
                        P.op("pool", lambda e: e.tensor_tensor(out=et[:, c0:c0 + 128], in0=et[:, c0:c0 + 128], in1=cmaskb[:], op=ALU.mult),
                             reads=[et.b, cmaskb.b], writes=[et.b])
                    for qt in range(max(r, 0), 4):
                        po = pO[qt // 2]
                        oc = (qt % 2) * 129
                        P.op("pe", lambda e: e.matmul(po[:, oc:oc + 129], lhsT=et[:, qt * 128:(qt + 1) * 128], rhs=vh_[:, kt, :],
                                                      start=(kt == 0 and qt % 2 == 0), stop=(kt == 4 * qb + qt), skip_group_check=True),
                             reads=[et.b, vh_.b], writes=[po.b])
                    if kt == 4 * qb + 3:
                        for i2 in range(2):
                            self.copy(("act", "dve")[i2], Oc[cc][:, 2 * i2:2 * i2 + 2, :].rearrange("p a e -> p (a e)"), pO[i2][:, 0:258], [pO[i2].b], [Oc[cc].b])
                        if cc == 1:
                            combine(qb)
                P.dma("sp", lambda e: e.dma_start(out=self.YT[hs, :], in_=yt_[:]), yt_.b, reads=[yt_.b])
            self.end_phase()

        with ExitStack() as es:
            self._stg = None
            Wo = self.tile(es, "aWo", [128, 8, D], BF16)
            s3 = inp["b_w_o"][jl].rearrange("(kc p) n -> p kc n", p=128)
            self.load_cast(es, [(Wo[:, kc:kc + 1, :], s3[:, kc:kc + 1, :], 128, 1, D) for kc in range(8)], Wo.b)
            xs = [self.tile(es, "cx%d" % i, [128, 8, TB], dma=True) for i in range(2)]
            ys = [self.tile(es, "cy%d" % i, [128, 8, TB], BF16, dma=True) for i in range(2)]
            for nb in range(T // TB):
                t0 = nb * TB
                x = xs[nb % 2]
                yb = ys[nb % 2]
                src = xsrc.rearrange("(j p) t -> p j t", p=128)[:, :, t0:t0 + TB]
                P.dma("sp", lambda e: e.dma_start(out=x[:], in_=src), x.b, writes=[x.b])
                srcy = self.YT.rearrange("(j p) t -> p j t", p=128)[:, :, t0:t0 + TB]
                P.dma("act", lambda e: e.dma_start(out=yb[:], in_=srcy), yb.b, writes=[yb.b])
                for nj in range(8):
                    ps = self.psum_next()
                    for kc in range(8):
                        P.op("pe", lambda e: e.matmul(ps[:, :], lhsT=Wo[:, kc, nj * 128:(nj + 1) * 128], rhs=yb[:, kc, :], start=(kc == 0), stop=(kc == 7)),
                             reads=[Wo.b, yb.b], writes=[ps.b])
                    P.op("dve", lambda e: e.scalar_tensor_tensor(out=x[:, nj, :], in0=ps[:, :], scalar=g1c[:, nj:nj + 1], in1=x[:, nj, :], op0=ALU.mult, op1=ALU.add),
                         reads=[ps.b, x.b, self.modc.b], writes=[x.b])
                dst = self.X.rearrange("(j p) t -> p j t", p=128)[:, :, t0:t0 + TB]
                P.dma("sp", lambda e: e.dma_start(out=dst, in_=x[:]), x.b, reads=[x.b])
            self.end_phase()


_CONSTS = None


def prepare_inputs(inputs):
    global _CONSTS
    if _CONSTS is None:
        _CONSTS = make_consts()
    f = lambda a: np.ascontiguousarray(np.asarray(a, np.float32))
    vecs = {}
    for l in range(4):
        vecs["ada_b%d" % l] = inputs["ada_b"][l]
        vecs["norm1_%d" % l] = inputs["norm1"][l]
        vecs["norm2_%d" % l] = inputs["norm2"][l]
        for i in range(3):
            vecs["cw%d_%d" % (i, l)] = inputs["ffn_conv_w"][l][i]
        vecs["cb_%d" % l] = inputs["ffn_conv_b"][l]
    vecs["final_norm"] = inputs["final_norm"]
    vecs["kv_norm"] = inputs["kv_norm"]
    for l in range(2):
        for i in range(6):
            vecs["mu%d_%d" % (i, l)] = inputs["a_mu"][l][i]
        vecs["w0_%d" % l] = inputs["a_w0"][l]
        vecs["a0_%d" % l] = inputs["a_a0"][l]
        vecs["k_k_%d" % l] = inputs["a_k_k"][l]
        vecs["k_a_%d" % l] = inputs["a_k_a"][l]
        vecs["r_k_%d" % l] = np.asarray(inputs["a_r_k"][l]).reshape(-1)
    cols = CP.pack(vecs)
    rows = np.concatenate([
        f(inputs["a_ln_w"][0]), f(inputs["a_ln_b"][0]), f(inputs["a_ln_w"][1]), f(inputs["a_ln_b"][1]),
        f(inputs["a_v0"][0]),
        np.tile(f(inputs["b_subln"][0]), 8), np.tile(f(inputs["b_subln"][1]), 8),
        f(inputs["b_lam"][0]).reshape(-1), f(inputs["b_lam"][1]).reshape(-1)])
    shared = dict(_CONSTS)
    shared["cols"] = cols
    shared["rows"] = rows
    for k in ("ada_w", "a_w_rkv", "a_w1", "a_w2", "a_a1", "a_a2", "a_v1", "a_v2", "a_g1", "a_g2", "a_w_o",
              "w_kv", "b_w_q", "b_w_o", "ffn_w_up", "ffn_w_down"):
        shared[k] = f(inputs[k])
    x = np.asarray(inputs["x"], np.float32)
    c = np.asarray(inputs["c"], np.float32)
    in_maps = []
    for b in range(NCORES):
        m = dict(shared)
        m["xT"] = np.ascontiguousarray(x[b].T)
        m["ccol"] = np.ascontiguousarray(c[b].reshape(8, 128).T)
        in_maps.append(m)
    return in_maps


_NC_CACHE = {}


def run(inputs, cfg, key="full", ncores=NCORES):
    if key not in _NC_CACHE:
        _NC_CACHE[key] = Builder(cfg).build()
    nc = _NC_CACHE[key]
    in_maps = prepare_inputs(inputs)[:ncores]
    res = run_bass_kernel_spmd(nc, in_maps, core_ids=list(range(ncores)))
    return res


def kernel(**inputs):
    cfg = {"layers": [0, 1, 2, 3]}
    res = run(inputs, cfg)
    out = np.stack([np.ascontiguousarray(r["outT"].T) for r in res.results], axis=0)
    return out.astype(np.float32)
```

```python
import math
import numpy as np
import concourse.bass as bass
import concourse.mybir as mybir
from concourse.bass_utils import run_bass_kernel_spmd
from contextlib import ExitStack
import types

F32 = mybir.dt.float32
BF16 = mybir.dt.bfloat16
AF = mybir.ActivationFunctionType
ALU = mybir.AluOpType
AX = mybir.AxisListType

D = 1024
T = 4096
NJ = 8
DFF = 2816
F2 = 5632
NF = 44
NG = 22
C0 = math.exp(-0.5)
NCORES = 8

ENGS = ["pe", "act", "dve", "pool", "sp"]


def freeze(fn):
    if fn.__closure__ is None:
        return fn
    cells = []
    for c in fn.__closure__:
        try:
            cells.append(types.CellType(c.cell_contents))
        except ValueError:
            cells.append(c)
    return types.FunctionType(fn.__code__, fn.__globals__, fn.__name__, fn.__defaults__, tuple(cells))


class Buf:
    __slots__ = ("name", "lw", "rd", "dsem", "excl")

    def __init__(self, name):
        self.name = name
        self.lw = None
        self.rd = {}
        self.dsem = None
        self.excl = False


class Tl:
    __slots__ = ("ap", "b")

    def __init__(self, ap, b):
        self.ap = ap
        self.b = b

    def __getitem__(self, k):
        return self.ap[k]


class Prog:
    def __init__(self, nc, es, n_dma_sems=40):
        self.nc = nc
        self.es = es
        self.q = {e: [] for e in ENGS}
        self.cnt = {e: 0 for e in ENGS}
        self.sems = {}
        self.semkey = 0
        self.esem = {e: self._newsem("c_" + e) for e in ENGS}
        self.seen = {e: {} for e in ENGS}
        self.bar = self._newsem("bar")
        self.nbar = 0
        self.dma_pool = [self._newsem("d%d" % i) for i in range(n_dma_sems)]
        self.dma_cnt = {k: 0 for k in self.dma_pool}
        self.dma_free = list(self.dma_pool)
        self.ninstr = 0

    def _newsem(self, name):
        s = self.es.enter_context(self.nc.semaphore(name))
        self.semkey += 1
        self.sems[self.semkey] = s
        return self.semkey

    def buf(self, name):
        return Buf(name)

    def dma_buf(self, name):
        b = Buf(name)
        b.dsem = self.dma_free.pop(0)
        return b

    def release(self, bufs):
        for b in bufs:
            if b.dsem is not None:
                self.dma_free.append(b.dsem)
                b.dsem = None

    def _waits(self, e, reads, writes, is_dma=False):
        need = {}
        seen = self.seen[e]

        def add(ev, raw):
            key, val, src = ev
            if src == e and not is_dma and (e in ("pe", "sp") or not raw):
                return
            if seen.get(key, 0) >= val:
                return
            if need.get(key, 0) < val:
                need[key] = val

        for b in reads:
            if b.lw is not None:
                add(b.lw, True)
            if b.excl:
                for src, ev in b.rd.items():
                    if src != e:
                        add(ev, False)
        for b in writes:
            if b.lw is not None:
                add(b.lw, False)
            for ev in b.rd.values():
                add(ev, False)
        out = []
        for key, val in need.items():
            seen[key] = val
            out.append((self.sems[key], val))
        return out

    def _emit(self, e, fn, waits, sem, inc):
        fn = freeze(fn)

        attach = (inc == 1 and e in ("act", "dve", "pool") and len(waits) > 0)

        def run(eng, fn=fn, waits=waits, sem=sem, inc=inc, attach=attach):
            for (s, v) in (waits[:-1] if attach else waits):
                eng.wait_ge(s, v)
            ins = fn(eng)
            if attach:
                ins._wait_ge(waits[-1][0], waits[-1][1])
            ins.then_inc(sem, inc)
        self.q[e].append(run)
        self.ninstr += 1 + len(waits)

    def op(self, e, fn, reads=(), writes=()):
        waits = self._waits(e, reads, writes)
        self.cnt[e] += 1
        key = self.esem[e]
        self._emit(e, fn, waits, self.sems[key], 1)
        ev = (key, self.cnt[e], e)
        for b in writes:
            b.lw = ev
            b.rd = {}
        for b in reads:
            if b not in writes:
                b.rd[e] = ev

    def dma(self, e, fn, owner, reads=(), writes=()):
        assert owner.dsem is not None, owner.name
        waits = self._waits(e, reads, writes, is_dma=True)
        key = owner.dsem
        self.dma_cnt[key] += 16
        self._emit(e, fn, waits, self.sems[key], 16)
        src = "dma%d" % key
        ev = (key, self.dma_cnt[key], src)
        for b in writes:
            b.lw = ev
            b.rd = {}
        for b in reads:
            if b not in writes:
                b.rd[src] = ev

    def barrier(self):
        g = "sp"
        gw = []
        for e in ENGS:
            if e == g or self.cnt[e] == 0:
                continue
            key = self.esem[e]
            if self.seen[g].get(key, 0) < self.cnt[e]:
                gw.append((self.sems[key], self.cnt[e]))
        for key in self.dma_pool:
            val = self.dma_cnt[key]
            if val > 0 and self.seen[g].get(key, 0) < val:
                gw.append((self.sems[key], val))
        self.nbar += 1
        bsem = self.sems[self.bar]

        def run_g(eng, gw=gw, bsem=bsem):
            for (s, v) in gw:
                eng.wait_ge(s, v)
            eng.sem_inc(bsem, 1)
        self.q[g].append(run_g)
        for e in ENGS:
            if e != g:
                self.q[e].append(lambda eng, sem=bsem, val=self.nbar: eng.wait_ge(sem, val))
        for e in ENGS:
            for e2 in ENGS:
                self.seen[e][self.esem[e2]] = self.cnt[e2]
            for key in self.dma_pool:
                self.seen[e][key] = self.dma_cnt[key]
        for e in ENGS:
            if self.cnt[e] > 12000:
                self.esem[e] = self._newsem("c_%s_%d" % (e, self.nbar))
                self.cnt[e] = 0

    def finish(self):
        nc = self.nc
        self.barrier()
        with nc.Block() as block:
            @block.tensor
            def _(eng):
                for f in self.q["pe"]:
                    f(eng)

            @block.scalar
            def _(eng):
                for f in self.q["act"]:
                    f(eng)

            @block.vector
            def _(eng):
                for f in self.q["dve"]:
                    f(eng)

            @block.gpsimd
            def _(eng):
                for f in self.q["pool"]:
                    f(eng)

            @block.sync
            def _(eng):
                for f in self.q["sp"]:
                    f(eng)


class ColPack:
    def __init__(self):
        self.off = {}
        self.n = 0
        self.items = []

    def add(self, name, length):
        assert length % 128 == 0
        self.off[name] = self.n
        self.n += length // 128
        self.items.append((name, length))

    def pack(self, vecs):
        arr = np.zeros((128, self.n), np.float32)
        for name, length in self.items:
            v = np.asarray(vecs[name], np.float32).reshape(length // 128, 128)
            arr[:, self.off[name]:self.off[name] + length // 128] = v.T
        return arr


def make_colpack():
    cp = ColPack()
    for l in range(4):
        cp.add("ada_b%d" % l, 6 * D)
        cp.add("norm1_%d" % l, D)
        cp.add("norm2_%d" % l, D)
        for i in range(3):
            cp.add("cw%d_%d" % (i, l), F2)
        cp.add("cb_%d" % l, F2)
    cp.add("final_norm", D)
    cp.add("kv_norm", D)
    for l in range(2):
        for i in range(6):
            cp.add("mu%d_%d" % (i, l), D)
        for nm in ("w0", "a0", "k_k", "k_a", "r_k"):
            cp.add("%s_%d" % (nm, l), D)
    return cp


CP = make_colpack()


def make_consts():
    c = {}
    c["ident"] = np.eye(128, dtype=np.float32)
    c["ones"] = np.ones((128, 128), np.float32)
    bd = np.zeros((128, 128), np.float32)
    bd[:64, :64] = 1
    bd[64:, 64:] = 1
    c["bd"] = bd
    ind = np.zeros((128, 8, 16), np.float32)
    for p in range(128):
        for j in range(8):
            ind[p, j, 2 * j + p // 64] = 1
    c["ind"] = ind.reshape(128, 128)
    s = np.arange(128)[:, None]
    t = np.arange(128)[None, :]
    strict = (t > s).astype(np.float32)
    incl = (t >= s).astype(np.float32)
    c["mask4"] = np.concatenate([strict, incl, strict, incl], axis=1)
    c["maskL"] = (t < s).astype(np.float32)
    pos = np.arange(T, dtype=np.float64)
    inv = 500000.0 ** (-np.arange(0, 16, 2, dtype=np.float64) / 16)
    ct = np.ones((128, T), np.float64)
    st = np.zeros((128, T), np.float64)
    rm = np.zeros((128, 128), np.float32)
    for cc in range(2):
        for d in range(16):
            p = cc * 64 + d
            ang = pos * inv[d % 8]
            ct[p] = np.cos(ang)
            st[p] = np.sin(ang)
            if d < 8:
                rm[p + 8, p] = -1.0
            else:
                rm[p - 8, p] = 1.0
    c["ropec"] = ct.astype(np.float32)
    c["ropes"] = st.astype(np.float32)
    c["roper"] = rm
    return c


class Builder:
    def __init__(self, cfg):
        self.cfg = cfg
        self.nc = bass.Bass("TRN2", target_bir_lowering=False)
        self.uid = 0
        self.rr = 0

    def dram_in(self, name, shape, dt=F32):
        return self.nc.dram_tensor(name, list(shape), dt, kind="ExternalInput").ap()

    def tile(self, es, name, shape, dt=F32, dma=False):
        self.uid += 1
        t = es.enter_context(self.nc.sbuf_tensor("%s_%d" % (name, self.uid), list(shape), dt))
        b = self.P.dma_buf(name) if dma else self.P.buf(name)
        if dma:
            self.phase_dma_bufs.append(b)
        return Tl(t, b)

    def eng_rr(self, engs=("dve", "pool", "act")):
        self.rr += 1
        return engs[self.rr % len(engs)]

    def copy(self, e, out_ap, in_ap, reads, writes):
        P = self.P
        if e == "act":
            P.op("act", lambda g: g.copy(out=out_ap, in_=in_ap), reads=reads, writes=writes)
        else:
            P.op(e, lambda g: g.tensor_copy(out=out_ap, in_=in_ap), reads=reads, writes=writes)

    def psum_next(self):
        self.psi = (self.psi + 1) % 6
        return self.ps[self.psi]

    def build(self):
        nc = self.nc
        cfg = self.cfg
        inp = {}
        inp["xT"] = self.dram_in("xT", [D, T])
        inp["ccol"] = self.dram_in("ccol", [128, 8])
        inp["cols"] = self.dram_in("cols", [128, CP.n])
        for k in ("ident", "ones", "bd", "ind", "maskL", "roper"):
            inp[k] = self.dram_in(k, [128, 128])
        inp["mask4"] = self.dram_in("mask4", [128, 512])
        inp["ropec"] = self.dram_in("ropec", [128, T])
        inp["ropes"] = self.dram_in("ropes", [128, T])
        inp["ada_w"] = self.dram_in("ada_w", [4, D, 6 * D])
        inp["a_w_rkv"] = self.dram_in("a_w_rkv", [2, 3, D, D])
        inp["a_w1"] = self.dram_in("a_w1", [2, D, 64])
        inp["a_w2"] = self.dram_in("a_w2", [2, 64, D])
        inp["a_a1"] = self.dram_in("a_a1", [2, D, 64])
        inp["a_a2"] = self.dram_in("a_a2", [2, 64, D])
        inp["a_v1"] = self.dram_in("a_v1", [1, D, 32])
        inp["a_v2"] = self.dram_in("a_v2", [1, 32, D])
        inp["a_g1"] = self.dram_in("a_g1", [2, D, 160])
        inp["a_g2"] = self.dram_in("a_g2", [2, 160, D])
        inp["a_w_o"] = self.dram_in("a_w_o", [2, D, D])
        inp["rows"] = self.dram_in("rows", [7 * D + 512])
        inp["w_kv"] = self.dram_in("w_kv", [D, 2 * D])
        inp["b_w_q"] = self.dram_in("b_w_q", [2, D, D])
        inp["b_w_o"] = self.dram_in("b_w_o", [2, D, D])
        inp["ffn_w_up"] = self.dram_in("ffn_w_up", [4, D, F2])
        inp["ffn_w_down"] = self.dram_in("ffn_w_down", [4, DFF, D])
        self.inp = inp
        self.outT = nc.dram_tensor("outT", [D, T], F32, kind="ExternalOutput").ap()
        self.X = nc.dram_tensor("Xs", [D, T], F32).ap()
        self.VF = nc.dram_tensor("VFs", [T, D], F32).ap()
        self.KT = nc.dram_tensor("KTs", [D, T], BF16).ap()
        self.VS = nc.dram_tensor("VSs", [T, D], BF16).ap()
        self.QT = nc.dram_tensor("QTs", [D, T], BF16).ap()
        self.YT = nc.dram_tensor("YTs", [D, T], BF16).ap()
        self.dbg = {}
        for name, shape in cfg.get("dbg", {}).items():
            self.dbg[name] = nc.dram_tensor("dbg_" + name, list(shape), F32, kind="ExternalOutput").ap()

        with ExitStack() as es:
            self.P = P = Prog(nc, es)
            self.phase_dma_bufs = []
            self.ps = []
            for i in range(7):
                t = es.enter_context(nc.psum_tensor("psb%d" % i, [128, 512], F32))
                self.ps.append(Tl(t, P.buf("psb%d" % i)))
                self.ps[-1].b.excl = True
            t = es.enter_context(nc.psum_tensor("psbf", [128, 1024], BF16))
            self.psbf = Tl(t, P.buf("psbf"))
            self.psbf.b.excl = True
            self.psi = 0
            g = es
            self.cols = self.tile(g, "cols", [128, CP.n], dma=True)
            self.ccol = self.tile(g, "ccol", [128, 8], dma=True)
            self.cst = {}
            for k in ("ident", "ones", "bd", "ind", "maskL", "roper"):
                self.cst[k] = self.tile(g, k, [128, 128], dma=True)
            self.cst["mask4"] = self.tile(g, "mask4", [128, 512], dma=True)
            P.dma("sp", lambda e: e.dma_start(out=self.cols[:], in_=inp["cols"]), self.cols.b, writes=[self.cols.b])
            P.dma("sp", lambda e: e.dma_start(out=self.ccol[:], in_=inp["ccol"]), self.ccol.b, writes=[self.ccol.b])
            for k, tl in self.cst.items():
                P.dma("act", lambda e, tl=tl, k=k: e.dma_start(out=tl[:], in_=inp[k]), tl.b, writes=[tl.b])
            self.eps = self.tile(g, "eps", [128, 1])
            P.op("pool", lambda e: e.memset(self.eps[:], 1e-6), writes=[self.eps.b])
            self.identb = self.tile(g, "identb", [128, 128], BF16)
            self.copy("pool", self.identb[:], self.cst["ident"][:], [self.cst["ident"].b], [self.identb.b])
            self.modc = self.tile(g, "modc", [128, 192])
            self.der = self.tile(g, "der", [128, 4, 2, 8])

            self.phase_mod()
            xsrc = inp["xT"]
            for l in cfg["layers"]:
                if cfg.get("mixer", True):
                    if l < 2:
                        self.phase_rwkv(l, xsrc)
                    else:
                        if l == 2 or cfg.get("force_kv", False):
                            self.phase_kv(xsrc)
                        self.phase_attn(l, xsrc)
                    xsrc = self.X
                fuse_final = (l == cfg["layers"][-1]) and cfg.get("ffn", True) and cfg.get("final", True) and cfg.get("fuse_final", True)
                if cfg.get("ffn", True):
                    self.phase_ffn(l, xsrc, fuse_final)
                    xsrc = self.X
            if not fuse_final:
                self.phase_final(xsrc, cfg.get("final", True))
            P.finish()
        return nc

    def col(self, name, j0=0, n=None):
        o = CP.off[name] + j0
        if n is None:
            n = 1
        return self.cols[:, o:o + n]

    def end_phase(self):
        self.P.barrier()
        self.P.release(self.phase_dma_bufs)
        self.phase_dma_bufs = []

    def phase_mod(self):
        P, nc, inp = self.P, self.nc, self.inp
        with ExitStack() as es:
            cact = self.tile(es, "cact", [128, 8])
            P.op("act", lambda e: e.activation(out=cact[:], in_=self.ccol[:], func=AF.Silu),
                 reads=[self.ccol.b], writes=[cact.b])
            A = [self.tile(es, "adaA%d" % i, [128, 8, 768], dma=True) for i in range(4)]
            psm = self.ps[0]
            it = 0
            for l in range(4):
                for blk in range(8):
                    a = A[it % 4]
                    it += 1
                    src = inp["ada_w"][l].rearrange("(kc p) n -> p kc n", p=128)[:, :, blk * 768:(blk + 1) * 768]
                    P.dma("sp" if it % 2 else "act", lambda e, a=a, src=src: e.dma_start(out=a[:], in_=src), a.b, writes=[a.b])
                    for n_ in range(6):
                        colidx = l * 48 + blk * 6 + n_
                        for kc in range(8):
                            P.op("pe", lambda e, a=a, kc=kc, n_=n_, colidx=colidx: e.matmul(
                                psm[:, colidx:colidx + 1], lhsT=a[:, kc, n_ * 128:(n_ + 1) * 128], rhs=cact[:, kc:kc + 1],
                                start=(kc == 0), stop=(kc == 7)), reads=[a.b, cact.b], writes=[psm.b])
            for l in range(4):
                o = CP.off["ada_b%d" % l]
                P.op("dve", lambda e, l=l, o=o: e.tensor_tensor(out=self.modc[:, l * 48:(l + 1) * 48], in0=psm[:, l * 48:(l + 1) * 48],
                                                                in1=self.cols[:, o:o + 48], op=ALU.add),
                     reads=[psm.b, self.cols.b], writes=[self.modc.b])
            for l in range(4):
                for which in range(2):
                    sc = self.modc[:, l * 48 + which * 24 + 8: l * 48 + which * 24 + 16]
                    nm = self.col("norm%d_%d" % (which + 1, l), 0, 8)
                    P.op("dve", lambda e, l=l, which=which, sc=sc, nm=nm: e.scalar_tensor_tensor(
                        out=self.der[:, l, which, :], in0=sc, scalar=1.0, in1=nm, op0=ALU.add, op1=ALU.mult),
                        reads=[self.modc.b, self.cols.b], writes=[self.der.b])
            if "modc" in self.dbg:
                dt = self.tile(es, "dbgm", [128, 192], dma=True)
                self.copy("dve", dt[:], self.modc[:], [self.modc.b], [dt.b])
                P.dma("sp", lambda e: e.dma_start(out=self.dbg["modc"], in_=dt[:]), dt.b, reads=[dt.b])
            self.end_phase()

    def modcol(self, l, i, j0=0, n=8):
        o = l * 48 + i * 8 + j0
        return self.modc[:, o:o + n]

    def rmsnorm(self, x, N, sq, G, S, out_ap, out_b, extra_reads=()):
        P = self.P
        ones = self.cst["ones"]
        P.op("act", lambda e: e.activation(out=sq[:], in_=x[:], func=AF.Square), reads=[x.b], writes=[sq.b])
        ps = self.psum_next()
        for j in range(8):
            P.op("pe", lambda e, j=j: e.matmul(ps[:, 0:N], lhsT=ones[:], rhs=sq[:, j, :], start=(j == 0), stop=(j == 7)),
                 reads=[ones.b, sq.b], writes=[ps.b])
        rs = self.rstd
        P.op("act", lambda e: e.activation(out=rs[:, 0:N], in_=ps[:, 0:N], func=AF.Sqrt, bias=self.eps[:], scale=1.0 / D),
             reads=[ps.b, self.eps.b], writes=[rs.b])
        P.op("dve", lambda e: e.reciprocal(out=rs[:, 0:N], in_=rs[:, 0:N]), reads=[rs.b], writes=[rs.b])
        P.op("dve", lambda e: e.tensor_tensor(out=sq[:], in0=x[:], in1=rs[:, 0:N].unsqueeze(1).to_broadcast([128, 8, N]), op=ALU.mult),
             reads=[x.b, rs.b], writes=[sq.b])
        if S is None:
            P.op("pool", lambda e: e.tensor_tensor(out=out_ap, in0=sq[:], in1=G.unsqueeze(2).to_broadcast([128, 8, N]), op=ALU.mult),
                 reads=[sq.b, self.cols.b, self.der.b] + list(extra_reads), writes=[out_b])
        else:
            P.op("pool", lambda e: e.tensor_tensor(out=sq[:], in0=sq[:], in1=G.unsqueeze(2).to_broadcast([128, 8, N]), op=ALU.mult),
                 reads=[sq.b, self.cols.b, self.der.b], writes=[sq.b])
            P.op("pool", lambda e: e.tensor_tensor(out=out_ap, in0=sq[:], in1=S.unsqueeze(2).to_broadcast([128, 8, N]), op=ALU.add),
                 reads=[sq.b, self.modc.b] + list(extra_reads), writes=[out_b])

    def load_cast(self, es_stage, pieces, wb):
        P = self.P
        if not hasattr(self, "_stg") or self._stg is None:
            self._stg = [self.tile(es_stage, "stg%d" % i, [128, 1024], dma=True) for i in range(getattr(self, "_stg_n", 2))]
            self._stgi = 0
        for (dst, src, p, a, b) in pieces:
            assert a * b <= 1024
            st = self._stg[self._stgi % len(self._stg)]
            self._stgi += 1
            sv = st[0:p, 0:a * b].rearrange("p (a b) -> p a b", a=a)
            q = ("sp", "act")[self._stgi % 2]
            P.dma(q, lambda e, sv=sv, src=src: e.dma_start(out=sv, in_=src), st.b, writes=[st.b])
            self.copy(self.eng_rr(("pool", "dve", "act")), dst, sv, [st.b], [wb])

    def phase_ffn(self, l, xsrc, fuse_final=False):
        P, nc, inp = self.P, self.nc, self.inp
        TB = 256
        NB = T // TB
        with ExitStack() as es:
            self._stg = None
            self._stg_n = 6
            ses = ExitStack()
            wup = self.tile(es, "wup", [128, 8, F2], BF16)
            wdn = self.tile(es, "wdn", [128, NG, D], BF16)
            pieces = []
            srcu = inp["ffn_w_up"][l].rearrange("(kc p) n -> p kc n", p=128)
            for kc in range(8):
                for (n0, n1) in ((0, 1024), (1024, 2048), (2048, 3072), (3072, 4096), (4096, 5120), (5120, F2)):
                    pieces.append((wup[:, kc:kc + 1, n0:n1], srcu[:, kc:kc + 1, n0:n1], 128, 1, n1 - n0))
            self.load_cast(ses, pieces, wup.b)
            srcd = inp["ffn_w_down"][l].rearrange("(kc p) n -> p kc n", p=128)
            pieces = [(wdn[:, i:i + 1, :], srcd[:, i:i + 1, :], 128, 1, D) for i in range(0, NG)]
            self.load_cast(ses, pieces, wdn.b)
            P.barrier()
            ses.close()
            self._stg = None
            self._stg_n = 2

            xs = [self.tile(es, "fx%d" % i, [128, 8, TB], dma=True) for i in range(2)]
            sq = self.tile(es, "fsq", [128, 8, TB])
            self.rstd = self.tile(es, "frstd", [128, TB])
            h2s = [self.tile(es, "fh2_%d" % i, [128, 8, TB + 2], BF16) for i in range(2)]
            for i in range(2):
                P.op("pool", lambda e: e.memset(h2s[i][:], 0.0), writes=[h2s[i].b])
            NCV = 6
            cv = [self.tile(es, "fcv%d" % i, [128, TB]) for i in range(NCV)]
            sgl = [self.tile(es, "fsg%d" % i, [128, TB]) for i in range(3)]
            hm = self.tile(es, "fhm", [128, NG, TB], BF16)
            G2 = self.der[:, l, 1, :]
            S2 = self.modcol(l, 3)
            g2c = self.modcol(l, 5)
            cw = [CP.off["cw%d_%d" % (i, l)] for i in range(3)]
            cb = CP.off["cb_%d" % l]
            xr = lambda ap: ap.rearrange("(j p) t -> p j t", p=128)

            def norm(nb):
                x = xs[nb % 2]
                h2 = h2s[nb % 2]
                t0 = nb * TB
                P.dma("sp", lambda e: e.dma_start(out=x[:], in_=xr(xsrc)[:, :, t0:t0 + TB]), x.b, writes=[x.b])
                if nb > 0:
                    hp = h2s[(nb - 1) % 2]
                    P.op("pool", lambda e: e.tensor_copy(out=h2[:, :, 0:2], in_=hp[:, :, TB:TB + 2]), reads=[hp.b], writes=[h2.b])
                self.rmsnorm(x, TB, sq, G2, S2, h2[:, :, 2:TB + 2], h2.b)

            def up(nb):
                h2 = h2s[nb % 2]
                ui = 0
                for i in range(NG):
                    for which in range(2):
                        n = i + which * NG
                        ps = self.psum_next()
                        for kc in range(8):
                            P.op("pe", lambda e: e.matmul(ps[:, 0:TB + 2], lhsT=wup[:, kc, n * 128:(n + 1) * 128], rhs=h2[:, kc, :],
                                                          start=(kc == 0), stop=(kc == 7)), reads=[wup.b, h2.b], writes=[ps.b])
                        c = cv[ui % NCV]
                        ui += 1
                        P.op("act", lambda e: e.activation(out=c[:], in_=ps[:, 2:TB + 2], func=AF.Identity,
                                                           bias=self.cols[:, cb + n:cb + n + 1], scale=self.cols[:, cw[2] + n:cw[2] + n + 1]),
                             reads=[ps.b, self.cols.b], writes=[c.b])
                        if which == 1:
                            sg_ = sgl[i % 3]
                            P.op("act", lambda e: e.activation(out=sg_[:], in_=cg[:], func=AF.Silu), reads=[cg.b], writes=[sg_.b])
                        P.op("dve", lambda e: e.scalar_tensor_tensor(out=c[:], in0=ps[:, 1:TB + 1], scalar=self.cols[:, cw[1] + n:cw[1] + n + 1],
                                                                     in1=c[:], op0=ALU.mult, op1=ALU.add), reads=[ps.b, c.b, self.cols.b], writes=[c.b])
                        P.op("dve", lambda e: e.scalar_tensor_tensor(out=c[:], in0=ps[:, 0:TB], scalar=self.cols[:, cw[0] + n:cw[0] + n + 1],
                                                                     in1=c[:], op0=ALU.mult, op1=ALU.add), reads=[ps.b, c.b, self.cols.b], writes=[c.b])
                        s_ = sgl[i % 3]
                        if which == 0:
                            cg = c
                        else:
                            P.op("pool", lambda e: e.tensor_tensor(out=hm[:, i, :], in0=s_[:], in1=c[:], op=ALU.mult),
                                 reads=[s_.b, c.b], writes=[hm.b])

            def down(nb):
                x = xs[nb % 2]
                t0 = nb * TB
                for nj in range(8):
                    ps = self.psum_next()
                    for i in range(NG):
                        P.op("pe", lambda e: e.matmul(ps[:, 0:TB], lhsT=wdn[:, i, nj * 128:(nj + 1) * 128], rhs=hm[:, i, :],
                                                      start=(i == 0), stop=(i == NG - 1)), reads=[wdn.b, hm.b], writes=[ps.b])
                    P.op("dve", lambda e: e.scalar_tensor_tensor(out=x[:, nj, :], in0=ps[:, 0:TB], scalar=g2c[:, nj:nj + 1],
                                                                 in1=x[:, nj, :], op0=ALU.mult, op1=ALU.add),
                         reads=[ps.b, x.b, self.modc.b], writes=[x.b])
                if fuse_final:
                    self.rmsnorm(x, TB, sq, self.col("final_norm", 0, 8), None, x[:], x.b)
                    P.dma("sp", lambda e: e.dma_start(out=xr(self.outT)[:, :, t0:t0 + TB], in_=x[:]), x.b, reads=[x.b])
                else:
                    P.dma("sp", lambda e: e.dma_start(out=xr(self.X)[:, :, t0:t0 + TB], in_=x[:]), x.b, reads=[x.b])

            norm(0)
            for nb in range(NB):
                up(nb)
                if nb + 1 < NB:
                    norm(nb + 1)
                down(nb)
            self.end_phase()

    def phase_final(self, xsrc, do_norm):
        P = self.P
        TB = 512
        with ExitStack() as es:
            xs = [self.tile(es, "nx%d" % i, [128, 8, TB], dma=True) for i in range(2)]
            sq = self.tile(es, "nsq", [128, 8, TB])
            self.rstd = self.tile(es, "nrstd", [128, TB])
            G = self.col("final_norm", 0, 8)
            for nb in range(T // TB):
                x = xs[nb % 2]
                t0 = nb * TB
                src = xsrc.rearrange("(j p) t -> p j t", p=128)[:, :, t0:t0 + TB]
                P.dma("sp", lambda e, x=x, src=src: e.dma_start(out=x[:], in_=src), x.b, writes=[x.b])
                if do_norm:
                    self.rmsnorm(x, TB, sq, G, None, x[:], x.b)
                dst = self.outT.rearrange("(j p) t -> p j t", p=128)[:, :, t0:t0 + TB]
                P.dma("act", lambda e, x=x, dst=dst: e.dma_start(out=dst, in_=x[:]), x.b, reads=[x.b])
            self.end_phase()

    def phase_rwkv(self, l, xsrc):
        P, nc, inp = self.P, self.nc, self.inp
        CH = 128
        NCH = self.cfg.get("nch", T // CH)
        ident = self.cst["ident"]
        with ExitStack() as es:
            self._stg = None
            reuse_stage = self.cfg.get("stage_reuse", True)
            self._stg_n = 6 if reuse_stage else 2
            ses = ExitStack() if reuse_stage else es
            Wr = self.tile(es, "Wr", [128, 8, D], BF16)
            Wk = self.tile(es, "Wk", [128, 8, D], BF16)
            Wv = self.tile(es, "Wv", [128, 8, D], BF16)
            Wo = self.tile(es, "Wo", [128, 8, D], BF16)
            w1b = self.tile(es, "w1b", [128, 8, 64], BF16)
            a1b = self.tile(es, "a1b", [128, 8, 64], BF16)
            g1b = self.tile(es, "g1b", [128, 8, 160], BF16)
            w2b = self.tile(es, "w2b", [64, 1, D], BF16)
            a2b = self.tile(es, "a2b", [64, 1, D], BF16)
            g2b = self.tile(es, "g2b", [128, 2, D], BF16)
            if l == 1:
                v1b = self.tile(es, "v1b", [128, 8, 32], BF16)
                v2b = self.tile(es, "v2b", [32, 1, D], BF16)
                v0r = self.tile(es, "v0r", [128, D], dma=True)
            lnw = self.tile(es, "lnw", [128, D], dma=True)
            lnb = self.tile(es, "lnb", [128, D], dma=True)
            omka = self.tile(es, "omka", [128, 8])
            eps2 = self.tile(es, "eps2", [128, 1])
            onesT = self.tile(es, "onesT", [128, 128])
            for W, src in ((Wr, inp["a_w_rkv"][l, 0]), (Wk, inp["a_w_rkv"][l, 1]), (Wv, inp["a_w_rkv"][l, 2]), (Wo, inp["a_w_o"][l])):
                s3 = src.rearrange("(kc p) n -> p kc n", p=128)
                self.load_cast(ses, [(W[:, kc:kc + 1, :], s3[:, kc:kc + 1, :], 128, 1, D) for kc in range(8)], W.b)
            self.load_cast(ses, [(w1b[:], inp["a_w1"][l].rearrange("(kc p) n -> p kc n", p=128), 128, 8, 64)], w1b.b)
            self.load_cast(ses, [(a1b[:], inp["a_a1"][l].rearrange("(kc p) n -> p kc n", p=128), 128, 8, 64)], a1b.b)
            sg1 = inp["a_g1"][l].rearrange("(kc p) n -> p kc n", p=128)
            self.load_cast(ses, [(g1b[:, 0:4, :], sg1[:, 0:4, :], 128, 4, 160), (g1b[:, 4:8, :], sg1[:, 4:8, :], 128, 4, 160)], g1b.b)
            self.load_cast(ses, [(w2b[:], inp["a_w2"][l].rearrange("(o p) n -> p o n", o=1), 64, 1, D)], w2b.b)
            self.load_cast(ses, [(a2b[:], inp["a_a2"][l].rearrange("(o p) n -> p o n", o=1), 64, 1, D)], a2b.b)
            self.load_cast(ses, [(g2b[:, 0:1, :], inp["a_g2"][l][0:128, :].rearrange("(o p) n -> p o n", o=1), 128, 1, D),
                                (g2b[0:32, 1:2, :], inp["a_g2"][l][128:160, :].rearrange("(o p) n -> p o n", o=1), 32, 1, D)], g2b.b)
            if l == 1:
                self.load_cast(ses, [(v1b[:], inp["a_v1"][0].rearrange("(kc p) n -> p kc n", p=128), 128, 8, 32)], v1b.b)
                self.load_cast(ses, [(v2b[:], inp["a_v2"][0].rearrange("(o p) n -> p o n", o=1), 32, 1, D)], v2b.b)
                P.dma("sp", lambda e: e.dma_start(out=v0r[:], in_=inp["rows"][4 * D:5 * D].partition_broadcast(128)), v0r.b, writes=[v0r.b])
            P.dma("sp", lambda e: e.dma_start(out=lnw[:], in_=inp["rows"][(2 * l) * D:(2 * l + 1) * D].partition_broadcast(128)), lnw.b, writes=[lnw.b])
            P.dma("sp", lambda e: e.dma_start(out=lnb[:], in_=inp["rows"][(2 * l + 1) * D:(2 * l + 2) * D].partition_broadcast(128)), lnb.b, writes=[lnb.b])
            P.op("dve", lambda e: e.tensor_scalar(out=omka[:], in0=self.col("k_a_%d" % l, 0, 8), scalar1=-1.0, scalar2=1.0, op0=ALU.mult, op1=ALU.add),
                 reads=[self.cols.b], writes=[omka.b])
            P.op("pool", lambda e: e.memset(eps2[:], 64e-5), writes=[eps2.b])
            P.op("pool", lambda e: e.memset(onesT[:], 1.0), writes=[onesT.b])

            if reuse_stage:
                P.barrier()
                ses.close()
                self._stg = None
            self._stg_n = 2
            Hbd = self.tile(es, "Hbd", [128, 8, 128])
            Hb = self.tile(es, "Hb", [128, 8, 128], BF16)
            if self.cfg.get("t_hb", True):
                P.op("pool", lambda e: e.memset(Hb[:], 0.0), writes=[Hb.b])
            Vb = self.tile(es, "Vb", [128, D], BF16)
            P.op("pool", lambda e: e.memset(Hbd[:], 0.0), writes=[Hbd.b])
            h = self.tile(es, "h", [128, 8, 129])
            P.op("pool", lambda e: e.memset(h[:], 0.0), writes=[h.b])
            x = self.tile(es, "rx", [128, 8, CH], dma=True)
            self.rstd = self.tile(es, "rrstd", [128, CH])
            xx = self.tile(es, "xx", [128, 8, CH])
            tmpm = [self.tile(es, "tmpm%d" % i, [128, 8, CH], dma=(i == 0)) for i in range(2)]
            sq = tmpm[1] if self.cfg.get("t_sq", True) else self.tile(es, "rsq", [128, 8, CH])
            xm = [self.tile(es, "xm%d" % i, [128, 8, CH], BF16) for i in range(6)]
            lwt = self.tile(es, "lwt", [64, CH], BF16)
            lat = self.tile(es, "lat", [64, CH], BF16)
            lgt = self.tile(es, "lgt", [128, 2, CH], BF16)
            V = self.tile(es, "V", [128, D], dma=True)
            Yt = self.tile(es, "Yt", [128, D], dma=True)
            if l == 1:
                lvt = self.tile(es, "lvt", [32, CH], BF16)
                VFt = Tl(tmpm[0][:].rearrange("p a t -> p (a t)"), tmpm[0].b)
                sgv = Tl(tmpm[1][:].rearrange("p a t -> p (a t)")[:, 0:512], tmpm[1].b)
            ogb = self.tile(es, "ogb", [128, D], BF16)
            ogT = self.tile(es, "ogT", [128, 8, CH], BF16)
            st = self.tile(es, "stat", [128, 6, 16])
            rkb = self.tile(es, "rkb", [128, 16])

            NSLOT = self.cfg.get("nslot%d" % l, 4)

            def mkslot(si):
                S = {}

                def pt(name, shape=(128, 128), dt=F32):
                    S[name] = self.tile(es, "s%d_%s" % (si, name), list(shape), dt)
                for nm in ("r", "k", "sg", "a", "kk", "x", "L", "eL"):
                    pt(nm)
                for nm in ("rh", "bts", "kts", "WT", "U"):
                    pt(nm, (128, 128), BF16)
                pt("AR", (128, 2, 128), BF16)
                pt("F3", (128, 3, 128), BF16)
                pt("T3", (128, 3, 128), BF16)
                for i in range(2):
                    pt("SC%d" % i, (128, 4, 128), BF16)
                    pt("Ym%d" % i, (128, 128), BF16)
                    pt("ZY%d_0" % i, (128, 2, 128), BF16)
                    pt("ZY%d_1" % i, (128, 2, 128), BF16)
                    pt("Tt%d" % i, (128, 128), BF16)
                    pt("AV%d" % i, (128, 64), BF16)
                return S
            slots = [mkslot(i) for i in range(NSLOT)]
            mask4 = self.cst["mask4"]; maskL = self.cst["maskL"]; bd = self.cst["bd"]; ind = self.cst["ind"]
            G1 = self.der[:, l, 0, :]
            S1 = self.modcol(l, 0)
            g1c = self.modcol(l, 2)
            psB = self.ps[6]

            def c_(name, j):
                return self.col("%s_%d" % (name, l), j, 1)

            for c in range(NCH):
                t0 = c * CH
                src = xsrc.rearrange("(j p) t -> p j t", p=128)[:, :, t0:t0 + CH]
                P.dma("sp", lambda e, src=src: e.dma_start(out=x[:], in_=src), x.b, writes=[x.b])
                self.rmsnorm(x, CH, sq, G1, S1, h[:, :, 1:CH + 1], h.b)
                P.op("pool", lambda e: e.tensor_tensor(out=xx[:], in0=h[:, :, 0:CH], in1=h[:, :, 1:CH + 1], op=ALU.subtract),
                     reads=[h.b], writes=[xx.b])
                for i in range(6):
                    tm = tmpm[i % 2]
                    mu = self.col("mu%d_%d" % (i, l), 0, 8)
                    P.op("dve", lambda e, tm=tm, mu=mu: e.tensor_tensor(out=tm[:], in0=xx[:], in1=mu.unsqueeze(2).to_broadcast([128, 8, CH]), op=ALU.mult),
                         reads=[xx.b, self.cols.b], writes=[tm.b])
                    P.op("pool", lambda e, tm=tm, i=i: e.tensor_tensor(out=xm[i][:], in0=tm[:], in1=h[:, :, 1:CH + 1], op=ALU.add),
                         reads=[tm.b, h.b], writes=[xm[i].b])
                P.op("pool", lambda e: e.tensor_copy(out=h[:, :, 0:1], in_=h[:, :, CH:CH + 1]), reads=[h.b], writes=[h.b])
                if l == 1:
                    P.dma("sp", lambda e, t0=t0: e.dma_start(out=VFt[:], in_=self.VF[t0:t0 + CH, :]), VFt.b, writes=[VFt.b])

                psl = self.psum_next()
                for kc in range(8):
                    P.op("pe", lambda e, kc=kc: e.matmul(psl[0:64, 0:128], lhsT=w1b[:, kc, :], rhs=xm[1][:, kc, :], start=(kc == 0), stop=(kc == 7)),
                         reads=[w1b.b, xm[1].b], writes=[psl.b])
                for kc in range(8):
                    P.op("pe", lambda e, kc=kc: e.matmul(psl[0:64, 128:256], lhsT=a1b[:, kc, :], rhs=xm[4][:, kc, :], start=(kc == 0), stop=(kc == 7)),
                         reads=[a1b.b, xm[4].b], writes=[psl.b])
                for kc in range(8):
                    P.op("pe", lambda e, kc=kc: e.matmul(psl[:, 256:384], lhsT=g1b[:, kc, 0:128], rhs=xm[5][:, kc, :], start=(kc == 0), stop=(kc == 7)),
                         reads=[g1b.b, xm[5].b], writes=[psl.b])
                for kc in range(8):
                    P.op("pe", lambda e, kc=kc: e.matmul(psl[0:32, 384:512], lhsT=g1b[:, kc, 128:160], rhs=xm[5][:, kc, :], start=(kc == 0), stop=(kc == 7)),
                         reads=[g1b.b, xm[5].b], writes=[psl.b])
                P.op("act", lambda e: e.activation(out=lwt[:], in_=psl[0:64, 0:128], func=AF.Tanh), reads=[psl.b], writes=[lwt.b])
                P.op("act", lambda e: e.copy(out=lat[:], in_=psl[0:64, 128:256]), reads=[psl.b], writes=[lat.b])
                P.op("act", lambda e: e.activation(out=lgt[:, 0, :], in_=psl[:, 256:384], func=AF.Sigmoid), reads=[psl.b], writes=[lgt.b])
                P.op("act", lambda e: e.activation(out=lgt[0:32, 1, :], in_=psl[0:32, 384:512], func=AF.Sigmoid), reads=[psl.b], writes=[lgt.b])
                if l == 1:
                    psv = self.psum_next()
                    for kc in range(8):
                        P.op("pe", lambda e, kc=kc: e.matmul(psv[0:32, 0:128], lhsT=v1b[:, kc, :], rhs=xm[3][:, kc, :], start=(kc == 0), stop=(kc == 7)),
                             reads=[v1b.b, xm[3].b], writes=[psv.b])
                    P.op("act", lambda e: e.copy(out=lvt[:], in_=psv[0:32, 0:128]), reads=[psv.b], writes=[lvt.b])
                for half in range(2):
                    ps = self.psum_next()
                    for kc in range(8):
                        P.op("pe", lambda e, ps=ps, kc=kc, half=half: e.matmul(ps[:, :], lhsT=xm[3][:, kc, :], rhs=Wv[:, kc, half * 512:(half + 1) * 512],
                                                                              start=(kc == 0), stop=(kc == 7)), reads=[xm[3].b, Wv.b], writes=[ps.b])
                    P.op("act", lambda e, ps=ps, half=half: e.copy(out=V[:, half * 512:(half + 1) * 512], in_=ps[:, :]), reads=[ps.b], writes=[V.b])
                if l == 1:
                    for half in range(2):
                        ps = self.psum_next()
                        hs = slice(half * 512, (half + 1) * 512)
                        P.op("pe", lambda e, ps=ps, hs=hs: e.matmul(ps[:, :], lhsT=lvt[:], rhs=v2b[0:32, 0, hs], start=True, stop=True),
                             reads=[lvt.b, v2b.b], writes=[ps.b])
                        P.op("dve", lambda e, ps=ps, hs=hs: e.tensor_tensor(out=sgv[:], in0=ps[:, :], in1=v0r[:, hs], op=ALU.add),
                             reads=[ps.b, v0r.b], writes=[sgv.b])
                        P.op("act", lambda e: e.activation(out=sgv[:], in_=sgv[:], func=AF.Sigmoid), reads=[sgv.b], writes=[sgv.b])
                        P.op("pool", lambda e, hs=hs: e.tensor_tensor(out=VFt[:, hs], in0=VFt[:, hs], in1=V[:, hs], op=ALU.subtract),
                             reads=[VFt.b, V.b], writes=[VFt.b])
                        P.op("pool", lambda e, hs=hs: e.tensor_tensor(out=VFt[:, hs], in0=VFt[:, hs], in1=sgv[:], op=ALU.mult),
                             reads=[VFt.b, sgv.b], writes=[VFt.b])
                        P.op("pool", lambda e, hs=hs: e.tensor_tensor(out=V[:, hs], in0=V[:, hs], in1=VFt[:, hs], op=ALU.add),
                             reads=[VFt.b, V.b], writes=[V.b])
                else:
                    P.dma("act", lambda e, t0=t0: e.dma_start(out=self.VF[t0:t0 + CH, :], in_=V[:]), V.b, reads=[V.b])
                if "rw_v" in self.dbg:
                    P.dma("act", lambda e, t0=t0: e.dma_start(out=self.dbg["rw_v"][t0:t0 + CH, :], in_=V[:]), V.b, reads=[V.b])

                if self.cfg.get("t_vb", True):
                    P.op("pool", lambda e: e.tensor_copy(out=Vb[:], in_=V[:]), reads=[V.b], writes=[Vb.b])

                def pair_gen(j, S):
                    CUT = self.cfg.get("rw_cut", 99)
                    EB = self.cfg.get("eng_b", "dve")

                    def smul(eng, out_ap, in_ap, sc_ap, rd, wr):
                        if eng == "act":
                            P.op("act", lambda e: e.activation(out=out_ap, in_=in_ap, func=AF.Copy, scale=sc_ap), reads=rd, writes=wr)
                        else:
                            P.op(eng, lambda e: e.tensor_scalar(out=out_ap, in0=in_ap, scalar1=sc_ap, scalar2=None, op0=ALU.mult), reads=rd, writes=wr)
                    js = slice(j * 128, (j + 1) * 128)
                    r_, k_, sg, a_, kk, tx, L_, eL = S["r"], S["k"], S["sg"], S["a"], S["kk"], S["x"], S["L"], S["eL"]
                    rh, bts, kts, WT, U_, AR, F3, T3 = S["rh"], S["bts"], S["kts"], S["WT"], S["U"], S["AR"], S["F3"], S["T3"]
                    SC = [S["SC0"], S["SC1"]]; Ym = [S["Ym0"], S["Ym1"]]; Tt = [S["Tt0"], S["Tt1"]]; AV = [S["AV0"], S["AV1"]]
                    ZY = [[S["ZY0_0"], S["ZY0_1"]], [S["ZY1_0"], S["ZY1_1"]]]
                    psA = self.psum_next()
                    for kc in range(8):
                        P.op("pe", lambda e: e.matmul(psA[:, 0:128], lhsT=Wr[:, kc, js], rhs=xm[0][:, kc, :], start=(kc == 0), stop=(kc == 7)),
                             reads=[Wr.b, xm[0].b], writes=[psA.b])
                    for kc in range(8):
                        P.op("pe", lambda e: e.matmul(psA[:, 128:256], lhsT=Wk[:, kc, js], rhs=xm[2][:, kc, :], start=(kc == 0), stop=(kc == 7)),
                             reads=[Wk.b, xm[2].b], writes=[psA.b])
                    P.op("pe", lambda e: e.matmul(psA[:, 256:384], lhsT=w2b[:, 0, js], rhs=lwt[:], start=True, stop=True), reads=[w2b.b, lwt.b], writes=[psA.b])
                    P.op("pe", lambda e: e.matmul(psA[:, 384:512], lhsT=a2b[:, 0, js], rhs=lat[:], start=True, stop=True), reads=[a2b.b, lat.b], writes=[psA.b])
                    P.op("act", lambda e: e.copy(out=r_[:], in_=psA[:, 0:128]), reads=[psA.b], writes=[r_.b])
                    P.op("act", lambda e: e.copy(out=k_[:], in_=psA[:, 128:256]), reads=[psA.b], writes=[k_.b])
                    P.op("act", lambda e: e.activation(out=sg[:], in_=psA[:, 256:384], func=AF.Sigmoid, bias=c_("w0", j), scale=1.0),
                         reads=[psA.b, self.cols.b], writes=[sg.b])
                    P.op("act", lambda e: e.activation(out=a_[:], in_=psA[:, 384:512], func=AF.Sigmoid, bias=c_("a0", j), scale=1.0),
                         reads=[psA.b, self.cols.b], writes=[a_.b])
                    yield
                    if CUT <= 1:
                        return
                    P.op("dve", lambda e: e.tensor_scalar(out=kk[:], in0=k_[:], scalar1=c_("k_k", j), scalar2=None, op0=ALU.mult),
                         reads=[k_.b, self.cols.b], writes=[kk.b])
                    P.op("pool", lambda e: e.tensor_tensor(out=tx[:], in0=kk[:], in1=kk[:], op=ALU.mult), reads=[kk.b], writes=[tx.b])
                    ps = self.psum_next()
                    P.op("pe", lambda e: e.matmul(ps[:, 0:128], lhsT=bd[:], rhs=tx[:], start=True, stop=True), reads=[bd.b, tx.b], writes=[ps.b])
                    P.op("act", lambda e: e.activation(out=tx[:], in_=ps[:, 0:128], func=AF.Sqrt), reads=[ps.b], writes=[tx.b])
                    yield
                    if CUT <= 2:
                        return
                    P.op("dve", lambda e: e.tensor_scalar(out=tx[:], in0=tx[:], scalar1=1e-12, scalar2=None, op0=ALU.max), reads=[tx.b], writes=[tx.b])
                    P.op("dve", lambda e: e.reciprocal(out=tx[:], in_=tx[:]), reads=[tx.b], writes=[tx.b])
                    P.op("pool", lambda e: e.tensor_tensor(out=kk[:], in0=kk[:], in1=tx[:], op=ALU.mult), reads=[kk.b, tx.b], writes=[kk.b])
                    P.op("dve", lambda e: e.tensor_scalar(out=tx[:], in0=a_[:], scalar1=c_("k_a", j), scalar2=omka[:, j:j + 1], op0=ALU.mult, op1=ALU.add),
                         reads=[a_.b, self.cols.b, omka.b], writes=[tx.b])
                    P.op("pool", lambda e: e.tensor_tensor(out=k_[:], in0=k_[:], in1=tx[:], op=ALU.mult), reads=[k_.b, tx.b], writes=[k_.b])
                    P.op("pool", lambda e: e.tensor_tensor(out=a_[:], in0=kk[:], in1=a_[:], op=ALU.mult), reads=[kk.b, a_.b], writes=[a_.b])
                    P.op("dve", lambda e: e.scalar_tensor_tensor(out=tx[:], in0=r_[:], scalar=c_("r_k", j), in1=k_[:], op0=ALU.mult, op1=ALU.mult),
                         reads=[r_.b, k_.b, self.cols.b], writes=[tx.b])
                    P.op("pe", lambda e: e.matmul(psB[:, 0:16], lhsT=tx[:], rhs=ind[:, j * 16:(j + 1) * 16], start=(j == 0), stop=(j == 7)),
                         reads=[tx.b, ind.b], writes=[psB.b])
                    yield
                    if CUT <= 3:
                        return
                    P.op("dve", lambda e: e.tensor_tensor_scan(out=L_[:], data0=onesT[:], data1=sg[:], initial=0.0, op0=ALU.mult, op1=ALU.add),
                         reads=[onesT.b, sg.b], writes=[L_.b])
                    P.op("pool", lambda e: e.tensor_tensor(out=sg[:], in0=L_[:], in1=sg[:], op=ALU.subtract), reads=[L_.b, sg.b], writes=[sg.b])
                    P.op("act", lambda e: e.activation(out=eL[:], in_=L_[:], func=AF.Exp, scale=-C0), reads=[L_.b], writes=[eL.b])
                    P.op("act", lambda e: e.activation(out=sg[:], in_=sg[:], func=AF.Exp, scale=-C0), reads=[sg.b], writes=[sg.b])
                    P.op("act", lambda e: e.activation(out=L_[:], in_=L_[:], func=AF.Exp, scale=C0), reads=[L_.b], writes=[L_.b])
                    yield
                    if CUT <= 4:
                        return
                    enL = L_
                    eE = sg
                    P.op("pool", lambda e: e.tensor_tensor(out=r_[:], in0=r_[:], in1=eL[:], op=ALU.mult), reads=[r_.b, eL.b], writes=[r_.b])
                    P.op("pool", lambda e: e.tensor_copy(out=rh[:], in_=r_[:]), reads=[r_.b], writes=[rh.b])
                    P.op("dve", lambda e: e.scalar_tensor_tensor(out=tx[:], in0=kk[:], scalar=-1.0, in1=eE[:], op0=ALU.mult, op1=ALU.mult),
                         reads=[kk.b, eE.b], writes=[tx.b])
                    P.op("pool", lambda e: e.tensor_copy(out=F3[:, 0, :], in_=tx[:]), reads=[tx.b], writes=[F3.b])
                    P.op("dve", lambda e: e.tensor_scalar(out=AR[:, 0, :], in0=tx[:], scalar1=enL[:, 63:64], scalar2=None, op0=ALU.mult),
                         reads=[tx.b, enL.b], writes=[AR.b])
                    P.op("dve", lambda e: e.tensor_scalar(out=AR[:, 1, :], in0=r_[:], scalar1=enL[:, 63:64], scalar2=None, op0=ALU.mult),
                         reads=[r_.b, enL.b], writes=[AR.b])
                    P.op("pool", lambda e: e.tensor_tensor(out=a_[:], in0=a_[:], in1=enL[:], op=ALU.mult), reads=[a_.b, enL.b], writes=[a_.b])
                    P.op("pool", lambda e: e.tensor_tensor(out=k_[:], in0=k_[:], in1=enL[:], op=ALU.mult), reads=[k_.b, enL.b], writes=[k_.b])
                    yield
                    if CUT <= 5:
                        return
                    smul(EB, bts[:], a_[:], eL[:, 63:64], [a_.b, eL.b], [bts.b])
                    smul(EB, kts[:], k_[:], eL[:, 63:64], [k_.b, eL.b], [kts.b])
                    smul(EB, F3[:, 1, :], a_[:], eL[:, 127:128], [a_.b, eL.b], [F3.b])
                    smul(EB, F3[:, 2, :], k_[:], eL[:, 127:128], [k_.b, eL.b], [F3.b])
                    yield
                    if CUT <= 6:
                        return
                    for q in range(3):
                        P.op("pe", lambda e: e.transpose(out=self.psbf[:, q * 128:(q + 1) * 128], in_=F3[:, q, :], identity=self.identb[:]),
                             reads=[F3.b, self.identb.b], writes=[self.psbf.b])
                    P.op("act", lambda e: e.copy(out=T3[:].rearrange("p a t -> p (a t)"), in_=self.psbf[:, 0:384]), reads=[self.psbf.b], writes=[T3.b])
                    yield
                    if CUT <= 7:
                        return
                    for par in range(2):
                        rows = slice(64 * par, 64 * par + 64)
                        pss = self.psum_next()
                        arv = AR[rows, :, :].rearrange("p a t -> p (a t)")
                        P.op("pe", lambda e: e.matmul(pss[:, 0:256], lhsT=bts[rows, :], rhs=arv, start=True, stop=True),
                             reads=[bts.b, AR.b], writes=[pss.b])
                        P.op("pe", lambda e: e.matmul(pss[:, 256:512], lhsT=kts[rows, :], rhs=arv, start=True, stop=True),
                             reads=[kts.b, AR.b], writes=[pss.b])
                        P.op("dve", lambda e: e.tensor_tensor(out=SC[par][:].rearrange("p a t -> p (a t)"), in0=pss[:, :], in1=mask4[:], op=ALU.mult),
                             reads=[pss.b, mask4.b], writes=[SC[par].b])
                        ps3 = self.psum_next()
                        P.op("pe", lambda e: e.matmul(ps3[:, 0:128], lhsT=AR[rows, 0, :], rhs=bts[rows, :], start=True, stop=True),
                             reads=[bts.b, AR.b], writes=[ps3.b])
                        P.op("dve", lambda e: e.tensor_tensor(out=Ym[par][:], in0=ps3[:, 0:128], in1=maskL[:], op=ALU.mult),
                             reads=[ps3.b, maskL.b], writes=[Ym[par].b])
                        P.op("pool", lambda e: e.tensor_tensor(out=Tt[par][:], in0=SC[par][:, 0, :], in1=self.identb[:], op=ALU.add),
                             reads=[SC[par].b, self.identb.b], writes=[Tt[par].b])
                        yield
                    cur = [(SC[0][:, 0, :], SC[0].b, Ym[0][:], Ym[0].b), (SC[1][:, 0, :], SC[1].b, Ym[1][:], Ym[1].b)]
                    for step in range(1, 7):
                        for par in range(2):
                            Zap, Zb, Yap, Yb = cur[par]
                            zy = ZY[par][step % 2]
                            psz = self.psum_next()
                            if step < 6:
                                P.op("pe", lambda e: e.matmul(psz[:, 0:128], lhsT=Yap, rhs=Zap, start=True, stop=True), reads=[Zb, Yb], writes=[psz.b])
                            P.op("pe", lambda e: e.matmul(psz[:, 128:256], lhsT=Zap, rhs=Yap, start=True, stop=True), reads=[Zb, Yb], writes=[psz.b])
                            ev = "act"
                            if step < 6:
                                self.copy(ev, zy[:].rearrange("p a t -> p (a t)"), psz[:, 0:256], [psz.b], [zy.b])
                            else:
                                self.copy(ev, zy[:, 1, :], psz[:, 128:256], [psz.b], [zy.b])
                            cur[par] = (zy[:, 0, :], zy.b, zy[:, 1, :], zy.b)
                            yield
                            tt = Tt[par]
                            psp = self.psum_next()
                            P.op("pe", lambda e: e.matmul(psp[:, 0:128], lhsT=zy[:, 1, :], rhs=tt[:], start=True, stop=True),
                                 reads=[zy.b, tt.b], writes=[psp.b])
                            P.op("dve", lambda e: e.tensor_tensor(out=tt[:], in0=psp[:, 0:128], in1=tt[:], op=ALU.add),
                                 reads=[psp.b, tt.b], writes=[tt.b])
                            yield
                    for par in range(2):
                        rows = slice(64 * par, 64 * par + 64)
                        hc = slice((2 * j + par) * 64, (2 * j + par + 1) * 64)
                        psw = self.psum_next()
                        P.op("pe", lambda e: e.matmul(psw[:, 0:128], lhsT=T3[:, 0, :], rhs=Tt[par][:], start=True, stop=True),
                             reads=[T3.b, Tt[par].b], writes=[psw.b])
                        P.op("pe", lambda e: e.matmul(psw[:, 128:192], lhsT=SC[par][:, 2, :], rhs=Vb[:, hc], start=True, stop=True),
                             reads=[SC[par].b, Vb.b], writes=[psw.b])
                        P.op("act", lambda e: e.copy(out=WT[rows, :], in_=psw[rows, 0:128]), reads=[psw.b], writes=[WT.b])
                        P.op("act", lambda e: e.copy(out=AV[par][:], in_=psw[:, 128:192]), reads=[psw.b], writes=[AV[par].b])
                        yield
                    psu = self.psum_next()
                    P.op("pe", lambda e: e.matmul(psu[:, 0:128], lhsT=WT[:], rhs=Hb[:, j, :], start=True, stop=True),
                         reads=[WT.b, Hb.b], writes=[psu.b])
                    for par in range(2):
                        cs = slice(64 * par, 64 * par + 64)
                        P.op("pe", lambda e: e.matmul(psu[:, cs], lhsT=Tt[par][:], rhs=AV[par][:], start=False, stop=(par == 1), skip_group_check=True),
                             reads=[Tt[par].b, AV[par].b], writes=[psu.b])
                    P.op("act", lambda e: e.copy(out=U_[:], in_=psu[:, 0:128]), reads=[psu.b], writes=[U_.b])
                    yield
                    if CUT <= 8:
                        return
                    psy = self.psum_next()
                    P.op("pe", lambda e: e.matmul(psy[:, 0:128], lhsT=rh[:], rhs=Hb[:, j, :], start=True, stop=True),
                         reads=[rh.b, Hb.b], writes=[psy.b])
                    for par in range(2):
                        hc = slice((2 * j + par) * 64, (2 * j + par + 1) * 64)
                        cs = slice(64 * par, 64 * par + 64)
                        P.op("pe", lambda e: e.matmul(psy[:, cs], lhsT=SC[par][:, 1, :], rhs=U_[:, cs], start=False, stop=False, skip_group_check=True),
                             reads=[SC[par].b, U_.b], writes=[psy.b])
                        P.op("pe", lambda e: e.matmul(psy[:, cs], lhsT=SC[par][:, 3, :], rhs=Vb[:, hc], start=False, stop=(par == 1), skip_group_check=True),
                             reads=[SC[par].b, Vb.b], writes=[psy.b])
                    P.op("act", lambda e: e.copy(out=Yt[:, js], in_=psy[:, 0:128]), reads=[psy.b], writes=[Yt.b])
                    psh = self.psum_next()
                    P.op("pe", lambda e: e.matmul(psh[:, 0:128], lhsT=T3[:, 2, :], rhs=Vb[:, js], start=True, stop=False),
                         reads=[T3.b, Vb.b], writes=[psh.b])
                    P.op("pe", lambda e: e.matmul(psh[:, 0:128], lhsT=T3[:, 1, :], rhs=U_[:], start=False, stop=True),
                         reads=[T3.b, U_.b], writes=[psh.b])
                    for par in range(2):
                        rows = slice(64 * par, 64 * par + 64)
                        P.op("dve", lambda e: e.scalar_tensor_tensor(out=Hbd[rows, j, rows], in0=Hbd[rows, j, rows], scalar=eL[rows, 127:128],
                                                                     in1=psh[rows, rows], op0=ALU.mult, op1=ALU.add),
                             reads=[Hbd.b, eL.b, psh.b], writes=[Hbd.b])
                        P.op("pool", lambda e: e.tensor_copy(out=Hb[rows, j, rows], in_=Hbd[rows, j, rows]), reads=[Hbd.b], writes=[Hb.b])
                    yield
                    if CUT <= 9:
                        return

                STAG = self.cfg.get("stagger", 0)
                pending = list(range(8))
                free_slots = list(range(NSLOT))
                active = []
                tick = 0
                next_start = 0
                while pending or active:
                    if pending and free_slots and tick >= next_start:
                        jn = pending.pop(0)
                        sl = free_slots.pop(0)
                        active.append((pair_gen(jn, slots[sl]), sl))
                        next_start = tick + STAG
                    for item in list(active):
                        try:
                            next(item[0])
                        except StopIteration:
                            active.remove(item)
                            free_slots.append(item[1])
                    tick += 1

                if "rw_o" in self.dbg:
                    P.dma("act", lambda e, t0=t0: e.dma_start(out=self.dbg["rw_o"][t0:t0 + CH, :], in_=Yt[:]), Yt.b, reads=[Yt.b])
                Y3 = Yt[:].rearrange("p (h n) -> p h n", n=64)
                S1t = tmpm[0][:].rearrange("p a t -> p (a t)")
                S1b = tmpm[0].b
                S2t = tmpm[1][:].rearrange("p a t -> p (a t)")
                S2b = tmpm[1].b
                P.op("act", lambda e: e.copy(out=rkb[:], in_=psB[:, 0:16]), reads=[psB.b], writes=[rkb.b])
                P.op("dve", lambda e: e.tensor_reduce(out=st[:, 0, :], in_=Y3, axis=AX.X, op=ALU.add), reads=[Yt.b], writes=[st.b])
                P.op("pool", lambda e: e.tensor_tensor(out=S1t, in0=Yt[:], in1=Yt[:], op=ALU.mult), reads=[Yt.b], writes=[S1b])
                P.op("dve", lambda e: e.tensor_reduce(out=st[:, 1, :], in_=S1t.rearrange("p (h n) -> p h n", n=64), axis=AX.X, op=ALU.add),
                     reads=[S1b], writes=[st.b])
                P.op("dve", lambda e: e.tensor_scalar(out=st[:, 2, :], in0=st[:, 0, :], scalar1=1.0 / 64, scalar2=None, op0=ALU.mult), reads=[st.b], writes=[st.b])
                P.op("dve", lambda e: e.tensor_tensor(out=st[:, 3, :], in0=st[:, 2, :], in1=st[:, 2, :], op=ALU.mult), reads=[st.b], writes=[st.b])
                P.op("dve", lambda e: e.scalar_tensor_tensor(out=st[:, 4, :], in0=st[:, 1, :], scalar=1.0 / 64, in1=st[:, 3, :], op0=ALU.mult, op1=ALU.subtract),
                     reads=[st.b], writes=[st.b])
                P.op("act", lambda e: e.activation(out=st[:, 5, :], in_=st[:, 4, :], func=AF.Sqrt, bias=eps2[:], scale=1.0), reads=[st.b, eps2.b], writes=[st.b])
                P.op("dve", lambda e: e.reciprocal(out=st[:, 5, :], in_=st[:, 5, :]), reads=[st.b], writes=[st.b])
                P.op("pool", lambda e: e.tensor_tensor(out=S1t.rearrange("p (h n) -> p h n", n=64), in0=Y3, in1=st[:, 2, :].unsqueeze(2).to_broadcast([128, 16, 64]), op=ALU.subtract),
                     reads=[Yt.b, st.b], writes=[S1b])
                P.op("dve", lambda e: e.tensor_tensor(out=S1t.rearrange("p (h n) -> p h n", n=64), in0=S1t.rearrange("p (h n) -> p h n", n=64),
                                                      in1=st[:, 5, :].unsqueeze(2).to_broadcast([128, 16, 64]), op=ALU.mult),
                     reads=[S1b, st.b], writes=[S1b])
                P.op("pool", lambda e: e.tensor_tensor(out=S1t, in0=S1t, in1=lnw[:], op=ALU.mult), reads=[S1b, lnw.b], writes=[S1b])
                P.op("dve", lambda e: e.tensor_tensor(out=S1t, in0=S1t, in1=lnb[:], op=ALU.add), reads=[S1b, lnb.b], writes=[S1b])
                P.op("pool", lambda e: e.tensor_tensor(out=S2t.rearrange("p (h n) -> p h n", n=64), in0=V[:].rearrange("p (h n) -> p h n", n=64),
                                                       in1=rkb[:].unsqueeze(2).to_broadcast([128, 16, 64]), op=ALU.mult),
                     reads=[V.b, rkb.b], writes=[S2b])
                P.op("dve", lambda e: e.tensor_tensor(out=S1t, in0=S1t, in1=S2t, op=ALU.add), reads=[S1b, S2b], writes=[S1b])
                for half in range(2):
                    ps = self.psum_next()
                    hs = slice(half * 512, (half + 1) * 512)
                    P.op("pe", lambda e: e.matmul(ps[:, :], lhsT=lgt[:, 0, :], rhs=g2b[:, 0, hs], start=True, stop=False),
                         reads=[lgt.b, g2b.b], writes=[ps.b])
                    P.op("pe", lambda e: e.matmul(ps[:, :], lhsT=lgt[0:32, 1, :], rhs=g2b[0:32, 1, hs], start=False, stop=True),
                         reads=[lgt.b, g2b.b], writes=[ps.b])
                    P.op("dve", lambda e: e.tensor_tensor(out=ogb[:, hs], in0=S1t[:, hs], in1=ps[:, :], op=ALU.mult), reads=[S1b, ps.b], writes=[ogb.b])
                for jj in range(8):
                    P.op("pe", lambda e, jj=jj: e.transpose(out=self.psbf[:, jj * 128:(jj + 1) * 128], in_=ogb[:, jj * 128:(jj + 1) * 128], identity=self.identb[:]),
                         reads=[ogb.b, self.identb.b], writes=[self.psbf.b])
                P.op("act", lambda e: e.copy(out=ogT[:].rearrange("p a t -> p (a t)"), in_=self.psbf[:, :]), reads=[self.psbf.b], writes=[ogT.b])
                for half in range(2):
                    ps = self.psum_next()
                    for jj in range(4):
                        nj = half * 4 + jj
                        for kc in range(8):
                            P.op("pe", lambda e, ps=ps, jj=jj, nj=nj, kc=kc: e.matmul(ps[:, jj * 128:(jj + 1) * 128], lhsT=Wo[:, kc, nj * 128:(nj + 1) * 128], rhs=ogT[:, kc, :],
                                                                                    start=(kc == 0), stop=(kc == 7)), reads=[Wo.b, ogT.b], writes=[ps.b])
                    for jj in range(4):
                        nj = half * 4 + jj
                        P.op("dve", lambda e, ps=ps, jj=jj, nj=nj: e.scalar_tensor_tensor(out=x[:, nj, :], in0=ps[:, jj * 128:(jj + 1) * 128], scalar=g1c[:, nj:nj + 1],
                                                                                         in1=x[:, nj, :], op0=ALU.mult, op1=ALU.add),
                             reads=[ps.b, x.b, self.modc.b], writes=[x.b])
                dst = self.X.rearrange("(j p) t -> p j t", p=128)[:, :, t0:t0 + CH]
                P.dma("sp", lambda e, dst=dst: e.dma_start(out=dst, in_=x[:]), x.b, reads=[x.b])
            self.end_phase()

    def rope_proj(self, es, W, hb, ncol0, dst_blk, CTt, STt, tmpA, tmpB):
        P = self.P
        roper = self.cst["roper"]
        tmpAs, tmpBs = tmpA, tmpB
        for hh in range(8):
            tmpA = tmpAs[hh % len(tmpAs)]
            tmpB = tmpBs[hh % len(tmpBs)]
            ps = self.psum_next()
            for kc in range(8):
                P.op("pe", lambda e: e.matmul(ps[:, :], lhsT=W[:, kc, ncol0 + hh * 128:ncol0 + (hh + 1) * 128], rhs=hb[:, kc, :],
                                              start=(kc == 0), stop=(kc == 7)), reads=[W.b, hb.b], writes=[ps.b])
            P.op("act", lambda e: e.copy(out=tmpA[:], in_=ps[:, :]), reads=[ps.b], writes=[tmpA.b])
            ps2 = self.psum_next()
            P.op("pe", lambda e: e.matmul(ps2[:, :], lhsT=roper[:], rhs=tmpA[:], start=True, stop=True), reads=[roper.b, tmpA.b], writes=[ps2.b])
            P.op("dve", lambda e: e.tensor_tensor(out=tmpB[:], in0=ps2[:, :], in1=STt[:], op=ALU.mult), reads=[ps2.b, STt.b], writes=[tmpB.b])
            P.op("pool", lambda e: e.tensor_tensor(out=tmpA[:], in0=tmpA[:], in1=CTt[:], op=ALU.mult), reads=[tmpA.b, CTt.b], writes=[tmpA.b])
            P.op("pool", lambda e: e.tensor_tensor(out=dst_blk[:, hh, :], in0=tmpA[:], in1=tmpB[:], op=ALU.add), reads=[tmpA.b, tmpB.b], writes=[dst_blk.b])

    def phase_kv(self, xsrc):
        P, nc, inp = self.P, self.nc, self.inp
        TB = 512
        with ExitStack() as es:
            self._stg = None
            Wkv = self.tile(es, "Wkv", [128, 8, 2 * D], BF16)
            s3 = inp["w_kv"].rearrange("(kc p) n -> p kc n", p=128)
            pieces = []
            for kc in range(8):
                for hf in range(2):
                    pieces.append((Wkv[:, kc:kc + 1, hf * D:(hf + 1) * D], s3[:, kc:kc + 1, hf * D:(hf + 1) * D], 128, 1, D))
            self.load_cast(es, pieces, Wkv.b)
            xs_ = [self.tile(es, "kx%d" % i, [128, 8, TB], dma=True) for i in range(2)]
            sq = self.tile(es, "ksq", [128, 8, TB])
            self.rstd = self.tile(es, "krstd", [128, TB])
            hbs_ = [self.tile(es, "khb%d" % i, [128, 8, TB], BF16) for i in range(2)]
            CTt = self.tile(es, "kCT", [128, TB], dma=True)
            STt = self.tile(es, "kST", [128, TB], dma=True)
            tmpA = [self.tile(es, "ktA%d" % i, [128, TB]) for i in range(3)]
            tmpB = [self.tile(es, "ktB%d" % i, [128, TB]) for i in range(3)]
            Kblks = [self.tile(es, "Kblk%d" % i, [128, 8, TB], BF16, dma=True) for i in range(2)]
            Vblks = [self.tile(es, "Vblk%d" % i, [128, 4, D], BF16, dma=True) for i in range(2)]
            G = self.col("kv_norm", 0, 8)
            for nb in range(T // TB):
                t0 = nb * TB
                x, hb, Kblk, Vblk = xs_[nb % 2], hbs_[nb % 2], Kblks[nb % 2], Vblks[nb % 2]
                src = xsrc.rearrange("(j p) t -> p j t", p=128)[:, :, t0:t0 + TB]
                P.dma("sp", lambda e: e.dma_start(out=x[:], in_=src), x.b, writes=[x.b])
                P.dma("act", lambda e: e.dma_start(out=CTt[:], in_=inp["ropec"][:, t0:t0 + TB]), CTt.b, writes=[CTt.b])
                P.dma("act", lambda e: e.dma_start(out=STt[:], in_=inp["ropes"][:, t0:t0 + TB]), STt.b, writes=[STt.b])
                self.rmsnorm(x, TB, sq, G, None, hb[:], hb.b)
                self.rope_proj(es, Wkv, hb, 0, Kblk, CTt, STt, tmpA, tmpB)
                dst = self.KT.rearrange("(j p) t -> p j t", p=128)[:, :, t0:t0 + TB]
                P.dma("sp", lambda e: e.dma_start(out=dst, in_=Kblk[:]), Kblk.b, reads=[Kblk.b])
                for tl in range(4):
                    for half in range(2):
                        ps = self.psum_next()
                        for kc in range(8):
                            P.op("pe", lambda e: e.matmul(ps[:, :], lhsT=hb[:, kc, tl * 128:(tl + 1) * 128], rhs=Wkv[:, kc, D + half * 512:D + (half + 1) * 512],
                                                          start=(kc == 0), stop=(kc == 7)), reads=[hb.b, Wkv.b], writes=[ps.b])
                        self.copy(("act", "dve")[half], Vblk[:, tl, half * 512:(half + 1) * 512], ps[:, :], [ps.b], [Vblk.b])
                dstv = self.VS[t0:t0 + TB, :].rearrange("(a p) e -> p a e", p=128)
                P.dma("sp", lambda e: e.dma_start(out=dstv, in_=Vblk[:]), Vblk.b, reads=[Vblk.b])
            self.end_phase()

    def phase_attn(self, l, xsrc):
        P, nc, inp = self.P, self.nc, self.inp
        jl = l - 2
        TB = 512
        lam_init = 0.8 - 0.6 * math.exp(-0.3 * l)
        G1 = self.der[:, l, 0, :]
        S1 = self.modcol(l, 0)
        g1c = self.modcol(l, 2)
        with ExitStack() as es:
            self._stg = None
            Wq = self.tile(es, "Wq", [128, 8, D], BF16)
            s3 = inp["b_w_q"][jl].rearrange("(kc p) n -> p kc n", p=128)
            self.load_cast(es, [(Wq[:, kc:kc + 1, :], s3[:, kc:kc + 1, :], 128, 1, D) for kc in range(8)], Wq.b)
            xs_ = [self.tile(es, "qx%d" % i, [128, 8, TB], dma=True) for i in range(2)]
            sq = self.tile(es, "qsq", [128, 8, TB])
            self.rstd = self.tile(es, "qrstd", [128, TB])
            hbs_ = [self.tile(es, "qhb%d" % i, [128, 8, TB], BF16) for i in range(2)]
            CTt = self.tile(es, "qCT", [128, TB], dma=True)
            STt = self.tile(es, "qST", [128, TB], dma=True)
            tmpA = [self.tile(es, "qtA%d" % i, [128, TB]) for i in range(3)]
            tmpB = [self.tile(es, "qtB%d" % i, [128, TB]) for i in range(3)]
            Qblks = [self.tile(es, "Qblk%d" % i, [128, 8, TB], BF16, dma=True) for i in range(2)]
            for nb in range(T // TB):
                t0 = nb * TB
                x, hb, Qblk = xs_[nb % 2], hbs_[nb % 2], Qblks[nb % 2]
                src = xsrc.rearrange("(j p) t -> p j t", p=128)[:, :, t0:t0 + TB]
                P.dma("sp", lambda e: e.dma_start(out=x[:], in_=src), x.b, writes=[x.b])
                P.dma("act", lambda e: e.dma_start(out=CTt[:], in_=inp["ropec"][:, t0:t0 + TB]), CTt.b, writes=[CTt.b])
                P.dma("act", lambda e: e.dma_start(out=STt[:], in_=inp["ropes"][:, t0:t0 + TB]), STt.b, writes=[STt.b])
                self.rmsnorm(x, TB, sq, G1, S1, hb[:], hb.b)
                self.rope_proj(es, Wq, hb, 0, Qblk, CTt, STt, tmpA, tmpB)
                dst = self.QT.rearrange("(j p) t -> p j t", p=128)[:, :, t0:t0 + TB]
                P.dma("sp", lambda e: e.dma_start(out=dst, in_=Qblk[:]), Qblk.b, reads=[Qblk.b])
            self.end_phase()

        with ExitStack() as es:
            NH = self.cfg.get("nheads", 8)
            NQB = self.cfg.get("nqb", T // TB)
            lamv = self.tile(es, "lamv", [128, 256], dma=True)
            o_l = 7 * D + jl * 256
            P.dma("sp", lambda e: e.dma_start(out=lamv[:], in_=inp["rows"][o_l:o_l + 256].partition_broadcast(128)), lamv.b, writes=[lamv.b])
            subw = self.tile(es, "subw", [128, 128], dma=True)
            o_s = 5 * D + jl * D
            P.dma("sp", lambda e: e.dma_start(out=subw[:], in_=inp["rows"][o_s:o_s + 128].partition_broadcast(128)), subw.b, writes=[subw.b])
            P.op("dve", lambda e: e.tensor_scalar(out=subw[:], in0=subw[:], scalar1=(1.0 - lam_init), scalar2=None, op0=ALU.mult), reads=[subw.b], writes=[subw.b])
            lt = self.tile(es, "lt", [128, 2, 64])
            ls = self.tile(es, "ls", [128, 4])
            P.op("dve", lambda e: e.tensor_tensor(out=lt[:, 0, :], in0=lamv[:, 0:64], in1=lamv[:, 64:128], op=ALU.mult), reads=[lamv.b], writes=[lt.b])
            P.op("dve", lambda e: e.tensor_tensor(out=lt[:, 1, :], in0=lamv[:, 128:192], in1=lamv[:, 192:256], op=ALU.mult), reads=[lamv.b], writes=[lt.b])
            P.op("dve", lambda e: e.tensor_reduce(out=ls[:, 0:2], in_=lt[:], axis=AX.X, op=ALU.add), reads=[lt.b], writes=[ls.b])
            P.op("act", lambda e: e.activation(out=ls[:, 0:2], in_=ls[:, 0:2], func=AF.Exp), reads=[ls.b], writes=[ls.b])
            P.op("dve", lambda e: e.tensor_tensor(out=ls[:, 2:3], in0=ls[:, 1:2], in1=ls[:, 0:1], op=ALU.subtract), reads=[ls.b], writes=[ls.b])
            P.op("dve", lambda e: e.tensor_scalar(out=ls[:, 3:4], in0=ls[:, 2:3], scalar1=-lam_init, scalar2=None, op0=ALU.add), reads=[ls.b], writes=[ls.b])
            neglam = ls[:, 3:4]
            cmaskb = self.tile(es, "cmaskb", [128, 128], BF16)
            self.copy("pool", cmaskb[:], self.cst["mask4"][:, 128:256], [self.cst["mask4"].b], [cmaskb.b])
            eps1 = self.eps

            KTh = [self.tile(es, "KTh%d" % i, [128, T], BF16, dma=True) for i in range(2)]
            QTh = [self.tile(es, "QTh%d" % i, [128, T], BF16, dma=True) for i in range(2)]
            Vh = [self.tile(es, "Vh%d" % i, [128, 32, 129], BF16, dma=True) for i in range(2)]
            YTh = [self.tile(es, "YTh%d" % i, [128, T], BF16, dma=True) for i in range(2)]
            for i in range(2):
                P.op("pool", lambda e: e.memset(Vh[i][:, :, 128:129], 1.0), writes=[Vh[i].b])
            ET = [self.tile(es, "ET%d" % i, [128, 512], BF16) for i in range(4)]
            Oc = [self.tile(es, "Oc%d" % i, [128, 4, 129]) for i in range(2)]
            rz = self.tile(es, "rz", [128, 2, 4])
            y = self.tile(es, "ay", [128, 4, 128])
            ysq = self.tile(es, "aysq", [128, 4, 128])
            ss = self.tile(es, "ass", [128, 4])
            ynb = self.tile(es, "aynb", [128, 4, 128], BF16)
            if "at_o" in self.dbg:
                self.dbg_tile = self.tile(es, "dbgt", [128, 4, 128], dma=True)
            eti = 0
            for hh in range(NH):
                kt_, qt_, vh_, yt_ = KTh[hh % 2], QTh[hh % 2], Vh[hh % 2], YTh[hh % 2]
                hs = slice(hh * 128, (hh + 1) * 128)
                P.dma("sp", lambda e: e.dma_start(out=kt_[:], in_=self.KT[hs, :]), kt_.b, writes=[kt_.b])
                P.dma("act", lambda e: e.dma_start(out=qt_[:], in_=self.QT[hs, :]), qt_.b, writes=[qt_.b])
                P.dma("sp", lambda e: e.dma_start(out=vh_[:, :, 0:128], in_=self.VS.rearrange("(kt p) e -> p kt e", p=128)[:, :, hs]), vh_.b, writes=[vh_.b])
                sbanks = [self.ps[0], self.ps[1], self.ps[6]]
                tasks = []
                for qb in range(NQB):
                    for cc in range(2):
                        for kt in range(4 * qb + 4):
                            tasks.append((qb, cc, kt))

                def score(ti):
                    qb, cc, kt = tasks[ti]
                    rows = slice(64 * cc, 64 * cc + 64)
                    c0 = max(kt - 4 * qb, 0) * 128
                    pS = sbanks[ti % 3]
                    P.op("pe", lambda e: e.matmul(pS[:, c0:512], lhsT=kt_[rows, kt * 128:(kt + 1) * 128], rhs=qt_[rows, qb * 512 + c0:(qb + 1) * 512],
                                                  start=True, stop=True), reads=[kt_.b, qt_.b], writes=[pS.b])

                def combine(qb):
                    qs = slice(qb * 512, (qb + 1) * 512)
                    P.op("dve", lambda e: e.reciprocal(out=rz[:, 0, :], in_=Oc[0][:, :, 128]), reads=[Oc[0].b], writes=[rz.b])
                    P.op("dve", lambda e: e.reciprocal(out=rz[:, 1, :], in_=Oc[1][:, :, 128]), reads=[Oc[1].b], writes=[rz.b])
                    P.op("dve", lambda e: e.tensor_scalar(out=rz[:, 1, :], in0=rz[:, 1, :], scalar1=neglam, scalar2=None, op0=ALU.mult), reads=[rz.b, ls.b], writes=[rz.b])
                    P.op("pool", lambda e: e.tensor_tensor(out=y[:], in0=Oc[0][:, :, 0:128], in1=rz[:, 0, :].unsqueeze(2).to_broadcast([128, 4, 128]), op=ALU.mult),
                         reads=[Oc[0].b, rz.b], writes=[y.b])
                    P.op("pool", lambda e: e.tensor_tensor(out=ysq[:], in0=Oc[1][:, :, 0:128], in1=rz[:, 1, :].unsqueeze(2).to_broadcast([128, 4, 128]), op=ALU.mult),
                         reads=[Oc[1].b, rz.b], writes=[ysq.b])
                    P.op("dve", lambda e: e.tensor_tensor(out=y[:], in0=y[:], in1=ysq[:], op=ALU.add), reads=[y.b, ysq.b], writes=[y.b])
                    if "at_o" in self.dbg:
                        dtl = self.dbg_tile
                        self.copy("dve", dtl[:], y[:], [y.b], [dtl.b])
                        dd = self.dbg["at_o"][qb * 512:(qb + 1) * 512, hs].rearrange("(a p) e -> p a e", p=128)
                        P.dma("act", lambda e: e.dma_start(out=dd, in_=dtl[:]), dtl.b, reads=[dtl.b])
                    P.op("pool", lambda e: e.tensor_tensor(out=ysq[:], in0=y[:], in1=y[:], op=ALU.mult), reads=[y.b], writes=[ysq.b])
                    P.op("dve", lambda e: e.tensor_reduce(out=ss[:], in_=ysq[:], axis=AX.X, op=ALU.add), reads=[ysq.b], writes=[ss.b])

                def combine_b(qb):
                    qs = slice(qb * 512, (qb + 1) * 512)
                    P.op("act", lambda e: e.activation(out=ss[:], in_=ss[:], func=AF.Sqrt, bias=eps1[:], scale=1.0 / 128), reads=[ss.b, eps1.b], writes=[ss.b])
                    P.op("dve", lambda e: e.reciprocal(out=ss[:], in_=ss[:]), reads=[ss.b], writes=[ss.b])
                    P.op("pool", lambda e: e.tensor_tensor(out=y[:], in0=y[:], in1=ss[:].unsqueeze(2).to_broadcast([128, 4, 128]), op=ALU.mult),
                         reads=[y.b, ss.b], writes=[y.b])
                    P.op("dve", lambda e: e.tensor_tensor(out=ynb[:], in0=y[:], in1=subw[:].unsqueeze(1).to_broadcast([128, 4, 128]), op=ALU.mult),
                         reads=[y.b, subw.b], writes=[ynb.b])
                    for qt in range(4):
                        P.op("pe", lambda e: e.transpose(out=self.psbf[:, qt * 128:(qt + 1) * 128], in_=ynb[:, qt, :], identity=self.identb[:]),
                             reads=[ynb.b, self.identb.b], writes=[self.psbf.b])
                    P.op("act", lambda e: e.copy(out=yt_[:, qs], in_=self.psbf[:, 0:512]), reads=[self.psbf.b], writes=[yt_.b])

                LOOK = 2
                for ti in range(min(LOOK, len(tasks))):
                    score(ti)
                pend_b = None
                for ti in range(len(tasks)):
                    if ti + LOOK < len(tasks):
                        score(ti + LOOK)
                    if pend_b is not None and ti >= pend_b[1]:
                        combine_b(pend_b[0])
                        pend_b = None
                    qb, cc, kt = tasks[ti]
                    r = kt - 4 * qb
                    c0 = max(r, 0) * 128
                    pS = sbanks[ti % 3]
                    pO = [self.ps[2 + 2 * cc], self.ps[3 + 2 * cc]]
                    et = ET[ti % 4]
                    P.op("act", lambda e: e.activation(out=et[:, c0:512], in_=pS[:, c0:512], func=AF.Exp, scale=0.125), reads=[pS.b], writes=[et.b])
                    if r >= 0:
                        P.op("pool", lambda e: e.tensor_tensor(out=et[:, c0:c0 + 128], in0=et[:, c0:c0 + 128], in1=cmaskb[:], op=ALU.mult),
                             reads=[et.b, cmaskb.b], writes=[et.b])
                    for qt in range(max(r, 0), 4):
                        po = pO[qt // 2]
                        oc = (qt % 2) * 129
                        P.op("pe", lambda e: e.matmul(po[:, oc:oc + 129], lhsT=et[:, qt * 128:(qt + 1) * 128], rhs=vh_[:, kt, :],
                                                      start=(kt == 0 and qt % 2 == 0), stop=(kt == 4 * qb + qt), skip_group_check=True),
                             reads=[et.b, vh_.b], writes=[po.b])
                    if kt == 4 * qb + 3:
                        for i2 in range(2):
                            self.copy(("act", "dve")[i2], Oc[cc][:, 2 * i2:2 * i2 + 2, :].rearrange("p a e -> p (a e)"), pO[i2][:, 0:258], [pO[i2].b], [Oc[cc].b])
                        if cc == 1:
                            combine(qb)
                            pend_b = (qb, ti + 2)
                if pend_b is not None:
                    combine_b(pend_b[0])
                    pend_b = None
                P.dma("sp", lambda e: e.dma_start(out=self.YT[hs, :], in_=yt_[:]), yt_.b, reads=[yt_.b])
            self.end_phase()

        with ExitStack() as es:
            self._stg = None
            Wo = self.tile(es, "aWo", [128, 8, D], BF16)
            s3 = inp["b_w_o"][jl].rearrange("(kc p) n -> p kc n", p=128)
            self.load_cast(es, [(Wo[:, kc:kc + 1, :], s3[:, kc:kc + 1, :], 128, 1, D) for kc in range(8)], Wo.b)
            xs = [self.tile(es, "cx%d" % i, [128, 8, TB], dma=True) for i in range(2)]
            ys = [self.tile(es, "cy%d" % i, [128, 8, TB], BF16, dma=True) for i in range(2)]
            for nb in range(T // TB):
                t0 = nb * TB
                x = xs[nb % 2]
                yb = ys[nb % 2]
                src = xsrc.rearrange("(j p) t -> p j t", p=128)[:, :, t0:t0 + TB]
                P.dma("sp", lambda e: e.dma_start(out=x[:], in_=src), x.b, writes=[x.b])
                srcy = self.YT.rearrange("(j p) t -> p j t", p=128)[:, :, t0:t0 + TB]
                P.dma("act", lambda e: e.dma_start(out=yb[:], in_=srcy), yb.b, writes=[yb.b])
                for nj in range(8):
                    ps = self.psum_next()
                    for kc in range(8):
                        P.op("pe", lambda e: e.matmul(ps[:, :], lhsT=Wo[:, kc, nj * 128:(nj + 1) * 128], rhs=yb[:, kc, :], start=(kc == 0), stop=(kc == 7)),
                             reads=[Wo.b, yb.b], writes=[ps.b])
                    P.op("dve", lambda e: e.scalar_tensor_tensor(out=x[:, nj, :], in0=ps[:, :], scalar=g1c[:, nj:nj + 1], in1=x[:, nj, :], op0=ALU.mult, op1=ALU.add),
                         reads=[ps.b, x.b, self.modc.b], writes=[x.b])
                dst = self.X.rearrange("(j p) t -> p j t", p=128)[:, :, t0:t0 + TB]
                P.dma("sp", lambda e: e.dma_start(out=dst, in_=x[:]), x.b, reads=[x.b])
            self.end_phase()


_CONSTS = None


def prepare_inputs(inputs):
    global _CONSTS
    if _CONSTS is None:
        _CONSTS = make_consts()
    f = lambda a: np.ascontiguousarray(np.asarray(a, np.float32))
    vecs = {}
    for l in range(4):
        vecs["ada_b%d" % l] = inputs["ada_b"][l]
        vecs["norm1_%d" % l] = inputs["norm1"][l]
        vecs["norm2_%d" % l] = inputs["norm2"][l]
        for i in range(3):
            vecs["cw%d_%d" % (i, l)] = inputs["ffn_conv_w"][l][i]
        vecs["cb_%d" % l] = inputs["ffn_conv_b"][l]
    vecs["final_norm"] = inputs["final_norm"]
    vecs["kv_norm"] = inputs["kv_norm"]
    for l in range(2):
        for i in range(6):
            vecs["mu%d_%d" % (i, l)] = inputs["a_mu"][l][i]
        vecs["w0_%d" % l] = inputs["a_w0"][l]
        vecs["a0_%d" % l] = inputs["a_a0"][l]
        vecs["k_k_%d" % l] = inputs["a_k_k"][l]
        vecs["k_a_%d" % l] = inputs["a_k_a"][l]
        vecs["r_k_%d" % l] = np.asarray(inputs["a_r_k"][l]).reshape(-1)
    cols = CP.pack(vecs)
    rows = np.concatenate([
        f(inputs["a_ln_w"][0]), f(inputs["a_ln_b"][0]), f(inputs["a_ln_w"][1]), f(inputs["a_ln_b"][1]),
        f(inputs["a_v0"][0]),
        np.tile(f(inputs["b_subln"][0]), 8), np.tile(f(inputs["b_subln"][1]), 8),
        f(inputs["b_lam"][0]).reshape(-1), f(inputs["b_lam"][1]).reshape(-1)])
    shared = dict(_CONSTS)
    shared["cols"] = cols
    shared["rows"] = rows
    for k in ("ada_w", "a_w_rkv", "a_w1", "a_w2", "a_a1", "a_a2", "a_v1", "a_v2", "a_g1", "a_g2", "a_w_o",
              "w_kv", "b_w_q", "b_w_o", "ffn_w_up", "ffn_w_down"):
        shared[k] = f(inputs[k])
    x = np.asarray(inputs["x"], np.float32)
    c = np.asarray(inputs["c"], np.float32)
    in_maps = []
    for b in range(NCORES):
        m = dict(shared)
        m["xT"] = np.ascontiguousarray(x[b].T)
        m["ccol"] = np.ascontiguousarray(c[b].reshape(8, 128).T)
        in_maps.append(m)
    return in_maps


_NC_CACHE = {}


def run(inputs, cfg, key="full", ncores=NCORES):
    if key not in _NC_CACHE:
        _NC_CACHE[key] = Builder(cfg).build()
    nc = _NC_CACHE[key]
    in_maps = prepare_inputs(inputs)[:ncores]
    res = run_bass_kernel_spmd(nc, in_maps, core_ids=list(range(ncores)))
    return res


def kernel(**inputs):
    cfg = {"layers": [0, 1, 2, 3]}
    res = run(inputs, cfg)
    out = np.stack([np.ascontiguousarray(r["outT"].T) for r in res.results], axis=0)
    return out.astype(np.float32)
```
